# Optimizing a Trainium2 kernel written in Bass

```python
import jax, jax.numpy as jnp
from jax import lax
import numpy as np

D_MODEL = 1024
BATCH = 2
SEQ = 8192
DEPTH = 2
DEC_BATCH = 16
DEC_SEQ = 2048
PAST_LEN = 128

N_MIXERS = 2
CONV_WIDTH = 3
MLSTM_HEADS = 8
QK_DIM = D_MODEL // (2 * MLSTM_HEADS)
V_DIM = D_MODEL // MLSTM_HEADS
QK_W = MLSTM_HEADS * QK_DIM
MLSTM_PROJ = 2 * QK_W + 2 * D_MODEL + 4 * MLSTM_HEADS
CHUNK = 128
D_FF = -(-8 * D_MODEL // (3 * 256)) * 256
EPS = 1e-6

kernel_name = "hybrid_conv_mlstm_bidir_encoder"


def rms_norm(x, g):
    xf = x.astype(jnp.float32)
    y = xf * lax.rsqrt(jnp.mean(xf * xf, axis=-1, keepdims=True) + EPS)
    return (y * g.astype(jnp.float32)).astype(x.dtype)


def short_conv_mixer(x, w_in, conv_w, w_out):
    S = x.shape[1]
    b, c, v = jnp.split(x @ w_in, 3, axis=-1)
    u = c * v
    pad = CONV_WIDTH // 2
    up = jnp.pad(u, ((0, 0), (pad, pad), (0, 0)))
    conv = sum(up[:, j:j + S] * conv_w[j] for j in range(CONV_WIDTH))
    return (b * conv) @ w_out


def mlstm_chunkwise(q, k, v, log_i, log_f):
    Bn, H, S, dk = q.shape
    dv = v.shape[-1]
    nc = S // CHUNK

    def to_chunks(a):
        return jnp.moveaxis(a.reshape((Bn, H, nc, CHUNK) + a.shape[3:]), 2, 0)

    xs = tuple(map(to_chunks, (q, k, v, log_i, log_f)))
    lower = jnp.tril(jnp.ones((CHUNK, CHUNK), dtype=bool))

    def step(carry, xs_c):
        C, n, m = carry
        qb, kb, vb, ib, fb = xs_c
        bcum = jnp.cumsum(fb, axis=-1)
        dmat = bcum[..., :, None] - bcum[..., None, :] + ib[..., None, :]
        dmat = jnp.where(lower, dmat, -jnp.inf)
        m_inter = bcum + m[..., None]
        m_t = jnp.maximum(m_inter, jnp.max(dmat, axis=-1))
        w_intra = jnp.exp(dmat - m_t[..., None])
        w_inter = jnp.exp(m_inter - m_t)
        scores = jnp.einsum('bhtd,bhsd->bhts', qb, kb) * w_intra
        num = (jnp.einsum('bhts,bhsv->bhtv', scores, vb)
               + w_inter[..., None] * jnp.einsum('bhtd,bhdv->bhtv', qb, C))
        den = jnp.sum(scores, axis=-1) + w_inter * jnp.einsum('bhtd,bhd->bht', qb, n)
        h = num / jnp.maximum(jnp.abs(den), jnp.exp(-m_t))[..., None]
        b_last = bcum[..., -1]
        d_last = b_last[..., None] - bcum + ib
        m_new = jnp.maximum(b_last + m, jnp.max(d_last, axis=-1))
        w_s = jnp.exp(d_last - m_new[..., None])
        decay = jnp.exp(b_last + m - m_new)
        C_new = decay[..., None, None] * C + jnp.einsum('bhs,bhsd,bhsv->bhdv', w_s, kb, vb)
        n_new = decay[..., None] * n + jnp.einsum('bhs,bhsd->bhd', w_s, kb)
        return (C_new, n_new, m_new), h

    init = (jnp.zeros((Bn, H, dk, dv), jnp.float32),
            jnp.zeros((Bn, H, dk), jnp.float32),
            jnp.zeros((Bn, H), jnp.float32))
    _, hs = lax.scan(step, init, xs)
    return jnp.moveaxis(hs, 0, 2).reshape(Bn, H, S, dv)


def mlstm_mixer(x, w_in, b_gate, norm_g, w_out):
    Bn, S, _ = x.shape
    H = MLSTM_HEADS
    proj = x @ w_in
    q, k, v, o, gates = jnp.split(
        proj, [QK_W, 2 * QK_W, 2 * QK_W + D_MODEL, 2 * QK_W + 2 * D_MODEL], axis=-1)

    def heads(a, d):
        return a.reshape(Bn, S, H, d).transpose(0, 2, 1, 3).astype(jnp.float32)

    qh = heads(q, QK_DIM) * (QK_DIM ** -0.5)
    kh = heads(k, QK_DIM)
    vh = heads(v, V_DIM)
    g = (gates.astype(jnp.float32) + b_gate.astype(jnp.float32)).transpose(0, 2, 1)
    i_f, f_f, i_b, f_b = jnp.split(g, 4, axis=1)
    h_fwd = mlstm_chunkwise(qh, kh, vh, i_f, jax.nn.log_sigmoid(f_f))
    flip = lambda a: jnp.flip(a, axis=2)
    h_bwd = flip(mlstm_chunkwise(flip(qh), flip(kh), flip(vh), flip(i_b),
                                 flip(jax.nn.log_sigmoid(f_b))))
    h = h_fwd + h_bwd
    h = h * lax.rsqrt(jnp.mean(h * h, axis=-1, keepdims=True) + EPS)
    h = h.transpose(0, 2, 1, 3).reshape(Bn, S, D_MODEL) * norm_g.astype(jnp.float32)
    h = (jax.nn.sigmoid(o.astype(jnp.float32)) * h).astype(x.dtype)
    return h @ w_out


def swiglu(x, w_in, w_out):
    gate, up = jnp.split(x @ w_in, 2, axis=-1)
    return (jax.nn.silu(gate) * up) @ w_out


def trunk(x, norms, conv_w_in, conv_w, conv_w_out,
          mlstm_w_in, mlstm_b_gate, mlstm_norm, mlstm_w_out, ffn_w_in, ffn_w_out):
    for i in range(DEPTH):
        j = i // N_MIXERS
        h = rms_norm(x, norms[i, 0])
        if i % N_MIXERS == 0:
            mix = short_conv_mixer(h, conv_w_in[j], conv_w[j], conv_w_out[j])
        else:
            mix = mlstm_mixer(h, mlstm_w_in[j], mlstm_b_gate[j], mlstm_norm[j], mlstm_w_out[j])
        x = x + rms_norm(mix, norms[i, 1])
        h = rms_norm(x, norms[i, 2])
        x = x + rms_norm(swiglu(h, ffn_w_in[i], ffn_w_out[i]), norms[i, 3])
    return x


def setup_inputs(seed: int = 0) -> dict:
    key = jax.random.key(seed)
    ks = jax.random.split(key, 16)
    n_conv = (DEPTH + N_MIXERS - 1) // N_MIXERS
    n_ml = DEPTH // N_MIXERS
    H = MLSTM_HEADS
    nrm = lambda k, s, scale: jax.random.normal(k, s, jnp.float32) * scale
    gb = jax.random.normal(ks[7], (n_ml, 4 * H), jnp.float32)
    gate_offset = jnp.concatenate([jnp.zeros((H,)), 3.0 * jnp.ones((H,)),
                                   jnp.zeros((H,)), 3.0 * jnp.ones((H,))]).astype(jnp.float32)
    return {
        "x_prompt": nrm(ks[0], (BATCH, SEQ, D_MODEL), 1.0),
        "x_sample": nrm(ks[1], (DEC_BATCH, DEC_SEQ, D_MODEL), 1.0),
        "norms": 1.0 + nrm(ks[2], (DEPTH, 4, D_MODEL), 0.05),
        "conv_w_in": nrm(ks[3], (n_conv, D_MODEL, 3 * D_MODEL), D_MODEL ** -0.5),
        "conv_w": nrm(ks[4], (n_conv, CONV_WIDTH, D_MODEL), CONV_WIDTH ** -0.5),
        "conv_w_out": nrm(ks[5], (n_conv, D_MODEL, D_MODEL), D_MODEL ** -0.5),
        "mlstm_w_in": nrm(ks[6], (n_ml, D_MODEL, MLSTM_PROJ), D_MODEL ** -0.5),
        "mlstm_b_gate": gate_offset + 0.3 * gb,
        "mlstm_norm": 1.0 + nrm(ks[8], (n_ml, D_MODEL), 0.05),
        "mlstm_w_out": nrm(ks[9], (n_ml, D_MODEL, D_MODEL), D_MODEL ** -0.5),
        "ffn_w_in": nrm(ks[10], (DEPTH, D_MODEL, 2 * D_FF), D_MODEL ** -0.5),
        "ffn_w_out": nrm(ks[11], (DEPTH, D_FF, D_MODEL), D_FF ** -0.5),
    }


def reference(x_prompt, x_sample, norms, conv_w_in, conv_w, conv_w_out,
              mlstm_w_in, mlstm_b_gate, mlstm_norm, mlstm_w_out, ffn_w_in, ffn_w_out):
    y_prompt = trunk(x_prompt, norms, conv_w_in, conv_w, conv_w_out,
                     mlstm_w_in, mlstm_b_gate, mlstm_norm, mlstm_w_out, ffn_w_in, ffn_w_out)
    y_sample = trunk(x_sample, norms, conv_w_in, conv_w, conv_w_out,
                     mlstm_w_in, mlstm_b_gate, mlstm_norm, mlstm_w_out, ffn_w_in, ffn_w_out)
    return (y_prompt, y_sample)
```

```python
import contextlib
import numpy as np
import concourse.bass as bass
import concourse.mybir as mybir
from concourse.bass_utils import run_bass_kernel_spmd

F32 = mybir.dt.float32
BF16 = mybir.dt.bfloat16
AF = mybir.ActivationFunctionType
ALU = mybir.AluOpType
AX = mybir.AxisListType

D = 1024
T = 2048
NSEG = 3
NBLK = 4
NCH = 16
DFF = 2816
KF = 22
EPS = 1e-6
NEG = -1.0e30
NCORES = 8


class _Rec:
    def __getattr__(self, name):
        def f(*a, **kw):
            self.call = (name, a, kw)
            return self
        return f


class Sched:
    ENGS = ("pe", "act", "dve", "pool", "sp")

    def __init__(self, nc):
        self.nc = nc
        self.ops = []
        self.lastw = {}
        self.readers = {}
        self.dma_keys = []
        self.pending = {}
        self.touched = set()
        self.tags = []
        self.reorder = True
        self.last_pe = None
        self.window = 64
        import os
        self.reorder_engs = tuple(os.environ.get("KREORDER", "none").split(","))
        self.xlat = 1000.0

    def alias(self, new_names, old_names):
        old = set(old_names)
        dset = set()
        for k, w in self.lastw.items():
            if k[0] in old:
                dset.add(w)
        for k, rs in self.readers.items():
            if k[0] in old:
                dset.update(rs)
        for n in new_names:
            self.pending[n] = set(self.pending.get(n, set())) | dset
            self.touched = {k for k in self.touched if k[0] != n}

    @staticmethod
    def _cost(eng, name, a, kw, dma):
        out = kw.get("out", a[0] if a else None)
        try:
            shp = out.shape
            free = 1
            for d_ in shp[1:]:
                free *= d_
        except Exception:
            free = 512
        if name == "collective_compute":
            return (500.0, 40000.0)
        if dma is not None:
            try:
                nbytes = out.nbytes()
            except Exception:
                nbytes = free * 4 * 128
            return (150.0, 2500.0 + nbytes / 120.0)
        if eng == "pe":
            return (max(64, free) * 0.50 + 35.0, 250.0)
        if eng == "act":
            return (210.0 + free * 0.65, 250.0)
        if eng == "dve":
            return (90.0 + free * 0.95, 250.0)
        if eng == "pool":
            return (250.0 + free * 2.6, 300.0)
        return (100.0, 100.0)

    def op(self, eng, fn, reads=(), writes=(), dma=None, inc=16, after=()):
        deps = set(after)
        for k in list(reads) + list(writes):
            if k[0] in self.pending and k not in self.touched:
                deps |= self.pending[k[0]]
                self.touched.add(k)
        for k in reads:
            w = self.lastw.get(k)
            if w is not None:
                deps.add(w)
        for k in writes:
            w = self.lastw.get(k)
            if w is not None:
                deps.add(w)
            for r in self.readers.get(k, ()):
                deps.add(r)
        i = len(self.ops)
        rec = _Rec()
        fn(rec)
        name, a, kw = rec.call
        fn = (lambda e, name=name, a=a, kw=kw: getattr(e, name)(*a, **kw))
        self.ops.append(dict(eng=eng, fn=fn, deps=deps, dma=dma, inc=inc))
        if eng == "pe":
            self.last_pe = i
        if dma is not None and dma not in self.dma_keys:
            self.dma_keys.append(dma)
        tag = ("d", dma) if dma is not None else ("e", eng)
        order = set()
        for k in reads:
            lst = self.readers.setdefault(k, [])
            for r in lst:
                if self.tags[r] == tag:
                    order.add(r)
            lst[:] = [r for r in lst if self.tags[r] != tag]
            lst.append(i)
        self.tags.append(tag)
        self.ops[i]["order"] = order
        self.ops[i]["cost"] = self._cost(eng, name, a, kw, dma)
        for k in writes:
            self.lastw[k] = i
            self.readers[k] = []
        return i

    def _list_schedule(self):
        ops = self.ops
        n = len(ops)
        full = {e: [] for e in self.ENGS}
        for i, o in enumerate(ops):
            full[o["eng"]].append(i)
        nxt = {e: 0 for e in self.ENGS}
        win = {e: [] for e in self.ENGS}
        done = [False] * n
        fin = [0.0] * n
        free_t = {e: 0.0 for e in self.ENGS}
        out = {e: [] for e in self.ENGS}
        preds = [list(o["deps"] | o["order"]) for o in ops]
        succ_eng = [set() for _ in range(n)]
        for i, o in enumerate(ops):
            for d in preds[i]:
                succ_eng[d].add(o["eng"])
        W = self.window
        xlat = self.xlat

        def refill(e):
            Wl = W if e in self.reorder_engs else 1
            w = win[e]
            f = full[e]
            while len(w) < Wl and nxt[e] < len(f):
                w.append(f[nxt[e]])
                nxt[e] += 1

        def best(e):
            bi = None
            bt = None
            ft = free_t[e]
            for i in win[e]:
                ok = True
                rt = 0.0
                for d in preds[i]:
                    if not done[d]:
                        ok = False
                        break
                    od = ops[d]
                    t = fin[d] + (0.0 if (od["eng"] == e and od["dma"] is None) else xlat)
                    if t > rt:
                        rt = t
                if not ok:
                    continue
                st = rt if rt > ft else ft
                if bt is None or st < bt - 1e-9:
                    bi, bt = i, st
                    if st <= ft + 1e-9:
                        break
            return bi, bt

        for e in self.ENGS:
            refill(e)
        cand = {}
        remaining = n
        while remaining:
            choice = None
            for e in self.ENGS:
                if not win[e]:
                    continue
                if e not in cand:
                    cand[e] = best(e)
                bi, bt = cand[e]
                if bi is None:
                    continue
                if choice is None or bt < choice[1]:
                    choice = (bi, bt, e)
            assert choice is not None, "scheduler deadlock"
            i, st, e = choice
            busy, lat = ops[i]["cost"]
            free_t[e] = st + busy
            fin[i] = st + busy + lat
            done[i] = True
            win[e].remove(i)
            refill(e)
            out[e].append(i)
            remaining -= 1
            cand.pop(e, None)
            for e2 in succ_eng[i]:
                cand.pop(e2, None)
        self.sim_time = max(fin) if fin else 0.0
        return out

    def emit(self, stack):
        nc = self.nc
        ops = self.ops
        per_eng_sched = self._list_schedule() if self.reorder else None
        needed = [False] * len(ops)
        for o in ops:
            if o["eng"] == "pe":
                o["wdeps"] = {d for d in o["deps"] if not (ops[d]["eng"] == "pe" and ops[d]["dma"] is None)}
            else:
                o["wdeps"] = o["deps"]
            for d in o["wdeps"]:
                needed[d] = True
        esem = {e: stack.enter_context(nc.semaphore("s_" + e)) for e in self.ENGS}
        dsem = {k: stack.enter_context(nc.semaphore("d_%d" % i)) for i, k in enumerate(self.dma_keys)}
        cnt = {e: 0 for e in self.ENGS}
        dcnt = {k: 0 for k in self.dma_keys}
        token = [None] * len(ops)
        per_eng = {e: [] for e in self.ENGS}
        if per_eng_sched is not None:
            seq = [i for e in self.ENGS for i in per_eng_sched[e]]
        else:
            seq = list(range(len(ops)))
        for i in seq:
            o = ops[i]
            per_eng[o["eng"]].append(i)
            if o["dma"] is not None:
                dcnt[o["dma"]] += o["inc"]
                token[i] = (("d", o["dma"]), dsem[o["dma"]], dcnt[o["dma"]])
            elif needed[i]:
                cnt[o["eng"]] += 1
                token[i] = (("e", o["eng"]), esem[o["eng"]], cnt[o["eng"]])
        self.stats = dict(sim_ms=getattr(self, "sim_time", 0.0) / 1e6, nops=len(ops), cnt=dict(cnt), ndma=len(self.dma_keys),
                          per_eng={e: len(v) for e, v in per_eng.items()})
        self._last_order = per_eng
        self._last_token = token
        block = stack.enter_context(nc.Block())
        handles = {"pe": block.tensor, "act": block.scalar, "dve": block.vector,
                   "pool": block.gpsimd, "sp": block.sync}
        final_d = dict(dcnt)
        final_e = dict(cnt)

        def make_body(e):
            def body(eng):
                seen = {}
                for i in per_eng[e]:
                    o = ops[i]
                    waits = {}
                    for d in o["wdeps"]:
                        t = token[d]
                        if t is None:
                            continue
                        name, sem, val = t
                        if seen.get(name, 0) >= val:
                            continue
                        if name not in waits or waits[name][1] < val:
                            waits[name] = (sem, val)
                    if o["dma"] is not None:
                        name, sem, val = token[i]
                        prev = val - o["inc"]
                        if prev > 0 and seen.get(name, 0) < prev:
                            waits[name] = (sem, prev)
                    for name, (sem, val) in waits.items():
                        eng.wait_ge(sem, val)
                        seen[name] = val
                    ins = o["fn"](eng)
                    t = token[i]
                    if t is not None:
                        ins.then_inc(t[1], o["inc"] if o["dma"] is not None else 1)
                if e == "sp":
                    for k, v in final_d.items():
                        if v:
                            eng.wait_ge(dsem[k], v)
                    for e2, v in final_e.items():
                        if v and e2 != "sp":
                            eng.wait_ge(esem[e2], v)
            return body

        for e in self.ENGS:
            handles[e](make_body(e))


def build_program(n_layers=2, do_mlstm=True, n_seg=NSEG):
    nc = bass.Bass("TRN2", target_bir_lowering=False)
    dr = lambda name, shape, dt=F32, kind="ExternalInput": nc.dram_tensor(name, shape, dt, kind=kind).ap()
    xin = dr("xin", [NSEG, T, D])
    halo = dr("halo", [NSEG, 2, D])
    yout = dr("yout", [NSEG, T, D], kind="ExternalOutput")
    wspec = {
        "w_cin": (24, 1024), "w_cout": (8, 1024), "w_qk": (8, 1024), "w_o": (8, 1024),
        "w_v": (4, 2048), "w_g": (1, 640), "w_mout": (8, 1024),
        "w_f1": (88, 1024), "w_f2": (16, DFF),
    }
    wf = {k: dr(k, [n, 128, w]) for k, (n, w) in wspec.items()}
    wb = {k: nc.dram_tensor("b" + k, [n, 128, w], BF16).ap() for k, (n, w) in wspec.items()}
    norms_d = dr("norms_t", [128, 64])
    convw_d = dr("convw_t", [128, 24])
    mnorm_d = dr("mnorm_t", [128, 8])
    bgate_d = dr("bgate_t", [40, 2])
    flp_d = dr("flp", [128, 8])
    fls_d = dr("fls", [40, 4])
    summ_f = nc.dram_tensor("summ_f", [128, 1032], F32).ap()
    summ_b = nc.dram_tensor("summ_b", [128, 1056], F32).ap()
    sout_f = nc.dram_tensor("sout_f", [512, 1032], F32).ap()
    sout_b = nc.dram_tensor("sout_b", [512, 1056], F32).ap()
    cin_d = nc.dram_tensor("cin_d", [128, 2064], F32).ap()

    st = contextlib.ExitStack()
    with st:
        SB = lambda name, shape, dt: st.enter_context(nc.sbuf_tensor(name, shape, dt))
        s = Sched(nc)
        xT = SB("xT", [128, 8, T], F32)
        ARENA_W = 22568
        arena = SB("arena", [128, ARENA_W], F32)

        def aview(off_b, nbytes, dt):
            a = arena[:, off_b // 4:(off_b + nbytes) // 4]
            return a.bitcast(dt) if dt != F32 else a

        wchunk = [SB("wch%d" % i, [128, 1024], BF16) for i in range(4)]
        wbig = [SB("wbig%d" % i, [128, DFF], BF16) for i in range(2)]
        wg_sb = SB("wg_sb", [128, 640], BF16)
        t2k = [SB("t2k%d" % i, [128, 512], F32) for i in range(6)]
        sqb = [SB("sqb%d" % i, [128, 512], BF16) for i in range(2)]
        ident_f = SB("ident_f", [128, 128], F32)
        ident_b = SB("ident_b", [128, 128], BF16)
        ones_b = SB("ones_b", [128, 128], BF16)
        ones_f = SB("ones_f", [128, 128], F32)
        mask = SB("mask", [128, 2, 128], F32)
        norms_sb = SB("norms_sb", [128, 64], F32)
        convw_sb = SB("convw_sb", [128, 24], F32)
        mnorm_sb = SB("mnorm_sb", [128, 8], F32)
        bgate_sb = SB("bgate_sb", [40, 2], F32)
        negbf = SB("negbf", [40, 1], F32)
        sel = SB("sel", [40, 16], F32)
        xTh = SB("xTh", [128, 8, 2], F32)
        e_tok = SB("e_tok", [128, NCH, 16], F32)
        cl_tok = SB("cl_tok", [128, NCH, 16], F32)
        dec_rep = SB("dec_rep", [128, NCH, 16], F32)
        decp = SB("decp", [128, NCH, 2, 4], F32)
        g_amax = SB("g_amax", [40, NCH], F32)
        g_bn = SB("g_bn", [40, NCH], F32)
        g_m = SB("g_m", [40, NCH], F32)
        g_mp = SB("g_mp", [40, NCH], F32)
        g_mout = SB("g_mout", [40, 1], F32)
        g_dec = SB("g_dec", [40, NCH], F32)
        g_bd = SB("g_bd", [40, NCH, 16], F32)
        PT = [SB("PT%d" % i, [128, 128], BF16) for i in range(8)]
        kp = [SB("kp%d" % i, [128, 128], BF16) for i in range(6)]
        Cp = [SB("Cp%d" % i, [128, 258], BF16) for i in range(3)]
        S_f = SB("S_f", [128, 258], F32)
        S_b = SB("S_b", [128, 258], F32)
        htmp = [SB("htmp%d" % i, [128, 128], F32) for i in range(3)]
        hs = [SB("hs%d" % i, [128, 128], F32) for i in range(3)]
        hn = [SB("hn%d" % i, [128, 128], BF16) for i in range(3)]
        hjunk = SB("hjunk", [128, 128], BF16)
        sm = [SB("sm%d" % i, [128, 8], F32) for i in range(8)]
        flp_sb = SB("flp_sb", [128, 8], F32)
        flpB = SB("flpB", [128, 8], F32)
        fls_sb = SB("fls_sb", [40, 4], F32)
        flsB = SB("flsB", [40, 4], F32)
        g_sv = SB("g_sv", [40, 2], F32)
        g_bd2 = SB("g_bd2", [40, 2, 16], F32)
        svrep = SB("svrep", [128, 32], F32)
        sc_p = SB("sc_p", [128, 16], F32)
        mq = SB("mq", [128, 4, 8], F32)
        Fq = SB("Fq", [128, 4, 8], F32)
        a1 = SB("a1", [128, 4, 8], F32)
        a2 = SB("a2", [128, 4, 8], F32)
        cm = SB("cm", [128, 8], F32)
        ct1 = SB("ct1", [128, 8], F32)
        ct2 = SB("ct2", [128, 8], F32)
        svq = SB("svq", [40, 4, 2], F32)
        ms = SB("ms", [40, 1], F32)
        st1 = SB("st1", [40, 1], F32)
        st2 = SB("st2", [40, 1], F32)

        psb = [st.enter_context(nc.psum_tensor("psb%d" % i, [128, 512], F32)) for i in range(8)]

        rr = {}

        def rot(name, n):
            i = rr.get(name, 0)
            rr[name] = i + 1
            return i % n

        accmode = {"wide": False}

        def acc_bank():
            if accmode["wide"]:
                return (0, 1, 2, 3, 4, 5, 7)[rot("accw", 7)]
            return rot("acc", 5)
        STAT = 5
        MISC = 6
        MISC2 = 7

        def P(b):
            return ("ps", b)

        s.op("pool", lambda e: e.memset(ident_f[:], 0.0), writes=[("ident_f",)])
        s.op("pool", lambda e: e.affine_select(out=ident_f[:], in_=ident_f[:], pattern=[[-1, 128]],
                                               compare_op=ALU.not_equal, fill=1.0, base=0, channel_multiplier=1),
             reads=[("ident_f",)], writes=[("ident_f",)])
        s.op("pool", lambda e: e.tensor_copy(out=ident_b[:], in_=ident_f[:]), reads=[("ident_f",)], writes=[("ident_b",)])
        s.op("pool", lambda e: e.memset(ones_b[:], 1.0), writes=[("ones_b",)])
        s.op("pool", lambda e: e.memset(ones_f[:], 1.0), writes=[("ones_f",)])
        s.op("pool", lambda e: e.affine_select(out=mask[:, 0, :], in_=ones_f[:], pattern=[[1, 128]],
                                               compare_op=ALU.is_ge, fill=0.0, base=0, channel_multiplier=-1),
             reads=[("ones_f",)], writes=[("mask", 0)])
        s.op("pool", lambda e: e.affine_select(out=mask[:, 1, :], in_=ones_f[:], pattern=[[-1, 128]],
                                               compare_op=ALU.is_ge, fill=0.0, base=0, channel_multiplier=1),
             reads=[("ones_f",)], writes=[("mask", 1)])
        s.op("pool", lambda e: e.tensor_copy(out=sel[:, 0:8], in_=ident_f[0:40, 0:8]), reads=[("ident_f",)], writes=[("sel", 0)])
        s.op("pool", lambda e: e.tensor_copy(out=sel[:, 8:16], in_=ident_f[0:40, 32:40]), reads=[("ident_f",)], writes=[("sel", 1)])
        s.op("pool", lambda e: e.memset(g_m[:], 0.0), writes=[("g", "m")])
        s.op("pool", lambda e: e.memset(g_mout[:], 0.0), writes=[("g", "mp")])
        s.op("pool", lambda e: e.memset(g_amax[:], 0.0), writes=[("g", "amax")])
        s.op("pool", lambda e: e.memset(g_bn[:], 0.0), writes=[("g", "bn")])
        s.op("sp", lambda e: e.dma_start(out=norms_sb[:], in_=norms_d), writes=[("norms",)], dma="c_norms")
        s.op("sp", lambda e: e.dma_start(out=convw_sb[:], in_=convw_d), writes=[("convw",)], dma="c_convw")
        s.op("sp", lambda e: e.dma_start(out=mnorm_sb[:], in_=mnorm_d), writes=[("mnorm",)], dma="c_mnorm")
        s.op("sp", lambda e: e.dma_start(out=bgate_sb[:], in_=bgate_d), writes=[("bgate",)], dma="c_bgate")
        s.op("dve", lambda e: e.tensor_scalar(out=negbf[:], in0=bgate_sb[:, 1:2], scalar1=-1.0, scalar2=None, op0=ALU.mult),
             reads=[("bgate",)], writes=[("negbf",)])

        cast_order = ["w_cin", "w_cout", "w_f1:0", "w_f2:0", "w_g", "w_qk", "w_v", "w_o", "w_mout", "w_f1:1", "w_f2:1"]
        cast_pieces = []
        for item in cast_order:
            if ":" in item:
                nm, l = item.split(":")
                l = int(l)
                n = wspec[nm][0] // 2
                lo, hi = l * n, (l + 1) * n
            else:
                nm = item
                lo, hi = 0, wspec[nm][0]
            step = 8 if wspec[nm][1] <= 1024 else 4
            for a in range(lo, hi, step):
                cast_pieces.append((nm, a, min(hi, a + step)))

        def flush_cast(n=1, gate=True):
            for _ in range(n):
                if not cast_pieces:
                    return
                nm, a, b = cast_pieces.pop(0)
                aft = [s.last_pe] if (gate and s.last_pe is not None) else []
                s.op("pool", lambda e: e.dma_start(out=wb[nm][a:b], in_=wf[nm][a:b]),
                     writes=[("wb", nm, j) for j in range(a, b)], dma=("cast", nm, a), after=aft)

        flush_cast(2, gate=False)

        def load_chunk(nm, j):
            sl = rot("wch", 4)
            s.op("sp", lambda e: e.dma_start(out=wchunk[sl][:], in_=wb[nm][j]),
                 reads=[("wb", nm, j)], writes=[("wch", sl)], dma=("wch", sl))
            return wchunk[sl], ("wch", sl)

        def load_big(nm, j, width):
            sl = rot("wbig", 2)
            s.op("sp", lambda e: e.dma_start(out=wbig[sl][:, 0:width], in_=wb[nm][j]),
                 reads=[("wb", nm, j)], writes=[("wbig", sl)], dma=("wbig", sl))
            return wbig[sl], ("wbig", sl)

        gcol = lambda l, n: (l * 4 + n) * 8

        def rms_T(src, srckeys, n, gc, dst, dstkeys, npart=128):
            for c in range(8):
                q = rot("sqb", 2)
                s.op("act", lambda e, c=c, q=q: e.activation(out=sqb[q][:, 0:n], in_=src(c), func=AF.Square),
                     reads=[srckeys(c)], writes=[("sqb", q)])
                s.op("pe", lambda e, c=c, q=q: e.matmul(psb[STAT][:, 0:n], lhsT=ones_b[:], rhs=sqb[q][:, 0:n],
                                                        start=(c == 0), stop=(c == 7)),
                     reads=[("sqb", q), ("ones_b",)], writes=[P(STAT)])
            r = rot("t2k", 6)
            s.op("act", lambda e: e.activation(out=t2k[r][:, 0:n], in_=psb[STAT][:, 0:n], func=AF.Sqrt, bias=EPS, scale=1.0 / D),
                 reads=[P(STAT)], writes=[("t2k", r)])
            s.op("dve", lambda e: e.reciprocal(out=t2k[r][:, 0:n], in_=t2k[r][:, 0:n]), reads=[("t2k", r)], writes=[("t2k", r)])
            for c in range(8):
                s.op("dve", lambda e, c=c: e.scalar_tensor_tensor(out=dst(c), in0=src(c), scalar=norms_sb[:, gc + c:gc + c + 1],
                                                                  in1=t2k[r][:, 0:n], op0=ALU.mult, op1=ALU.mult),
                     reads=[srckeys(c), ("t2k", r), ("norms",)], writes=[dstkeys(c)])

        def out_proj_block(wres, wreskeys, nk, rhs, rhskeys, ybuf, ykey, col0, n, gc, statbank):
            for oc in range(8):
                b = acc_bank()
                for kc in range(nk):
                    s.op("pe", lambda e, oc=oc, kc=kc, b=b: e.matmul(psb[b][:, 0:n], lhsT=wres(oc, kc), rhs=rhs(kc),
                                                                     start=(kc == 0), stop=(kc == nk - 1)),
                         reads=[wreskeys(oc), rhskeys(kc)], writes=[P(b)])
                s.op("act", lambda e, oc=oc, b=b: e.copy(out=ybuf(oc), in_=psb[b][:, 0:n]), reads=[P(b)], writes=[(ykey, oc, col0)])
                q = rot("sqb", 2)
                s.op("pool", lambda e, oc=oc, q=q: e.tensor_tensor(out=sqb[q][:, 0:n], in0=ybuf(oc), in1=ybuf(oc), op=ALU.mult),
                     reads=[(ykey, oc, col0)], writes=[("sqb", q)])
                s.op("pe", lambda e, oc=oc, q=q: e.matmul(psb[statbank][:, 0:n], lhsT=ones_b[:], rhs=sqb[q][:, 0:n],
                                                          start=(oc == 0), stop=(oc == 7)),
                     reads=[("sqb", q), ("ones_b",)], writes=[P(statbank)])

        def resid_block(ybuf, ykey, col0, n, gc, statbank):
            r = rot("t2k", 6)
            s.op("act", lambda e: e.activation(out=t2k[r][:, 0:n], in_=psb[statbank][:, 0:n], func=AF.Sqrt, bias=EPS, scale=1.0 / D),
                 reads=[P(statbank)], writes=[("t2k", r)])
            s.op("dve", lambda e: e.reciprocal(out=t2k[r][:, 0:n], in_=t2k[r][:, 0:n]), reads=[("t2k", r)], writes=[("t2k", r)])
            for oc in range(8):
                t = rot("t2k", 6)
                while t == r:
                    t = rot("t2k", 6)
                s.op("dve", lambda e, oc=oc, t=t: e.scalar_tensor_tensor(out=t2k[t][:, 0:n], in0=ybuf(oc),
                                                                         scalar=norms_sb[:, gc + oc:gc + oc + 1],
                                                                         in1=t2k[r][:, 0:n], op0=ALU.mult, op1=ALU.mult),
                     reads=[(ykey, oc, col0), ("t2k", r), ("norms",)], writes=[("t2k", t)])
                s.op("pool", lambda e, oc=oc, t=t: e.tensor_tensor(out=xT[:, oc, col0:col0 + n], in0=xT[:, oc, col0:col0 + n],
                                                                   in1=t2k[t][:, 0:n], op=ALU.add),
                     reads=[("t2k", t), ("xT", oc, col0 // 512)], writes=[("xT", oc, col0 // 512)])

        xkey = lambda c, blk: ("xT", c, blk)

        import os as _os
        XQ = _os.environ.get("KXQ", "pool")

        def load_blocks(seg, blks):
            for blk in blks:
                for tt in range(blk * 4, blk * 4 + 4):
                    for hf in range(2):
                        r = rot("t2k", 6)
                        s.op(XQ, lambda e: e.dma_start(out=t2k[r][:], in_=xin[seg, tt * 128:(tt + 1) * 128, hf * 512:(hf + 1) * 512]),
                             writes=[("t2k", r)], dma=("xl", r))
                        bk = acc_bank()
                        for q in range(4):
                            s.op("pe", lambda e: e.transpose(out=psb[bk][:, q * 128:(q + 1) * 128], in_=t2k[r][:, q * 128:(q + 1) * 128], identity=ident_f[:]),
                                 reads=[("t2k", r), ("ident_f",)], writes=[P(bk)])
                        outap = xT[:, hf * 4:(hf + 1) * 4, tt * 128:(tt + 1) * 128]
                        inap = psb[bk][:].rearrange("p (q t) -> p q t", q=4)
                        wk = [xkey(hf * 4 + q, tt // 4) for q in range(4)]
                        if (tt + hf) % 2 == 0:
                            s.op("act", lambda e: e.copy(out=outap, in_=inap), reads=[P(bk)], writes=wk)
                        else:
                            s.op("dve", lambda e: e.tensor_copy(out=outap, in_=inap), reads=[P(bk)], writes=wk)

        def load_halo(seg):
            for hf in range(2):
                r = rot("t2k", 6)
                s.op(XQ, lambda e: e.dma_start(out=t2k[r][0:2, :], in_=halo[seg, :, hf * 512:(hf + 1) * 512]), writes=[("t2k", r)], dma=("xl", r))
                for q in range(4):
                    c = hf * 4 + q
                    s.op("pe", lambda e: e.transpose(out=psb[MISC][:, c * 2:(c + 1) * 2], in_=t2k[r][0:2, q * 128:(q + 1) * 128],
                                                     identity=ident_f[0:2, 0:2]),
                         reads=[("t2k", r), ("ident_f",)], writes=[P(MISC)])
            s.op("act", lambda e: e.copy(out=xTh[:], in_=psb[MISC][:, 0:16].rearrange("p (c t) -> p c t", t=2)), reads=[P(MISC)], writes=[("xTh",)])

        def store_blocks(seg, blks):
            for blk in blks:
                for tt in range(blk * 4, blk * 4 + 4):
                    for hf in range(2):
                        bk = acc_bank()
                        for q in range(4):
                            c = hf * 4 + q
                            s.op("pe", lambda e: e.transpose(out=psb[bk][:, q * 128:(q + 1) * 128], in_=xT[:, c, tt * 128:(tt + 1) * 128], identity=ident_f[:]),
                                 reads=[xkey(c, tt // 4), ("ident_f",)], writes=[P(bk)])
                        r = rot("t2k", 6)
                        if (tt + hf) % 2 == 0:
                            s.op("act", lambda e: e.copy(out=t2k[r][:], in_=psb[bk][:]), reads=[P(bk)], writes=[("t2k", r)])
                        else:
                            s.op("dve", lambda e: e.tensor_copy(out=t2k[r][:], in_=psb[bk][:]), reads=[P(bk)], writes=[("t2k", r)])
                        s.op(XQ, lambda e: e.dma_start(out=yout[seg, tt * 128:(tt + 1) * 128, hf * 512:(hf + 1) * 512], in_=t2k[r][:]),
                             reads=[("t2k", r)], writes=[("yout", seg, tt, hf)], dma=("xs", r))

        for seg in range(n_seg):
            if seg == 0:
                load_blocks(0, range(NBLK))
                load_halo(0)

            for layer in range(n_layers):
                if layer == 0:
                    xnT = aview(0, 32800, BF16).rearrange("p (c t) -> p c t", c=8)
                    zT = aview(32800, 32768, BF16).rearrange("p (c t) -> p c t", c=8)
                    c_sb = aview(65568, 8200, F32)
                    u_sb = aview(73768, 8200, F32)
                    s.alias(["xnT", "zT", "c_sb", "u_sb"], ["xnT", "zT", "c_sb", "u_sb", "wres", "mix", "xn2", "act", "y", "hT", "gt", "pair"])
                    for blk in range(NBLK):
                        rms_T(lambda c, blk=blk: xT[:, c, blk * 512:(blk + 1) * 512], lambda c, blk=blk: xkey(c, blk), 512, gcol(0, 0),
                              lambda c, blk=blk: xnT[:, c, 1 + blk * 512:1 + (blk + 1) * 512], lambda c, blk=blk: ("xnT", c, blk))
                    rms_T(lambda c: xTh[:, c, :], lambda c: ("xTh",), 2, gcol(0, 0),
                          lambda c: xnT[:, c, 0:2050:2049], lambda c: ("xnT", c, "h"))
                    for cc in range(8):
                        if seg == 0:
                            flush_cast(1)
                        wc, kwc = load_chunk("w_cin", 3 * cc + 0)
                        wv_, kwv = load_chunk("w_cin", 3 * cc + 1)
                        wb_, kwb = load_chunk("w_cin", 3 * cc + 2)
                        for blk in range(NBLK):
                            b = acc_bank()
                            for kc in range(8):
                                s.op("pe", lambda e, kc=kc, b=b, blk=blk, wc=wc: e.matmul(psb[b][:], lhsT=wc[:, kc * 128:(kc + 1) * 128],
                                                                                       rhs=xnT[:, kc, 1 + blk * 512:1 + (blk + 1) * 512],
                                                                                       start=(kc == 0), stop=(kc == 7)),
                                     reads=[kwc, ("xnT", kc, blk)], writes=[P(b)])
                            s.op("act", lambda e, b=b, blk=blk: e.copy(out=c_sb[:, 1 + blk * 512:1 + (blk + 1) * 512], in_=psb[b][:]),
                                 reads=[P(b)], writes=[("c_sb", blk)])
                        for kc in range(8):
                            s.op("pe", lambda e, kc=kc, wc=wc: e.matmul(psb[MISC][:, 0:2], lhsT=wc[:, kc * 128:(kc + 1) * 128], rhs=xnT[:, kc, 0:2050:2049],
                                                                     start=(kc == 0), stop=(kc == 7)),
                                 reads=[kwc, ("xnT", kc, "h")], writes=[P(MISC)])
                        s.op("act", lambda e: e.copy(out=c_sb[:, 0:2050:2049], in_=psb[MISC][:, 0:2]), reads=[P(MISC)], writes=[("c_sb", "h")])
                        for blk in range(NBLK):
                            b = acc_bank()
                            for kc in range(8):
                                s.op("pe", lambda e, kc=kc, b=b, blk=blk, wv_=wv_: e.matmul(psb[b][:], lhsT=wv_[:, kc * 128:(kc + 1) * 128],
                                                                                         rhs=xnT[:, kc, 1 + blk * 512:1 + (blk + 1) * 512],
                                                                                         start=(kc == 0), stop=(kc == 7)),
                                     reads=[kwv, ("xnT", kc, blk)], writes=[P(b)])
                            s.op("dve", lambda e, b=b, blk=blk: e.tensor_tensor(out=u_sb[:, 1 + blk * 512:1 + (blk + 1) * 512], in0=psb[b][:],
                                                                               in1=c_sb[:, 1 + blk * 512:1 + (blk + 1) * 512], op=ALU.mult),
                                 reads=[P(b), ("c_sb", blk)], writes=[("u_sb", blk)])
                        for kc in range(8):
                            s.op("pe", lambda e, kc=kc, wv_=wv_: e.matmul(psb[MISC][:, 0:2], lhsT=wv_[:, kc * 128:(kc + 1) * 128], rhs=xnT[:, kc, 0:2050:2049],
                                                                       start=(kc == 0), stop=(kc == 7)),
                                 reads=[kwv, ("xnT", kc, "h")], writes=[P(MISC)])
                        s.op("dve", lambda e: e.tensor_tensor(out=u_sb[:, 0:2050:2049], in0=psb[MISC][:, 0:2], in1=c_sb[:, 0:2050:2049], op=ALU.mult),
                             reads=[P(MISC), ("c_sb", "h")], writes=[("u_sb", "h")])
                        ukeys = [("u_sb", k) for k in (0, 1, 2, 3, "h")]
                        ckeys = [("c_sb", k) for k in (0, 1, 2, 3, "h")]
                        cw = lambda j, cc=cc: convw_sb[:, cc * 3 + j:cc * 3 + j + 1]
                        s.op("act", lambda e, cw=cw: e.activation(out=c_sb[:, 1:2049], in_=u_sb[:, 0:2048], func=AF.Copy, scale=cw(0)),
                             reads=ukeys + [("convw",)], writes=ckeys)
                        s.op("dve", lambda e, cw=cw: e.scalar_tensor_tensor(out=c_sb[:, 1:2049], in0=u_sb[:, 1:2049], scalar=cw(1), in1=c_sb[:, 1:2049],
                                                                          op0=ALU.mult, op1=ALU.add),
                             reads=ukeys + ckeys + [("convw",)], writes=ckeys)
                        s.op("dve", lambda e, cw=cw: e.scalar_tensor_tensor(out=c_sb[:, 1:2049], in0=u_sb[:, 2:2050], scalar=cw(2), in1=c_sb[:, 1:2049],
                                                                          op0=ALU.mult, op1=ALU.add),
                             reads=ukeys + ckeys + [("convw",)], writes=ckeys)
                        for blk in range(NBLK):
                            b = acc_bank()
                            for kc in range(8):
                                s.op("pe", lambda e, kc=kc, b=b, blk=blk, wb_=wb_: e.matmul(psb[b][:], lhsT=wb_[:, kc * 128:(kc + 1) * 128],
                                                                                         rhs=xnT[:, kc, 1 + blk * 512:1 + (blk + 1) * 512],
                                                                                         start=(kc == 0), stop=(kc == 7)),
                                     reads=[kwb, ("xnT", kc, blk)], writes=[P(b)])
                            s.op("dve", lambda e, b=b, blk=blk, cc=cc: e.tensor_tensor(out=zT[:, cc, blk * 512:(blk + 1) * 512], in0=psb[b][:],
                                                                                      in1=c_sb[:, 1 + blk * 512:1 + (blk + 1) * 512], op=ALU.mult),
                                 reads=[P(b)] + ckeys, writes=[("zT", cc, blk)])
                    mixer_in = zT
                    mixer_key = "zT"
                    wout_name = "w_cout"
                else:
                    if seg == 0:
                        flush_cast(100)
                        s.op("sp", lambda e: e.dma_start(out=wg_sb[:], in_=wb["w_g"][0]), reads=[("wb", "w_g", 0)], writes=[("wg",)], dma="c_wg")
                    xnT = aview(0, 32768, BF16).rearrange("p (c t) -> p c t", c=8)
                    hT = aview(32800, 32768, BF16).rearrange("p (c t) -> p c t", c=8)
                    s.alias(["xnT", "hT", "gt", "pair"], ["xnT", "zT", "c_sb", "u_sb", "wres", "mix", "xn2", "act", "y", "hT", "gt", "pair"])
                    for blk in range(NBLK):
                        rms_T(lambda c, blk=blk: xT[:, c, blk * 512:(blk + 1) * 512], lambda c, blk=blk: xkey(c, blk), 512, gcol(1, 0),
                              lambda c, blk=blk: xnT[:, c, blk * 512:(blk + 1) * 512], lambda c, blk=blk: ("xnT", c, blk))
                    GI = aview(32800, 8192, F32)
                    CS = aview(32800 + 8192, 8192, F32)
                    SP = aview(32800 + 16384, 8192, F32)
                    GI3 = GI.rearrange("p (c t) -> p c t", t=128)
                    SP3 = SP.rearrange("p (c t) -> p c t", t=128)
                    CS3 = CS.rearrange("p (c t) -> p c t", t=128)
                    for blk in range(NBLK):
                        cols = slice(blk * 512, (blk + 1) * 512)
                        for gi in range(2):
                            b = acc_bank()
                            for kc in range(8):
                                s.op("pe", lambda e: e.matmul(psb[b][0:40, :], lhsT=wg_sb[:, gi * 320 + kc * 40:gi * 320 + (kc + 1) * 40],
                                                              rhs=xnT[:, kc, cols], start=(kc == 0), stop=(kc == 7)),
                                     reads=[("wg",), ("xnT", kc, blk)], writes=[P(b)])
                            if gi == 0:
                                s.op("act", lambda e: e.activation(out=GI[0:40, cols], in_=psb[b][0:40, :], func=AF.Identity, bias=bgate_sb[:, 0:1]),
                                     reads=[P(b), ("bgate",)], writes=[("gt", "GI")])
                            else:
                                s.op("act", lambda e: e.activation(out=SP[0:40, cols], in_=psb[b][0:40, :], func=AF.Exp, bias=negbf[:, 0:1], scale=-1.0),
                                     reads=[P(b), ("negbf",)], writes=[("gt", "SP")])
                    s.op("act", lambda e: e.activation(out=SP[0:40, :], in_=SP[0:40, :], func=AF.Ln, bias=1.0), reads=[("gt", "SP")], writes=[("gt", "SP")])
                    for c in range(NCH):
                        s.op("dve", lambda e: e.tensor_tensor_scan(out=CS[0:40, c * 128:(c + 1) * 128], data0=ones_f[0:40, :], data1=SP[0:40, c * 128:(c + 1) * 128],
                                                                   initial=0.0, op0=ALU.mult, op1=ALU.add),
                             reads=[("gt", "SP"), ("ones_f",)], writes=[("gt", "CS")])
                    s.op("dve", lambda e: e.tensor_copy(out=g_bn[:], in_=CS3[0:40, :, 127]), reads=[("gt", "CS")], writes=[("g", "bn")])
                    s.op("dve", lambda e: e.tensor_tensor(out=CS3[32:40], in0=g_bn[32:40, :].unsqueeze(2).broadcast_to([8, NCH, 128]), in1=CS3[32:40], op=ALU.subtract),
                         reads=[("gt", "CS"), ("g", "bn")], writes=[("gt", "CS")])
                    s.op("dve", lambda e: e.tensor_tensor(out=CS[32:40, :], in0=CS[32:40, :], in1=SP[32:40, :], op=ALU.add),
                         reads=[("gt", "CS"), ("gt", "SP")], writes=[("gt", "CS")])
                    s.op("dve", lambda e: e.tensor_tensor(out=GI[0:40, :], in0=GI[0:40, :], in1=CS[0:40, :], op=ALU.add),
                         reads=[("gt", "GI"), ("gt", "CS")], writes=[("gt", "GI")])
                    s.op("dve", lambda e: e.tensor_reduce(out=g_amax[:], in_=GI3[0:40], axis=AX.X, op=ALU.max), reads=[("gt", "GI")], writes=[("g", "amax")])
                    tabkeys = [("e_tok", h_, d_) for h_ in range(2) for d_ in range(2)]
                    clkeys = [("cl_tok", h_, d_) for h_ in range(2) for d_ in range(2)]
                    dpkeys = [("decp", 0), ("decp", 1)]

                    def gate_tables(m_in):
                        s.op("dve", lambda e: e.memset(g_mp[:], NEG), writes=[("g", "mp")])
                        if m_in is not None:
                            s.op("dve", lambda e: e.tensor_copy(out=g_mp[0:8, 0:1], in_=m_in[0:8, :]), reads=[("cs", "ms"), ("g", "mp")], writes=[("g", "mp")])
                            s.op("dve", lambda e: e.tensor_copy(out=g_mp[32:40, NCH - 1:NCH], in_=m_in[32:40, :]), reads=[("cs", "ms"), ("g", "mp")], writes=[("g", "mp")])
                        for c in range(NCH):
                            s.op("dve", lambda e: e.tensor_tensor(out=g_m[0:8, c:c + 1], in0=g_mp[0:8, c:c + 1], in1=g_amax[0:8, c:c + 1], op=ALU.max),
                                 reads=[("g", "mp"), ("g", "amax")], writes=[("g", "m")])
                            dst = g_mp[0:8, c + 1:c + 2] if c < NCH - 1 else g_mout[0:8, :]
                            s.op("dve", lambda e: e.tensor_tensor(out=dst, in0=g_m[0:8, c:c + 1], in1=g_bn[0:8, c:c + 1], op=ALU.subtract),
                                 reads=[("g", "m"), ("g", "bn")], writes=[("g", "mp")])
                        for c in range(NCH - 1, -1, -1):
                            s.op("dve", lambda e: e.tensor_tensor(out=g_m[32:40, c:c + 1], in0=g_mp[32:40, c:c + 1], in1=g_amax[32:40, c:c + 1], op=ALU.max),
                                 reads=[("g", "mp"), ("g", "amax")], writes=[("g", "m")])
                            dst = g_mp[32:40, c - 1:c] if c > 0 else g_mout[32:40, :]
                            s.op("dve", lambda e: e.tensor_tensor(out=dst, in0=g_m[32:40, c:c + 1], in1=g_bn[32:40, c:c + 1], op=ALU.subtract),
                                 reads=[("g", "m"), ("g", "bn")], writes=[("g", "mp")])
                        s.op("dve", lambda e: e.tensor_tensor(out=g_dec[:], in0=g_mp[:], in1=g_m[:], op=ALU.subtract), reads=[("g", "mp"), ("g", "m")], writes=[("g", "dec")])
                        s.op("dve", lambda e: e.tensor_scalar(out=g_dec[:], in0=g_dec[:], scalar1=-100.0, scalar2=None, op0=ALU.max), reads=[("g", "dec")], writes=[("g", "dec")])
                        s.op("act", lambda e: e.activation(out=g_dec[:], in_=g_dec[:], func=AF.Exp), reads=[("g", "dec")], writes=[("g", "dec")])
                        for src3, skey, dstt, dkey in ((GI3, ("gt", "GI"), e_tok, "e_tok"), (CS3, ("gt", "CS"), cl_tok, "cl_tok")):
                            s.op("dve", lambda e: e.tensor_tensor(out=SP3[0:40], in0=src3[0:40], in1=g_m[:].unsqueeze(2).broadcast_to([40, NCH, 128]), op=ALU.subtract),
                                 reads=[skey, ("g", "m"), ("gt", "SP")], writes=[("gt", "SP")])
                            s.op("act", lambda e: e.activation(out=SP[0:40, :], in_=SP[0:40, :], func=AF.Exp), reads=[("gt", "SP")], writes=[("gt", "SP")])
                            for half in range(2):
                                for c8 in range(8):
                                    c = half * 8 + c8
                                    s.op("pe", lambda e: e.transpose(out=psb[MISC][:, c8 * 40:(c8 + 1) * 40], in_=SP[0:40, c * 128:(c + 1) * 128],
                                                                     identity=ident_f[0:40, 0:40]),
                                         reads=[("gt", "SP"), ("ident_f",)], writes=[P(MISC)])
                                m3 = psb[MISC][:, 0:320].rearrange("p (c j) -> p c j", j=40)
                                for d in range(2):
                                    s.op("act", lambda e: e.copy(out=dstt[:, half * 8:(half + 1) * 8, d * 8:(d + 1) * 8], in_=m3[:, :, d * 32:d * 32 + 8]),
                                         reads=[P(MISC)], writes=[(dkey, half, d)])
                        s.op("dve", lambda e: e.tensor_tensor(out=g_bd[:], in0=g_dec[:].unsqueeze(2).broadcast_to([40, NCH, 16]),
                                                              in1=sel[:].unsqueeze(1).broadcast_to([40, NCH, 16]), op=ALU.mult),
                             reads=[("g", "dec"), ("sel", 0), ("sel", 1)], writes=[("g", "bd")])
                        s.op("pe", lambda e: e.matmul(psb[MISC2][:, 0:256], lhsT=ones_f[0:40, :], rhs=g_bd[:].rearrange("p c j -> p (c j)"), start=True, stop=True),
                             reads=[("g", "bd"), ("ones_f",)], writes=[P(MISC2)])
                        s.op("act", lambda e: e.copy(out=dec_rep[:].rearrange("p c j -> p (c j)"), in_=psb[MISC2][:, 0:256]), reads=[P(MISC2)], writes=[("dec_rep",)])
                        dr4 = dec_rep[:].rearrange("p c (d g two) -> p c d g two", d=2, two=2)
                        s.op("dve", lambda e: e.tensor_copy(out=decp[0:64], in_=dr4[0:64, :, :, :, 0]), reads=[("dec_rep",)], writes=[("decp", 0)])
                        s.op("dve", lambda e: e.tensor_copy(out=decp[64:128], in_=dr4[64:128, :, :, :, 1]), reads=[("dec_rep",)], writes=[("decp", 1)])

                    PB = 65568
                    qT = aview(PB, 4096, BF16)
                    kT = aview(PB + 4096, 4096, BF16)
                    v_aug = aview(PB + 8192, 8256, BF16).rearrange("p (c h f) -> p c h f", c=NCH, h=2)
                    Cb_st = aview(PB + 8192 + 8256, 8256, BF16).rearrange("p (c f) -> p c f", c=NCH)

                    qstate = {}

                    def qk_proj(g, which, blks=None):
                        dst, sc = ((qT, 0.125), (kT, 1.0))[which]
                        if blks is None or (g, which) not in qstate:
                            qstate[(g, which)] = load_chunk("w_qk", 2 * g + which)
                        wq, kwq = qstate[(g, which)]
                        for blk in (range(NBLK) if blks is None else blks):
                            b = acc_bank()
                            for kc in range(8):
                                s.op("pe", lambda e: e.matmul(psb[b][:], lhsT=wq[:, kc * 128:(kc + 1) * 128],
                                                              rhs=xnT[:, kc, blk * 512:(blk + 1) * 512], start=(kc == 0), stop=(kc == 7)),
                                     reads=[kwq, ("xnT", kc, blk)], writes=[P(b)])
                            s.op("act", lambda e: e.activation(out=dst[:, blk * 512:(blk + 1) * 512], in_=psb[b][:], func=AF.Copy, scale=sc),
                                 reads=[P(b)], writes=[("pair", "qk", which, blk)])

                    def v_proj(g):
                        wv_, kwv = load_big("w_v", g, 2048)
                        s.op("pool", lambda e: e.memset(v_aug[:, :, :, 128:129], 1.0), writes=[("pair", "v1")])
                        for c in range(NCH):
                            b = acc_bank()
                            for kc in range(8):
                                s.op("pe", lambda e: e.matmul(psb[b][:, 0:256], lhsT=xnT[:, kc, c * 128:(c + 1) * 128],
                                                              rhs=wv_[:, kc * 256:(kc + 1) * 256], start=(kc == 0), stop=(kc == 7)),
                                     reads=[kwv, ("xnT", kc, c // 4)], writes=[P(b)])
                            s.op("act", lambda e: e.copy(out=v_aug[:, c, :, 0:128], in_=psb[b][:, 0:256].rearrange("p (h f) -> p h f", h=2)),
                                 reads=[P(b)], writes=[("pair", "v", c)])

                    def kprime(g, c, d):
                        pbank = psb[MISC][:].bitcast(BF16)
                        s.op("pe", lambda e: e.transpose(out=pbank[:, 0:128], in_=kT[:, c * 128:(c + 1) * 128], identity=ident_b[:]),
                             reads=[("pair", "qk", 1, c // 4), ("ident_b",)], writes=[P(MISC)])
                        q = rot("kp", 6)
                        s.op("dve", lambda e: e.tensor_tensor(out=kp[q][:].rearrange("p (h f) -> p h f", h=2),
                                                              in0=pbank[:, 0:128].rearrange("p (h f) -> p h f", h=2),
                                                              in1=e_tok[:, c, d * 8 + 2 * g:d * 8 + 2 * g + 2].unsqueeze(2).broadcast_to([128, 2, 64]),
                                                              op=ALU.mult),
                             reads=[P(MISC)] + tabkeys, writes=[("kp", q)])
                        return q

                    def state_step(g, c, d, S, Skey, q):
                        b = acc_bank()
                        s.op("pe", lambda e: e.matmul(psb[b][:, 0:258], lhsT=kp[q][:], rhs=v_aug[:, c, :, :].rearrange("p h f -> p (h f)"), start=True, stop=True),
                             reads=[("kp", q), ("pair", "v", c), ("pair", "v1")], writes=[P(b)])
                        s.op("dve", lambda e: e.scalar_tensor_tensor(out=S[:], in0=S[:], scalar=decp[:, c, d, g:g + 1], in1=psb[b][:, 0:258],
                                                                     op0=ALU.mult, op1=ALU.add),
                             reads=[P(b), Skey] + dpkeys, writes=[Skey])

                    if do_mlstm:
                        qk_proj(0, 1)
                        v_proj(0)
                        if seg != 0:
                            qk_proj(0, 0)
                    gate_tables(None)
                    if seg == 0 and do_mlstm:
                        for g in range(4):
                            if g > 0:
                                qk_proj(g, 1)
                                v_proj(g)
                            s.op("pool", lambda e: e.memset(S_b[:], 0.0), writes=[("S_b",)])
                            for c in range(NCH - 1, -1, -1):
                                q = kprime(g, c, 1)
                                state_step(g, c, 1, S_b, ("S_b",), q)
                            s.op("sp", lambda e: e.dma_start(out=summ_b[:, g * 258:(g + 1) * 258], in_=S_b[:]), reads=[("S_b",)], writes=[("summ_in", 4 + g)], dma="sm")
                            s.op("pool", lambda e: e.memset(S_f[:], 0.0), writes=[("S_f",)])
                            for c in range(NCH):
                                q = kprime(g, c, 0)
                                state_step(g, c, 0, S_f, ("S_f",), q)
                            s.op("sp", lambda e: e.dma_start(out=summ_f[:, g * 258:(g + 1) * 258], in_=S_f[:]), reads=[("S_f",)], writes=[("summ_in", g)], dma="sm")
                        s.op("dve", lambda e: e.tensor_copy(out=g_sv[:, 0:1], in_=g_mout[:]), reads=[("g", "mp")], writes=[("cs", "sv")])
                        s.op("dve", lambda e: e.tensor_reduce(out=g_sv[:, 1:2], in_=g_bn[:], axis=AX.X, op=ALU.add, negate=True), reads=[("g", "bn"), ("cs", "sv")], writes=[("cs", "sv")])
                        s.op("dve", lambda e: e.tensor_tensor(out=g_bd2[:], in0=g_sv[:].unsqueeze(2).broadcast_to([40, 2, 16]),
                                                              in1=sel[:].unsqueeze(1).broadcast_to([40, 2, 16]), op=ALU.mult),
                             reads=[("cs", "sv"), ("sel", 0), ("sel", 1)], writes=[("cs", "bd2")])
                        s.op("pe", lambda e: e.matmul(psb[MISC2][:, 0:32], lhsT=ones_f[0:40, :], rhs=g_bd2[:].rearrange("p a j -> p (a j)"), start=True, stop=True),
                             reads=[("cs", "bd2"), ("ones_f",)], writes=[P(MISC2)])
                        s.op("act", lambda e: e.copy(out=svrep[:], in_=psb[MISC2][:, 0:32]), reads=[P(MISC2)], writes=[("cs", "svrep")])
                        sv4 = svrep[:].rearrange("p (a d g two) -> p a d g two", a=2, d=2, two=2)
                        scp4 = sc_p[:].rearrange("p (a d g) -> p a d g", a=2, d=2)
                        s.op("dve", lambda e: e.tensor_copy(out=scp4[0:64], in_=sv4[0:64, :, :, :, 0]), reads=[("cs", "svrep")], writes=[("cs", "scp", 0)])
                        s.op("dve", lambda e: e.tensor_copy(out=scp4[64:128], in_=sv4[64:128, :, :, :, 1]), reads=[("cs", "svrep")], writes=[("cs", "scp", 1)])
                        s.op("sp", lambda e: e.dma_start(out=summ_b[:, 1032:1048], in_=sc_p[:]), reads=[("cs", "scp", 0), ("cs", "scp", 1)], writes=[("summ_in", 8)], dma="sm")
                        s.op("pool", lambda e: e.memset(ct1[:], 0.0), writes=[("cs", "ct1")])
                        s.op("sp", lambda e: e.dma_start(out=summ_b[:, 1048:1056], in_=ct1[:]), reads=[("cs", "ct1")], writes=[("summ_in", 9)], dma="sm")
                        s.op("sp", lambda e: e.dma_start(out=summ_b[0:40, 1048:1050], in_=g_sv[:]), reads=[("cs", "sv")], writes=[("summ_in", 9)], dma="sm")
                        s.op("pool", lambda e: e.collective_compute("AllGather", ALU.bypass, replica_groups=[[0, 1, 2, 3], [4, 5, 6, 7]],
                                                                    ins=[summ_f], outs=[sout_f]),
                             reads=[("summ_in", j) for j in range(10)], writes=[("summ_out",)], dma="ag", inc=1)
                        s.op("pool", lambda e: e.collective_compute("AllGather", ALU.bypass, replica_groups=[[0, 1, 2, 3], [4, 5, 6, 7]],
                                                                    ins=[summ_b], outs=[sout_b]),
                             reads=[("summ_in", j) for j in range(10)], writes=[("summ_out",)], dma="ag2", inc=1)
                        s.op("sp", lambda e: e.dma_start(out=flp_sb[:], in_=flp_d), writes=[("cs", "flp")], dma="c_flp")
                        s.op("sp", lambda e: e.dma_start(out=fls_sb[:], in_=fls_d), writes=[("cs", "fls")], dma="c_fls")
                        s.op("dve", lambda e: e.tensor_scalar(out=flpB[:], in0=flp_sb[:], scalar1=-NEG, scalar2=NEG, op0=ALU.mult, op1=ALU.add), reads=[("cs", "flp")], writes=[("cs", "flpB")])
                        s.op("dve", lambda e: e.tensor_scalar(out=flsB[:], in0=fls_sb[:], scalar1=-NEG, scalar2=NEG, op0=ALU.mult, op1=ALU.add), reads=[("cs", "fls")], writes=[("cs", "flsB")])
                        s.op("dve", lambda e: e.memset(svq[:], 0.0), writes=[("cs", "svq")])
                        for k in range(4):
                            for d in range(2):
                                qm = k if d == 0 else 3 - k
                                s.op("sp", lambda e: e.dma_start(out=mq[:, k, d * 4:(d + 1) * 4], in_=sout_b[qm * 128:(qm + 1) * 128, 1032 + d * 4:1032 + (d + 1) * 4]),
                                     reads=[("summ_out",)], writes=[("cs", "mq", k, d)], dma="cq")
                                s.op("sp", lambda e: e.dma_start(out=Fq[:, k, d * 4:(d + 1) * 4], in_=sout_b[qm * 128:(qm + 1) * 128, 1040 + d * 4:1040 + (d + 1) * 4]),
                                     reads=[("summ_out",)], writes=[("cs", "Fq", k, d)], dma="cq")
                                r0 = d * 32
                                s.op("sp", lambda e: e.dma_start(out=svq[r0:r0 + 8, k, :], in_=sout_b[qm * 128 + r0:qm * 128 + r0 + 8, 1048:1050]),
                                     reads=[("summ_out",), ("cs", "svq")], writes=[("cs", "svq")], dma="cq")
                        cs_ = lambda nm: [("cs", nm)]
                        s.op("dve", lambda e: e.memset(cm[:], NEG), writes=cs_("cm"))
                        s.op("dve", lambda e: e.memset(ms[:], NEG), writes=cs_("ms"))
                        v3 = lambda t: t[:].rearrange("p (d g) -> p d g", d=2)
                        for k in range(4):
                            flb = flp_sb[:, k * 2:(k + 1) * 2].unsqueeze(2).broadcast_to([128, 2, 4])
                            flBb = flpB[:, k * 2:(k + 1) * 2].unsqueeze(2).broadcast_to([128, 2, 4])
                            mqk = mq[:, k, :].rearrange("p (d g) -> p d g", d=2)
                            Fqk = Fq[:, k, :].rearrange("p (d g) -> p d g", d=2)
                            rk = [("cs", "mq", k_, d_) for k_ in range(4) for d_ in range(2)] + [("cs", "Fq", k_, d_) for k_ in range(4) for d_ in range(2)] + [("cs", "svq"), ("cs", "flp"), ("cs", "flpB")]
                            s.op("dve", lambda e: e.tensor_tensor(out=v3(ct1), in0=Fqk, in1=flb, op=ALU.mult), reads=rk, writes=cs_("ct1"))
                            s.op("dve", lambda e: e.tensor_tensor(out=v3(ct2), in0=mqk, in1=flb, op=ALU.mult), reads=rk, writes=cs_("ct2"))
                            s.op("dve", lambda e: e.tensor_tensor(out=v3(ct2), in0=v3(ct2), in1=flBb, op=ALU.add), reads=rk + cs_("ct2"), writes=cs_("ct2"))
                            s.op("dve", lambda e: e.tensor_tensor(out=ct1[:], in0=ct1[:], in1=cm[:], op=ALU.add), reads=cs_("ct1") + cs_("cm"), writes=cs_("ct1"))
                            s.op("dve", lambda e: e.tensor_tensor(out=cm[:], in0=ct1[:], in1=ct2[:], op=ALU.max), reads=cs_("ct1") + cs_("ct2") + cs_("cm"), writes=cs_("cm"))
                            s.op("dve", lambda e: e.tensor_tensor(out=ct1[:], in0=ct1[:], in1=cm[:], op=ALU.subtract), reads=cs_("ct1") + cs_("cm"), writes=cs_("ct1"))
                            s.op("dve", lambda e: e.tensor_scalar(out=ct1[:], in0=ct1[:], scalar1=-100.0, scalar2=None, op0=ALU.max), reads=cs_("ct1"), writes=cs_("ct1"))
                            s.op("act", lambda e: e.activation(out=a1[:, k, :], in_=ct1[:], func=AF.Exp), reads=cs_("ct1"), writes=[("cs", "a1", k)])
                            s.op("dve", lambda e: e.tensor_tensor(out=ct2[:], in0=ct2[:], in1=cm[:], op=ALU.subtract), reads=cs_("ct2") + cs_("cm"), writes=cs_("ct2"))
                            s.op("dve", lambda e: e.tensor_scalar(out=ct2[:], in0=ct2[:], scalar1=-100.0, scalar2=None, op0=ALU.max), reads=cs_("ct2"), writes=cs_("ct2"))
                            s.op("act", lambda e: e.activation(out=ct2[:], in_=ct2[:], func=AF.Exp), reads=cs_("ct2"), writes=cs_("ct2"))
                            s.op("dve", lambda e: e.tensor_tensor(out=a2[:, k, :].rearrange("p (d g) -> p d g", d=2), in0=v3(ct2), in1=flb, op=ALU.mult),
                                 reads=cs_("ct2") + rk, writes=[("cs", "a2", k)])
                            rs = rk + [("cs", "fls"), ("cs", "flsB")]
                            s.op("dve", lambda e: e.tensor_tensor(out=st1[:], in0=svq[:, k, 1:2], in1=fls_sb[:, k:k + 1], op=ALU.mult), reads=rs, writes=cs_("st1"))
                            s.op("dve", lambda e: e.tensor_tensor(out=st2[:], in0=svq[:, k, 0:1], in1=fls_sb[:, k:k + 1], op=ALU.mult), reads=rs, writes=cs_("st2"))
                            s.op("dve", lambda e: e.tensor_tensor(out=st2[:], in0=st2[:], in1=flsB[:, k:k + 1], op=ALU.add), reads=rs + cs_("st2"), writes=cs_("st2"))
                            s.op("dve", lambda e: e.tensor_tensor(out=st1[:], in0=st1[:], in1=ms[:], op=ALU.add), reads=cs_("st1") + cs_("ms"), writes=cs_("st1"))
                            s.op("dve", lambda e: e.tensor_tensor(out=ms[:], in0=st1[:], in1=st2[:], op=ALU.max), reads=cs_("st1") + cs_("st2") + cs_("ms"), writes=cs_("ms"))
                        for d in range(2):
                            for g in range(4):
                                ra = rot("t2k", 6)
                                s.op("pool", lambda e: e.memset(t2k[ra][:, 0:258], 0.0), writes=[("t2k", ra)])
                                for k in range(4):
                                    qm = k if d == 0 else 3 - k
                                    rb = rot("t2k", 6)
                                    while rb == ra:
                                        rb = rot("t2k", 6)
                                    s.op(XQ, lambda e: e.dma_start(out=t2k[rb][:, 0:258], in_=(sout_f if d == 0 else sout_b)[qm * 128:(qm + 1) * 128, g * 258:(g + 1) * 258]),
                                         reads=[("summ_out",)], writes=[("t2k", rb)], dma=("xl", rb))
                                    col = d * 4 + g
                                    s.op("dve", lambda e: e.tensor_scalar(out=t2k[rb][:, 0:258], in0=t2k[rb][:, 0:258], scalar1=a2[:, k, col:col + 1], scalar2=None, op0=ALU.mult),
                                         reads=[("t2k", rb), ("cs", "a2", k)], writes=[("t2k", rb)])
                                    s.op("dve", lambda e: e.scalar_tensor_tensor(out=t2k[ra][:, 0:258], in0=t2k[ra][:, 0:258], scalar=a1[:, k, col:col + 1], in1=t2k[rb][:, 0:258],
                                                                                 op0=ALU.mult, op1=ALU.add),
                                         reads=[("t2k", ra), ("t2k", rb), ("cs", "a1", k)], writes=[("t2k", ra)])
                                s.op(XQ, lambda e: e.dma_start(out=cin_d[:, (d * 4 + g) * 258:(d * 4 + g + 1) * 258], in_=t2k[ra][:, 0:258]),
                                     reads=[("t2k", ra)], writes=[("cin", d, g)], dma=("xs", ra))
                        gate_tables(ms)

                    s.alias(["hT"], ["gt"])
                    accmode["wide"] = True
                    for g in range(4 if do_mlstm else 0):
                        q_interleave = True
                        if g > 0 or seg == 0:
                            qk_proj(g, 1)
                            v_proj(g)
                        else:
                            q_interleave = False
                        if seg == 0:
                            s.op("sp", lambda e: e.dma_start(out=S_b[:], in_=cin_d[:, (4 + g) * 258:(5 + g) * 258]), reads=[("cin", 1, g)], writes=[("S_b",)], dma="c_sb")
                        else:
                            s.op("pool", lambda e: e.memset(S_b[:], 0.0), writes=[("S_b",)])
                        for c in range(NCH - 1, -1, -1):
                            s.op("act", lambda e, c=c: e.activation(out=Cb_st[:, c, :], in_=S_b[:], func=AF.Copy, scale=decp[:, c, 1, g:g + 1]),
                                 reads=[("S_b",)] + dpkeys, writes=[("pair", "Cb", c)])
                            if c > 0:
                                q = kprime(g, c, 1)
                                state_step(g, c, 1, S_b, ("S_b",), q)
                            if q_interleave and c % 4 == 0:
                                qk_proj(g, 0, blks=[3 - c // 4])
                        if seg == 0:
                            s.op("sp", lambda e: e.dma_start(out=S_f[:], in_=cin_d[:, g * 258:(g + 1) * 258]), reads=[("cin", 0, g)], writes=[("S_f",)], dma="c_sf")
                        else:
                            s.op("pool", lambda e: e.memset(S_f[:], 0.0), writes=[("S_f",)])
                        for c in range(NCH):
                            csl = slice(c * 128, (c + 1) * 128)
                            if c % 4 == 0:
                                so = []
                                for hh in range(2):
                                    wo, kwo = load_chunk("w_o", 2 * g + hh)
                                    b = acc_bank()
                                    for kc in range(8):
                                        s.op("pe", lambda e, kc=kc, b=b, wo=wo: e.matmul(psb[b][:], lhsT=wo[:, kc * 128:(kc + 1) * 128],
                                                                                      rhs=xnT[:, kc, (c // 4) * 512:(c // 4 + 1) * 512], start=(kc == 0), stop=(kc == 7)),
                                             reads=[kwo, ("xnT", kc, c // 4)], writes=[P(b)])
                                    r = rot("t2k", 6)
                                    s.op("act", lambda e, b=b, r=r: e.activation(out=t2k[r][:], in_=psb[b][:], func=AF.Sigmoid), reads=[P(b)], writes=[("t2k", r)])
                                    so.append(r)
                            cq = rot("Cp", 3)
                            s.op("act", lambda e, c=c, cq=cq: e.activation(out=Cp[cq][:], in_=S_f[:], func=AF.Copy, scale=decp[:, c, 0, g:g + 1]),
                                 reads=[("S_f",)] + dpkeys, writes=[("Cp", cq)])
                            for hh in range(2):
                                hd = 2 * g + hh
                                rows = slice(hh * 64, (hh + 1) * 64)
                                bS = acc_bank()
                                s.op("pe", lambda e, bS=bS, rows=rows, csl=csl: e.matmul(psb[bS][:, 0:128], lhsT=kT[rows, csl], rhs=qT[rows, csl], start=True, stop=True),
                                     reads=[("pair", "qk", 0, c // 4), ("pair", "qk", 1, c // 4)], writes=[P(bS)])
                                pts = []
                                for d in range(2):
                                    pq = rot("PT", 8)
                                    s.op("dve", lambda e, bS=bS, pq=pq, d=d, hd=hd, c=c: e.scalar_tensor_tensor(out=PT[pq][:], in0=psb[bS][:, 0:128],
                                                                                                            scalar=e_tok[:, c, d * 8 + hd:d * 8 + hd + 1],
                                                                                                            in1=mask[:, d, :], op0=ALU.mult, op1=ALU.mult),
                                         reads=[P(bS), ("mask", d)] + tabkeys, writes=[("PT", pq)])
                                    pts.append(pq)
                                bO = acc_bank()
                                vrhs = v_aug[:, c, hh, :]
                                s.op("pe", lambda e, bO=bO, pq=pts[0], vrhs=vrhs: e.matmul(psb[bO][:, 0:129], lhsT=PT[pq][:], rhs=vrhs, start=True, stop=False),
                                     reads=[("PT", pts[0]), ("pair", "v", c), ("pair", "v1")], writes=[P(bO)])
                                s.op("pe", lambda e, bO=bO, rows=rows, csl=csl, cq=cq, hh=hh: e.matmul(psb[bO][:, 0:129], lhsT=qT[rows, csl],
                                                                                                   rhs=Cp[cq][rows, hh * 129:(hh + 1) * 129], start=False, stop=True),
                                     reads=[("pair", "qk", 0, c // 4), ("Cp", cq)], writes=[P(bO)])
                                s.op("pe", lambda e, bO=bO, pq=pts[1], vrhs=vrhs: e.matmul(psb[bO][:, 129:258], lhsT=PT[pq][:], rhs=vrhs, start=True, stop=False),
                                     reads=[("PT", pts[1]), ("pair", "v", c), ("pair", "v1")], writes=[P(bO)])
                                s.op("pe", lambda e, bO=bO, rows=rows, csl=csl, hh=hh, c=c: e.matmul(psb[bO][:, 129:258], lhsT=qT[rows, csl],
                                                                                                 rhs=Cb_st[rows, c, hh * 129:(hh + 1) * 129], start=False, stop=True),
                                     reads=[("pair", "qk", 0, c // 4), ("pair", "Cb", c)], writes=[P(bO)])
                                m = rot("sm", 8)
                                den = psb[bO][:, 128:258:129]
                                s.op("dve", lambda e, m=m, den=den, c=c, hd=hd: e.scalar_tensor_tensor(out=sm[m][:, 0:2], in0=den, scalar=-1.0, in1=cl_tok[:, c, hd:16:8],
                                                                                                  op0=ALU.mult, op1=ALU.max),
                                     reads=[P(bO)] + clkeys, writes=[("sm", m)])
                                s.op("dve", lambda e, m=m, den=den: e.tensor_tensor(out=sm[m][:, 0:2], in0=sm[m][:, 0:2], in1=den, op=ALU.max),
                                     reads=[P(bO), ("sm", m)], writes=[("sm", m)])
                                s.op("dve", lambda e, m=m: e.reciprocal(out=sm[m][:, 0:2], in_=sm[m][:, 0:2]), reads=[("sm", m)], writes=[("sm", m)])
                                hq = rot("hq", 3)
                                s.op("dve", lambda e, m=m, hq=hq, bO=bO: e.tensor_scalar(out=htmp[hq][:], in0=psb[bO][:, 0:128], scalar1=sm[m][:, 0:1], scalar2=None, op0=ALU.mult),
                                     reads=[P(bO), ("sm", m)], writes=[("htmp", hq)])
                                s.op("dve", lambda e, m=m, hq=hq, bO=bO: e.scalar_tensor_tensor(out=hs[hq][:], in0=psb[bO][:, 129:257], scalar=sm[m][:, 1:2], in1=htmp[hq][:],
                                                                                              op0=ALU.mult, op1=ALU.add),
                                     reads=[P(bO), ("sm", m), ("htmp", hq)], writes=[("hs", hq)])
                                s.op("act", lambda e, m=m, hq=hq: e.activation(out=hjunk[:], in_=hs[hq][:], func=AF.Square, accum_out=sm[m][:, 2:3]),
                                     reads=[("hs", hq)], writes=[("sm", m), ("hjunk",)])
                                s.op("act", lambda e, m=m: e.activation(out=sm[m][:, 3:4], in_=sm[m][:, 2:3], func=AF.Sqrt, bias=EPS, scale=1.0 / 128),
                                     reads=[("sm", m)], writes=[("sm", m)])
                                s.op("dve", lambda e, m=m: e.reciprocal(out=sm[m][:, 3:4], in_=sm[m][:, 3:4]), reads=[("sm", m)], writes=[("sm", m)])
                                s.op("dve", lambda e, m=m, hq=hq: e.tensor_scalar(out=hn[hq][:], in0=hs[hq][:], scalar1=sm[m][:, 3:4], scalar2=None, op0=ALU.mult),
                                     reads=[("hs", hq), ("sm", m)], writes=[("hn", hq)])
                                pO = psb[bO][:].bitcast(BF16)
                                s.op("pe", lambda e, pO=pO, hq=hq: e.transpose(out=pO[:, 768:896], in_=hn[hq][:], identity=ident_b[:]),
                                     reads=[("hn", hq), ("ident_b",)], writes=[P(bO)])
                                s.op("dve", lambda e, pO=pO, hd=hd, csl=csl, r=so[hh], c=c: e.scalar_tensor_tensor(out=hT[:, hd, csl], in0=pO[:, 768:896], scalar=mnorm_sb[:, hd:hd + 1],
                                                                                                              in1=t2k[r][:, (c % 4) * 128:(c % 4 + 1) * 128], op0=ALU.mult, op1=ALU.mult),
                                     reads=[P(bO), ("t2k", so[hh]), ("mnorm",)], writes=[("hT", hd, c // 4)])
                            if c < NCH - 1:
                                q = kprime(g, c, 0)
                                state_step(g, c, 0, S_f, ("S_f",), q)
                    accmode["wide"] = False
                    mixer_in = hT
                    mixer_key = "hT"
                    wout_name = "w_mout"

                wres = aview(0, 16384, BF16).rearrange("p (o k) -> p o k", o=8)
                mix = aview(16384, 16384, F32).rearrange("p (o t) -> p o t", o=8)
                s.alias(["wres", "mix"], ["xnT"])
                for oc in range(8):
                    s.op("sp", lambda e, oc=oc: e.dma_start(out=wres[:, oc, :], in_=wb[wout_name][oc]),
                         reads=[("wb", wout_name, oc)], writes=[("wres", oc)], dma=("wres", oc))
                for blk in range(NBLK):
                    cols = slice(blk * 512, (blk + 1) * 512)
                    out_proj_block(lambda oc, kc: wres[:, oc, kc * 128:(kc + 1) * 128], lambda oc: ("wres", oc), 8,
                                   lambda kc, cols=cols: mixer_in[:, kc, cols], lambda kc, blk=blk: (mixer_key, kc, blk),
                                   lambda oc: mix[:, oc, :], "mix", 0, 512, gcol(layer, 1), STAT)
                    r = rot("t2k", 6)
                    s.op("act", lambda e, r=r: e.activation(out=t2k[r][:], in_=psb[STAT][:], func=AF.Sqrt, bias=EPS, scale=1.0 / D),
                         reads=[P(STAT)], writes=[("t2k", r)])
                    s.op("dve", lambda e, r=r: e.reciprocal(out=t2k[r][:], in_=t2k[r][:]), reads=[("t2k", r)], writes=[("t2k", r)])
                    for oc in range(8):
                        t = rot("t2k", 6)
                        while t == r:
                            t = rot("t2k", 6)
                        gc = gcol(layer, 1)
                        s.op("dve", lambda e, oc=oc, t=t, r=r, gc=gc: e.scalar_tensor_tensor(out=t2k[t][:], in0=mix[:, oc, :], scalar=norms_sb[:, gc + oc:gc + oc + 1],
                                                                                       in1=t2k[r][:], op0=ALU.mult, op1=ALU.mult),
                             reads=[("mix", oc, 0), ("t2k", r), ("norms",)], writes=[("t2k", t)])
                        s.op("dve", lambda e, oc=oc, t=t, cols=cols: e.tensor_tensor(out=xT[:, oc, cols], in0=xT[:, oc, cols], in1=t2k[t][:], op=ALU.add),
                             reads=[("t2k", t), xkey(oc, blk)], writes=[xkey(oc, blk)])

                xn2 = aview(0, 16384, BF16).rearrange("p (c t) -> p c t", c=8)
                ybuf = aview(0, 32768, F32).rearrange("p (o t) -> p o t", o=8)
                act = aview(32800, 45056, BF16).rearrange("p (k t) -> p k t", k=KF)
                for half in range(2):
                    s.alias(["xn2", "act"], ["wres", "mix", "zT", "hT", "y", "xn2", "act", "xnT", "c_sb", "u_sb", "gt", "pair"])
                    for sub in range(2):
                        blk = half * 2 + sub
                        rms_T(lambda c, blk=blk: xT[:, c, blk * 512:(blk + 1) * 512], lambda c, blk=blk: xkey(c, blk), 512, gcol(layer, 2),
                              lambda c, sub=sub: xn2[:, c, sub * 512:(sub + 1) * 512], lambda c, sub=sub: ("xn2", c, sub))
                    for j in range(KF):
                        if seg == 0 and (half * KF + j) % 3 == 0:
                            flush_cast(1)
                        wg_, kwg = load_chunk("w_f1", layer * 44 + 2 * j)
                        wu_, kwu = load_chunk("w_f1", layer * 44 + 2 * j + 1)
                        for sub in range(2):
                            cols = slice(sub * 512, (sub + 1) * 512)
                            b = acc_bank()
                            for kc in range(8):
                                s.op("pe", lambda e, kc=kc, b=b, cols=cols, wg_=wg_: e.matmul(psb[b][:], lhsT=wg_[:, kc * 128:(kc + 1) * 128], rhs=xn2[:, kc, cols],
                                                                                           start=(kc == 0), stop=(kc == 7)),
                                     reads=[kwg, ("xn2", kc, sub)], writes=[P(b)])
                            r = rot("t2k", 6)
                            s.op("act", lambda e, b=b, r=r: e.activation(out=t2k[r][:], in_=psb[b][:], func=AF.Silu), reads=[P(b)], writes=[("t2k", r)])
                            b2 = acc_bank()
                            for kc in range(8):
                                s.op("pe", lambda e, kc=kc, b2=b2, cols=cols, wu_=wu_: e.matmul(psb[b2][:], lhsT=wu_[:, kc * 128:(kc + 1) * 128], rhs=xn2[:, kc, cols],
                                                                                             start=(kc == 0), stop=(kc == 7)),
                                     reads=[kwu, ("xn2", kc, sub)], writes=[P(b2)])
                            s.op("dve", lambda e, b2=b2, r=r, j=j, cols=cols: e.tensor_tensor(out=act[:, j, cols], in0=psb[b2][:], in1=t2k[r][:], op=ALU.mult),
                                 reads=[P(b2), ("t2k", r)], writes=[("act", j, sub)])
                    s.alias(["y"], ["xn2"])
                    for oc in range(8):
                        w2, kw2 = load_big("w_f2", layer * 8 + oc, DFF)
                        for sub in range(2):
                            cols = slice(sub * 512, (sub + 1) * 512)
                            statbank = STAT if sub == 0 else MISC
                            b = acc_bank()
                            for kc in range(KF):
                                s.op("pe", lambda e, kc=kc, b=b, cols=cols, w2=w2: e.matmul(psb[b][:], lhsT=w2[:, kc * 128:(kc + 1) * 128], rhs=act[:, kc, cols],
                                                                                         start=(kc == 0), stop=(kc == KF - 1)),
                                     reads=[kw2, ("act", kc, sub)], writes=[P(b)])
                            s.op("act", lambda e, b=b, oc=oc, cols=cols: e.copy(out=ybuf[:, oc, cols], in_=psb[b][:]), reads=[P(b)], writes=[("y", oc, sub)])
                            q = rot("sqb", 2)
                            s.op("pool", lambda e, oc=oc, q=q, cols=cols: e.tensor_tensor(out=sqb[q][:], in0=ybuf[:, oc, cols], in1=ybuf[:, oc, cols], op=ALU.mult),
                                 reads=[("y", oc, sub)], writes=[("sqb", q)])
                            s.op("pe", lambda e, oc=oc, q=q, statbank=statbank: e.matmul(psb[statbank][:], lhsT=ones_b[:], rhs=sqb[q][:], start=(oc == 0), stop=(oc == 7)),
                                 reads=[("sqb", q), ("ones_b",)], writes=[P(statbank)])
                    for sub in range(2):
                        blk = half * 2 + sub
                        cols = slice(sub * 512, (sub + 1) * 512)
                        xcols = slice(blk * 512, (blk + 1) * 512)
                        statbank = STAT if sub == 0 else MISC
                        r = rot("t2k", 6)
                        s.op("act", lambda e, r=r, statbank=statbank: e.activation(out=t2k[r][:], in_=psb[statbank][:], func=AF.Sqrt, bias=EPS, scale=1.0 / D),
                             reads=[P(statbank)], writes=[("t2k", r)])
                        s.op("dve", lambda e, r=r: e.reciprocal(out=t2k[r][:], in_=t2k[r][:]), reads=[("t2k", r)], writes=[("t2k", r)])
                        gc = gcol(layer, 3)
                        for oc in range(8):
                            t = rot("t2k", 6)
                            while t == r:
                                t = rot("t2k", 6)
                            s.op("dve", lambda e, oc=oc, t=t, r=r, gc=gc, cols=cols: e.scalar_tensor_tensor(out=t2k[t][:], in0=ybuf[:, oc, cols], scalar=norms_sb[:, gc + oc:gc + oc + 1],
                                                                                                    in1=t2k[r][:], op0=ALU.mult, op1=ALU.mult),
                                 reads=[("y", oc, sub), ("t2k", r), ("norms",)], writes=[("t2k", t)])
                            s.op("dve", lambda e, oc=oc, t=t, xcols=xcols: e.tensor_tensor(out=xT[:, oc, xcols], in0=xT[:, oc, xcols], in1=t2k[t][:], op=ALU.add),
                                 reads=[("t2k", t), xkey(oc, blk)], writes=[xkey(oc, blk)])
                    if layer == n_layers - 1:
                        store_blocks(seg, [half * 2, half * 2 + 1])
                        if seg + 1 < n_seg:
                            load_blocks(seg + 1, [half * 2, half * 2 + 1])
                            if half == 1:
                                load_halo(seg + 1)

            if n_layers == 0:
                store_blocks(seg, range(NBLK))
                if seg + 1 < n_seg:
                    load_blocks(seg + 1, range(NBLK))
                    load_halo(seg + 1)
        s.emit(st)
        import os
        if os.environ.get("KDEBUG"):
            print("SCHED", s.stats)
    return nc


def _chunks(W, col_lists):
    K = W.shape[0]
    kcn = K // 128
    out = []
    for cols in col_lists:
        sub = W[:, cols]
        w = sub.shape[1]
        out.append(sub.reshape(kcn, 128, w).transpose(1, 0, 2).reshape(128, kcn * w))
    return np.ascontiguousarray(np.stack(out, 0), dtype=np.float32)


def prep_weights(inp):
    r = lambda a, b: list(range(a, b))
    cw_in = inp["conv_w_in"][0]
    cin_lists = []
    for cc in range(8):
        cin_lists += [r(1024 + cc * 128, 1024 + (cc + 1) * 128), r(2048 + cc * 128, 2048 + (cc + 1) * 128), r(cc * 128, (cc + 1) * 128)]
    w = {}
    w["w_cin"] = _chunks(cw_in, cin_lists)
    w["w_cout"] = _chunks(inp["conv_w_out"][0], [r(o * 128, (o + 1) * 128) for o in range(8)])
    mw = inp["mlstm_w_in"][0]
    qk_lists = []
    for g in range(4):
        qk_lists += [r(g * 128, (g + 1) * 128), r(512 + g * 128, 512 + (g + 1) * 128)]
    w["w_qk"] = _chunks(mw, qk_lists)
    w["w_o"] = _chunks(mw, [r(2048 + h * 128, 2048 + (h + 1) * 128) for h in range(8)])
    w["w_v"] = _chunks(mw, [r(1024 + g * 256, 1024 + (g + 1) * 256) for g in range(4)])
    gcols = mw[:, 3072:3104]
    gi = np.zeros((1024, 40), np.float32)
    gf = np.zeros((1024, 40), np.float32)
    gi[:, 0:8] = gcols[:, 0:8]
    gf[:, 0:8] = gcols[:, 8:16]
    gi[:, 32:40] = gcols[:, 16:24]
    gf[:, 32:40] = gcols[:, 24:32]
    wgi = _chunks(gi, [r(0, 40)])[0]
    wgf = _chunks(gf, [r(0, 40)])[0]
    w["w_g"] = np.ascontiguousarray(np.concatenate([wgi, wgf], axis=1)[None], dtype=np.float32)
    w["w_mout"] = _chunks(inp["mlstm_w_out"][0], [r(o * 128, (o + 1) * 128) for o in range(8)])
    f1 = []
    for l in range(2):
        lists = []
        for j in range(KF):
            lists += [r(j * 128, (j + 1) * 128), r(DFF + j * 128, DFF + (j + 1) * 128)]
        f1.append(_chunks(inp["ffn_w_in"][l], lists))
    w["w_f1"] = np.ascontiguousarray(np.concatenate(f1, 0))
    f2 = [_chunks(inp["ffn_w_out"][l], [r(o * 128, (o + 1) * 128) for o in range(8)]) for l in range(2)]
    w["w_f2"] = np.ascontiguousarray(np.concatenate(f2, 0))
    nr = inp["norms"].reshape(8, 8, 128)
    w["norms_t"] = np.ascontiguousarray(nr.transpose(2, 0, 1).reshape(128, 64), dtype=np.float32)
    cw = inp["conv_w"][0].reshape(3, 8, 128)
    w["convw_t"] = np.ascontiguousarray(cw.transpose(2, 1, 0).reshape(128, 24), dtype=np.float32)
    w["mnorm_t"] = np.ascontiguousarray(inp["mlstm_norm"][0].reshape(8, 128).T, dtype=np.float32)
    bg = inp["mlstm_b_gate"][0]
    bt = np.zeros((40, 2), np.float32)
    bt[0:8, 0] = bg[0:8]
    bt[0:8, 1] = bg[8:16]
    bt[32:40, 0] = bg[16:24]
    bt[32:40, 1] = bg[24:32]
    w["bgate_t"] = bt
    return w


def kernel(x_prompt, x_sample, norms, conv_w_in, conv_w, conv_w_out, mlstm_w_in, mlstm_b_gate,
           mlstm_norm, mlstm_w_out, ffn_w_in, ffn_w_out, _n_layers=2, _do_mlstm=True, _n_seg=NSEG):
    inp = dict(norms=np.asarray(norms, np.float32), conv_w_in=np.asarray(conv_w_in, np.float32),
               conv_w=np.asarray(conv_w, np.float32), conv_w_out=np.asarray(conv_w_out, np.float32),
               mlstm_w_in=np.asarray(mlstm_w_in, np.float32), mlstm_b_gate=np.asarray(mlstm_b_gate, np.float32),
               mlstm_norm=np.asarray(mlstm_norm, np.float32), mlstm_w_out=np.asarray(mlstm_w_out, np.float32),
               ffn_w_in=np.asarray(ffn_w_in, np.float32), ffn_w_out=np.asarray(ffn_w_out, np.float32))
    xp = np.asarray(x_prompt, np.float32)
    xs = np.asarray(x_sample, np.float32)
    w = prep_weights(inp)
    in_maps = []
    for r in range(NCORES):
        b, qd = r // 4, r % 4
        xin = np.empty((NSEG, T, D), np.float32)
        halo = np.zeros((NSEG, 2, D), np.float32)
        xin[0] = xp[b, qd * T:(qd + 1) * T]
        if qd > 0:
            halo[0, 0] = xp[b, qd * T - 1]
        if qd < 3:
            halo[0, 1] = xp[b, (qd + 1) * T]
        xin[1] = xs[2 * r]
        xin[2] = xs[2 * r + 1]
        m = dict(w)
        flp = np.zeros((128, 4, 2), np.float32)
        fls = np.zeros((40, 4), np.float32)
        for k in range(4):
            ff = 1.0 if k < qd else 0.0
            fb = 1.0 if (3 - k) > qd else 0.0
            flp[:, k, 0] = ff
            flp[:, k, 1] = fb
            fls[0:8, k] = ff
            fls[32:40, k] = fb
        m["flp"] = flp.reshape(128, 8)
        m["fls"] = fls
        m["xin"] = xin
        m["halo"] = halo
        in_maps.append(m)
    nc = build_program(n_layers=_n_layers, do_mlstm=_do_mlstm, n_seg=_n_seg)
    res = run_bass_kernel_spmd(nc, in_maps, core_ids=list(range(NCORES)))
    y_prompt = np.empty_like(xp)
    y_sample = np.empty_like(xs)
    for r in range(NCORES):
        y = res.results[r]["yout"]
        b, qd = r // 4, r % 4
        y_prompt[b, qd * T:(qd + 1) * T] = y[0]
        y_sample[2 * r] = y[1]
        y_sample[2 * r + 1] = y[2]
    return (y_prompt, y_sample)
```

```python
import contextlib
import numpy as np
import concourse.bass as bass
import concourse.mybir as mybir
from concourse.bass_utils import run_bass_kernel_spmd

F32 = mybir.dt.float32
BF16 = mybir.dt.bfloat16
AF = mybir.ActivationFunctionType
ALU = mybir.AluOpType
AX = mybir.AxisListType

D = 1024
T = 2048
NSEG = 3
NBLK = 4
NCH = 16
DFF = 2816
KF = 22
EPS = 1e-6
NEG = -1.0e30
NCORES = 8


class _Rec:
    def __getattr__(self, name):
        def f(*a, **kw):
            self.call = (name, a, kw)
            return self
        return f


class Sched:
    ENGS = ("pe", "act", "dve", "pool", "sp")

    def __init__(self, nc):
        self.nc = nc
        self.ops = []
        self.lastw = {}
        self.readers = {}
        self.dma_keys = []
        self.pending = {}
        self.touched = set()
        self.tags = []
        self.reorder = True
        self.last_pe = None
        self.window = 64
        import os
        self.reorder_engs = tuple(os.environ.get("KREORDER", "pe,act,dve,pool").split(","))
        self.xlat = 1000.0

    def alias(self, new_names, old_names):
        old = set(old_names)
        dset = set()
        for k, w in self.lastw.items():
            if k[0] in old:
                dset.add(w)
        for k, rs in self.readers.items():
            if k[0] in old:
                dset.update(rs)
        for n in new_names:
            self.pending[n] = set(self.pending.get(n, set())) | dset
            self.touched = {k for k in self.touched if k[0] != n}

    @staticmethod
    def _cost(eng, name, a, kw, dma):
        out = kw.get("out", a[0] if a else None)
        try:
            shp = out.shape
            free = 1
            for d_ in shp[1:]:
                free *= d_
        except Exception:
            free = 512
        if name == "collective_compute":
            return (500.0, 40000.0)
        if dma is not None:
            try:
                nbytes = out.nbytes()
            except Exception:
                nbytes = free * 4 * 128
            return (150.0, 2500.0 + nbytes / 120.0)
        if eng == "pe":
            return (max(64, free) * 0.50 + 35.0, 250.0)
        if eng == "act":
            return (210.0 + free * 0.65, 250.0)
        if eng == "dve":
            return (90.0 + free * 0.95, 250.0)
        if eng == "pool":
            return (250.0 + free * 2.6, 300.0)
        return (100.0, 100.0)

    def op(self, eng, fn, reads=(), writes=(), dma=None, inc=16, after=()):
        deps = set(after)
        for k in list(reads) + list(writes):
            if k[0] in self.pending and k not in self.touched:
                deps |= self.pending[k[0]]
                self.touched.add(k)
        for k in reads:
            w = self.lastw.get(k)
            if w is not None:
                deps.add(w)
        for k in writes:
            w = self.lastw.get(k)
            if w is not None:
                deps.add(w)
            for r in self.readers.get(k, ()):
                deps.add(r)
        i = len(self.ops)
        rec = _Rec()
        fn(rec)
        name, a, kw = rec.call
        fn = (lambda e, name=name, a=a, kw=kw: getattr(e, name)(*a, **kw))
        self.ops.append(dict(eng=eng, fn=fn, deps=deps, dma=dma, inc=inc))
        if eng == "pe":
            self.last_pe = i
        if dma is not None and dma not in self.dma_keys:
            self.dma_keys.append(dma)
        tag = ("d", dma) if dma is not None else ("e", eng)
        order = set()
        for k in reads:
            lst = self.readers.setdefault(k, [])
            for r in lst:
                if self.tags[r] == tag:
                    order.add(r)
            lst[:] = [r for r in lst if self.tags[r] != tag]
            lst.append(i)
        self.tags.append(tag)
        self.ops[i]["order"] = order
        self.ops[i]["cost"] = self._cost(eng, name, a, kw, dma)
        for k in writes:
            self.lastw[k] = i
            self.readers[k] = []
        return i

    def _list_schedule(self):
        ops = self.ops
        n = len(ops)
        full = {e: [] for e in self.ENGS}
        for i, o in enumerate(ops):
            full[o["eng"]].append(i)
        nxt = {e: 0 for e in self.ENGS}
        win = {e: [] for e in self.ENGS}
        done = [False] * n
        fin = [0.0] * n
        free_t = {e: 0.0 for e in self.ENGS}
        out = {e: [] for e in self.ENGS}
        preds = [list(o["deps"] | o["order"]) for o in ops]
        succ_eng = [set() for _ in range(n)]
        for i, o in enumerate(ops):
            for d in preds[i]:
                succ_eng[d].add(o["eng"])
        W = self.window
        xlat = self.xlat

        def refill(e):
            Wl = W if e in self.reorder_engs else 1
            w = win[e]
            f = full[e]
            while len(w) < Wl and nxt[e] < len(f):
                w.append(f[nxt[e]])
                nxt[e] += 1

        def best(e):
            bi = None
            bt = None
            ft = free_t[e]
            for i in win[e]:
                ok = True
                rt = 0.0
                for d in preds[i]:
                    if not done[d]:
                        ok = False
                        break
                    od = ops[d]
                    t = fin[d] + (0.0 if (od["eng"] == e and od["dma"] is None) else xlat)
                    if t > rt:
                        rt = t
                if not ok:
                    continue
                st = rt if rt > ft else ft
                if bt is None or st < bt - 1e-9:
                    bi, bt = i, st
                    if st <= ft + 1e-9:
                        break
            return bi, bt

        for e in self.ENGS:
            refill(e)
        cand = {}
        remaining = n
        while remaining:
            choice = None
            for e in self.ENGS:
                if not win[e]:
                    continue
                if e not in cand:
                    cand[e] = best(e)
                bi, bt = cand[e]
                if bi is None:
                    continue
                if choice is None or bt < choice[1]:
                    choice = (bi, bt, e)
            assert choice is not None, "scheduler deadlock"
            i, st, e = choice
            busy, lat = ops[i]["cost"]
            free_t[e] = st + busy
            fin[i] = st + busy + lat
            done[i] = True
            win[e].remove(i)
            refill(e)
            out[e].append(i)
            remaining -= 1
            cand.pop(e, None)
            for e2 in succ_eng[i]:
                cand.pop(e2, None)
        self.sim_time = max(fin) if fin else 0.0
        return out

    def emit(self, stack):
        nc = self.nc
        ops = self.ops
        per_eng_sched = self._list_schedule() if self.reorder else None
        needed = [False] * len(ops)
        for o in ops:
            if o["eng"] == "pe":
                o["wdeps"] = {d for d in o["deps"] if not (ops[d]["eng"] == "pe" and ops[d]["dma"] is None)}
            else:
                o["wdeps"] = o["deps"]
            for d in o["wdeps"]:
                needed[d] = True
        esem = {e: stack.enter_context(nc.semaphore("s_" + e)) for e in self.ENGS}
        dsem = {k: stack.enter_context(nc.semaphore("d_%d" % i)) for i, k in enumerate(self.dma_keys)}
        cnt = {e: 0 for e in self.ENGS}
        dcnt = {k: 0 for k in self.dma_keys}
        token = [None] * len(ops)
        per_eng = {e: [] for e in self.ENGS}
        if per_eng_sched is not None:
            seq = [i for e in self.ENGS for i in per_eng_sched[e]]
        else:
            seq = list(range(len(ops)))
        for i in seq:
            o = ops[i]
            per_eng[o["eng"]].append(i)
            if o["dma"] is not None:
                dcnt[o["dma"]] += o["inc"]
                token[i] = (("d", o["dma"]), dsem[o["dma"]], dcnt[o["dma"]])
            elif needed[i]:
                cnt[o["eng"]] += 1
                token[i] = (("e", o["eng"]), esem[o["eng"]], cnt[o["eng"]])
        self.stats = dict(sim_ms=getattr(self, "sim_time", 0.0) / 1e6, nops=len(ops), cnt=dict(cnt), ndma=len(self.dma_keys),
                          per_eng={e: len(v) for e, v in per_eng.items()})
        self._last_order = per_eng
        self._last_token = token
        block = stack.enter_context(nc.Block())
        handles = {"pe": block.tensor, "act": block.scalar, "dve": block.vector,
                   "pool": block.gpsimd, "sp": block.sync}
        final_d = dict(dcnt)
        final_e = dict(cnt)

        def make_body(e):
            def body(eng):
                seen = {}
                for i in per_eng[e]:
                    o = ops[i]
                    waits = {}
                    for d in o["wdeps"]:
                        t = token[d]
                        if t is None:
                            continue
                        name, sem, val = t
                        if seen.get(name, 0) >= val:
                            continue
                        if name not in waits or waits[name][1] < val:
                            waits[name] = (sem, val)
                    if o["dma"] is not None:
                        name, sem, val = token[i]
                        prev = val - o["inc"]
                        if prev > 0 and seen.get(name, 0) < prev:
                            waits[name] = (sem, prev)
                    for name, (sem, val) in waits.items():
                        eng.wait_ge(sem, val)
                        seen[name] = val
                    ins = o["fn"](eng)
                    t = token[i]
                    if t is not None:
                        ins.then_inc(t[1], o["inc"] if o["dma"] is not None else 1)
                if e == "sp":
                    for k, v in final_d.items():
                        if v:
                            eng.wait_ge(dsem[k], v)
                    for e2, v in final_e.items():
                        if v and e2 != "sp":
                            eng.wait_ge(esem[e2], v)
            return body

        for e in self.ENGS:
            handles[e](make_body(e))


def build_program(n_layers=2, do_mlstm=True, n_seg=NSEG):
    nc = bass.Bass("TRN2", target_bir_lowering=False)
    dr = lambda name, shape, dt=F32, kind="ExternalInput": nc.dram_tensor(name, shape, dt, kind=kind).ap()
    xin = dr("xin", [NSEG, T, D])
    halo = dr("halo", [NSEG, 2, D])
    yout = dr("yout", [NSEG, T, D], kind="ExternalOutput")
    wspec = {
        "w_cin": (24, 1024), "w_cout": (8, 1024), "w_qk": (8, 1024), "w_o": (8, 1024),
        "w_v": (4, 2048), "w_g": (1, 640), "w_mout": (8, 1024),
        "w_f1": (88, 1024), "w_f2": (16, DFF),
    }
    wf = {k: dr(k, [n, 128, w]) for k, (n, w) in wspec.items()}
    wb = {k: nc.dram_tensor("b" + k, [n, 128, w], BF16).ap() for k, (n, w) in wspec.items()}
    norms_d = dr("norms_t", [128, 64])
    convw_d = dr("convw_t", [128, 24])
    mnorm_d = dr("mnorm_t", [128, 8])
    bgate_d = dr("bgate_t", [40, 2])
    flp_d = dr("flp", [128, 8])
    fls_d = dr("fls", [40, 4])
    summ_f = nc.dram_tensor("summ_f", [128, 1032], F32).ap()
    summ_b = nc.dram_tensor("summ_b", [128, 1056], F32).ap()
    sout_f = nc.dram_tensor("sout_f", [512, 1032], F32).ap()
    sout_b = nc.dram_tensor("sout_b", [512, 1056], F32).ap()
    cin_d = nc.dram_tensor("cin_d", [128, 2064], F32).ap()

    st = contextlib.ExitStack()
    with st:
        SB = lambda name, shape, dt: st.enter_context(nc.sbuf_tensor(name, shape, dt))
        s = Sched(nc)
        xT = SB("xT", [128, 8, T], F32)
        ARENA_W = 22568
        arena = SB("arena", [128, ARENA_W], F32)

        def aview(off_b, nbytes, dt):
            a = arena[:, off_b // 4:(off_b + nbytes) // 4]
            return a.bitcast(dt) if dt != F32 else a

        wchunk = [SB("wch%d" % i, [128, 1024], BF16) for i in range(4)]
        wbig = [SB("wbig%d" % i, [128, DFF], BF16) for i in range(2)]
        wg_sb = SB("wg_sb", [128, 640], BF16)
        t2k = [SB("t2k%d" % i, [128, 512], F32) for i in range(6)]
        sqb = [SB("sqb%d" % i, [128, 512], BF16) for i in range(2)]
        ident_f = SB("ident_f", [128, 128], F32)
        ident_b = SB("ident_b", [128, 128], BF16)
        ones_b = SB("ones_b", [128, 128], BF16)
        ones_f = SB("ones_f", [128, 128], F32)
        mask = SB("mask", [128, 2, 128], F32)
        norms_sb = SB("norms_sb", [128, 64], F32)
        convw_sb = SB("convw_sb", [128, 24], F32)
        mnorm_sb = SB("mnorm_sb", [128, 8], F32)
        bgate_sb = SB("bgate_sb", [40, 2], F32)
        negbf = SB("negbf", [40, 1], F32)
        sel = SB("sel", [40, 16], F32)
        xTh = SB("xTh", [128, 8, 2], F32)
        e_tok = SB("e_tok", [128, NCH, 16], F32)
        cl_tok = SB("cl_tok", [128, NCH, 16], F32)
        dec_rep = SB("dec_rep", [128, NCH, 16], F32)
        decp = SB("decp", [128, NCH, 2, 4], F32)
        g_amax = SB("g_amax", [40, NCH], F32)
        g_bn = SB("g_bn", [40, NCH], F32)
        g_m = SB("g_m", [40, NCH], F32)
        g_mp = SB("g_mp", [40, NCH], F32)
        g_mout = SB("g_mout", [40, 1], F32)
        g_dec = SB("g_dec", [40, NCH], F32)
        g_bd = SB("g_bd", [40, NCH, 16], F32)
        PT = [SB("PT%d" % i, [128, 128], BF16) for i in range(8)]
        kp = [SB("kp%d" % i, [128, 128], BF16) for i in range(6)]
        Cp = [SB("Cp%d" % i, [128, 258], BF16) for i in range(3)]
        S_f = SB("S_f", [128, 258], F32)
        S_b = SB("S_b", [128, 258], F32)
        htmp = [SB("htmp%d" % i, [128, 128], F32) for i in range(3)]
        hs = [SB("hs%d" % i, [128, 128], F32) for i in range(3)]
        hn = [SB("hn%d" % i, [128, 128], BF16) for i in range(3)]
        hjunk = SB("hjunk", [128, 128], BF16)
        sm = [SB("sm%d" % i, [128, 8], F32) for i in range(8)]
        flp_sb = SB("flp_sb", [128, 8], F32)
        flpB = SB("flpB", [128, 8], F32)
        fls_sb = SB("fls_sb", [40, 4], F32)
        flsB = SB("flsB", [40, 4], F32)
        g_sv = SB("g_sv", [40, 2], F32)
        g_bd2 = SB("g_bd2", [40, 2, 16], F32)
        svrep = SB("svrep", [128, 32], F32)
        sc_p = SB("sc_p", [128, 16], F32)
        mq = SB("mq", [128, 4, 8], F32)
        Fq = SB("Fq", [128, 4, 8], F32)
        a1 = SB("a1", [128, 4, 8], F32)
        a2 = SB("a2", [128, 4, 8], F32)
        cm = SB("cm", [128, 8], F32)
        ct1 = SB("ct1", [128, 8], F32)
        ct2 = SB("ct2", [128, 8], F32)
        svq = SB("svq", [40, 4, 2], F32)
        ms = SB("ms", [40, 1], F32)
        st1 = SB("st1", [40, 1], F32)
        st2 = SB("st2", [40, 1], F32)

        psb = [st.enter_context(nc.psum_tensor("psb%d" % i, [128, 512], F32)) for i in range(8)]

        rr = {}

        def rot(name, n):
            i = rr.get(name, 0)
            rr[name] = i + 1
            return i % n

        accmode = {"wide": False}

        def acc_bank():
            if accmode["wide"]:
                return (0, 1, 2, 3, 4, 5, 7)[rot("accw", 7)]
            return rot("acc", 5)
        STAT = 5
        MISC = 6
        MISC2 = 7

        def P(b):
            return ("ps", b)

        s.op("pool", lambda e: e.memset(ident_f[:], 0.0), writes=[("ident_f",)])
        s.op("pool", lambda e: e.affine_select(out=ident_f[:], in_=ident_f[:], pattern=[[-1, 128]],
                                               compare_op=ALU.not_equal, fill=1.0, base=0, channel_multiplier=1),
             reads=[("ident_f",)], writes=[("ident_f",)])
        s.op("pool", lambda e: e.tensor_copy(out=ident_b[:], in_=ident_f[:]), reads=[("ident_f",)], writes=[("ident_b",)])
        s.op("pool", lambda e: e.memset(ones_b[:], 1.0), writes=[("ones_b",)])
        s.op("pool", lambda e: e.memset(ones_f[:], 1.0), writes=[("ones_f",)])
        s.op("pool", lambda e: e.affine_select(out=mask[:, 0, :], in_=ones_f[:], pattern=[[1, 128]],
                                               compare_op=ALU.is_ge, fill=0.0, base=0, channel_multiplier=-1),
             reads=[("ones_f",)], writes=[("mask", 0)])
        s.op("pool", lambda e: e.affine_select(out=mask[:, 1, :], in_=ones_f[:], pattern=[[-1, 128]],
                                               compare_op=ALU.is_ge, fill=0.0, base=0, channel_multiplier=1),
             reads=[("ones_f",)], writes=[("mask", 1)])
        s.op("pool", lambda e: e.tensor_copy(out=sel[:, 0:8], in_=ident_f[0:40, 0:8]), reads=[("ident_f",)], writes=[("sel", 0)])
        s.op("pool", lambda e: e.tensor_copy(out=sel[:, 8:16], in_=ident_f[0:40, 32:40]), reads=[("ident_f",)], writes=[("sel", 1)])
        s.op("pool", lambda e: e.memset(g_m[:], 0.0), writes=[("g", "m")])
        s.op("pool", lambda e: e.memset(g_mout[:], 0.0), writes=[("g", "mp")])
        s.op("pool", lambda e: e.memset(g_amax[:], 0.0), writes=[("g", "amax")])
        s.op("pool", lambda e: e.memset(g_bn[:], 0.0), writes=[("g", "bn")])
        s.op("sp", lambda e: e.dma_start(out=norms_sb[:], in_=norms_d), writes=[("norms",)], dma="c_norms")
        s.op("sp", lambda e: e.dma_start(out=convw_sb[:], in_=convw_d), writes=[("convw",)], dma="c_convw")
        s.op("sp", lambda e: e.dma_start(out=mnorm_sb[:], in_=mnorm_d), writes=[("mnorm",)], dma="c_mnorm")
        s.op("sp", lambda e: e.dma_start(out=bgate_sb[:], in_=bgate_d), writes=[("bgate",)], dma="c_bgate")
        s.op("dve", lambda e: e.tensor_scalar(out=negbf[:], in0=bgate_sb[:, 1:2], scalar1=-1.0, scalar2=None, op0=ALU.mult),
             reads=[("bgate",)], writes=[("negbf",)])

        cast_order = ["w_cin", "w_cout", "w_f1:0", "w_f2:0", "w_g", "w_qk", "w_v", "w_o", "w_mout", "w_f1:1", "w_f2:1"]
        cast_pieces = []
        for item in cast_order:
            if ":" in item:
                nm, l = item.split(":")
                l = int(l)
                n = wspec[nm][0] // 2
                lo, hi = l * n, (l + 1) * n
            else:
                nm = item
                lo, hi = 0, wspec[nm][0]
            step = 8 if wspec[nm][1] <= 1024 else 4
            for a in range(lo, hi, step):
                cast_pieces.append((nm, a, min(hi, a + step)))

        def flush_cast(n=1, gate=True):
            for _ in range(n):
                if not cast_pieces:
                    return
                nm, a, b = cast_pieces.pop(0)
                aft = [s.last_pe] if (gate and s.last_pe is not None) else []
                s.op("pool", lambda e: e.dma_start(out=wb[nm][a:b], in_=wf[nm][a:b]),
                     writes=[("wb", nm, j) for j in range(a, b)], dma=("cast", nm, a), after=aft)

        flush_cast(2, gate=False)

        def load_chunk(nm, j):
            sl = rot("wch", 4)
            s.op("sp", lambda e: e.dma_start(out=wchunk[sl][:], in_=wb[nm][j]),
                 reads=[("wb", nm, j)], writes=[("wch", sl)], dma=("wch", sl))
            return wchunk[sl], ("wch", sl)

        def load_big(nm, j, width):
            sl = rot("wbig", 2)
            s.op("sp", lambda e: e.dma_start(out=wbig[sl][:, 0:width], in_=wb[nm][j]),
                 reads=[("wb", nm, j)], writes=[("wbig", sl)], dma=("wbig", sl))
            return wbig[sl], ("wbig", sl)

        gcol = lambda l, n: (l * 4 + n) * 8

        def rms_T(src, srckeys, n, gc, dst, dstkeys, npart=128):
            for c in range(8):
                q = rot("sqb", 2)
                s.op("act", lambda e, c=c, q=q: e.activation(out=sqb[q][:, 0:n], in_=src(c), func=AF.Square),
                     reads=[srckeys(c)], writes=[("sqb", q)])
                s.op("pe", lambda e, c=c, q=q: e.matmul(psb[STAT][:, 0:n], lhsT=ones_b[:], rhs=sqb[q][:, 0:n],
                                                        start=(c == 0), stop=(c == 7)),
                     reads=[("sqb", q), ("ones_b",)], writes=[P(STAT)])
            r = rot("t2k", 6)
            s.op("act", lambda e: e.activation(out=t2k[r][:, 0:n], in_=psb[STAT][:, 0:n], func=AF.Sqrt, bias=EPS, scale=1.0 / D),
                 reads=[P(STAT)], writes=[("t2k", r)])
            s.op("dve", lambda e: e.reciprocal(out=t2k[r][:, 0:n], in_=t2k[r][:, 0:n]), reads=[("t2k", r)], writes=[("t2k", r)])
            for c in range(8):
                s.op("dve", lambda e, c=c: e.scalar_tensor_tensor(out=dst(c), in0=src(c), scalar=norms_sb[:, gc + c:gc + c + 1],
                                                                  in1=t2k[r][:, 0:n], op0=ALU.mult, op1=ALU.mult),
                     reads=[srckeys(c), ("t2k", r), ("norms",)], writes=[dstkeys(c)])

        def out_proj_block(wres, wreskeys, nk, rhs, rhskeys, ybuf, ykey, col0, n, gc, statbank):
            for oc in range(8):
                b = acc_bank()
                for kc in range(nk):
                    s.op("pe", lambda e, oc=oc, kc=kc, b=b: e.matmul(psb[b][:, 0:n], lhsT=wres(oc, kc), rhs=rhs(kc),
                                                                     start=(kc == 0), stop=(kc == nk - 1)),
                         reads=[wreskeys(oc), rhskeys(kc)], writes=[P(b)])
                s.op("act", lambda e, oc=oc, b=b: e.copy(out=ybuf(oc), in_=psb[b][:, 0:n]), reads=[P(b)], writes=[(ykey, oc, col0)])
                q = rot("sqb", 2)
                s.op("pool", lambda e, oc=oc, q=q: e.tensor_tensor(out=sqb[q][:, 0:n], in0=ybuf(oc), in1=ybuf(oc), op=ALU.mult),
                     reads=[(ykey, oc, col0)], writes=[("sqb", q)])
                s.op("pe", lambda e, oc=oc, q=q: e.matmul(psb[statbank][:, 0:n], lhsT=ones_b[:], rhs=sqb[q][:, 0:n],
                                                          start=(oc == 0), stop=(oc == 7)),
                     reads=[("sqb", q), ("ones_b",)], writes=[P(statbank)])

        def resid_block(ybuf, ykey, col0, n, gc, statbank):
            r = rot("t2k", 6)
            s.op("act", lambda e: e.activation(out=t2k[r][:, 0:n], in_=psb[statbank][:, 0:n], func=AF.Sqrt, bias=EPS, scale=1.0 / D),
                 reads=[P(statbank)], writes=[("t2k", r)])
            s.op("dve", lambda e: e.reciprocal(out=t2k[r][:, 0:n], in_=t2k[r][:, 0:n]), reads=[("t2k", r)], writes=[("t2k", r)])
            for oc in range(8):
                t = rot("t2k", 6)
                while t == r:
                    t = rot("t2k", 6)
                s.op("dve", lambda e, oc=oc, t=t: e.scalar_tensor_tensor(out=t2k[t][:, 0:n], in0=ybuf(oc),
                                                                         scalar=norms_sb[:, gc + oc:gc + oc + 1],
                                                                         in1=t2k[r][:, 0:n], op0=ALU.mult, op1=ALU.mult),
                     reads=[(ykey, oc, col0), ("t2k", r), ("norms",)], writes=[("t2k", t)])
                s.op("pool", lambda e, oc=oc, t=t: e.tensor_tensor(out=xT[:, oc, col0:col0 + n], in0=xT[:, oc, col0:col0 + n],
                                                                   in1=t2k[t][:, 0:n], op=ALU.add),
                     reads=[("t2k", t), ("xT", oc, col0 // 512)], writes=[("xT", oc, col0 // 512)])

        xkey = lambda c, blk: ("xT", c, blk)

        import os as _os
        XQ = _os.environ.get("KXQ", "pool")

        def load_blocks(seg, blks):
            for blk in blks:
                for tt in range(blk * 4, blk * 4 + 4):
                    for hf in range(2):
                        r = rot("t2k", 6)
                        s.op(XQ, lambda e: e.dma_start(out=t2k[r][:], in_=xin[seg, tt * 128:(tt + 1) * 128, hf * 512:(hf + 1) * 512]),
                             writes=[("t2k", r)], dma=("xl", r))
                        bk = acc_bank()
                        for q in range(4):
                            s.op("pe", lambda e: e.transpose(out=psb[bk][:, q * 128:(q + 1) * 128], in_=t2k[r][:, q * 128:(q + 1) * 128], identity=ident_f[:]),
                                 reads=[("t2k", r), ("ident_f",)], writes=[P(bk)])
                        outap = xT[:, hf * 4:(hf + 1) * 4, tt * 128:(tt + 1) * 128]
                        inap = psb[bk][:].rearrange("p (q t) -> p q t", q=4)
                        wk = [xkey(hf * 4 + q, tt // 4) for q in range(4)]
                        if (tt + hf) % 2 == 0:
                            s.op("act", lambda e: e.copy(out=outap, in_=inap), reads=[P(bk)], writes=wk)
                        else:
                            s.op("dve", lambda e: e.tensor_copy(out=outap, in_=inap), reads=[P(bk)], writes=wk)

        def load_halo(seg):
            for hf in range(2):
                r = rot("t2k", 6)
                s.op(XQ, lambda e: e.dma_start(out=t2k[r][0:2, :], in_=halo[seg, :, hf * 512:(hf + 1) * 512]), writes=[("t2k", r)], dma=("xl", r))
                for q in range(4):
                    c = hf * 4 + q
                    s.op("pe", lambda e: e.transpose(out=psb[MISC][:, c * 2:(c + 1) * 2], in_=t2k[r][0:2, q * 128:(q + 1) * 128],
                                                     identity=ident_f[0:2, 0:2]),
                         reads=[("t2k", r), ("ident_f",)], writes=[P(MISC)])
            s.op("act", lambda e: e.copy(out=xTh[:], in_=psb[MISC][:, 0:16].rearrange("p (c t) -> p c t", t=2)), reads=[P(MISC)], writes=[("xTh",)])

        def store_blocks(seg, blks):
            for blk in blks:
                for tt in range(blk * 4, blk * 4 + 4):
                    for hf in range(2):
                        bk = acc_bank()
                        for q in range(4):
                            c = hf * 4 + q
                            s.op("pe", lambda e: e.transpose(out=psb[bk][:, q * 128:(q + 1) * 128], in_=xT[:, c, tt * 128:(tt + 1) * 128], identity=ident_f[:]),
                                 reads=[xkey(c, tt // 4), ("ident_f",)], writes=[P(bk)])
                        r = rot("t2k", 6)
                        if (tt + hf) % 2 == 0:
                            s.op("act", lambda e: e.copy(out=t2k[r][:], in_=psb[bk][:]), reads=[P(bk)], writes=[("t2k", r)])
                        else:
                            s.op("dve", lambda e: e.tensor_copy(out=t2k[r][:], in_=psb[bk][:]), reads=[P(bk)], writes=[("t2k", r)])
                        s.op(XQ, lambda e: e.dma_start(out=yout[seg, tt * 128:(tt + 1) * 128, hf * 512:(hf + 1) * 512], in_=t2k[r][:]),
                             reads=[("t2k", r)], writes=[("yout", seg, tt, hf)], dma=("xs", r))

        for seg in range(n_seg):
            if seg == 0:
                load_blocks(0, range(NBLK))
                load_halo(0)

            for layer in range(n_layers):
                if layer == 0:
                    xnT = aview(0, 32800, BF16).rearrange("p (c t) -> p c t", c=8)
                    zT = aview(32800, 32768, BF16).rearrange("p (c t) -> p c t", c=8)
                    c_sb = aview(65568, 8200, F32)
                    u_sb = aview(73768, 8200, F32)
                    s.alias(["xnT", "zT", "c_sb", "u_sb"], ["xnT", "zT", "c_sb", "u_sb", "wres", "mix", "xn2", "act", "y", "hT", "gt", "pair"])
                    for blk in range(NBLK):
                        rms_T(lambda c, blk=blk: xT[:, c, blk * 512:(blk + 1) * 512], lambda c, blk=blk: xkey(c, blk), 512, gcol(0, 0),
                              lambda c, blk=blk: xnT[:, c, 1 + blk * 512:1 + (blk + 1) * 512], lambda c, blk=blk: ("xnT", c, blk))
                    rms_T(lambda c: xTh[:, c, :], lambda c: ("xTh",), 2, gcol(0, 0),
                          lambda c: xnT[:, c, 0:2050:2049], lambda c: ("xnT", c, "h"))
                    for cc in range(8):
                        if seg == 0:
                            flush_cast(1)
                        wc, kwc = load_chunk("w_cin", 3 * cc + 0)
                        wv_, kwv = load_chunk("w_cin", 3 * cc + 1)
                        wb_, kwb = load_chunk("w_cin", 3 * cc + 2)
                        for blk in range(NBLK):
                            b = acc_bank()
                            for kc in range(8):
                                s.op("pe", lambda e, kc=kc, b=b, blk=blk, wc=wc: e.matmul(psb[b][:], lhsT=wc[:, kc * 128:(kc + 1) * 128],
                                                                                       rhs=xnT[:, kc, 1 + blk * 512:1 + (blk + 1) * 512],
                                                                                       start=(kc == 0), stop=(kc == 7)),
                                     reads=[kwc, ("xnT", kc, blk)], writes=[P(b)])
                            s.op("act", lambda e, b=b, blk=blk: e.copy(out=c_sb[:, 1 + blk * 512:1 + (blk + 1) * 512], in_=psb[b][:]),
                                 reads=[P(b)], writes=[("c_sb", blk)])
                        for kc in range(8):
                            s.op("pe", lambda e, kc=kc, wc=wc: e.matmul(psb[MISC][:, 0:2], lhsT=wc[:, kc * 128:(kc + 1) * 128], rhs=xnT[:, kc, 0:2050:2049],
                                                                     start=(kc == 0), stop=(kc == 7)),
                                 reads=[kwc, ("xnT", kc, "h")], writes=[P(MISC)])
                        s.op("act", lambda e: e.copy(out=c_sb[:, 0:2050:2049], in_=psb[MISC][:, 0:2]), reads=[P(MISC)], writes=[("c_sb", "h")])
                        for blk in range(NBLK):
                            b = acc_bank()
                            for kc in range(8):
                                s.op("pe", lambda e, kc=kc, b=b, blk=blk, wv_=wv_: e.matmul(psb[b][:], lhsT=wv_[:, kc * 128:(kc + 1) * 128],
                                                                                         rhs=xnT[:, kc, 1 + blk * 512:1 + (blk + 1) * 512],
                                                                                         start=(kc == 0), stop=(kc == 7)),
                                     reads=[kwv, ("xnT", kc, blk)], writes=[P(b)])
                            s.op("dve", lambda e, b=b, blk=blk: e.tensor_tensor(out=u_sb[:, 1 + blk * 512:1 + (blk + 1) * 512], in0=psb[b][:],
                                                                               in1=c_sb[:, 1 + blk * 512:1 + (blk + 1) * 512], op=ALU.mult),
                                 reads=[P(b), ("c_sb", blk)], writes=[("u_sb", blk)])
                        for kc in range(8):
                            s.op("pe", lambda e, kc=kc, wv_=wv_: e.matmul(psb[MISC][:, 0:2], lhsT=wv_[:, kc * 128:(kc + 1) * 128], rhs=xnT[:, kc, 0:2050:2049],
                                                                       start=(kc == 0), stop=(kc == 7)),
                                 reads=[kwv, ("xnT", kc, "h")], writes=[P(MISC)])
                        s.op("dve", lambda e: e.tensor_tensor(out=u_sb[:, 0:2050:2049], in0=psb[MISC][:, 0:2], in1=c_sb[:, 0:2050:2049], op=ALU.mult),
                             reads=[P(MISC), ("c_sb", "h")], writes=[("u_sb", "h")])
                        ukeys = [("u_sb", k) for k in (0, 1, 2, 3, "h")]
                        ckeys = [("c_sb", k) for k in (0, 1, 2, 3, "h")]
                        cw = lambda j, cc=cc: convw_sb[:, cc * 3 + j:cc * 3 + j + 1]
                        s.op("act", lambda e, cw=cw: e.activation(out=c_sb[:, 1:2049], in_=u_sb[:, 0:2048], func=AF.Copy, scale=cw(0)),
                             reads=ukeys + [("convw",)], writes=ckeys)
                        s.op("dve", lambda e, cw=cw: e.scalar_tensor_tensor(out=c_sb[:, 1:2049], in0=u_sb[:, 1:2049], scalar=cw(1), in1=c_sb[:, 1:2049],
                                                                          op0=ALU.mult, op1=ALU.add),
                             reads=ukeys + ckeys + [("convw",)], writes=ckeys)
                        s.op("dve", lambda e, cw=cw: e.scalar_tensor_tensor(out=c_sb[:, 1:2049], in0=u_sb[:, 2:2050], scalar=cw(2), in1=c_sb[:, 1:2049],
                                                                          op0=ALU.mult, op1=ALU.add),
                             reads=ukeys + ckeys + [("convw",)], writes=ckeys)
                        for blk in range(NBLK):
                            b = acc_bank()
                            for kc in range(8):
                                s.op("pe", lambda e, kc=kc, b=b, blk=blk, wb_=wb_: e.matmul(psb[b][:], lhsT=wb_[:, kc * 128:(kc + 1) * 128],
                                                                                         rhs=xnT[:, kc, 1 + blk * 512:1 + (blk + 1) * 512],
                                                                                         start=(kc == 0), stop=(kc == 7)),
                                     reads=[kwb, ("xnT", kc, blk)], writes=[P(b)])
                            s.op("dve", lambda e, b=b, blk=blk, cc=cc: e.tensor_tensor(out=zT[:, cc, blk * 512:(blk + 1) * 512], in0=psb[b][:],
                                                                                      in1=c_sb[:, 1 + blk * 512:1 + (blk + 1) * 512], op=ALU.mult),
                                 reads=[P(b)] + ckeys, writes=[("zT", cc, blk)])
                    mixer_in = zT
                    mixer_key = "zT"
                    wout_name = "w_cout"
                else:
                    if seg == 0:
                        flush_cast(100)
                        s.op("sp", lambda e: e.dma_start(out=wg_sb[:], in_=wb["w_g"][0]), reads=[("wb", "w_g", 0)], writes=[("wg",)], dma="c_wg")
                    xnT = aview(0, 32768, BF16).rearrange("p (c t) -> p c t", c=8)
                    hT = aview(32800, 32768, BF16).rearrange("p (c t) -> p c t", c=8)
                    s.alias(["xnT", "hT", "gt", "pair"], ["xnT", "zT", "c_sb", "u_sb", "wres", "mix", "xn2", "act", "y", "hT", "gt", "pair"])
                    for blk in range(NBLK):
                        rms_T(lambda c, blk=blk: xT[:, c, blk * 512:(blk + 1) * 512], lambda c, blk=blk: xkey(c, blk), 512, gcol(1, 0),
                              lambda c, blk=blk: xnT[:, c, blk * 512:(blk + 1) * 512], lambda c, blk=blk: ("xnT", c, blk))
                    GI = aview(32800, 8192, F32)
                    CS = aview(32800 + 8192, 8192, F32)
                    SP = aview(32800 + 16384, 8192, F32)
                    GI3 = GI.rearrange("p (c t) -> p c t", t=128)
                    SP3 = SP.rearrange("p (c t) -> p c t", t=128)
                    CS3 = CS.rearrange("p (c t) -> p c t", t=128)
                    for blk in range(NBLK):
                        cols = slice(blk * 512, (blk + 1) * 512)
                        for gi in range(2):
                            b = acc_bank()
                            for kc in range(8):
                                s.op("pe", lambda e: e.matmul(psb[b][0:40, :], lhsT=wg_sb[:, gi * 320 + kc * 40:gi * 320 + (kc + 1) * 40],
                                                              rhs=xnT[:, kc, cols], start=(kc == 0), stop=(kc == 7)),
                                     reads=[("wg",), ("xnT", kc, blk)], writes=[P(b)])
                            if gi == 0:
                                s.op("act", lambda e: e.activation(out=GI[0:40, cols], in_=psb[b][0:40, :], func=AF.Identity, bias=bgate_sb[:, 0:1]),
                                     reads=[P(b), ("bgate",)], writes=[("gt", "GI")])
                            else:
                                s.op("act", lambda e: e.activation(out=SP[0:40, cols], in_=psb[b][0:40, :], func=AF.Exp, bias=negbf[:, 0:1], scale=-1.0),
                                     reads=[P(b), ("negbf",)], writes=[("gt", "SP")])
                    s.op("act", lambda e: e.activation(out=SP[0:40, :], in_=SP[0:40, :], func=AF.Ln, bias=1.0), reads=[("gt", "SP")], writes=[("gt", "SP")])
                    for c in range(NCH):
                        s.op("dve", lambda e: e.tensor_tensor_scan(out=CS[0:40, c * 128:(c + 1) * 128], data0=ones_f[0:40, :], data1=SP[0:40, c * 128:(c + 1) * 128],
                                                                   initial=0.0, op0=ALU.mult, op1=ALU.add),
                             reads=[("gt", "SP"), ("ones_f",)], writes=[("gt", "CS")])
                    s.op("dve", lambda e: e.tensor_copy(out=g_bn[:], in_=CS3[0:40, :, 127]), reads=[("gt", "CS")], writes=[("g", "bn")])
                    s.op("dve", lambda e: e.tensor_tensor(out=CS3[32:40], in0=g_bn[32:40, :].unsqueeze(2).broadcast_to([8, NCH, 128]), in1=CS3[32:40], op=ALU.subtract),
                         reads=[("gt", "CS"), ("g", "bn")], writes=[("gt", "CS")])
                    s.op("dve", lambda e: e.tensor_tensor(out=CS[32:40, :], in0=CS[32:40, :], in1=SP[32:40, :], op=ALU.add),
                         reads=[("gt", "CS"), ("gt", "SP")], writes=[("gt", "CS")])
                    s.op("dve", lambda e: e.tensor_tensor(out=GI[0:40, :], in0=GI[0:40, :], in1=CS[0:40, :], op=ALU.add),
                         reads=[("gt", "GI"), ("gt", "CS")], writes=[("gt", "GI")])
                    s.op("dve", lambda e: e.tensor_reduce(out=g_amax[:], in_=GI3[0:40], axis=AX.X, op=ALU.max), reads=[("gt", "GI")], writes=[("g", "amax")])
                    tabkeys = [("e_tok", h_, d_) for h_ in range(2) for d_ in range(2)]
                    clkeys = [("cl_tok", h_, d_) for h_ in range(2) for d_ in range(2)]
                    dpkeys = [("decp", 0), ("decp", 1)]

                    def gate_tables(m_in):
                        s.op("dve", lambda e: e.memset(g_mp[:], NEG), writes=[("g", "mp")])
                        if m_in is not None:
                            s.op("dve", lambda e: e.tensor_copy(out=g_mp[0:8, 0:1], in_=m_in[0:8, :]), reads=[("cs", "ms"), ("g", "mp")], writes=[("g", "mp")])
                            s.op("dve", lambda e: e.tensor_copy(out=g_mp[32:40, NCH - 1:NCH], in_=m_in[32:40, :]), reads=[("cs", "ms"), ("g", "mp")], writes=[("g", "mp")])
                        for c in range(NCH):
                            s.op("dve", lambda e: e.tensor_tensor(out=g_m[0:8, c:c + 1], in0=g_mp[0:8, c:c + 1], in1=g_amax[0:8, c:c + 1], op=ALU.max),
                                 reads=[("g", "mp"), ("g", "amax")], writes=[("g", "m")])
                            dst = g_mp[0:8, c + 1:c + 2] if c < NCH - 1 else g_mout[0:8, :]
                            s.op("dve", lambda e: e.tensor_tensor(out=dst, in0=g_m[0:8, c:c + 1], in1=g_bn[0:8, c:c + 1], op=ALU.subtract),
                                 reads=[("g", "m"), ("g", "bn")], writes=[("g", "mp")])
                        for c in range(NCH - 1, -1, -1):
                            s.op("dve", lambda e: e.tensor_tensor(out=g_m[32:40, c:c + 1], in0=g_mp[32:40, c:c + 1], in1=g_amax[32:40, c:c + 1], op=ALU.max),
                                 reads=[("g", "mp"), ("g", "amax")], writes=[("g", "m")])
                            dst = g_mp[32:40, c - 1:c] if c > 0 else g_mout[32:40, :]
                            s.op("dve", lambda e: e.tensor_tensor(out=dst, in0=g_m[32:40, c:c + 1], in1=g_bn[32:40, c:c + 1], op=ALU.subtract),
                                 reads=[("g", "m"), ("g", "bn")], writes=[("g", "mp")])
                        s.op("dve", lambda e: e.tensor_tensor(out=g_dec[:], in0=g_mp[:], in1=g_m[:], op=ALU.subtract), reads=[("g", "mp"), ("g", "m")], writes=[("g", "dec")])
                        s.op("dve", lambda e: e.tensor_scalar(out=g_dec[:], in0=g_dec[:], scalar1=-100.0, scalar2=None, op0=ALU.max), reads=[("g", "dec")], writes=[("g", "dec")])
                        s.op("act", lambda e: e.activation(out=g_dec[:], in_=g_dec[:], func=AF.Exp), reads=[("g", "dec")], writes=[("g", "dec")])
                        for src3, skey, dstt, dkey in ((GI3, ("gt", "GI"), e_tok, "e_tok"), (CS3, ("gt", "CS"), cl_tok, "cl_tok")):
                            s.op("dve", lambda e: e.tensor_tensor(out=SP3[0:40], in0=src3[0:40], in1=g_m[:].unsqueeze(2).broadcast_to([40, NCH, 128]), op=ALU.subtract),
                                 reads=[skey, ("g", "m"), ("gt", "SP")], writes=[("gt", "SP")])
                            s.op("act", lambda e: e.activation(out=SP[0:40, :], in_=SP[0:40, :], func=AF.Exp), reads=[("gt", "SP")], writes=[("gt", "SP")])
                            for half in range(2):
                                for c8 in range(8):
                                    c = half * 8 + c8
                                    s.op("pe", lambda e: e.transpose(out=psb[MISC][:, c8 * 40:(c8 + 1) * 40], in_=SP[0:40, c * 128:(c + 1) * 128],
                                                                     identity=ident_f[0:40, 0:40]),
                                         reads=[("gt", "SP"), ("ident_f",)], writes=[P(MISC)])
                                m3 = psb[MISC][:, 0:320].rearrange("p (c j) -> p c j", j=40)
                                for d in range(2):
                                    s.op("act", lambda e: e.copy(out=dstt[:, half * 8:(half + 1) * 8, d * 8:(d + 1) * 8], in_=m3[:, :, d * 32:d * 32 + 8]),
                                         reads=[P(MISC)], writes=[(dkey, half, d)])
                        s.op("dve", lambda e: e.tensor_tensor(out=g_bd[:], in0=g_dec[:].unsqueeze(2).broadcast_to([40, NCH, 16]),
                                                              in1=sel[:].unsqueeze(1).broadcast_to([40, NCH, 16]), op=ALU.mult),
                             reads=[("g", "dec"), ("sel", 0), ("sel", 1)], writes=[("g", "bd")])
                        s.op("pe", lambda e: e.matmul(psb[MISC2][:, 0:256], lhsT=ones_f[0:40, :], rhs=g_bd[:].rearrange("p c j -> p (c j)"), start=True, stop=True),
                             reads=[("g", "bd"), ("ones_f",)], writes=[P(MISC2)])
                        s.op("act", lambda e: e.copy(out=dec_rep[:].rearrange("p c j -> p (c j)"), in_=psb[MISC2][:, 0:256]), reads=[P(MISC2)], writes=[("dec_rep",)])
                        dr4 = dec_rep[:].rearrange("p c (d g two) -> p c d g two", d=2, two=2)
                        s.op("dve", lambda e: e.tensor_copy(out=decp[0:64], in_=dr4[0:64, :, :, :, 0]), reads=[("dec_rep",)], writes=[("decp", 0)])
                        s.op("dve", lambda e: e.tensor_copy(out=decp[64:128], in_=dr4[64:128, :, :, :, 1]), reads=[("dec_rep",)], writes=[("decp", 1)])

                    PB = 65568
                    qT = aview(PB, 4096, BF16)
                    kT = aview(PB + 4096, 4096, BF16)
                    v_aug = aview(PB + 8192, 8256, BF16).rearrange("p (c h f) -> p c h f", c=NCH, h=2)
                    Cb_st = aview(PB + 8192 + 8256, 8256, BF16).rearrange("p (c f) -> p c f", c=NCH)

                    qstate = {}

                    def qk_proj(g, which, blks=None):
                        dst, sc = ((qT, 0.125), (kT, 1.0))[which]
                        if blks is None or (g, which) not in qstate:
                            qstate[(g, which)] = load_chunk("w_qk", 2 * g + which)
                        wq, kwq = qstate[(g, which)]
                        for blk in (range(NBLK) if blks is None else blks):
                            b = acc_bank()
                            for kc in range(8):
                                s.op("pe", lambda e: e.matmul(psb[b][:], lhsT=wq[:, kc * 128:(kc + 1) * 128],
                                                              rhs=xnT[:, kc, blk * 512:(blk + 1) * 512], start=(kc == 0), stop=(kc == 7)),
                                     reads=[kwq, ("xnT", kc, blk)], writes=[P(b)])
                            s.op("act", lambda e: e.activation(out=dst[:, blk * 512:(blk + 1) * 512], in_=psb[b][:], func=AF.Copy, scale=sc),
                                 reads=[P(b)], writes=[("pair", "qk", which, blk)])

                    def v_proj(g):
                        wv_, kwv = load_big("w_v", g, 2048)
                        s.op("pool", lambda e: e.memset(v_aug[:, :, :, 128:129], 1.0), writes=[("pair", "v1")])
                        for c in range(NCH):
                            b = acc_bank()
                            for kc in range(8):
                                s.op("pe", lambda e: e.matmul(psb[b][:, 0:256], lhsT=xnT[:, kc, c * 128:(c + 1) * 128],
                                                              rhs=wv_[:, kc * 256:(kc + 1) * 256], start=(kc == 0), stop=(kc == 7)),
                                     reads=[kwv, ("xnT", kc, c // 4)], writes=[P(b)])
                            s.op("act", lambda e: e.copy(out=v_aug[:, c, :, 0:128], in_=psb[b][:, 0:256].rearrange("p (h f) -> p h f", h=2)),
                                 reads=[P(b)], writes=[("pair", "v", c)])

                    def kprime(g, c, d):
                        pbank = psb[MISC][:].bitcast(BF16)
                        s.op("pe", lambda e: e.transpose(out=pbank[:, 0:128], in_=kT[:, c * 128:(c + 1) * 128], identity=ident_b[:]),
                             reads=[("pair", "qk", 1, c // 4), ("ident_b",)], writes=[P(MISC)])
                        q = rot("kp", 6)
                        s.op("dve", lambda e: e.tensor_tensor(out=kp[q][:].rearrange("p (h f) -> p h f", h=2),
                                                              in0=pbank[:, 0:128].rearrange("p (h f) -> p h f", h=2),
                                                              in1=e_tok[:, c, d * 8 + 2 * g:d * 8 + 2 * g + 2].unsqueeze(2).broadcast_to([128, 2, 64]),
                                                              op=ALU.mult),
                             reads=[P(MISC)] + tabkeys, writes=[("kp", q)])
                        return q

                    def state_step(g, c, d, S, Skey, q):
                        b = acc_bank()
                        s.op("pe", lambda e: e.matmul(psb[b][:, 0:258], lhsT=kp[q][:], rhs=v_aug[:, c, :, :].rearrange("p h f -> p (h f)"), start=True, stop=True),
                             reads=[("kp", q), ("pair", "v", c), ("pair", "v1")], writes=[P(b)])
                        s.op("dve", lambda e: e.scalar_tensor_tensor(out=S[:], in0=S[:], scalar=decp[:, c, d, g:g + 1], in1=psb[b][:, 0:258],
                                                                     op0=ALU.mult, op1=ALU.add),
                             reads=[P(b), Skey] + dpkeys, writes=[Skey])

                    if do_mlstm:
                        qk_proj(0, 1)
                        v_proj(0)
                        if seg != 0:
                            qk_proj(0, 0)
                    gate_tables(None)
                    if seg == 0 and do_mlstm:
                        for g in range(4):
                            if g > 0:
                                qk_proj(g, 1)
                                v_proj(g)
                            s.op("pool", lambda e: e.memset(S_b[:], 0.0), writes=[("S_b",)])
                            for c in range(NCH - 1, -1, -1):
                                q = kprime(g, c, 1)
                                state_step(g, c, 1, S_b, ("S_b",), q)
                            s.op("sp", lambda e: e.dma_start(out=summ_b[:, g * 258:(g + 1) * 258], in_=S_b[:]), reads=[("S_b",)], writes=[("summ_in", 4 + g)], dma="sm")
                            s.op("pool", lambda e: e.memset(S_f[:], 0.0), writes=[("S_f",)])
                            for c in range(NCH):
                                q = kprime(g, c, 0)
                                state_step(g, c, 0, S_f, ("S_f",), q)
                            s.op("sp", lambda e: e.dma_start(out=summ_f[:, g * 258:(g + 1) * 258], in_=S_f[:]), reads=[("S_f",)], writes=[("summ_in", g)], dma="sm")
                        s.op("dve", lambda e: e.tensor_copy(out=g_sv[:, 0:1], in_=g_mout[:]), reads=[("g", "mp")], writes=[("cs", "sv")])
                        s.op("dve", lambda e: e.tensor_reduce(out=g_sv[:, 1:2], in_=g_bn[:], axis=AX.X, op=ALU.add, negate=True), reads=[("g", "bn"), ("cs", "sv")], writes=[("cs", "sv")])
                        s.op("dve", lambda e: e.tensor_tensor(out=g_bd2[:], in0=g_sv[:].unsqueeze(2).broadcast_to([40, 2, 16]),
                                                              in1=sel[:].unsqueeze(1).broadcast_to([40, 2, 16]), op=ALU.mult),
                             reads=[("cs", "sv"), ("sel", 0), ("sel", 1)], writes=[("cs", "bd2")])
                        s.op("pe", lambda e: e.matmul(psb[MISC2][:, 0:32], lhsT=ones_f[0:40, :], rhs=g_bd2[:].rearrange("p a j -> p (a j)"), start=True, stop=True),
                             reads=[("cs", "bd2"), ("ones_f",)], writes=[P(MISC2)])
                        s.op("act", lambda e: e.copy(out=svrep[:], in_=psb[MISC2][:, 0:32]), reads=[P(MISC2)], writes=[("cs", "svrep")])
                        sv4 = svrep[:].rearrange("p (a d g two) -> p a d g two", a=2, d=2, two=2)
                        scp4 = sc_p[:].rearrange("p (a d g) -> p a d g", a=2, d=2)
                        s.op("dve", lambda e: e.tensor_copy(out=scp4[0:64], in_=sv4[0:64, :, :, :, 0]), reads=[("cs", "svrep")], writes=[("cs", "scp", 0)])
                        s.op("dve", lambda e: e.tensor_copy(out=scp4[64:128], in_=sv4[64:128, :, :, :, 1]), reads=[("cs", "svrep")], writes=[("cs", "scp", 1)])
                        s.op("sp", lambda e: e.dma_start(out=summ_b[:, 1032:1048], in_=sc_p[:]), reads=[("cs", "scp", 0), ("cs", "scp", 1)], writes=[("summ_in", 8)], dma="sm")
                        s.op("pool", lambda e: e.memset(ct1[:], 0.0), writes=[("cs", "ct1")])
                        s.op("sp", lambda e: e.dma_start(out=summ_b[:, 1048:1056], in_=ct1[:]), reads=[("cs", "ct1")], writes=[("summ_in", 9)], dma="sm")
                        s.op("sp", lambda e: e.dma_start(out=summ_b[0:40, 1048:1050], in_=g_sv[:]), reads=[("cs", "sv")], writes=[("summ_in", 9)], dma="sm")
                        s.op("pool", lambda e: e.collective_compute("AllGather", ALU.bypass, replica_groups=[[0, 1, 2, 3], [4, 5, 6, 7]],
                                                                    ins=[summ_f], outs=[sout_f]),
                             reads=[("summ_in", j) for j in range(10)], writes=[("summ_out",)], dma="ag", inc=1)
                        s.op("pool", lambda e: e.collective_compute("AllGather", ALU.bypass, replica_groups=[[0, 1, 2, 3], [4, 5, 6, 7]],
                                                                    ins=[summ_b], outs=[sout_b]),
                             reads=[("summ_in", j) for j in range(10)], writes=[("summ_out",)], dma="ag2", inc=1)
                        s.op("sp", lambda e: e.dma_start(out=flp_sb[:], in_=flp_d), writes=[("cs", "flp")], dma="c_flp")
                        s.op("sp", lambda e: e.dma_start(out=fls_sb[:], in_=fls_d), writes=[("cs", "fls")], dma="c_fls")
                        s.op("dve", lambda e: e.tensor_scalar(out=flpB[:], in0=flp_sb[:], scalar1=-NEG, scalar2=NEG, op0=ALU.mult, op1=ALU.add), reads=[("cs", "flp")], writes=[("cs", "flpB")])
                        s.op("dve", lambda e: e.tensor_scalar(out=flsB[:], in0=fls_sb[:], scalar1=-NEG, scalar2=NEG, op0=ALU.mult, op1=ALU.add), reads=[("cs", "fls")], writes=[("cs", "flsB")])
                        s.op("dve", lambda e: e.memset(svq[:], 0.0), writes=[("cs", "svq")])
                        for k in range(4):
                            for d in range(2):
                                qm = k if d == 0 else 3 - k
                                s.op("sp", lambda e: e.dma_start(out=mq[:, k, d * 4:(d + 1) * 4], in_=sout_b[qm * 128:(qm + 1) * 128, 1032 + d * 4:1032 + (d + 1) * 4]),
                                     reads=[("summ_out",)], writes=[("cs", "mq", k, d)], dma="cq")
                                s.op("sp", lambda e: e.dma_start(out=Fq[:, k, d * 4:(d + 1) * 4], in_=sout_b[qm * 128:(qm + 1) * 128, 1040 + d * 4:1040 + (d + 1) * 4]),
                                     reads=[("summ_out",)], writes=[("cs", "Fq", k, d)], dma="cq")
                                r0 = d * 32
                                s.op("sp", lambda e: e.dma_start(out=svq[r0:r0 + 8, k, :], in_=sout_b[qm * 128 + r0:qm * 128 + r0 + 8, 1048:1050]),
                                     reads=[("summ_out",), ("cs", "svq")], writes=[("cs", "svq")], dma="cq")
                        cs_ = lambda nm: [("cs", nm)]
                        s.op("dve", lambda e: e.memset(cm[:], NEG), writes=cs_("cm"))
                        s.op("dve", lambda e: e.memset(ms[:], NEG), writes=cs_("ms"))
                        v3 = lambda t: t[:].rearrange("p (d g) -> p d g", d=2)
                        for k in range(4):
                            flb = flp_sb[:, k * 2:(k + 1) * 2].unsqueeze(2).broadcast_to([128, 2, 4])
                            flBb = flpB[:, k * 2:(k + 1) * 2].unsqueeze(2).broadcast_to([128, 2, 4])
                            mqk = mq[:, k, :].rearrange("p (d g) -> p d g", d=2)
                            Fqk = Fq[:, k, :].rearrange("p (d g) -> p d g", d=2)
                            rk = [("cs", "mq", k_, d_) for k_ in range(4) for d_ in range(2)] + [("cs", "Fq", k_, d_) for k_ in range(4) for d_ in range(2)] + [("cs", "svq"), ("cs", "flp"), ("cs", "flpB")]
                            s.op("dve", lambda e: e.tensor_tensor(out=v3(ct1), in0=Fqk, in1=flb, op=ALU.mult), reads=rk, writes=cs_("ct1"))
                            s.op("dve", lambda e: e.tensor_tensor(out=v3(ct2), in0=mqk, in1=flb, op=ALU.mult), reads=rk, writes=cs_("ct2"))
                            s.op("dve", lambda e: e.tensor_tensor(out=v3(ct2), in0=v3(ct2), in1=flBb, op=ALU.add), reads=rk + cs_("ct2"), writes=cs_("ct2"))
                            s.op("dve", lambda e: e.tensor_tensor(out=ct1[:], in0=ct1[:], in1=cm[:], op=ALU.add), reads=cs_("ct1") + cs_("cm"), writes=cs_("ct1"))
                            s.op("dve", lambda e: e.tensor_tensor(out=cm[:], in0=ct1[:], in1=ct2[:], op=ALU.max), reads=cs_("ct1") + cs_("ct2") + cs_("cm"), writes=cs_("cm"))
                            s.op("dve", lambda e: e.tensor_tensor(out=ct1[:], in0=ct1[:], in1=cm[:], op=ALU.subtract), reads=cs_("ct1") + cs_("cm"), writes=cs_("ct1"))
                            s.op("dve", lambda e: e.tensor_scalar(out=ct1[:], in0=ct1[:], scalar1=-100.0, scalar2=None, op0=ALU.max), reads=cs_("ct1"), writes=cs_("ct1"))
                            s.op("act", lambda e: e.activation(out=a1[:, k, :], in_=ct1[:], func=AF.Exp), reads=cs_("ct1"), writes=[("cs", "a1", k)])
                            s.op("dve", lambda e: e.tensor_tensor(out=ct2[:], in0=ct2[:], in1=cm[:], op=ALU.subtract), reads=cs_("ct2") + cs_("cm"), writes=cs_("ct2"))
                            s.op("dve", lambda e: e.tensor_scalar(out=ct2[:], in0=ct2[:], scalar1=-100.0, scalar2=None, op0=ALU.max), reads=cs_("ct2"), writes=cs_("ct2"))
                            s.op("act", lambda e: e.activation(out=ct2[:], in_=ct2[:], func=AF.Exp), reads=cs_("ct2"), writes=cs_("ct2"))
                            s.op("dve", lambda e: e.tensor_tensor(out=a2[:, k, :].rearrange("p (d g) -> p d g", d=2), in0=v3(ct2), in1=flb, op=ALU.mult),
                                 reads=cs_("ct2") + rk, writes=[("cs", "a2", k)])
                            rs = rk + [("cs", "fls"), ("cs", "flsB")]
                            s.op("dve", lambda e: e.tensor_tensor(out=st1[:], in0=svq[:, k, 1:2], in1=fls_sb[:, k:k + 1], op=ALU.mult), reads=rs, writes=cs_("st1"))
                            s.op("dve", lambda e: e.tensor_tensor(out=st2[:], in0=svq[:, k, 0:1], in1=fls_sb[:, k:k + 1], op=ALU.mult), reads=rs, writes=cs_("st2"))
                            s.op("dve", lambda e: e.tensor_tensor(out=st2[:], in0=st2[:], in1=flsB[:, k:k + 1], op=ALU.add), reads=rs + cs_("st2"), writes=cs_("st2"))
                            s.op("dve", lambda e: e.tensor_tensor(out=st1[:], in0=st1[:], in1=ms[:], op=ALU.add), reads=cs_("st1") + cs_("ms"), writes=cs_("st1"))
                            s.op("dve", lambda e: e.tensor_tensor(out=ms[:], in0=st1[:], in1=st2[:], op=ALU.max), reads=cs_("st1") + cs_("st2") + cs_("ms"), writes=cs_("ms"))
                        for d in range(2):
                            for g in range(4):
                                ra = rot("t2k", 6)
                                s.op("pool", lambda e: e.memset(t2k[ra][:, 0:258], 0.0), writes=[("t2k", ra)])
                                for k in range(4):
                                    qm = k if d == 0 else 3 - k
                                    rb = rot("t2k", 6)
                                    while rb == ra:
                                        rb = rot("t2k", 6)
                                    s.op(XQ, lambda e: e.dma_start(out=t2k[rb][:, 0:258], in_=(sout_f if d == 0 else sout_b)[qm * 128:(qm + 1) * 128, g * 258:(g + 1) * 258]),
                                         reads=[("summ_out",)], writes=[("t2k", rb)], dma=("xl", rb))
                                    col = d * 4 + g
                                    s.op("dve", lambda e: e.tensor_scalar(out=t2k[rb][:, 0:258], in0=t2k[rb][:, 0:258], scalar1=a2[:, k, col:col + 1], scalar2=None, op0=ALU.mult),
                                         reads=[("t2k", rb), ("cs", "a2", k)], writes=[("t2k", rb)])
                                    s.op("dve", lambda e: e.scalar_tensor_tensor(out=t2k[ra][:, 0:258], in0=t2k[ra][:, 0:258], scalar=a1[:, k, col:col + 1], in1=t2k[rb][:, 0:258],
                                                                                 op0=ALU.mult, op1=ALU.add),
                                         reads=[("t2k", ra), ("t2k", rb), ("cs", "a1", k)], writes=[("t2k", ra)])
                                s.op(XQ, lambda e: e.dma_start(out=cin_d[:, (d * 4 + g) * 258:(d * 4 + g + 1) * 258], in_=t2k[ra][:, 0:258]),
                                     reads=[("t2k", ra)], writes=[("cin", d, g)], dma=("xs", ra))
                        gate_tables(ms)

                    s.alias(["hT"], ["gt"])
                    accmode["wide"] = True
                    for g in range(4 if do_mlstm else 0):
                        q_interleave = True
                        if g > 0 or seg == 0:
                            qk_proj(g, 1)
                            v_proj(g)
                        else:
                            q_interleave = False
                        if seg == 0:
                            s.op("sp", lambda e: e.dma_start(out=S_b[:], in_=cin_d[:, (4 + g) * 258:(5 + g) * 258]), reads=[("cin", 1, g)], writes=[("S_b",)], dma="c_sb")
                        else:
                            s.op("pool", lambda e: e.memset(S_b[:], 0.0), writes=[("S_b",)])
                        for c in range(NCH - 1, -1, -1):
                            s.op("act", lambda e, c=c: e.activation(out=Cb_st[:, c, :], in_=S_b[:], func=AF.Copy, scale=decp[:, c, 1, g:g + 1]),
                                 reads=[("S_b",)] + dpkeys, writes=[("pair", "Cb", c)])
                            if c > 0:
                                q = kprime(g, c, 1)
                                state_step(g, c, 1, S_b, ("S_b",), q)
                            if q_interleave and c % 4 == 0:
                                qk_proj(g, 0, blks=[3 - c // 4])
                        if seg == 0:
                            s.op("sp", lambda e: e.dma_start(out=S_f[:], in_=cin_d[:, g * 258:(g + 1) * 258]), reads=[("cin", 0, g)], writes=[("S_f",)], dma="c_sf")
                        else:
                            s.op("pool", lambda e: e.memset(S_f[:], 0.0), writes=[("S_f",)])
                        for c in range(NCH):
                            csl = slice(c * 128, (c + 1) * 128)
                            if c % 4 == 0:
                                so = []
                                for hh in range(2):
                                    wo, kwo = load_chunk("w_o", 2 * g + hh)
                                    b = acc_bank()
                                    for kc in range(8):
                                        s.op("pe", lambda e, kc=kc, b=b, wo=wo: e.matmul(psb[b][:], lhsT=wo[:, kc * 128:(kc + 1) * 128],
                                                                                      rhs=xnT[:, kc, (c // 4) * 512:(c // 4 + 1) * 512], start=(kc == 0), stop=(kc == 7)),
                                             reads=[kwo, ("xnT", kc, c // 4)], writes=[P(b)])
                                    r = rot("t2k", 6)
                                    s.op("act", lambda e, b=b, r=r: e.activation(out=t2k[r][:], in_=psb[b][:], func=AF.Sigmoid), reads=[P(b)], writes=[("t2k", r)])
                                    so.append(r)
                            cq = rot("Cp", 3)
                            s.op("act", lambda e, c=c, cq=cq: e.activation(out=Cp[cq][:], in_=S_f[:], func=AF.Copy, scale=decp[:, c, 0, g:g + 1]),
                                 reads=[("S_f",)] + dpkeys, writes=[("Cp", cq)])
                            U = []
                            for hh in range(2):
                                u = dict(hh=hh, hd=2 * g + hh, rows=slice(hh * 64, (hh + 1) * 64), bS=acc_bank())
                                U.append(u)
                            for u in U:
                                s.op("pe", lambda e: e.matmul(psb[u["bS"]][:, 0:128], lhsT=kT[u["rows"], csl], rhs=qT[u["rows"], csl], start=True, stop=True),
                                     reads=[("pair", "qk", 0, c // 4), ("pair", "qk", 1, c // 4)], writes=[P(u["bS"])])
                            for u in U:
                                u["pts"] = []
                                for d in range(2):
                                    pq = rot("PT", 8)
                                    s.op("dve", lambda e: e.scalar_tensor_tensor(out=PT[pq][:], in0=psb[u["bS"]][:, 0:128],
                                                                                 scalar=e_tok[:, c, d * 8 + u["hd"]:d * 8 + u["hd"] + 1],
                                                                                 in1=mask[:, d, :], op0=ALU.mult, op1=ALU.mult),
                                         reads=[P(u["bS"]), ("mask", d)] + tabkeys, writes=[("PT", pq)])
                                    u["pts"].append(pq)
                            for u in U:
                                u["bO"] = acc_bank()
                                bO, hh, rows = u["bO"], u["hh"], u["rows"]
                                vrhs = v_aug[:, c, hh, :]
                                s.op("pe", lambda e: e.matmul(psb[bO][:, 0:129], lhsT=PT[u["pts"][0]][:], rhs=vrhs, start=True, stop=False),
                                     reads=[("PT", u["pts"][0]), ("pair", "v", c), ("pair", "v1")], writes=[P(bO)])
                                s.op("pe", lambda e: e.matmul(psb[bO][:, 0:129], lhsT=qT[rows, csl], rhs=Cp[cq][rows, hh * 129:(hh + 1) * 129], start=False, stop=True),
                                     reads=[("pair", "qk", 0, c // 4), ("Cp", cq)], writes=[P(bO)])
                                s.op("pe", lambda e: e.matmul(psb[bO][:, 129:258], lhsT=PT[u["pts"][1]][:], rhs=vrhs, start=True, stop=False),
                                     reads=[("PT", u["pts"][1]), ("pair", "v", c), ("pair", "v1")], writes=[P(bO)])
                                s.op("pe", lambda e: e.matmul(psb[bO][:, 129:258], lhsT=qT[rows, csl], rhs=Cb_st[rows, c, hh * 129:(hh + 1) * 129], start=False, stop=True),
                                     reads=[("pair", "qk", 0, c // 4), ("pair", "Cb", c)], writes=[P(bO)])
                            for u in U:
                                u["m"] = rot("sm", 8)
                                m, bO, hd = u["m"], u["bO"], u["hd"]
                                den = psb[bO][:, 128:258:129]
                                s.op("dve", lambda e: e.scalar_tensor_tensor(out=sm[m][:, 0:2], in0=den, scalar=-1.0, in1=cl_tok[:, c, hd:16:8], op0=ALU.mult, op1=ALU.max),
                                     reads=[P(bO)] + clkeys, writes=[("sm", m)])
                                s.op("dve", lambda e: e.tensor_tensor(out=sm[m][:, 0:2], in0=sm[m][:, 0:2], in1=den, op=ALU.max), reads=[P(bO), ("sm", m)], writes=[("sm", m)])
                                s.op("dve", lambda e: e.reciprocal(out=sm[m][:, 0:2], in_=sm[m][:, 0:2]), reads=[("sm", m)], writes=[("sm", m)])
                            for u in U:
                                u["hq"] = rot("hq", 3)
                                m, bO, hq = u["m"], u["bO"], u["hq"]
                                s.op("dve", lambda e: e.tensor_scalar(out=htmp[hq][:], in0=psb[bO][:, 0:128], scalar1=sm[m][:, 0:1], scalar2=None, op0=ALU.mult),
                                     reads=[P(bO), ("sm", m)], writes=[("htmp", hq)])
                                s.op("dve", lambda e: e.scalar_tensor_tensor(out=hs[hq][:], in0=psb[bO][:, 129:257], scalar=sm[m][:, 1:2], in1=htmp[hq][:], op0=ALU.mult, op1=ALU.add),
                                     reads=[P(bO), ("sm", m), ("htmp", hq)], writes=[("hs", hq)])
                            for u in U:
                                m, hq = u["m"], u["hq"]
                                s.op("act", lambda e: e.activation(out=hjunk[:], in_=hs[hq][:], func=AF.Square, accum_out=sm[m][:, 2:3]),
                                     reads=[("hs", hq)], writes=[("sm", m), ("hjunk",)])
                                s.op("act", lambda e: e.activation(out=sm[m][:, 3:4], in_=sm[m][:, 2:3], func=AF.Sqrt, bias=EPS, scale=1.0 / 128),
                                     reads=[("sm", m)], writes=[("sm", m)])
                            for u in U:
                                m, hq = u["m"], u["hq"]
                                s.op("dve", lambda e: e.reciprocal(out=sm[m][:, 3:4], in_=sm[m][:, 3:4]), reads=[("sm", m)], writes=[("sm", m)])
                                s.op("dve", lambda e: e.tensor_scalar(out=hn[hq][:], in0=hs[hq][:], scalar1=sm[m][:, 3:4], scalar2=None, op0=ALU.mult),
                                     reads=[("hs", hq), ("sm", m)], writes=[("hn", hq)])
                            for u in U:
                                pO = psb[u["bO"]][:].bitcast(BF16)
                                s.op("pe", lambda e: e.transpose(out=pO[:, 768:896], in_=hn[u["hq"]][:], identity=ident_b[:]),
                                     reads=[("hn", u["hq"]), ("ident_b",)], writes=[P(u["bO"])])
                            for u in U:
                                pO = psb[u["bO"]][:].bitcast(BF16)
                                hd, r = u["hd"], so[u["hh"]]
                                s.op("dve", lambda e: e.scalar_tensor_tensor(out=hT[:, hd, csl], in0=pO[:, 768:896], scalar=mnorm_sb[:, hd:hd + 1],
                                                                             in1=t2k[r][:, (c % 4) * 128:(c % 4 + 1) * 128], op0=ALU.mult, op1=ALU.mult),
                                     reads=[P(u["bO"]), ("t2k", r), ("mnorm",)], writes=[("hT", hd, c // 4)])
                            if c < NCH - 1:
                                q = kprime(g, c, 0)
                                state_step(g, c, 0, S_f, ("S_f",), q)
                    accmode["wide"] = False
                    mixer_in = hT
                    mixer_key = "hT"
                    wout_name = "w_mout"

                wres = aview(0, 16384, BF16).rearrange("p (o k) -> p o k", o=8)
                mix = aview(16384, 16384, F32).rearrange("p (o t) -> p o t", o=8)
                s.alias(["wres", "mix"], ["xnT"])
                for oc in range(8):
                    s.op("sp", lambda e, oc=oc: e.dma_start(out=wres[:, oc, :], in_=wb[wout_name][oc]),
                         reads=[("wb", wout_name, oc)], writes=[("wres", oc)], dma=("wres", oc))
                for blk in range(NBLK):
                    cols = slice(blk * 512, (blk + 1) * 512)
                    out_proj_block(lambda oc, kc: wres[:, oc, kc * 128:(kc + 1) * 128], lambda oc: ("wres", oc), 8,
                                   lambda kc, cols=cols: mixer_in[:, kc, cols], lambda kc, blk=blk: (mixer_key, kc, blk),
                                   lambda oc: mix[:, oc, :], "mix", 0, 512, gcol(layer, 1), STAT)
                    r = rot("t2k", 6)
                    s.op("act", lambda e, r=r: e.activation(out=t2k[r][:], in_=psb[STAT][:], func=AF.Sqrt, bias=EPS, scale=1.0 / D),
                         reads=[P(STAT)], writes=[("t2k", r)])
                    s.op("dve", lambda e, r=r: e.reciprocal(out=t2k[r][:], in_=t2k[r][:]), reads=[("t2k", r)], writes=[("t2k", r)])
                    for oc in range(8):
                        t = rot("t2k", 6)
                        while t == r:
                            t = rot("t2k", 6)
                        gc = gcol(layer, 1)
                        s.op("dve", lambda e, oc=oc, t=t, r=r, gc=gc: e.scalar_tensor_tensor(out=t2k[t][:], in0=mix[:, oc, :], scalar=norms_sb[:, gc + oc:gc + oc + 1],
                                                                                       in1=t2k[r][:], op0=ALU.mult, op1=ALU.mult),
                             reads=[("mix", oc, 0), ("t2k", r), ("norms",)], writes=[("t2k", t)])
                        s.op("dve", lambda e, oc=oc, t=t, cols=cols: e.tensor_tensor(out=xT[:, oc, cols], in0=xT[:, oc, cols], in1=t2k[t][:], op=ALU.add),
                             reads=[("t2k", t), xkey(oc, blk)], writes=[xkey(oc, blk)])

                xn2 = aview(0, 16384, BF16).rearrange("p (c t) -> p c t", c=8)
                ybuf = aview(0, 32768, F32).rearrange("p (o t) -> p o t", o=8)
                act = aview(32800, 45056, BF16).rearrange("p (k t) -> p k t", k=KF)
                for half in range(2):
                    s.alias(["xn2", "act"], ["wres", "mix", "zT", "hT", "y", "xn2", "act", "xnT", "c_sb", "u_sb", "gt", "pair"])
                    for sub in range(2):
                        blk = half * 2 + sub
                        rms_T(lambda c, blk=blk: xT[:, c, blk * 512:(blk + 1) * 512], lambda c, blk=blk: xkey(c, blk), 512, gcol(layer, 2),
                              lambda c, sub=sub: xn2[:, c, sub * 512:(sub + 1) * 512], lambda c, sub=sub: ("xn2", c, sub))
                    for j in range(KF):
                        if seg == 0 and (half * KF + j) % 3 == 0:
                            flush_cast(1)
                        wg_, kwg = load_chunk("w_f1", layer * 44 + 2 * j)
                        wu_, kwu = load_chunk("w_f1", layer * 44 + 2 * j + 1)
                        for sub in range(2):
                            cols = slice(sub * 512, (sub + 1) * 512)
                            b = acc_bank()
                            for kc in range(8):
                                s.op("pe", lambda e, kc=kc, b=b, cols=cols, wg_=wg_: e.matmul(psb[b][:], lhsT=wg_[:, kc * 128:(kc + 1) * 128], rhs=xn2[:, kc, cols],
                                                                                           start=(kc == 0), stop=(kc == 7)),
                                     reads=[kwg, ("xn2", kc, sub)], writes=[P(b)])
                            r = rot("t2k", 6)
                            s.op("act", lambda e, b=b, r=r: e.activation(out=t2k[r][:], in_=psb[b][:], func=AF.Silu), reads=[P(b)], writes=[("t2k", r)])
                            b2 = acc_bank()
                            for kc in range(8):
                                s.op("pe", lambda e, kc=kc, b2=b2, cols=cols, wu_=wu_: e.matmul(psb[b2][:], lhsT=wu_[:, kc * 128:(kc + 1) * 128], rhs=xn2[:, kc, cols],
                                                                                             start=(kc == 0), stop=(kc == 7)),
                                     reads=[kwu, ("xn2", kc, sub)], writes=[P(b2)])
                            s.op("dve", lambda e, b2=b2, r=r, j=j, cols=cols: e.tensor_tensor(out=act[:, j, cols], in0=psb[b2][:], in1=t2k[r][:], op=ALU.mult),
                                 reads=[P(b2), ("t2k", r)], writes=[("act", j, sub)])
                    s.alias(["y"], ["xn2"])
                    for oc in range(8):
                        w2, kw2 = load_big("w_f2", layer * 8 + oc, DFF)
                        for sub in range(2):
                            cols = slice(sub * 512, (sub + 1) * 512)
                            statbank = STAT if sub == 0 else MISC
                            b = acc_bank()
                            for kc in range(KF):
                                s.op("pe", lambda e, kc=kc, b=b, cols=cols, w2=w2: e.matmul(psb[b][:], lhsT=w2[:, kc * 128:(kc + 1) * 128], rhs=act[:, kc, cols],
                                                                                         start=(kc == 0), stop=(kc == KF - 1)),
                                     reads=[kw2, ("act", kc, sub)], writes=[P(b)])
                            s.op("act", lambda e, b=b, oc=oc, cols=cols: e.copy(out=ybuf[:, oc, cols], in_=psb[b][:]), reads=[P(b)], writes=[("y", oc, sub)])
                            q = rot("sqb", 2)
                            s.op("pool", lambda e, oc=oc, q=q, cols=cols: e.tensor_tensor(out=sqb[q][:], in0=ybuf[:, oc, cols], in1=ybuf[:, oc, cols], op=ALU.mult),
                                 reads=[("y", oc, sub)], writes=[("sqb", q)])
                            s.op("pe", lambda e, oc=oc, q=q, statbank=statbank: e.matmul(psb[statbank][:], lhsT=ones_b[:], rhs=sqb[q][:], start=(oc == 0), stop=(oc == 7)),
                                 reads=[("sqb", q), ("ones_b",)], writes=[P(statbank)])
                    for sub in range(2):
                        blk = half * 2 + sub
                        cols = slice(sub * 512, (sub + 1) * 512)
                        xcols = slice(blk * 512, (blk + 1) * 512)
                        statbank = STAT if sub == 0 else MISC
                        r = rot("t2k", 6)
                        s.op("act", lambda e, r=r, statbank=statbank: e.activation(out=t2k[r][:], in_=psb[statbank][:], func=AF.Sqrt, bias=EPS, scale=1.0 / D),
                             reads=[P(statbank)], writes=[("t2k", r)])
                        s.op("dve", lambda e, r=r: e.reciprocal(out=t2k[r][:], in_=t2k[r][:]), reads=[("t2k", r)], writes=[("t2k", r)])
                        gc = gcol(layer, 3)
                        for oc in range(8):
                            t = rot("t2k", 6)
                            while t == r:
                                t = rot("t2k", 6)
                            s.op("dve", lambda e, oc=oc, t=t, r=r, gc=gc, cols=cols: e.scalar_tensor_tensor(out=t2k[t][:], in0=ybuf[:, oc, cols], scalar=norms_sb[:, gc + oc:gc + oc + 1],
                                                                                                    in1=t2k[r][:], op0=ALU.mult, op1=ALU.mult),
                                 reads=[("y", oc, sub), ("t2k", r), ("norms",)], writes=[("t2k", t)])
                            s.op("dve", lambda e, oc=oc, t=t, xcols=xcols: e.tensor_tensor(out=xT[:, oc, xcols], in0=xT[:, oc, xcols], in1=t2k[t][:], op=ALU.add),
                                 reads=[("t2k", t), xkey(oc, blk)], writes=[xkey(oc, blk)])
                    if layer == n_layers - 1:
                        store_blocks(seg, [half * 2, half * 2 + 1])
                        if seg + 1 < n_seg:
                            load_blocks(seg + 1, [half * 2, half * 2 + 1])
                            if half == 1:
                                load_halo(seg + 1)

            if n_layers == 0:
                store_blocks(seg, range(NBLK))
                if seg + 1 < n_seg:
                    load_blocks(seg + 1, range(NBLK))
                    load_halo(seg + 1)
        s.emit(st)
        import os
        if os.environ.get("KDEBUG"):
            print("SCHED", s.stats)
    return nc


def _chunks(W, col_lists):
    K = W.shape[0]
    kcn = K // 128
    out = []
    for cols in col_lists:
        sub = W[:, cols]
        w = sub.shape[1]
        out.append(sub.reshape(kcn, 128, w).transpose(1, 0, 2).reshape(128, kcn * w))
    return np.ascontiguousarray(np.stack(out, 0), dtype=np.float32)


def prep_weights(inp):
    r = lambda a, b: list(range(a, b))
    cw_in = inp["conv_w_in"][0]
    cin_lists = []
    for cc in range(8):
        cin_lists += [r(1024 + cc * 128, 1024 + (cc + 1) * 128), r(2048 + cc * 128, 2048 + (cc + 1) * 128), r(cc * 128, (cc + 1) * 128)]
    w = {}
    w["w_cin"] = _chunks(cw_in, cin_lists)
    w["w_cout"] = _chunks(inp["conv_w_out"][0], [r(o * 128, (o + 1) * 128) for o in range(8)])
    mw = inp["mlstm_w_in"][0]
    qk_lists = []
    for g in range(4):
        qk_lists += [r(g * 128, (g + 1) * 128), r(512 + g * 128, 512 + (g + 1) * 128)]
    w["w_qk"] = _chunks(mw, qk_lists)
    w["w_o"] = _chunks(mw, [r(2048 + h * 128, 2048 + (h + 1) * 128) for h in range(8)])
    w["w_v"] = _chunks(mw, [r(1024 + g * 256, 1024 + (g + 1) * 256) for g in range(4)])
    gcols = mw[:, 3072:3104]
    gi = np.zeros((1024, 40), np.float32)
    gf = np.zeros((1024, 40), np.float32)
    gi[:, 0:8] = gcols[:, 0:8]
    gf[:, 0:8] = gcols[:, 8:16]
    gi[:, 32:40] = gcols[:, 16:24]
    gf[:, 32:40] = gcols[:, 24:32]
    wgi = _chunks(gi, [r(0, 40)])[0]
    wgf = _chunks(gf, [r(0, 40)])[0]
    w["w_g"] = np.ascontiguousarray(np.concatenate([wgi, wgf], axis=1)[None], dtype=np.float32)
    w["w_mout"] = _chunks(inp["mlstm_w_out"][0], [r(o * 128, (o + 1) * 128) for o in range(8)])
    f1 = []
    for l in range(2):
        lists = []
        for j in range(KF):
            lists += [r(j * 128, (j + 1) * 128), r(DFF + j * 128, DFF + (j + 1) * 128)]
        f1.append(_chunks(inp["ffn_w_in"][l], lists))
    w["w_f1"] = np.ascontiguousarray(np.concatenate(f1, 0))
    f2 = [_chunks(inp["ffn_w_out"][l], [r(o * 128, (o + 1) * 128) for o in range(8)]) for l in range(2)]
    w["w_f2"] = np.ascontiguousarray(np.concatenate(f2, 0))
    nr = inp["norms"].reshape(8, 8, 128)
    w["norms_t"] = np.ascontiguousarray(nr.transpose(2, 0, 1).reshape(128, 64), dtype=np.float32)
    cw = inp["conv_w"][0].reshape(3, 8, 128)
    w["convw_t"] = np.ascontiguousarray(cw.transpose(2, 1, 0).reshape(128, 24), dtype=np.float32)
    w["mnorm_t"] = np.ascontiguousarray(inp["mlstm_norm"][0].reshape(8, 128).T, dtype=np.float32)
    bg = inp["mlstm_b_gate"][0]
    bt = np.zeros((40, 2), np.float32)
    bt[0:8, 0] = bg[0:8]
    bt[0:8, 1] = bg[8:16]
    bt[32:40, 0] = bg[16:24]
    bt[32:40, 1] = bg[24:32]
    w["bgate_t"] = bt
    return w


def kernel(x_prompt, x_sample, norms, conv_w_in, conv_w, conv_w_out, mlstm_w_in, mlstm_b_gate,
           mlstm_norm, mlstm_w_out, ffn_w_in, ffn_w_out, _n_layers=2, _do_mlstm=True, _n_seg=NSEG):
    inp = dict(norms=np.asarray(norms, np.float32), conv_w_in=np.asarray(conv_w_in, np.float32),
               conv_w=np.asarray(conv_w, np.float32), conv_w_out=np.asarray(conv_w_out, np.float32),
               mlstm_w_in=np.asarray(mlstm_w_in, np.float32), mlstm_b_gate=np.asarray(mlstm_b_gate, np.float32),
               mlstm_norm=np.asarray(mlstm_norm, np.float32), mlstm_w_out=np.asarray(mlstm_w_out, np.float32),
               ffn_w_in=np.asarray(ffn_w_in, np.float32), ffn_w_out=np.asarray(ffn_w_out, np.float32))
    xp = np.asarray(x_prompt, np.float32)
    xs = np.asarray(x_sample, np.float32)
    w = prep_weights(inp)
    in_maps = []
    for r in range(NCORES):
        b, qd = r // 4, r % 4
        xin = np.empty((NSEG, T, D), np.float32)
        halo = np.zeros((NSEG, 2, D), np.float32)
        xin[0] = xp[b, qd * T:(qd + 1) * T]
        if qd > 0:
            halo[0, 0] = xp[b, qd * T - 1]
        if qd < 3:
            halo[0, 1] = xp[b, (qd + 1) * T]
        xin[1] = xs[2 * r]
        xin[2] = xs[2 * r + 1]
        m = dict(w)
        flp = np.zeros((128, 4, 2), np.float32)
        fls = np.zeros((40, 4), np.float32)
        for k in range(4):
            ff = 1.0 if k < qd else 0.0
            fb = 1.0 if (3 - k) > qd else 0.0
            flp[:, k, 0] = ff
            flp[:, k, 1] = fb
            fls[0:8, k] = ff
            fls[32:40, k] = fb
        m["flp"] = flp.reshape(128, 8)
        m["fls"] = fls
        m["xin"] = xin
        m["halo"] = halo
        in_maps.append(m)
    nc = build_program(n_layers=_n_layers, do_mlstm=_do_mlstm, n_seg=_n_seg)
    res = run_bass_kernel_spmd(nc, in_maps, core_ids=list(range(NCORES)))
    y_prompt = np.empty_like(xp)
    y_sample = np.empty_like(xs)
    for r in range(NCORES):
        y = res.results[r]["yout"]
        b, qd = r // 4, r % 4
        y_prompt[b, qd * T:(qd + 1) * T] = y[0]
        y_sample[2 * r] = y[1]
        y_sample[2 * r + 1] = y[2]
    return (y_prompt, y_sample)
```

```python
import contextlib
import numpy as np
import concourse.bass as bass
import concourse.mybir as mybir
from concourse.bass_utils import run_bass_kernel_spmd

F32 = mybir.dt.float32
BF16 = mybir.dt.bfloat16
AF = mybir.ActivationFunctionType
ALU = mybir.AluOpType
AX = mybir.AxisListType

D = 1024
T = 2048
NSEG = 3
NBLK = 4
NCH = 16
DFF = 2816
KF = 22
EPS = 1e-6
NEG = -1.0e30
NCORES = 8


class _Rec:
    def __getattr__(self, name):
        def f(*a, **kw):
            self.call = (name, a, kw)
            return self
        return f


class Sched:
    ENGS = ("pe", "act", "dve", "pool", "sp")

    def __init__(self, nc):
        self.nc = nc
        self.ops = []
        self.lastw = {}
        self.readers = {}
        self.dma_keys = []
        self.pending = {}
        self.touched = set()
        self.tags = []
        self.reorder = True
        self.last_pe = None
        self.window = 64
        import os
        self.reorder_engs = tuple(os.environ.get("KREORDER", "pe,act,dve,pool").split(","))
        self.xlat = 1000.0

    def alias(self, new_names, old_names):
        old = set(old_names)
        dset = set()
        for k, w in self.lastw.items():
            if k[0] in old:
                dset.add(w)
        for k, rs in self.readers.items():
            if k[0] in old:
                dset.update(rs)
        for n in new_names:
            self.pending[n] = set(self.pending.get(n, set())) | dset
            self.touched = {k for k in self.touched if k[0] != n}

    @staticmethod
    def _cost(eng, name, a, kw, dma):
        out = kw.get("out", a[0] if a else None)
        try:
            shp = out.shape
            free = 1
            for d_ in shp[1:]:
                free *= d_
        except Exception:
            free = 512
        if name == "collective_compute":
            return (500.0, 40000.0)
        if dma is not None:
            try:
                nbytes = out.nbytes()
            except Exception:
                nbytes = free * 4 * 128
            return (150.0, 2500.0 + nbytes / 120.0)
        if eng == "pe":
            return (max(64, free) * 0.50 + 35.0, 250.0)
        if eng == "act":
            return (210.0 + free * 0.65, 250.0)
        if eng == "dve":
            return (90.0 + free * 0.95, 250.0)
        if eng == "pool":
            return (250.0 + free * 2.6, 300.0)
        return (100.0, 100.0)

    def op(self, eng, fn, reads=(), writes=(), dma=None, inc=16, after=()):
        deps = set(after)
        for k in list(reads) + list(writes):
            if k[0] in self.pending and k not in self.touched:
                deps |= self.pending[k[0]]
                self.touched.add(k)
        for k in reads:
            w = self.lastw.get(k)
            if w is not None:
                deps.add(w)
        for k in writes:
            w = self.lastw.get(k)
            if w is not None:
                deps.add(w)
            for r in self.readers.get(k, ()):
                deps.add(r)
        i = len(self.ops)
        rec = _Rec()
        fn(rec)
        name, a, kw = rec.call
        fn = (lambda e, name=name, a=a, kw=kw: getattr(e, name)(*a, **kw))
        self.ops.append(dict(eng=eng, fn=fn, deps=deps, dma=dma, inc=inc))
        if eng == "pe":
            self.last_pe = i
        if dma is not None and dma not in self.dma_keys:
            self.dma_keys.append(dma)
        tag = ("d", dma) if dma is not None else ("e", eng)
        order = set()
        for k in reads:
            lst = self.readers.setdefault(k, [])
            for r in lst:
                if self.tags[r] == tag:
                    order.add(r)
            lst[:] = [r for r in lst if self.tags[r] != tag]
            lst.append(i)
        self.tags.append(tag)
        self.ops[i]["order"] = order
        self.ops[i]["cost"] = self._cost(eng, name, a, kw, dma)
        for k in writes:
            self.lastw[k] = i
            self.readers[k] = []
        return i

    def _list_schedule(self):
        ops = self.ops
        n = len(ops)
        full = {e: [] for e in self.ENGS}
        for i, o in enumerate(ops):
            full[o["eng"]].append(i)
        nxt = {e: 0 for e in self.ENGS}
        win = {e: [] for e in self.ENGS}
        done = [False] * n
        fin = [0.0] * n
        free_t = {e: 0.0 for e in self.ENGS}
        out = {e: [] for e in self.ENGS}
        preds = [list(o["deps"] | o["order"]) for o in ops]
        succ_eng = [set() for _ in range(n)]
        for i, o in enumerate(ops):
            for d in preds[i]:
                succ_eng[d].add(o["eng"])
        W = self.window
        xlat = self.xlat

        def refill(e):
            Wl = W if e in self.reorder_engs else 1
            w = win[e]
            f = full[e]
            while len(w) < Wl and nxt[e] < len(f):
                w.append(f[nxt[e]])
                nxt[e] += 1

        def best(e):
            bi = None
            bt = None
            ft = free_t[e]
            for i in win[e]:
                ok = True
                rt = 0.0
                for d in preds[i]:
                    if not done[d]:
                        ok = False
                        break
                    od = ops[d]
                    t = fin[d] + (0.0 if (od["eng"] == e and od["dma"] is None) else xlat)
                    if t > rt:
                        rt = t
                if not ok:
                    continue
                st = rt if rt > ft else ft
                if bt is None or st < bt - 1e-9:
                    bi, bt = i, st
                    if st <= ft + 1e-9:
                        break
            return bi, bt

        for e in self.ENGS:
            refill(e)
        cand = {}
        remaining = n
        while remaining:
            choice = None
            for e in self.ENGS:
                if not win[e]:
                    continue
                if e not in cand:
                    cand[e] = best(e)
                bi, bt = cand[e]
                if bi is None:
                    continue
                if choice is None or bt < choice[1]:
                    choice = (bi, bt, e)
            assert choice is not None, "scheduler deadlock"
            i, st, e = choice
            busy, lat = ops[i]["cost"]
            free_t[e] = st + busy
            fin[i] = st + busy + lat
            done[i] = True
            win[e].remove(i)
            refill(e)
            out[e].append(i)
            remaining -= 1
            cand.pop(e, None)
            for e2 in succ_eng[i]:
                cand.pop(e2, None)
        self.sim_time = max(fin) if fin else 0.0
        return out

    def emit(self, stack):
        nc = self.nc
        ops = self.ops
        per_eng_sched = self._list_schedule() if self.reorder else None
        needed = [False] * len(ops)
        for o in ops:
            if o["eng"] == "pe":
                o["wdeps"] = {d for d in o["deps"] if not (ops[d]["eng"] == "pe" and ops[d]["dma"] is None)}
            else:
                o["wdeps"] = o["deps"]
            for d in o["wdeps"]:
                needed[d] = True
        esem = {e: stack.enter_context(nc.semaphore("s_" + e)) for e in self.ENGS}
        dsem = {k: stack.enter_context(nc.semaphore("d_%d" % i)) for i, k in enumerate(self.dma_keys)}
        cnt = {e: 0 for e in self.ENGS}
        dcnt = {k: 0 for k in self.dma_keys}
        token = [None] * len(ops)
        per_eng = {e: [] for e in self.ENGS}
        if per_eng_sched is not None:
            seq = [i for e in self.ENGS for i in per_eng_sched[e]]
        else:
            seq = list(range(len(ops)))
        for i in seq:
            o = ops[i]
            per_eng[o["eng"]].append(i)
            if o["dma"] is not None:
                dcnt[o["dma"]] += o["inc"]
                token[i] = (("d", o["dma"]), dsem[o["dma"]], dcnt[o["dma"]])
            elif needed[i]:
                cnt[o["eng"]] += 1
                token[i] = (("e", o["eng"]), esem[o["eng"]], cnt[o["eng"]])
        self.stats = dict(sim_ms=getattr(self, "sim_time", 0.0) / 1e6, nops=len(ops), cnt=dict(cnt), ndma=len(self.dma_keys),
                          per_eng={e: len(v) for e, v in per_eng.items()})
        self._last_order = per_eng
        self._last_token = token
        block = stack.enter_context(nc.Block())
        handles = {"pe": block.tensor, "act": block.scalar, "dve": block.vector,
                   "pool": block.gpsimd, "sp": block.sync}
        final_d = dict(dcnt)
        final_e = dict(cnt)

        def make_body(e):
            def body(eng):
                seen = {}
                for i in per_eng[e]:
                    o = ops[i]
                    waits = {}
                    for d in o["wdeps"]:
                        t = token[d]
                        if t is None:
                            continue
                        name, sem, val = t
                        if seen.get(name, 0) >= val:
                            continue
                        if name not in waits or waits[name][1] < val:
                            waits[name] = (sem, val)
                    if o["dma"] is not None:
                        name, sem, val = token[i]
                        prev = val - o["inc"]
                        if prev > 0 and seen.get(name, 0) < prev:
                            waits[name] = (sem, prev)
                    for name, (sem, val) in waits.items():
                        eng.wait_ge(sem, val)
                        seen[name] = val
                    ins = o["fn"](eng)
                    t = token[i]
                    if t is not None:
                        ins.then_inc(t[1], o["inc"] if o["dma"] is not None else 1)
                if e == "sp":
                    for k, v in final_d.items():
                        if v:
                            eng.wait_ge(dsem[k], v)
                    for e2, v in final_e.items():
                        if v and e2 != "sp":
                            eng.wait_ge(esem[e2], v)
            return body

        for e in self.ENGS:
            handles[e](make_body(e))


def build_program(n_layers=2, do_mlstm=True, n_seg=NSEG):
    nc = bass.Bass("TRN2", target_bir_lowering=False)
    dr = lambda name, shape, dt=F32, kind="ExternalInput": nc.dram_tensor(name, shape, dt, kind=kind).ap()
    xin = dr("xin", [NSEG, T, D])
    halo = dr("halo", [NSEG, 2, D])
    yout = dr("yout", [NSEG, T, D], kind="ExternalOutput")
    wspec = {
        "w_cin": (24, 1024), "w_cout": (8, 1024), "w_qk": (8, 1024), "w_o": (8, 1024),
        "w_v": (4, 2048), "w_g": (1, 640), "w_mout": (8, 1024),
        "w_f1": (88, 1024), "w_f2": (16, DFF),
    }
    wf = {k: dr(k, [n, 128, w]) for k, (n, w) in wspec.items()}
    wb = {k: nc.dram_tensor("b" + k, [n, 128, w], BF16).ap() for k, (n, w) in wspec.items()}
    norms_d = dr("norms_t", [128, 64])
    convw_d = dr("convw_t", [128, 24])
    mnorm_d = dr("mnorm_t", [128, 8])
    bgate_d = dr("bgate_t", [40, 2])
    flp_d = dr("flp", [128, 8])
    fls_d = dr("fls", [40, 4])
    summ_f = nc.dram_tensor("summ_f", [128, 1032], F32).ap()
    summ_b = nc.dram_tensor("summ_b", [128, 1056], F32).ap()
    sout_f = nc.dram_tensor("sout_f", [512, 1032], F32).ap()
    sout_b = nc.dram_tensor("sout_b", [512, 1056], F32).ap()
    cin_d = nc.dram_tensor("cin_d", [128, 2064], F32).ap()

    st = contextlib.ExitStack()
    with st:
        SB = lambda name, shape, dt: st.enter_context(nc.sbuf_tensor(name, shape, dt))
        s = Sched(nc)
        xT = SB("xT", [128, 8, T], F32)
        ARENA_W = 22568
        arena = SB("arena", [128, ARENA_W], F32)

        def aview(off_b, nbytes, dt):
            a = arena[:, off_b // 4:(off_b + nbytes) // 4]
            return a.bitcast(dt) if dt != F32 else a

        wchunk = [SB("wch%d" % i, [128, 1024], BF16) for i in range(4)]
        wbig = [SB("wbig%d" % i, [128, DFF], BF16) for i in range(2)]
        wg_sb = SB("wg_sb", [128, 640], BF16)
        t2k = [SB("t2k%d" % i, [128, 512], F32) for i in range(6)]
        sqb = [SB("sqb%d" % i, [128, 512], BF16) for i in range(2)]
        ident_f = SB("ident_f", [128, 128], F32)
        ident_b = SB("ident_b", [128, 128], BF16)
        ones_b = SB("ones_b", [128, 128], BF16)
        ones_f = SB("ones_f", [128, 128], F32)
        mask = SB("mask", [128, 2, 128], F32)
        norms_sb = SB("norms_sb", [128, 64], F32)
        convw_sb = SB("convw_sb", [128, 24], F32)
        mnorm_sb = SB("mnorm_sb", [128, 8], F32)
        bgate_sb = SB("bgate_sb", [40, 2], F32)
        negbf = SB("negbf", [40, 1], F32)
        sel = SB("sel", [40, 16], F32)
        xTh = SB("xTh", [128, 8, 2], F32)
        e_tok = SB("e_tok", [128, NCH, 16], F32)
        cl_tok = SB("cl_tok", [128, NCH, 16], F32)
        dec_rep = SB("dec_rep", [128, NCH, 16], F32)
        decp = SB("decp", [128, NCH, 2, 4], F32)
        g_amax = SB("g_amax", [40, NCH], F32)
        g_bn = SB("g_bn", [40, NCH], F32)
        g_m = SB("g_m", [40, NCH], F32)
        g_mp = SB("g_mp", [40, NCH], F32)
        g_mout = SB("g_mout", [40, 1], F32)
        g_dec = SB("g_dec", [40, NCH], F32)
        g_bd = SB("g_bd", [40, NCH, 16], F32)
        PT = [SB("PT%d" % i, [128, 128], BF16) for i in range(8)]
        kp = [SB("kp%d" % i, [128, 128], BF16) for i in range(6)]
        Cp = [SB("Cp%d" % i, [128, 258], BF16) for i in range(3)]
        S_f = SB("S_f", [128, 258], F32)
        S_b = SB("S_b", [128, 258], F32)
        hs = [SB("hs%d" % i, [128, 128], F32) for i in range(4)]
        hn = [SB("hn%d" % i, [128, 128], BF16) for i in range(4)]
        hjunk = SB("hjunk", [128, 128], BF16)
        sm = [SB("sm%d" % i, [128, 8], F32) for i in range(8)]
        flp_sb = SB("flp_sb", [128, 8], F32)
        flpB = SB("flpB", [128, 8], F32)
        fls_sb = SB("fls_sb", [40, 4], F32)
        flsB = SB("flsB", [40, 4], F32)
        g_sv = SB("g_sv", [40, 2], F32)
        g_bd2 = SB("g_bd2", [40, 2, 16], F32)
        svrep = SB("svrep", [128, 32], F32)
        sc_p = SB("sc_p", [128, 16], F32)
        mq = SB("mq", [128, 4, 8], F32)
        Fq = SB("Fq", [128, 4, 8], F32)
        a1 = SB("a1", [128, 4, 8], F32)
        a2 = SB("a2", [128, 4, 8], F32)
        cm = SB("cm", [128, 8], F32)
        ct1 = SB("ct1", [128, 8], F32)
        ct2 = SB("ct2", [128, 8], F32)
        svq = SB("svq", [40, 4, 2], F32)
        ms = SB("ms", [40, 1], F32)
        st1 = SB("st1", [40, 1], F32)
        st2 = SB("st2", [40, 1], F32)

        psb = [st.enter_context(nc.psum_tensor("psb%d" % i, [128, 512], F32)) for i in range(8)]

        rr = {}

        def rot(name, n):
            i = rr.get(name, 0)
            rr[name] = i + 1
            return i % n

        accmode = {"wide": False}

        def acc_bank():
            if accmode["wide"]:
                return (0, 1, 2, 3, 4, 5, 7)[rot("accw", 7)]
            return rot("acc", 5)
        STAT = 5
        MISC = 6
        MISC2 = 7

        def P(b):
            return ("ps", b)

        s.op("pool", lambda e: e.memset(ident_f[:], 0.0), writes=[("ident_f",)])
        s.op("pool", lambda e: e.affine_select(out=ident_f[:], in_=ident_f[:], pattern=[[-1, 128]],
                                               compare_op=ALU.not_equal, fill=1.0, base=0, channel_multiplier=1),
             reads=[("ident_f",)], writes=[("ident_f",)])
        s.op("pool", lambda e: e.tensor_copy(out=ident_b[:], in_=ident_f[:]), reads=[("ident_f",)], writes=[("ident_b",)])
        s.op("pool", lambda e: e.memset(ones_b[:], 1.0), writes=[("ones_b",)])
        s.op("pool", lambda e: e.memset(ones_f[:], 1.0), writes=[("ones_f",)])
        s.op("pool", lambda e: e.affine_select(out=mask[:, 0, :], in_=ones_f[:], pattern=[[1, 128]],
                                               compare_op=ALU.is_ge, fill=0.0, base=0, channel_multiplier=-1),
             reads=[("ones_f",)], writes=[("mask", 0)])
        s.op("pool", lambda e: e.affine_select(out=mask[:, 1, :], in_=ones_f[:], pattern=[[-1, 128]],
                                               compare_op=ALU.is_ge, fill=0.0, base=0, channel_multiplier=1),
             reads=[("ones_f",)], writes=[("mask", 1)])
        s.op("pool", lambda e: e.tensor_copy(out=sel[:, 0:8], in_=ident_f[0:40, 0:8]), reads=[("ident_f",)], writes=[("sel", 0)])
        s.op("pool", lambda e: e.tensor_copy(out=sel[:, 8:16], in_=ident_f[0:40, 32:40]), reads=[("ident_f",)], writes=[("sel", 1)])
        s.op("pool", lambda e: e.memset(g_m[:], 0.0), writes=[("g", "m")])
        s.op("pool", lambda e: e.memset(g_mout[:], 0.0), writes=[("g", "mp")])
        s.op("pool", lambda e: e.memset(g_amax[:], 0.0), writes=[("g", "amax")])
        s.op("pool", lambda e: e.memset(g_bn[:], 0.0), writes=[("g", "bn")])
        s.op("sp", lambda e: e.dma_start(out=norms_sb[:], in_=norms_d), writes=[("norms",)], dma="c_norms")
        s.op("sp", lambda e: e.dma_start(out=convw_sb[:], in_=convw_d), writes=[("convw",)], dma="c_convw")
        s.op("sp", lambda e: e.dma_start(out=mnorm_sb[:], in_=mnorm_d), writes=[("mnorm",)], dma="c_mnorm")
        s.op("sp", lambda e: e.dma_start(out=bgate_sb[:], in_=bgate_d), writes=[("bgate",)], dma="c_bgate")
        s.op("dve", lambda e: e.tensor_scalar(out=negbf[:], in0=bgate_sb[:, 1:2], scalar1=-1.0, scalar2=None, op0=ALU.mult),
             reads=[("bgate",)], writes=[("negbf",)])

        cast_order = ["w_cin", "w_cout", "w_f1:0", "w_f2:0", "w_g", "w_qk", "w_v", "w_o", "w_mout", "w_f1:1", "w_f2:1"]
        cast_pieces = []
        for item in cast_order:
            if ":" in item:
                nm, l = item.split(":")
                l = int(l)
                n = wspec[nm][0] // 2
                lo, hi = l * n, (l + 1) * n
            else:
                nm = item
                lo, hi = 0, wspec[nm][0]
            step = 8 if wspec[nm][1] <= 1024 else 4
            for a in range(lo, hi, step):
                cast_pieces.append((nm, a, min(hi, a + step)))

        def flush_cast(n=1, gate=True):
            for _ in range(n):
                if not cast_pieces:
                    return
                nm, a, b = cast_pieces.pop(0)
                aft = [s.last_pe] if (gate and s.last_pe is not None) else []
                s.op("pool", lambda e: e.dma_start(out=wb[nm][a:b], in_=wf[nm][a:b]),
                     writes=[("wb", nm, j) for j in range(a, b)], dma=("cast", nm, a), after=aft)

        flush_cast(2, gate=False)

        def load_chunk(nm, j):
            sl = rot("wch", 4)
            s.op("sp", lambda e: e.dma_start(out=wchunk[sl][:], in_=wb[nm][j]),
                 reads=[("wb", nm, j)], writes=[("wch", sl)], dma=("wch", sl))
            return wchunk[sl], ("wch", sl)

        def load_big(nm, j, width):
            sl = rot("wbig", 2)
            s.op("sp", lambda e: e.dma_start(out=wbig[sl][:, 0:width], in_=wb[nm][j]),
                 reads=[("wb", nm, j)], writes=[("wbig", sl)], dma=("wbig", sl))
            return wbig[sl], ("wbig", sl)

        gcol = lambda l, n: (l * 4 + n) * 8

        def rms_T(src, srckeys, n, gc, dst, dstkeys, npart=128):
            for c in range(8):
                q = rot("sqb", 2)
                s.op("act", lambda e, c=c, q=q: e.activation(out=sqb[q][:, 0:n], in_=src(c), func=AF.Square),
                     reads=[srckeys(c)], writes=[("sqb", q)])
                s.op("pe", lambda e, c=c, q=q: e.matmul(psb[STAT][:, 0:n], lhsT=ones_b[:], rhs=sqb[q][:, 0:n],
                                                        start=(c == 0), stop=(c == 7)),
                     reads=[("sqb", q), ("ones_b",)], writes=[P(STAT)])
            r = rot("t2k", 6)
            s.op("act", lambda e: e.activation(out=t2k[r][:, 0:n], in_=psb[STAT][:, 0:n], func=AF.Sqrt, bias=EPS, scale=1.0 / D),
                 reads=[P(STAT)], writes=[("t2k", r)])
            s.op("dve", lambda e: e.reciprocal(out=t2k[r][:, 0:n], in_=t2k[r][:, 0:n]), reads=[("t2k", r)], writes=[("t2k", r)])
            for c in range(8):
                s.op("dve", lambda e, c=c: e.scalar_tensor_tensor(out=dst(c), in0=src(c), scalar=norms_sb[:, gc + c:gc + c + 1],
                                                                  in1=t2k[r][:, 0:n], op0=ALU.mult, op1=ALU.mult),
                     reads=[srckeys(c), ("t2k", r), ("norms",)], writes=[dstkeys(c)])

        def out_proj_block(wres, wreskeys, nk, rhs, rhskeys, ybuf, ykey, col0, n, gc, statbank):
            for oc in range(8):
                b = acc_bank()
                for kc in range(nk):
                    s.op("pe", lambda e, oc=oc, kc=kc, b=b: e.matmul(psb[b][:, 0:n], lhsT=wres(oc, kc), rhs=rhs(kc),
                                                                     start=(kc == 0), stop=(kc == nk - 1)),
                         reads=[wreskeys(oc), rhskeys(kc)], writes=[P(b)])
                s.op("act", lambda e, oc=oc, b=b: e.copy(out=ybuf(oc), in_=psb[b][:, 0:n]), reads=[P(b)], writes=[(ykey, oc, col0)])
                q = rot("sqb", 2)
                s.op("pool", lambda e, oc=oc, q=q: e.tensor_tensor(out=sqb[q][:, 0:n], in0=ybuf(oc), in1=ybuf(oc), op=ALU.mult),
                     reads=[(ykey, oc, col0)], writes=[("sqb", q)])
                s.op("pe", lambda e, oc=oc, q=q: e.matmul(psb[statbank][:, 0:n], lhsT=ones_b[:], rhs=sqb[q][:, 0:n],
                                                          start=(oc == 0), stop=(oc == 7)),
                     reads=[("sqb", q), ("ones_b",)], writes=[P(statbank)])

        def resid_block(ybuf, ykey, col0, n, gc, statbank):
            r = rot("t2k", 6)
            s.op("act", lambda e: e.activation(out=t2k[r][:, 0:n], in_=psb[statbank][:, 0:n], func=AF.Sqrt, bias=EPS, scale=1.0 / D),
                 reads=[P(statbank)], writes=[("t2k", r)])
            s.op("dve", lambda e: e.reciprocal(out=t2k[r][:, 0:n], in_=t2k[r][:, 0:n]), reads=[("t2k", r)], writes=[("t2k", r)])
            for oc in range(8):
                t = rot("t2k", 6)
                while t == r:
                    t = rot("t2k", 6)
                s.op("dve", lambda e, oc=oc, t=t: e.scalar_tensor_tensor(out=t2k[t][:, 0:n], in0=ybuf(oc),
                                                                         scalar=norms_sb[:, gc + oc:gc + oc + 1],
                                                                         in1=t2k[r][:, 0:n], op0=ALU.mult, op1=ALU.mult),
                     reads=[(ykey, oc, col0), ("t2k", r), ("norms",)], writes=[("t2k", t)])
                s.op("pool", lambda e, oc=oc, t=t: e.tensor_tensor(out=xT[:, oc, col0:col0 + n], in0=xT[:, oc, col0:col0 + n],
                                                                   in1=t2k[t][:, 0:n], op=ALU.add),
                     reads=[("t2k", t), ("xT", oc, col0 // 512)], writes=[("xT", oc, col0 // 512)])

        xkey = lambda c, blk: ("xT", c, blk)

        import os as _os
        XQ = _os.environ.get("KXQ", "pool")

        def load_blocks(seg, blks):
            for blk in blks:
                for tt in range(blk * 4, blk * 4 + 4):
                    for hf in range(2):
                        r = rot("t2k", 6)
                        s.op(XQ, lambda e: e.dma_start(out=t2k[r][:], in_=xin[seg, tt * 128:(tt + 1) * 128, hf * 512:(hf + 1) * 512]),
                             writes=[("t2k", r)], dma=("xl", r))
                        bk = acc_bank()
                        for q in range(4):
                            s.op("pe", lambda e: e.transpose(out=psb[bk][:, q * 128:(q + 1) * 128], in_=t2k[r][:, q * 128:(q + 1) * 128], identity=ident_f[:]),
                                 reads=[("t2k", r), ("ident_f",)], writes=[P(bk)])
                        outap = xT[:, hf * 4:(hf + 1) * 4, tt * 128:(tt + 1) * 128]
                        inap = psb[bk][:].rearrange("p (q t) -> p q t", q=4)
                        wk = [xkey(hf * 4 + q, tt // 4) for q in range(4)]
                        if (tt + hf) % 2 == 0:
                            s.op("act", lambda e: e.copy(out=outap, in_=inap), reads=[P(bk)], writes=wk)
                        else:
                            s.op("dve", lambda e: e.tensor_copy(out=outap, in_=inap), reads=[P(bk)], writes=wk)

        def load_halo(seg):
            for hf in range(2):
                r = rot("t2k", 6)
                s.op(XQ, lambda e: e.dma_start(out=t2k[r][0:2, :], in_=halo[seg, :, hf * 512:(hf + 1) * 512]), writes=[("t2k", r)], dma=("xl", r))
                for q in range(4):
                    c = hf * 4 + q
                    s.op("pe", lambda e: e.transpose(out=psb[MISC][:, c * 2:(c + 1) * 2], in_=t2k[r][0:2, q * 128:(q + 1) * 128],
                                                     identity=ident_f[0:2, 0:2]),
                         reads=[("t2k", r), ("ident_f",)], writes=[P(MISC)])
            s.op("act", lambda e: e.copy(out=xTh[:], in_=psb[MISC][:, 0:16].rearrange("p (c t) -> p c t", t=2)), reads=[P(MISC)], writes=[("xTh",)])

        def store_blocks(seg, blks):
            for blk in blks:
                for tt in range(blk * 4, blk * 4 + 4):
                    for hf in range(2):
                        bk = acc_bank()
                        for q in range(4):
                            c = hf * 4 + q
                            s.op("pe", lambda e: e.transpose(out=psb[bk][:, q * 128:(q + 1) * 128], in_=xT[:, c, tt * 128:(tt + 1) * 128], identity=ident_f[:]),
                                 reads=[xkey(c, tt // 4), ("ident_f",)], writes=[P(bk)])
                        r = rot("t2k", 6)
                        if (tt + hf) % 2 == 0:
                            s.op("act", lambda e: e.copy(out=t2k[r][:], in_=psb[bk][:]), reads=[P(bk)], writes=[("t2k", r)])
                        else:
                            s.op("dve", lambda e: e.tensor_copy(out=t2k[r][:], in_=psb[bk][:]), reads=[P(bk)], writes=[("t2k", r)])
                        s.op(XQ, lambda e: e.dma_start(out=yout[seg, tt * 128:(tt + 1) * 128, hf * 512:(hf + 1) * 512], in_=t2k[r][:]),
                             reads=[("t2k", r)], writes=[("yout", seg, tt, hf)], dma=("xs", r))

        for seg in range(n_seg):
            if seg == 0:
                load_blocks(0, range(NBLK))
                load_halo(0)

            for layer in range(n_layers):
                if layer == 0:
                    xnT = aview(0, 32800, BF16).rearrange("p (c t) -> p c t", c=8)
                    zT = aview(32800, 32768, BF16).rearrange("p (c t) -> p c t", c=8)
                    c_sb = aview(65568, 8200, F32)
                    u_sb = aview(73768, 8200, F32)
                    s.alias(["xnT", "zT", "c_sb", "u_sb"], ["xnT", "zT", "c_sb", "u_sb", "wres", "mix", "xn2", "act", "y", "hT", "gt", "pair"])
                    for blk in range(NBLK):
                        rms_T(lambda c, blk=blk: xT[:, c, blk * 512:(blk + 1) * 512], lambda c, blk=blk: xkey(c, blk), 512, gcol(0, 0),
                              lambda c, blk=blk: xnT[:, c, 1 + blk * 512:1 + (blk + 1) * 512], lambda c, blk=blk: ("xnT", c, blk))
                    rms_T(lambda c: xTh[:, c, :], lambda c: ("xTh",), 2, gcol(0, 0),
                          lambda c: xnT[:, c, 0:2050:2049], lambda c: ("xnT", c, "h"))
                    for cc in range(8):
                        if seg == 0:
                            flush_cast(1)
                        wc, kwc = load_chunk("w_cin", 3 * cc + 0)
                        wv_, kwv = load_chunk("w_cin", 3 * cc + 1)
                        wb_, kwb = load_chunk("w_cin", 3 * cc + 2)
                        for blk in range(NBLK):
                            b = acc_bank()
                            for kc in range(8):
                                s.op("pe", lambda e, kc=kc, b=b, blk=blk, wc=wc: e.matmul(psb[b][:], lhsT=wc[:, kc * 128:(kc + 1) * 128],
                                                                                       rhs=xnT[:, kc, 1 + blk * 512:1 + (blk + 1) * 512],
                                                                                       start=(kc == 0), stop=(kc == 7)),
                                     reads=[kwc, ("xnT", kc, blk)], writes=[P(b)])
                            s.op("act", lambda e, b=b, blk=blk: e.copy(out=c_sb[:, 1 + blk * 512:1 + (blk + 1) * 512], in_=psb[b][:]),
                                 reads=[P(b)], writes=[("c_sb", blk)])
                        for kc in range(8):
                            s.op("pe", lambda e, kc=kc, wc=wc: e.matmul(psb[MISC][:, 0:2], lhsT=wc[:, kc * 128:(kc + 1) * 128], rhs=xnT[:, kc, 0:2050:2049],
                                                                     start=(kc == 0), stop=(kc == 7)),
                                 reads=[kwc, ("xnT", kc, "h")], writes=[P(MISC)])
                        s.op("act", lambda e: e.copy(out=c_sb[:, 0:2050:2049], in_=psb[MISC][:, 0:2]), reads=[P(MISC)], writes=[("c_sb", "h")])
                        for blk in range(NBLK):
                            b = acc_bank()
                            for kc in range(8):
                                s.op("pe", lambda e, kc=kc, b=b, blk=blk, wv_=wv_: e.matmul(psb[b][:], lhsT=wv_[:, kc * 128:(kc + 1) * 128],
                                                                                         rhs=xnT[:, kc, 1 + blk * 512:1 + (blk + 1) * 512],
                                                                                         start=(kc == 0), stop=(kc == 7)),
                                     reads=[kwv, ("xnT", kc, blk)], writes=[P(b)])
                            s.op("dve", lambda e, b=b, blk=blk: e.tensor_tensor(out=u_sb[:, 1 + blk * 512:1 + (blk + 1) * 512], in0=psb[b][:],
                                                                               in1=c_sb[:, 1 + blk * 512:1 + (blk + 1) * 512], op=ALU.mult),
                                 reads=[P(b), ("c_sb", blk)], writes=[("u_sb", blk)])
                        for kc in range(8):
                            s.op("pe", lambda e, kc=kc, wv_=wv_: e.matmul(psb[MISC][:, 0:2], lhsT=wv_[:, kc * 128:(kc + 1) * 128], rhs=xnT[:, kc, 0:2050:2049],
                                                                       start=(kc == 0), stop=(kc == 7)),
                                 reads=[kwv, ("xnT", kc, "h")], writes=[P(MISC)])
                        s.op("dve", lambda e: e.tensor_tensor(out=u_sb[:, 0:2050:2049], in0=psb[MISC][:, 0:2], in1=c_sb[:, 0:2050:2049], op=ALU.mult),
                             reads=[P(MISC), ("c_sb", "h")], writes=[("u_sb", "h")])
                        ukeys = [("u_sb", k) for k in (0, 1, 2, 3, "h")]
                        ckeys = [("c_sb", k) for k in (0, 1, 2, 3, "h")]
                        cw = lambda j, cc=cc: convw_sb[:, cc * 3 + j:cc * 3 + j + 1]
                        s.op("act", lambda e, cw=cw: e.activation(out=c_sb[:, 1:2049], in_=u_sb[:, 0:2048], func=AF.Copy, scale=cw(0)),
                             reads=ukeys + [("convw",)], writes=ckeys)
                        s.op("dve", lambda e, cw=cw: e.scalar_tensor_tensor(out=c_sb[:, 1:2049], in0=u_sb[:, 1:2049], scalar=cw(1), in1=c_sb[:, 1:2049],
                                                                          op0=ALU.mult, op1=ALU.add),
                             reads=ukeys + ckeys + [("convw",)], writes=ckeys)
                        s.op("dve", lambda e, cw=cw: e.scalar_tensor_tensor(out=c_sb[:, 1:2049], in0=u_sb[:, 2:2050], scalar=cw(2), in1=c_sb[:, 1:2049],
                                                                          op0=ALU.mult, op1=ALU.add),
                             reads=ukeys + ckeys + [("convw",)], writes=ckeys)
                        for blk in range(NBLK):
                            b = acc_bank()
                            for kc in range(8):
                                s.op("pe", lambda e, kc=kc, b=b, blk=blk, wb_=wb_: e.matmul(psb[b][:], lhsT=wb_[:, kc * 128:(kc + 1) * 128],
                                                                                         rhs=xnT[:, kc, 1 + blk * 512:1 + (blk + 1) * 512],
                                                                                         start=(kc == 0), stop=(kc == 7)),
                                     reads=[kwb, ("xnT", kc, blk)], writes=[P(b)])
                            s.op("dve", lambda e, b=b, blk=blk, cc=cc: e.tensor_tensor(out=zT[:, cc, blk * 512:(blk + 1) * 512], in0=psb[b][:],
                                                                                      in1=c_sb[:, 1 + blk * 512:1 + (blk + 1) * 512], op=ALU.mult),
                                 reads=[P(b)] + ckeys, writes=[("zT", cc, blk)])
                    mixer_in = zT
                    mixer_key = "zT"
                    wout_name = "w_cout"
                else:
                    if seg == 0:
                        flush_cast(100)
                        s.op("sp", lambda e: e.dma_start(out=wg_sb[:], in_=wb["w_g"][0]), reads=[("wb", "w_g", 0)], writes=[("wg",)], dma="c_wg")
                    xnT = aview(0, 32768, BF16).rearrange("p (c t) -> p c t", c=8)
                    hT = aview(32800, 32768, BF16).rearrange("p (c t) -> p c t", c=8)
                    s.alias(["xnT", "hT", "gt", "pair"], ["xnT", "zT", "c_sb", "u_sb", "wres", "mix", "xn2", "act", "y", "hT", "gt", "pair"])
                    for blk in range(NBLK):
                        rms_T(lambda c, blk=blk: xT[:, c, blk * 512:(blk + 1) * 512], lambda c, blk=blk: xkey(c, blk), 512, gcol(1, 0),
                              lambda c, blk=blk: xnT[:, c, blk * 512:(blk + 1) * 512], lambda c, blk=blk: ("xnT", c, blk))
                    GI = aview(32800, 8192, F32)
                    CS = aview(32800 + 8192, 8192, F32)
                    SP = aview(32800 + 16384, 8192, F32)
                    GI3 = GI.rearrange("p (c t) -> p c t", t=128)
                    SP3 = SP.rearrange("p (c t) -> p c t", t=128)
                    CS3 = CS.rearrange("p (c t) -> p c t", t=128)
                    for blk in range(NBLK):
                        cols = slice(blk * 512, (blk + 1) * 512)
                        for gi in range(2):
                            b = acc_bank()
                            for kc in range(8):
                                s.op("pe", lambda e: e.matmul(psb[b][0:40, :], lhsT=wg_sb[:, gi * 320 + kc * 40:gi * 320 + (kc + 1) * 40],
                                                              rhs=xnT[:, kc, cols], start=(kc == 0), stop=(kc == 7)),
                                     reads=[("wg",), ("xnT", kc, blk)], writes=[P(b)])
                            if gi == 0:
                                s.op("act", lambda e: e.activation(out=GI[0:40, cols], in_=psb[b][0:40, :], func=AF.Identity, bias=bgate_sb[:, 0:1]),
                                     reads=[P(b), ("bgate",)], writes=[("gt", "GI")])
                            else:
                                s.op("act", lambda e: e.activation(out=SP[0:40, cols], in_=psb[b][0:40, :], func=AF.Exp, bias=negbf[:, 0:1], scale=-1.0),
                                     reads=[P(b), ("negbf",)], writes=[("gt", "SP")])
                    s.op("act", lambda e: e.activation(out=SP[0:40, :], in_=SP[0:40, :], func=AF.Ln, bias=1.0), reads=[("gt", "SP")], writes=[("gt", "SP")])
                    for c in range(NCH):
                        s.op("dve", lambda e: e.tensor_tensor_scan(out=CS[0:40, c * 128:(c + 1) * 128], data0=ones_f[0:40, :], data1=SP[0:40, c * 128:(c + 1) * 128],
                                                                   initial=0.0, op0=ALU.mult, op1=ALU.add),
                             reads=[("gt", "SP"), ("ones_f",)], writes=[("gt", "CS")])
                    s.op("dve", lambda e: e.tensor_copy(out=g_bn[:], in_=CS3[0:40, :, 127]), reads=[("gt", "CS")], writes=[("g", "bn")])
                    s.op("dve", lambda e: e.tensor_tensor(out=CS3[32:40], in0=g_bn[32:40, :].unsqueeze(2).broadcast_to([8, NCH, 128]), in1=CS3[32:40], op=ALU.subtract),
                         reads=[("gt", "CS"), ("g", "bn")], writes=[("gt", "CS")])
                    s.op("dve", lambda e: e.tensor_tensor(out=CS[32:40, :], in0=CS[32:40, :], in1=SP[32:40, :], op=ALU.add),
                         reads=[("gt", "CS"), ("gt", "SP")], writes=[("gt", "CS")])
                    s.op("dve", lambda e: e.tensor_tensor(out=GI[0:40, :], in0=GI[0:40, :], in1=CS[0:40, :], op=ALU.add),
                         reads=[("gt", "GI"), ("gt", "CS")], writes=[("gt", "GI")])
                    s.op("dve", lambda e: e.tensor_reduce(out=g_amax[:], in_=GI3[0:40], axis=AX.X, op=ALU.max), reads=[("gt", "GI")], writes=[("g", "amax")])
                    tabkeys = [("e_tok", h_, d_) for h_ in range(2) for d_ in range(2)]
                    clkeys = [("cl_tok", h_, d_) for h_ in range(2) for d_ in range(2)]
                    dpkeys = [("decp", 0), ("decp", 1)]

                    def gate_tables(m_in):
                        s.op("dve", lambda e: e.memset(g_mp[:], NEG), writes=[("g", "mp")])
                        if m_in is not None:
                            s.op("dve", lambda e: e.tensor_copy(out=g_mp[0:8, 0:1], in_=m_in[0:8, :]), reads=[("cs", "ms"), ("g", "mp")], writes=[("g", "mp")])
                            s.op("dve", lambda e: e.tensor_copy(out=g_mp[32:40, NCH - 1:NCH], in_=m_in[32:40, :]), reads=[("cs", "ms"), ("g", "mp")], writes=[("g", "mp")])
                        for c in range(NCH):
                            s.op("dve", lambda e: e.tensor_tensor(out=g_m[0:8, c:c + 1], in0=g_mp[0:8, c:c + 1], in1=g_amax[0:8, c:c + 1], op=ALU.max),
                                 reads=[("g", "mp"), ("g", "amax")], writes=[("g", "m")])
                            dst = g_mp[0:8, c + 1:c + 2] if c < NCH - 1 else g_mout[0:8, :]
                            s.op("dve", lambda e: e.tensor_tensor(out=dst, in0=g_m[0:8, c:c + 1], in1=g_bn[0:8, c:c + 1], op=ALU.subtract),
                                 reads=[("g", "m"), ("g", "bn")], writes=[("g", "mp")])
                        for c in range(NCH - 1, -1, -1):
                            s.op("dve", lambda e: e.tensor_tensor(out=g_m[32:40, c:c + 1], in0=g_mp[32:40, c:c + 1], in1=g_amax[32:40, c:c + 1], op=ALU.max),
                                 reads=[("g", "mp"), ("g", "amax")], writes=[("g", "m")])
                            dst = g_mp[32:40, c - 1:c] if c > 0 else g_mout[32:40, :]
                            s.op("dve", lambda e: e.tensor_tensor(out=dst, in0=g_m[32:40, c:c + 1], in1=g_bn[32:40, c:c + 1], op=ALU.subtract),
                                 reads=[("g", "m"), ("g", "bn")], writes=[("g", "mp")])
                        s.op("dve", lambda e: e.tensor_tensor(out=g_dec[:], in0=g_mp[:], in1=g_m[:], op=ALU.subtract), reads=[("g", "mp"), ("g", "m")], writes=[("g", "dec")])
                        s.op("dve", lambda e: e.tensor_scalar(out=g_dec[:], in0=g_dec[:], scalar1=-100.0, scalar2=None, op0=ALU.max), reads=[("g", "dec")], writes=[("g", "dec")])
                        s.op("act", lambda e: e.activation(out=g_dec[:], in_=g_dec[:], func=AF.Exp), reads=[("g", "dec")], writes=[("g", "dec")])
                        for src3, skey, dstt, dkey in ((GI3, ("gt", "GI"), e_tok, "e_tok"), (CS3, ("gt", "CS"), cl_tok, "cl_tok")):
                            s.op("dve", lambda e: e.tensor_tensor(out=SP3[0:40], in0=src3[0:40], in1=g_m[:].unsqueeze(2).broadcast_to([40, NCH, 128]), op=ALU.subtract),
                                 reads=[skey, ("g", "m"), ("gt", "SP")], writes=[("gt", "SP")])
                            s.op("act", lambda e: e.activation(out=SP[0:40, :], in_=SP[0:40, :], func=AF.Exp), reads=[("gt", "SP")], writes=[("gt", "SP")])
                            for half in range(2):
                                for c8 in range(8):
                                    c = half * 8 + c8
                                    s.op("pe", lambda e: e.transpose(out=psb[MISC][:, c8 * 40:(c8 + 1) * 40], in_=SP[0:40, c * 128:(c + 1) * 128],
                                                                     identity=ident_f[0:40, 0:40]),
                                         reads=[("gt", "SP"), ("ident_f",)], writes=[P(MISC)])
                                m3 = psb[MISC][:, 0:320].rearrange("p (c j) -> p c j", j=40)
                                for d in range(2):
                                    s.op("act", lambda e: e.copy(out=dstt[:, half * 8:(half + 1) * 8, d * 8:(d + 1) * 8], in_=m3[:, :, d * 32:d * 32 + 8]),
                                         reads=[P(MISC)], writes=[(dkey, half, d)])
                        s.op("dve", lambda e: e.tensor_tensor(out=g_bd[:], in0=g_dec[:].unsqueeze(2).broadcast_to([40, NCH, 16]),
                                                              in1=sel[:].unsqueeze(1).broadcast_to([40, NCH, 16]), op=ALU.mult),
                             reads=[("g", "dec"), ("sel", 0), ("sel", 1)], writes=[("g", "bd")])
                        s.op("pe", lambda e: e.matmul(psb[MISC2][:, 0:256], lhsT=ones_f[0:40, :], rhs=g_bd[:].rearrange("p c j -> p (c j)"), start=True, stop=True),
                             reads=[("g", "bd"), ("ones_f",)], writes=[P(MISC2)])
                        s.op("act", lambda e: e.copy(out=dec_rep[:].rearrange("p c j -> p (c j)"), in_=psb[MISC2][:, 0:256]), reads=[P(MISC2)], writes=[("dec_rep",)])
                        dr4 = dec_rep[:].rearrange("p c (d g two) -> p c d g two", d=2, two=2)
                        s.op("dve", lambda e: e.tensor_copy(out=decp[0:64], in_=dr4[0:64, :, :, :, 0]), reads=[("dec_rep",)], writes=[("decp", 0)])
                        s.op("dve", lambda e: e.tensor_copy(out=decp[64:128], in_=dr4[64:128, :, :, :, 1]), reads=[("dec_rep",)], writes=[("decp", 1)])

                    PB = 65568
                    qT = aview(PB, 4096, BF16)
                    kT = aview(PB + 4096, 4096, BF16)
                    v_aug = aview(PB + 8192, 8256, BF16).rearrange("p (c h f) -> p c h f", c=NCH, h=2)
                    Cb_st = aview(PB + 8192 + 8256, 8256, BF16).rearrange("p (c f) -> p c f", c=NCH)

                    qstate = {}

                    def qk_proj(g, which, blks=None):
                        dst, sc = ((qT, 0.125), (kT, 1.0))[which]
                        if blks is None or (g, which) not in qstate:
                            qstate[(g, which)] = load_chunk("w_qk", 2 * g + which)
                        wq, kwq = qstate[(g, which)]
                        for blk in (range(NBLK) if blks is None else blks):
                            b = acc_bank()
                            for kc in range(8):
                                s.op("pe", lambda e: e.matmul(psb[b][:], lhsT=wq[:, kc * 128:(kc + 1) * 128],
                                                              rhs=xnT[:, kc, blk * 512:(blk + 1) * 512], start=(kc == 0), stop=(kc == 7)),
                                     reads=[kwq, ("xnT", kc, blk)], writes=[P(b)])
                            s.op("act", lambda e: e.activation(out=dst[:, blk * 512:(blk + 1) * 512], in_=psb[b][:], func=AF.Copy, scale=sc),
                                 reads=[P(b)], writes=[("pair", "qk", which, blk)])

                    def v_proj(g):
                        wv_, kwv = load_big("w_v", g, 2048)
                        s.op("pool", lambda e: e.memset(v_aug[:, :, :, 128:129], 1.0), writes=[("pair", "v1")])
                        for c in range(NCH):
                            b = acc_bank()
                            for kc in range(8):
                                s.op("pe", lambda e: e.matmul(psb[b][:, 0:256], lhsT=xnT[:, kc, c * 128:(c + 1) * 128],
                                                              rhs=wv_[:, kc * 256:(kc + 1) * 256], start=(kc == 0), stop=(kc == 7)),
                                     reads=[kwv, ("xnT", kc, c // 4)], writes=[P(b)])
                            s.op("act", lambda e: e.copy(out=v_aug[:, c, :, 0:128], in_=psb[b][:, 0:256].rearrange("p (h f) -> p h f", h=2)),
                                 reads=[P(b)], writes=[("pair", "v", c)])

                    def kprime(g, c, d):
                        pbank = psb[MISC][:].bitcast(BF16)
                        s.op("pe", lambda e: e.transpose(out=pbank[:, 0:128], in_=kT[:, c * 128:(c + 1) * 128], identity=ident_b[:]),
                             reads=[("pair", "qk", 1, c // 4), ("ident_b",)], writes=[P(MISC)])
                        q = rot("kp", 6)
                        s.op("dve", lambda e: e.tensor_tensor(out=kp[q][:].rearrange("p (h f) -> p h f", h=2),
                                                              in0=pbank[:, 0:128].rearrange("p (h f) -> p h f", h=2),
                                                              in1=e_tok[:, c, d * 8 + 2 * g:d * 8 + 2 * g + 2].unsqueeze(2).broadcast_to([128, 2, 64]),
                                                              op=ALU.mult),
                             reads=[P(MISC)] + tabkeys, writes=[("kp", q)])
                        return q

                    def state_step(g, c, d, S, Skey, q):
                        b = acc_bank()
                        s.op("pe", lambda e: e.matmul(psb[b][:, 0:258], lhsT=kp[q][:], rhs=v_aug[:, c, :, :].rearrange("p h f -> p (h f)"), start=True, stop=True),
                             reads=[("kp", q), ("pair", "v", c), ("pair", "v1")], writes=[P(b)])
                        s.op("dve", lambda e: e.scalar_tensor_tensor(out=S[:], in0=S[:], scalar=decp[:, c, d, g:g + 1], in1=psb[b][:, 0:258],
                                                                     op0=ALU.mult, op1=ALU.add),
                             reads=[P(b), Skey] + dpkeys, writes=[Skey])

                    if do_mlstm:
                        qk_proj(0, 1)
                        v_proj(0)
                        if seg != 0:
                            qk_proj(0, 0)
                    gate_tables(None)
                    if seg == 0 and do_mlstm:
                        for g in range(4):
                            if g > 0:
                                qk_proj(g, 1)
                                v_proj(g)
                            s.op("pool", lambda e: e.memset(S_b[:], 0.0), writes=[("S_b",)])
                            for c in range(NCH - 1, -1, -1):
                                q = kprime(g, c, 1)
                                state_step(g, c, 1, S_b, ("S_b",), q)
                            s.op("sp", lambda e: e.dma_start(out=summ_b[:, g * 258:(g + 1) * 258], in_=S_b[:]), reads=[("S_b",)], writes=[("summ_in", 4 + g)], dma="sm")
                            s.op("pool", lambda e: e.memset(S_f[:], 0.0), writes=[("S_f",)])
                            for c in range(NCH):
                                q = kprime(g, c, 0)
                                state_step(g, c, 0, S_f, ("S_f",), q)
                            s.op("sp", lambda e: e.dma_start(out=summ_f[:, g * 258:(g + 1) * 258], in_=S_f[:]), reads=[("S_f",)], writes=[("summ_in", g)], dma="sm")
                        s.op("dve", lambda e: e.tensor_copy(out=g_sv[:, 0:1], in_=g_mout[:]), reads=[("g", "mp")], writes=[("cs", "sv")])
                        s.op("dve", lambda e: e.tensor_reduce(out=g_sv[:, 1:2], in_=g_bn[:], axis=AX.X, op=ALU.add, negate=True), reads=[("g", "bn"), ("cs", "sv")], writes=[("cs", "sv")])
                        s.op("dve", lambda e: e.tensor_tensor(out=g_bd2[:], in0=g_sv[:].unsqueeze(2).broadcast_to([40, 2, 16]),
                                                              in1=sel[:].unsqueeze(1).broadcast_to([40, 2, 16]), op=ALU.mult),
                             reads=[("cs", "sv"), ("sel", 0), ("sel", 1)], writes=[("cs", "bd2")])
                        s.op("pe", lambda e: e.matmul(psb[MISC2][:, 0:32], lhsT=ones_f[0:40, :], rhs=g_bd2[:].rearrange("p a j -> p (a j)"), start=True, stop=True),
                             reads=[("cs", "bd2"), ("ones_f",)], writes=[P(MISC2)])
                        s.op("act", lambda e: e.copy(out=svrep[:], in_=psb[MISC2][:, 0:32]), reads=[P(MISC2)], writes=[("cs", "svrep")])
                        sv4 = svrep[:].rearrange("p (a d g two) -> p a d g two", a=2, d=2, two=2)
                        scp4 = sc_p[:].rearrange("p (a d g) -> p a d g", a=2, d=2)
                        s.op("dve", lambda e: e.tensor_copy(out=scp4[0:64], in_=sv4[0:64, :, :, :, 0]), reads=[("cs", "svrep")], writes=[("cs", "scp", 0)])
                        s.op("dve", lambda e: e.tensor_copy(out=scp4[64:128], in_=sv4[64:128, :, :, :, 1]), reads=[("cs", "svrep")], writes=[("cs", "scp", 1)])
                        s.op("sp", lambda e: e.dma_start(out=summ_b[:, 1032:1048], in_=sc_p[:]), reads=[("cs", "scp", 0), ("cs", "scp", 1)], writes=[("summ_in", 8)], dma="sm")
                        s.op("pool", lambda e: e.memset(ct1[:], 0.0), writes=[("cs", "ct1")])
                        s.op("sp", lambda e: e.dma_start(out=summ_b[:, 1048:1056], in_=ct1[:]), reads=[("cs", "ct1")], writes=[("summ_in", 9)], dma="sm")
                        s.op("sp", lambda e: e.dma_start(out=summ_b[0:40, 1048:1050], in_=g_sv[:]), reads=[("cs", "sv")], writes=[("summ_in", 9)], dma="sm")
                        s.op("pool", lambda e: e.collective_compute("AllGather", ALU.bypass, replica_groups=[[0, 1, 2, 3], [4, 5, 6, 7]],
                                                                    ins=[summ_f], outs=[sout_f]),
                             reads=[("summ_in", j) for j in range(10)], writes=[("summ_out",)], dma="ag", inc=1)
                        s.op("pool", lambda e: e.collective_compute("AllGather", ALU.bypass, replica_groups=[[0, 1, 2, 3], [4, 5, 6, 7]],
                                                                    ins=[summ_b], outs=[sout_b]),
                             reads=[("summ_in", j) for j in range(10)], writes=[("summ_out",)], dma="ag2", inc=1)
                        s.op("sp", lambda e: e.dma_start(out=flp_sb[:], in_=flp_d), writes=[("cs", "flp")], dma="c_flp")
                        s.op("sp", lambda e: e.dma_start(out=fls_sb[:], in_=fls_d), writes=[("cs", "fls")], dma="c_fls")
                        s.op("dve", lambda e: e.tensor_scalar(out=flpB[:], in0=flp_sb[:], scalar1=-NEG, scalar2=NEG, op0=ALU.mult, op1=ALU.add), reads=[("cs", "flp")], writes=[("cs", "flpB")])
                        s.op("dve", lambda e: e.tensor_scalar(out=flsB[:], in0=fls_sb[:], scalar1=-NEG, scalar2=NEG, op0=ALU.mult, op1=ALU.add), reads=[("cs", "fls")], writes=[("cs", "flsB")])
                        s.op("dve", lambda e: e.memset(svq[:], 0.0), writes=[("cs", "svq")])
                        for k in range(4):
                            for d in range(2):
                                qm = k if d == 0 else 3 - k
                                s.op("sp", lambda e: e.dma_start(out=mq[:, k, d * 4:(d + 1) * 4], in_=sout_b[qm * 128:(qm + 1) * 128, 1032 + d * 4:1032 + (d + 1) * 4]),
                                     reads=[("summ_out",)], writes=[("cs", "mq", k, d)], dma="cq")
                                s.op("sp", lambda e: e.dma_start(out=Fq[:, k, d * 4:(d + 1) * 4], in_=sout_b[qm * 128:(qm + 1) * 128, 1040 + d * 4:1040 + (d + 1) * 4]),
                                     reads=[("summ_out",)], writes=[("cs", "Fq", k, d)], dma="cq")
                                r0 = d * 32
                                s.op("sp", lambda e: e.dma_start(out=svq[r0:r0 + 8, k, :], in_=sout_b[qm * 128 + r0:qm * 128 + r0 + 8, 1048:1050]),
                                     reads=[("summ_out",), ("cs", "svq")], writes=[("cs", "svq")], dma="cq")
                        cs_ = lambda nm: [("cs", nm)]
                        s.op("dve", lambda e: e.memset(cm[:], NEG), writes=cs_("cm"))
                        s.op("dve", lambda e: e.memset(ms[:], NEG), writes=cs_("ms"))
                        v3 = lambda t: t[:].rearrange("p (d g) -> p d g", d=2)
                        for k in range(4):
                            flb = flp_sb[:, k * 2:(k + 1) * 2].unsqueeze(2).broadcast_to([128, 2, 4])
                            flBb = flpB[:, k * 2:(k + 1) * 2].unsqueeze(2).broadcast_to([128, 2, 4])
                            mqk = mq[:, k, :].rearrange("p (d g) -> p d g", d=2)
                            Fqk = Fq[:, k, :].rearrange("p (d g) -> p d g", d=2)
                            rk = [("cs", "mq", k_, d_) for k_ in range(4) for d_ in range(2)] + [("cs", "Fq", k_, d_) for k_ in range(4) for d_ in range(2)] + [("cs", "svq"), ("cs", "flp"), ("cs", "flpB")]
                            s.op("dve", lambda e: e.tensor_tensor(out=v3(ct1), in0=Fqk, in1=flb, op=ALU.mult), reads=rk, writes=cs_("ct1"))
                            s.op("dve", lambda e: e.tensor_tensor(out=v3(ct2), in0=mqk, in1=flb, op=ALU.mult), reads=rk, writes=cs_("ct2"))
                            s.op("dve", lambda e: e.tensor_tensor(out=v3(ct2), in0=v3(ct2), in1=flBb, op=ALU.add), reads=rk + cs_("ct2"), writes=cs_("ct2"))
                            s.op("dve", lambda e: e.tensor_tensor(out=ct1[:], in0=ct1[:], in1=cm[:], op=ALU.add), reads=cs_("ct1") + cs_("cm"), writes=cs_("ct1"))
                            s.op("dve", lambda e: e.tensor_tensor(out=cm[:], in0=ct1[:], in1=ct2[:], op=ALU.max), reads=cs_("ct1") + cs_("ct2") + cs_("cm"), writes=cs_("cm"))
                            s.op("dve", lambda e: e.tensor_tensor(out=ct1[:], in0=ct1[:], in1=cm[:], op=ALU.subtract), reads=cs_("ct1") + cs_("cm"), writes=cs_("ct1"))
                            s.op("dve", lambda e: e.tensor_scalar(out=ct1[:], in0=ct1[:], scalar1=-100.0, scalar2=None, op0=ALU.max), reads=cs_("ct1"), writes=cs_("ct1"))
                            s.op("act", lambda e: e.activation(out=a1[:, k, :], in_=ct1[:], func=AF.Exp), reads=cs_("ct1"), writes=[("cs", "a1", k)])
                            s.op("dve", lambda e: e.tensor_tensor(out=ct2[:], in0=ct2[:], in1=cm[:], op=ALU.subtract), reads=cs_("ct2") + cs_("cm"), writes=cs_("ct2"))
                            s.op("dve", lambda e: e.tensor_scalar(out=ct2[:], in0=ct2[:], scalar1=-100.0, scalar2=None, op0=ALU.max), reads=cs_("ct2"), writes=cs_("ct2"))
                            s.op("act", lambda e: e.activation(out=ct2[:], in_=ct2[:], func=AF.Exp), reads=cs_("ct2"), writes=cs_("ct2"))
                            s.op("dve", lambda e: e.tensor_tensor(out=a2[:, k, :].rearrange("p (d g) -> p d g", d=2), in0=v3(ct2), in1=flb, op=ALU.mult),
                                 reads=cs_("ct2") + rk, writes=[("cs", "a2", k)])
                            rs = rk + [("cs", "fls"), ("cs", "flsB")]
                            s.op("dve", lambda e: e.tensor_tensor(out=st1[:], in0=svq[:, k, 1:2], in1=fls_sb[:, k:k + 1], op=ALU.mult), reads=rs, writes=cs_("st1"))
                            s.op("dve", lambda e: e.tensor_tensor(out=st2[:], in0=svq[:, k, 0:1], in1=fls_sb[:, k:k + 1], op=ALU.mult), reads=rs, writes=cs_("st2"))
                            s.op("dve", lambda e: e.tensor_tensor(out=st2[:], in0=st2[:], in1=flsB[:, k:k + 1], op=ALU.add), reads=rs + cs_("st2"), writes=cs_("st2"))
                            s.op("dve", lambda e: e.tensor_tensor(out=st1[:], in0=st1[:], in1=ms[:], op=ALU.add), reads=cs_("st1") + cs_("ms"), writes=cs_("st1"))
                            s.op("dve", lambda e: e.tensor_tensor(out=ms[:], in0=st1[:], in1=st2[:], op=ALU.max), reads=cs_("st1") + cs_("st2") + cs_("ms"), writes=cs_("ms"))
                        for d in range(2):
                            for g in range(4):
                                ra = rot("t2k", 6)
                                s.op("pool", lambda e: e.memset(t2k[ra][:, 0:258], 0.0), writes=[("t2k", ra)])
                                for k in range(4):
                                    qm = k if d == 0 else 3 - k
                                    rb = rot("t2k", 6)
                                    while rb == ra:
                                        rb = rot("t2k", 6)
                                    s.op(XQ, lambda e: e.dma_start(out=t2k[rb][:, 0:258], in_=(sout_f if d == 0 else sout_b)[qm * 128:(qm + 1) * 128, g * 258:(g + 1) * 258]),
                                         reads=[("summ_out",)], writes=[("t2k", rb)], dma=("xl", rb))
                                    col = d * 4 + g
                                    s.op("dve", lambda e: e.tensor_scalar(out=t2k[rb][:, 0:258], in0=t2k[rb][:, 0:258], scalar1=a2[:, k, col:col + 1], scalar2=None, op0=ALU.mult),
                                         reads=[("t2k", rb), ("cs", "a2", k)], writes=[("t2k", rb)])
                                    s.op("dve", lambda e: e.scalar_tensor_tensor(out=t2k[ra][:, 0:258], in0=t2k[ra][:, 0:258], scalar=a1[:, k, col:col + 1], in1=t2k[rb][:, 0:258],
                                                                                 op0=ALU.mult, op1=ALU.add),
                                         reads=[("t2k", ra), ("t2k", rb), ("cs", "a1", k)], writes=[("t2k", ra)])
                                s.op(XQ, lambda e: e.dma_start(out=cin_d[:, (d * 4 + g) * 258:(d * 4 + g + 1) * 258], in_=t2k[ra][:, 0:258]),
                                     reads=[("t2k", ra)], writes=[("cin", d, g)], dma=("xs", ra))
                        gate_tables(ms)

                    s.alias(["hT"], ["gt"])
                    accmode["wide"] = True
                    for g in range(4 if do_mlstm else 0):
                        q_interleave = True
                        if g > 0 or seg == 0:
                            qk_proj(g, 1)
                            v_proj(g)
                        else:
                            q_interleave = False
                        if seg == 0:
                            s.op("sp", lambda e: e.dma_start(out=S_b[:], in_=cin_d[:, (4 + g) * 258:(5 + g) * 258]), reads=[("cin", 1, g)], writes=[("S_b",)], dma="c_sb")
                        else:
                            s.op("pool", lambda e: e.memset(S_b[:], 0.0), writes=[("S_b",)])
                        for c in range(NCH - 1, -1, -1):
                            s.op("act", lambda e, c=c: e.activation(out=Cb_st[:, c, :], in_=S_b[:], func=AF.Copy, scale=decp[:, c, 1, g:g + 1]),
                                 reads=[("S_b",)] + dpkeys, writes=[("pair", "Cb", c)])
                            if c > 0:
                                q = kprime(g, c, 1)
                                state_step(g, c, 1, S_b, ("S_b",), q)
                            if q_interleave and c % 4 == 0:
                                qk_proj(g, 0, blks=[3 - c // 4])
                        if seg == 0:
                            s.op("sp", lambda e: e.dma_start(out=S_f[:], in_=cin_d[:, g * 258:(g + 1) * 258]), reads=[("cin", 0, g)], writes=[("S_f",)], dma="c_sf")
                        else:
                            s.op("pool", lambda e: e.memset(S_f[:], 0.0), writes=[("S_f",)])
                        for c in range(0, NCH, 2):
                            if c % 4 == 0:
                                so = []
                                for hh in range(2):
                                    wo, kwo = load_chunk("w_o", 2 * g + hh)
                                    b = acc_bank()
                                    for kc in range(8):
                                        s.op("pe", lambda e, kc=kc, b=b, wo=wo: e.matmul(psb[b][:], lhsT=wo[:, kc * 128:(kc + 1) * 128],
                                                                                      rhs=xnT[:, kc, (c // 4) * 512:(c // 4 + 1) * 512], start=(kc == 0), stop=(kc == 7)),
                                             reads=[kwo, ("xnT", kc, c // 4)], writes=[P(b)])
                                    r = rot("t2k", 6)
                                    s.op("act", lambda e, b=b, r=r: e.activation(out=t2k[r][:], in_=psb[b][:], func=AF.Sigmoid), reads=[P(b)], writes=[("t2k", r)])
                                    so.append(r)
                            U = []
                            for cc_ in (c, c + 1):
                                cq_ = rot("Cp", 3)
                                s.op("act", lambda e: e.activation(out=Cp[cq_][:], in_=S_f[:], func=AF.Copy, scale=decp[:, cc_, 0, g:g + 1]),
                                     reads=[("S_f",)] + dpkeys, writes=[("Cp", cq_)])
                                for hh in range(2):
                                    U.append(dict(c=cc_, csl=slice(cc_ * 128, (cc_ + 1) * 128), cq=cq_, hh=hh, hd=2 * g + hh,
                                                  rows=slice(hh * 64, (hh + 1) * 64), bS=acc_bank()))
                                if cc_ < NCH - 1:
                                    q = kprime(g, cc_, 0)
                                    state_step(g, cc_, 0, S_f, ("S_f",), q)
                            for u in U:
                                s.op("pe", lambda e: e.matmul(psb[u["bS"]][:, 0:128], lhsT=kT[u["rows"], u["csl"]], rhs=qT[u["rows"], u["csl"]], start=True, stop=True),
                                     reads=[("pair", "qk", 0, u["c"] // 4), ("pair", "qk", 1, u["c"] // 4)], writes=[P(u["bS"])])
                            for u in U:
                                u["pts"] = []
                                for d in range(2):
                                    pq = rot("PT", 8)
                                    s.op("dve", lambda e: e.scalar_tensor_tensor(out=PT[pq][:], in0=psb[u["bS"]][:, 0:128],
                                                                                 scalar=e_tok[:, u["c"], d * 8 + u["hd"]:d * 8 + u["hd"] + 1],
                                                                                 in1=mask[:, d, :], op0=ALU.mult, op1=ALU.mult),
                                         reads=[P(u["bS"]), ("mask", d)] + tabkeys, writes=[("PT", pq)])
                                    u["pts"].append(pq)
                            for u in U:
                                u["bO"] = acc_bank()
                                bO, hh, rows = u["bO"], u["hh"], u["rows"]
                                c, csl, cq = u["c"], u["csl"], u["cq"]
                                vrhs = v_aug[:, c, hh, :]
                                s.op("pe", lambda e: e.matmul(psb[bO][:, 0:129], lhsT=PT[u["pts"][0]][:], rhs=vrhs, start=True, stop=False),
                                     reads=[("PT", u["pts"][0]), ("pair", "v", c), ("pair", "v1")], writes=[P(bO)])
                                s.op("pe", lambda e: e.matmul(psb[bO][:, 0:129], lhsT=qT[rows, csl], rhs=Cp[cq][rows, hh * 129:(hh + 1) * 129], start=False, stop=True),
                                     reads=[("pair", "qk", 0, c // 4), ("Cp", cq)], writes=[P(bO)])
                                s.op("pe", lambda e: e.matmul(psb[bO][:, 129:258], lhsT=PT[u["pts"][1]][:], rhs=vrhs, start=True, stop=False),
                                     reads=[("PT", u["pts"][1]), ("pair", "v", c), ("pair", "v1")], writes=[P(bO)])
                                s.op("pe", lambda e: e.matmul(psb[bO][:, 129:258], lhsT=qT[rows, csl], rhs=Cb_st[rows, c, hh * 129:(hh + 1) * 129], start=False, stop=True),
                                     reads=[("pair", "qk", 0, c // 4), ("pair", "Cb", c)], writes=[P(bO)])
                            for u in U:
                                u["m"] = rot("sm", 8)
                                m, bO, hd = u["m"], u["bO"], u["hd"]
                                c = u["c"]
                                den = psb[bO][:, 128:258:129]
                                s.op("dve", lambda e: e.scalar_tensor_tensor(out=sm[m][:, 0:2], in0=den, scalar=-1.0, in1=cl_tok[:, c, hd:16:8], op0=ALU.mult, op1=ALU.max),
                                     reads=[P(bO)] + clkeys, writes=[("sm", m)])
                                s.op("dve", lambda e: e.tensor_tensor(out=sm[m][:, 0:2], in0=sm[m][:, 0:2], in1=den, op=ALU.max), reads=[P(bO), ("sm", m)], writes=[("sm", m)])
                                s.op("dve", lambda e: e.reciprocal(out=sm[m][:, 0:2], in_=sm[m][:, 0:2]), reads=[("sm", m)], writes=[("sm", m)])
                            for u in U:
                                u["hq"] = rot("hq", 4)
                                m, bO, hq = u["m"], u["bO"], u["hq"]
                                s.op("dve", lambda e: e.tensor_scalar(out=hs[hq][:], in0=psb[bO][:, 0:128], scalar1=sm[m][:, 0:1], scalar2=None, op0=ALU.mult),
                                     reads=[P(bO), ("sm", m)], writes=[("hs", hq)])
                                s.op("dve", lambda e: e.scalar_tensor_tensor(out=hs[hq][:], in0=psb[bO][:, 129:257], scalar=sm[m][:, 1:2], in1=hs[hq][:], op0=ALU.mult, op1=ALU.add),
                                     reads=[P(bO), ("sm", m), ("hs", hq)], writes=[("hs", hq)])
                            for u in U:
                                m, hq = u["m"], u["hq"]
                                s.op("act", lambda e: e.activation(out=hjunk[:], in_=hs[hq][:], func=AF.Square, accum_out=sm[m][:, 2:3]),
                                     reads=[("hs", hq)], writes=[("sm", m), ("hjunk",)])
                                s.op("act", lambda e: e.activation(out=sm[m][:, 3:4], in_=sm[m][:, 2:3], func=AF.Sqrt, bias=EPS, scale=1.0 / 128),
                                     reads=[("sm", m)], writes=[("sm", m)])
                            for u in U:
                                m, hq = u["m"], u["hq"]
                                s.op("dve", lambda e: e.reciprocal(out=sm[m][:, 3:4], in_=sm[m][:, 3:4]), reads=[("sm", m)], writes=[("sm", m)])
                                s.op("dve", lambda e: e.tensor_scalar(out=hn[hq][:], in0=hs[hq][:], scalar1=sm[m][:, 3:4], scalar2=None, op0=ALU.mult),
                                     reads=[("hs", hq), ("sm", m)], writes=[("hn", hq)])
                            for u in U:
                                pO = psb[u["bO"]][:].bitcast(BF16)
                                s.op("pe", lambda e: e.transpose(out=pO[:, 768:896], in_=hn[u["hq"]][:], identity=ident_b[:]),
                                     reads=[("hn", u["hq"]), ("ident_b",)], writes=[P(u["bO"])])
                            for u in U:
                                pO = psb[u["bO"]][:].bitcast(BF16)
                                hd, r = u["hd"], so[u["hh"]]
                                c, csl = u["c"], u["csl"]
                                s.op("dve", lambda e: e.scalar_tensor_tensor(out=hT[:, hd, csl], in0=pO[:, 768:896], scalar=mnorm_sb[:, hd:hd + 1],
                                                                             in1=t2k[r][:, (c % 4) * 128:(c % 4 + 1) * 128], op0=ALU.mult, op1=ALU.mult),
                                     reads=[P(u["bO"]), ("t2k", r), ("mnorm",)], writes=[("hT", hd, c // 4)])

                    accmode["wide"] = False
                    mixer_in = hT
                    mixer_key = "hT"
                    wout_name = "w_mout"

                wres = aview(0, 16384, BF16).rearrange("p (o k) -> p o k", o=8)
                mix = aview(16384, 16384, F32).rearrange("p (o t) -> p o t", o=8)
                s.alias(["wres", "mix"], ["xnT"])
                for oc in range(8):
                    s.op("sp", lambda e, oc=oc: e.dma_start(out=wres[:, oc, :], in_=wb[wout_name][oc]),
                         reads=[("wb", wout_name, oc)], writes=[("wres", oc)], dma=("wres", oc))
                for blk in range(NBLK):
                    cols = slice(blk * 512, (blk + 1) * 512)
                    out_proj_block(lambda oc, kc: wres[:, oc, kc * 128:(kc + 1) * 128], lambda oc: ("wres", oc), 8,
                                   lambda kc, cols=cols: mixer_in[:, kc, cols], lambda kc, blk=blk: (mixer_key, kc, blk),
                                   lambda oc: mix[:, oc, :], "mix", 0, 512, gcol(layer, 1), STAT)
                    r = rot("t2k", 6)
                    s.op("act", lambda e, r=r: e.activation(out=t2k[r][:], in_=psb[STAT][:], func=AF.Sqrt, bias=EPS, scale=1.0 / D),
                         reads=[P(STAT)], writes=[("t2k", r)])
                    s.op("dve", lambda e, r=r: e.reciprocal(out=t2k[r][:], in_=t2k[r][:]), reads=[("t2k", r)], writes=[("t2k", r)])
                    for oc in range(8):
                        t = rot("t2k", 6)
                        while t == r:
                            t = rot("t2k", 6)
                        gc = gcol(layer, 1)
                        s.op("dve", lambda e, oc=oc, t=t, r=r, gc=gc: e.scalar_tensor_tensor(out=t2k[t][:], in0=mix[:, oc, :], scalar=norms_sb[:, gc + oc:gc + oc + 1],
                                                                                       in1=t2k[r][:], op0=ALU.mult, op1=ALU.mult),
                             reads=[("mix", oc, 0), ("t2k", r), ("norms",)], writes=[("t2k", t)])
                        s.op("dve", lambda e, oc=oc, t=t, cols=cols: e.tensor_tensor(out=xT[:, oc, cols], in0=xT[:, oc, cols], in1=t2k[t][:], op=ALU.add),
                             reads=[("t2k", t), xkey(oc, blk)], writes=[xkey(oc, blk)])

                xn2 = aview(0, 16384, BF16).rearrange("p (c t) -> p c t", c=8)
                ybuf = aview(0, 32768, F32).rearrange("p (o t) -> p o t", o=8)
                act = aview(32800, 45056, BF16).rearrange("p (k t) -> p k t", k=KF)
                for half in range(2):
                    s.alias(["xn2", "act"], ["wres", "mix", "zT", "hT", "y", "xn2", "act", "xnT", "c_sb", "u_sb", "gt", "pair"])
                    for sub in range(2):
                        blk = half * 2 + sub
                        rms_T(lambda c, blk=blk: xT[:, c, blk * 512:(blk + 1) * 512], lambda c, blk=blk: xkey(c, blk), 512, gcol(layer, 2),
                              lambda c, sub=sub: xn2[:, c, sub * 512:(sub + 1) * 512], lambda c, sub=sub: ("xn2", c, sub))
                    for j in range(KF):
                        if seg == 0 and (half * KF + j) % 3 == 0:
                            flush_cast(1)
                        wg_, kwg = load_chunk("w_f1", layer * 44 + 2 * j)
                        wu_, kwu = load_chunk("w_f1", layer * 44 + 2 * j + 1)
                        for sub in range(2):
                            cols = slice(sub * 512, (sub + 1) * 512)
                            b = acc_bank()
                            for kc in range(8):
                                s.op("pe", lambda e, kc=kc, b=b, cols=cols, wg_=wg_: e.matmul(psb[b][:], lhsT=wg_[:, kc * 128:(kc + 1) * 128], rhs=xn2[:, kc, cols],
                                                                                           start=(kc == 0), stop=(kc == 7)),
                                     reads=[kwg, ("xn2", kc, sub)], writes=[P(b)])
                            r = rot("t2k", 6)
                            s.op("act", lambda e, b=b, r=r: e.activation(out=t2k[r][:], in_=psb[b][:], func=AF.Silu), reads=[P(b)], writes=[("t2k", r)])
                            b2 = acc_bank()
                            for kc in range(8):
                                s.op("pe", lambda e, kc=kc, b2=b2, cols=cols, wu_=wu_: e.matmul(psb[b2][:], lhsT=wu_[:, kc * 128:(kc + 1) * 128], rhs=xn2[:, kc, cols],
                                                                                             start=(kc == 0), stop=(kc == 7)),
                                     reads=[kwu, ("xn2", kc, sub)], writes=[P(b2)])
                            s.op("dve", lambda e, b2=b2, r=r, j=j, cols=cols: e.tensor_tensor(out=act[:, j, cols], in0=psb[b2][:], in1=t2k[r][:], op=ALU.mult),
                                 reads=[P(b2), ("t2k", r)], writes=[("act", j, sub)])
                    s.alias(["y"], ["xn2"])
                    for oc in range(8):
                        w2, kw2 = load_big("w_f2", layer * 8 + oc, DFF)
                        for sub in range(2):
                            cols = slice(sub * 512, (sub + 1) * 512)
                            statbank = STAT if sub == 0 else MISC
                            b = acc_bank()
                            for kc in range(KF):
                                s.op("pe", lambda e, kc=kc, b=b, cols=cols, w2=w2: e.matmul(psb[b][:], lhsT=w2[:, kc * 128:(kc + 1) * 128], rhs=act[:, kc, cols],
                                                                                         start=(kc == 0), stop=(kc == KF - 1)),
                                     reads=[kw2, ("act", kc, sub)], writes=[P(b)])
                            s.op("act", lambda e, b=b, oc=oc, cols=cols: e.copy(out=ybuf[:, oc, cols], in_=psb[b][:]), reads=[P(b)], writes=[("y", oc, sub)])
                            q = rot("sqb", 2)
                            s.op("pool", lambda e, oc=oc, q=q, cols=cols: e.tensor_tensor(out=sqb[q][:], in0=ybuf[:, oc, cols], in1=ybuf[:, oc, cols], op=ALU.mult),
                                 reads=[("y", oc, sub)], writes=[("sqb", q)])
                            s.op("pe", lambda e, oc=oc, q=q, statbank=statbank: e.matmul(psb[statbank][:], lhsT=ones_b[:], rhs=sqb[q][:], start=(oc == 0), stop=(oc == 7)),
                                 reads=[("sqb", q), ("ones_b",)], writes=[P(statbank)])
                    for sub in range(2):
                        blk = half * 2 + sub
                        cols = slice(sub * 512, (sub + 1) * 512)
                        xcols = slice(blk * 512, (blk + 1) * 512)
                        statbank = STAT if sub == 0 else MISC
                        r = rot("t2k", 6)
                        s.op("act", lambda e, r=r, statbank=statbank: e.activation(out=t2k[r][:], in_=psb[statbank][:], func=AF.Sqrt, bias=EPS, scale=1.0 / D),
                             reads=[P(statbank)], writes=[("t2k", r)])
                        s.op("dve", lambda e, r=r: e.reciprocal(out=t2k[r][:], in_=t2k[r][:]), reads=[("t2k", r)], writes=[("t2k", r)])
                        gc = gcol(layer, 3)
                        for oc in range(8):
                            t = rot("t2k", 6)
                            while t == r:
                                t = rot("t2k", 6)
                            s.op("dve", lambda e, oc=oc, t=t, r=r, gc=gc, cols=cols: e.scalar_tensor_tensor(out=t2k[t][:], in0=ybuf[:, oc, cols], scalar=norms_sb[:, gc + oc:gc + oc + 1],
                                                                                                    in1=t2k[r][:], op0=ALU.mult, op1=ALU.mult),
                                 reads=[("y", oc, sub), ("t2k", r), ("norms",)], writes=[("t2k", t)])
                            s.op("dve", lambda e, oc=oc, t=t, xcols=xcols: e.tensor_tensor(out=xT[:, oc, xcols], in0=xT[:, oc, xcols], in1=t2k[t][:], op=ALU.add),
                                 reads=[("t2k", t), xkey(oc, blk)], writes=[xkey(oc, blk)])
                    if layer == n_layers - 1:
                        store_blocks(seg, [half * 2, half * 2 + 1])
                        if seg + 1 < n_seg:
                            load_blocks(seg + 1, [half * 2, half * 2 + 1])
                            if half == 1:
                                load_halo(seg + 1)

            if n_layers == 0:
                store_blocks(seg, range(NBLK))
                if seg + 1 < n_seg:
                    load_blocks(seg + 1, range(NBLK))
                    load_halo(seg + 1)
        s.emit(st)
        import os
        if os.environ.get("KDEBUG"):
            print("SCHED", s.stats)
    return nc


def _chunks(W, col_lists):
    K = W.shape[0]
    kcn = K // 128
    out = []
    for cols in col_lists:
        sub = W[:, cols]
        w = sub.shape[1]
        out.append(sub.reshape(kcn, 128, w).transpose(1, 0, 2).reshape(128, kcn * w))
    return np.ascontiguousarray(np.stack(out, 0), dtype=np.float32)


def prep_weights(inp):
    r = lambda a, b: list(range(a, b))
    cw_in = inp["conv_w_in"][0]
    cin_lists = []
    for cc in range(8):
        cin_lists += [r(1024 + cc * 128, 1024 + (cc + 1) * 128), r(2048 + cc * 128, 2048 + (cc + 1) * 128), r(cc * 128, (cc + 1) * 128)]
    w = {}
    w["w_cin"] = _chunks(cw_in, cin_lists)
    w["w_cout"] = _chunks(inp["conv_w_out"][0], [r(o * 128, (o + 1) * 128) for o in range(8)])
    mw = inp["mlstm_w_in"][0]
    qk_lists = []
    for g in range(4):
        qk_lists += [r(g * 128, (g + 1) * 128), r(512 + g * 128, 512 + (g + 1) * 128)]
    w["w_qk"] = _chunks(mw, qk_lists)
    w["w_o"] = _chunks(mw, [r(2048 + h * 128, 2048 + (h + 1) * 128) for h in range(8)])
    w["w_v"] = _chunks(mw, [r(1024 + g * 256, 1024 + (g + 1) * 256) for g in range(4)])
    gcols = mw[:, 3072:3104]
    gi = np.zeros((1024, 40), np.float32)
    gf = np.zeros((1024, 40), np.float32)
    gi[:, 0:8] = gcols[:, 0:8]
    gf[:, 0:8] = gcols[:, 8:16]
    gi[:, 32:40] = gcols[:, 16:24]
    gf[:, 32:40] = gcols[:, 24:32]
    wgi = _chunks(gi, [r(0, 40)])[0]
    wgf = _chunks(gf, [r(0, 40)])[0]
    w["w_g"] = np.ascontiguousarray(np.concatenate([wgi, wgf], axis=1)[None], dtype=np.float32)
    w["w_mout"] = _chunks(inp["mlstm_w_out"][0], [r(o * 128, (o + 1) * 128) for o in range(8)])
    f1 = []
    for l in range(2):
        lists = []
        for j in range(KF):
            lists += [r(j * 128, (j + 1) * 128), r(DFF + j * 128, DFF + (j + 1) * 128)]
        f1.append(_chunks(inp["ffn_w_in"][l], lists))
    w["w_f1"] = np.ascontiguousarray(np.concatenate(f1, 0))
    f2 = [_chunks(inp["ffn_w_out"][l], [r(o * 128, (o + 1) * 128) for o in range(8)]) for l in range(2)]
    w["w_f2"] = np.ascontiguousarray(np.concatenate(f2, 0))
    nr = inp["norms"].reshape(8, 8, 128)
    w["norms_t"] = np.ascontiguousarray(nr.transpose(2, 0, 1).reshape(128, 64), dtype=np.float32)
    cw = inp["conv_w"][0].reshape(3, 8, 128)
    w["convw_t"] = np.ascontiguousarray(cw.transpose(2, 1, 0).reshape(128, 24), dtype=np.float32)
    w["mnorm_t"] = np.ascontiguousarray(inp["mlstm_norm"][0].reshape(8, 128).T, dtype=np.float32)
    bg = inp["mlstm_b_gate"][0]
    bt = np.zeros((40, 2), np.float32)
    bt[0:8, 0] = bg[0:8]
    bt[0:8, 1] = bg[8:16]
    bt[32:40, 0] = bg[16:24]
    bt[32:40, 1] = bg[24:32]
    w["bgate_t"] = bt
    return w


def kernel(x_prompt, x_sample, norms, conv_w_in, conv_w, conv_w_out, mlstm_w_in, mlstm_b_gate,
           mlstm_norm, mlstm_w_out, ffn_w_in, ffn_w_out, _n_layers=2, _do_mlstm=True, _n_seg=NSEG):
    inp = dict(norms=np.asarray(norms, np.float32), conv_w_in=np.asarray(conv_w_in, np.float32),
               conv_w=np.asarray(conv_w, np.float32), conv_w_out=np.asarray(conv_w_out, np.float32),
               mlstm_w_in=np.asarray(mlstm_w_in, np.float32), mlstm_b_gate=np.asarray(mlstm_b_gate, np.float32),
               mlstm_norm=np.asarray(mlstm_norm, np.float32), mlstm_w_out=np.asarray(mlstm_w_out, np.float32),
               ffn_w_in=np.asarray(ffn_w_in, np.float32), ffn_w_out=np.asarray(ffn_w_out, np.float32))
    xp = np.asarray(x_prompt, np.float32)
    xs = np.asarray(x_sample, np.float32)
    w = prep_weights(inp)
    in_maps = []
    for r in range(NCORES):
        b, qd = r // 4, r % 4
        xin = np.empty((NSEG, T, D), np.float32)
        halo = np.zeros((NSEG, 2, D), np.float32)
        xin[0] = xp[b, qd * T:(qd + 1) * T]
        if qd > 0:
            halo[0, 0] = xp[b, qd * T - 1]
        if qd < 3:
            halo[0, 1] = xp[b, (qd + 1) * T]
        xin[1] = xs[2 * r]
        xin[2] = xs[2 * r + 1]
        m = dict(w)
        flp = np.zeros((128, 4, 2), np.float32)
        fls = np.zeros((40, 4), np.float32)
        for k in range(4):
            ff = 1.0 if k < qd else 0.0
            fb = 1.0 if (3 - k) > qd else 0.0
            flp[:, k, 0] = ff
            flp[:, k, 1] = fb
            fls[0:8, k] = ff
            fls[32:40, k] = fb
        m["flp"] = flp.reshape(128, 8)
        m["fls"] = fls
        m["xin"] = xin
        m["halo"] = halo
        in_maps.append(m)
    nc = build_program(n_layers=_n_layers, do_mlstm=_do_mlstm, n_seg=_n_seg)
    res = run_bass_kernel_spmd(nc, in_maps, core_ids=list(range(NCORES)))
    y_prompt = np.empty_like(xp)
    y_sample = np.empty_like(xs)
    for r in range(NCORES):
        y = res.results[r]["yout"]
        b, qd = r // 4, r % 4
        y_prompt[b, qd * T:(qd + 1) * T] = y[0]
        y_sample[2 * r] = y[1]
        y_sample[2 * r + 1] = y[2]
    return (y_prompt, y_sample)
```

```python
import contextlib
import numpy as np
import concourse.bass as bass
import concourse.mybir as mybir
from concourse.bass_utils import run_bass_kernel_spmd

F32 = mybir.dt.float32
BF16 = mybir.dt.bfloat16
AF = mybir.ActivationFunctionType
ALU = mybir.AluOpType
AX = mybir.AxisListType

D = 1024
T = 2048
NSEG = 3
NBLK = 4
NCH = 16
DFF = 2816
KF = 22
EPS = 1e-6
NEG = -1.0e30
NCORES = 8


class _Rec:
    def __getattr__(self, name):
        def f(*a, **kw):
            self.call = (name, a, kw)
            return self
        return f


class Sched:
    ENGS = ("pe", "act", "dve", "pool", "sp")

    def __init__(self, nc):
        self.nc = nc
        self.ops = []
        self.lastw = {}
        self.readers = {}
        self.dma_keys = []
        self.pending = {}
        self.touched = set()
        self.tags = []
        self.reorder = True
        self.last_pe = None
        self.window = 64
        import os
        self.reorder_engs = tuple(os.environ.get("KREORDER", "pe,act,dve,pool").split(","))
        self.xlat = 1000.0

    def alias(self, new_names, old_names):
        old = set(old_names)
        dset = set()
        for k, w in self.lastw.items():
            if k[0] in old:
                dset.add(w)
        for k, rs in self.readers.items():
            if k[0] in old:
                dset.update(rs)
        for n in new_names:
            self.pending[n] = set(self.pending.get(n, set())) | dset
            self.touched = {k for k in self.touched if k[0] != n}

    @staticmethod
    def _cost(eng, name, a, kw, dma):
        out = kw.get("out", a[0] if a else None)
        try:
            shp = out.shape
            free = 1
            for d_ in shp[1:]:
                free *= d_
        except Exception:
            free = 512
        if name == "collective_compute":
            return (500.0, 40000.0)
        if dma is not None:
            try:
                nbytes = out.nbytes()
            except Exception:
                nbytes = free * 4 * 128
            return (150.0, 2500.0 + nbytes / 120.0)
        if eng == "pe":
            return (max(64, free) * 0.50 + 35.0, 250.0)
        if eng == "act":
            return (210.0 + free * 0.65, 250.0)
        if eng == "dve":
            return (90.0 + free * 0.95, 250.0)
        if eng == "pool":
            return (250.0 + free * 2.6, 300.0)
        return (100.0, 100.0)

    def op(self, eng, fn, reads=(), writes=(), dma=None, inc=16, after=()):
        deps = set(after)
        for k in list(reads) + list(writes):
            if k[0] in self.pending and k not in self.touched:
                deps |= self.pending[k[0]]
                self.touched.add(k)
        for k in reads:
            w = self.lastw.get(k)
            if w is not None:
                deps.add(w)
        for k in writes:
            w = self.lastw.get(k)
            if w is not None:
                deps.add(w)
            for r in self.readers.get(k, ()):
                deps.add(r)
        i = len(self.ops)
        rec = _Rec()
        fn(rec)
        name, a, kw = rec.call
        fn = (lambda e, name=name, a=a, kw=kw: getattr(e, name)(*a, **kw))
        self.ops.append(dict(eng=eng, fn=fn, deps=deps, dma=dma, inc=inc))
        if eng == "pe":
            self.last_pe = i
        if dma is not None and dma not in self.dma_keys:
            self.dma_keys.append(dma)
        tag = ("d", dma) if dma is not None else ("e", eng)
        order = set()
        for k in reads:
            lst = self.readers.setdefault(k, [])
            for r in lst:
                if self.tags[r] == tag:
                    order.add(r)
            lst[:] = [r for r in lst if self.tags[r] != tag]
            lst.append(i)
        self.tags.append(tag)
        self.ops[i]["order"] = order
        self.ops[i]["cost"] = self._cost(eng, name, a, kw, dma)
        for k in writes:
            self.lastw[k] = i
            self.readers[k] = []
        return i

    def _list_schedule(self):
        ops = self.ops
        n = len(ops)
        full = {e: [] for e in self.ENGS}
        for i, o in enumerate(ops):
            full[o["eng"]].append(i)
        nxt = {e: 0 for e in self.ENGS}
        win = {e: [] for e in self.ENGS}
        done = [False] * n
        fin = [0.0] * n
        free_t = {e: 0.0 for e in self.ENGS}
        out = {e: [] for e in self.ENGS}
        preds = [list(o["deps"] | o["order"]) for o in ops]
        succ_eng = [set() for _ in range(n)]
        for i, o in enumerate(ops):
            for d in preds[i]:
                succ_eng[d].add(o["eng"])
        W = self.window
        xlat = self.xlat

        def refill(e):
            Wl = W if e in self.reorder_engs else 1
            w = win[e]
            f = full[e]
            while len(w) < Wl and nxt[e] < len(f):
                w.append(f[nxt[e]])
                nxt[e] += 1

        def best(e):
            bi = None
            bt = None
            ft = free_t[e]
            for i in win[e]:
                ok = True
                rt = 0.0
                for d in preds[i]:
                    if not done[d]:
                        ok = False
                        break
                    od = ops[d]
                    t = fin[d] + (0.0 if (od["eng"] == e and od["dma"] is None) else xlat)
                    if t > rt:
                        rt = t
                if not ok:
                    continue
                st = rt if rt > ft else ft
                if bt is None or st < bt - 1e-9:
                    bi, bt = i, st
                    if st <= ft + 1e-9:
                        break
            return bi, bt

        for e in self.ENGS:
            refill(e)
        cand = {}
        remaining = n
        while remaining:
            choice = None
            for e in self.ENGS:
                if not win[e]:
                    continue
                if e not in cand:
                    cand[e] = best(e)
                bi, bt = cand[e]
                if bi is None:
                    continue
                if choice is None or bt < choice[1]:
                    choice = (bi, bt, e)
            assert choice is not None, "scheduler deadlock"
            i, st, e = choice
            busy, lat = ops[i]["cost"]
            free_t[e] = st + busy
            fin[i] = st + busy + lat
            done[i] = True
            win[e].remove(i)
            refill(e)
            out[e].append(i)
            remaining -= 1
            cand.pop(e, None)
            for e2 in succ_eng[i]:
                cand.pop(e2, None)
        self.sim_time = max(fin) if fin else 0.0
        return out

    def emit(self, stack):
        nc = self.nc
        ops = self.ops
        per_eng_sched = self._list_schedule() if self.reorder else None
        needed = [False] * len(ops)
        for o in ops:
            if o["eng"] == "pe":
                o["wdeps"] = {d for d in o["deps"] if not (ops[d]["eng"] == "pe" and ops[d]["dma"] is None)}
            else:
                o["wdeps"] = o["deps"]
            for d in o["wdeps"]:
                needed[d] = True
        esem = {e: stack.enter_context(nc.semaphore("s_" + e)) for e in self.ENGS}
        dsem = {k: stack.enter_context(nc.semaphore("d_%d" % i)) for i, k in enumerate(self.dma_keys)}
        cnt = {e: 0 for e in self.ENGS}
        dcnt = {k: 0 for k in self.dma_keys}
        token = [None] * len(ops)
        per_eng = {e: [] for e in self.ENGS}
        if per_eng_sched is not None:
            seq = [i for e in self.ENGS for i in per_eng_sched[e]]
        else:
            seq = list(range(len(ops)))
        for i in seq:
            o = ops[i]
            per_eng[o["eng"]].append(i)
            if o["dma"] is not None:
                dcnt[o["dma"]] += o["inc"]
                token[i] = (("d", o["dma"]), dsem[o["dma"]], dcnt[o["dma"]])
            elif needed[i]:
                cnt[o["eng"]] += 1
                token[i] = (("e", o["eng"]), esem[o["eng"]], cnt[o["eng"]])
        self.stats = dict(sim_ms=getattr(self, "sim_time", 0.0) / 1e6, nops=len(ops), cnt=dict(cnt), ndma=len(self.dma_keys),
                          per_eng={e: len(v) for e, v in per_eng.items()})
        self._last_order = per_eng
        self._last_token = token
        block = stack.enter_context(nc.Block())
        handles = {"pe": block.tensor, "act": block.scalar, "dve": block.vector,
                   "pool": block.gpsimd, "sp": block.sync}
        final_d = dict(dcnt)
        final_e = dict(cnt)

        def make_body(e):
            def body(eng):
                seen = {}
                for i in per_eng[e]:
                    o = ops[i]
                    waits = {}
                    for d in o["wdeps"]:
                        t = token[d]
                        if t is None:
                            continue
                        name, sem, val = t
                        if seen.get(name, 0) >= val:
                            continue
                        if name not in waits or waits[name][1] < val:
                            waits[name] = (sem, val)
                    if o["dma"] is not None:
                        name, sem, val = token[i]
                        prev = val - o["inc"]
                        if prev > 0 and seen.get(name, 0) < prev:
                            waits[name] = (sem, prev)
                    for name, (sem, val) in waits.items():
                        eng.wait_ge(sem, val)
                        seen[name] = val
                    ins = o["fn"](eng)
                    t = token[i]
                    if t is not None:
                        ins.then_inc(t[1], o["inc"] if o["dma"] is not None else 1)
                if e == "sp":
                    for k, v in final_d.items():
                        if v:
                            eng.wait_ge(dsem[k], v)
                    for e2, v in final_e.items():
                        if v and e2 != "sp":
                            eng.wait_ge(esem[e2], v)
            return body

        for e in self.ENGS:
            handles[e](make_body(e))


def build_program(n_layers=2, do_mlstm=True, n_seg=NSEG):
    nc = bass.Bass("TRN2", target_bir_lowering=False)
    dr = lambda name, shape, dt=F32, kind="ExternalInput": nc.dram_tensor(name, shape, dt, kind=kind).ap()
    xin = dr("xin", [NSEG, T, D])
    halo = dr("halo", [NSEG, 2, D])
    yout = dr("yout", [NSEG, T, D], kind="ExternalOutput")
    wspec = {
        "w_cin": (24, 1024), "w_cout": (8, 1024), "w_qk": (8, 1024), "w_o": (8, 1024),
        "w_v": (4, 2048), "w_g": (1, 640), "w_mout": (8, 1024),
        "w_f1": (88, 1024), "w_f2": (16, DFF),
    }
    wf = {k: dr(k, [n, 128, w]) for k, (n, w) in wspec.items()}
    wb = {k: nc.dram_tensor("b" + k, [n, 128, w], BF16).ap() for k, (n, w) in wspec.items()}
    norms_d = dr("norms_t", [128, 64])
    convw_d = dr("convw_t", [128, 24])
    mnorm_d = dr("mnorm_t", [128, 8])
    bgate_d = dr("bgate_t", [40, 2])
    flp_d = dr("flp", [128, 8])
    fls_d = dr("fls", [40, 4])
    summ_f = nc.dram_tensor("summ_f", [128, 1032], F32).ap()
    summ_b = nc.dram_tensor("summ_b", [128, 1056], F32).ap()
    sout_f = nc.dram_tensor("sout_f", [512, 1032], F32).ap()
    sout_b = nc.dram_tensor("sout_b", [512, 1056], F32).ap()
    cin_d = nc.dram_tensor("cin_d", [128, 2064], F32).ap()

    st = contextlib.ExitStack()
    with st:
        SB = lambda name, shape, dt: st.enter_context(nc.sbuf_tensor(name, shape, dt))
        s = Sched(nc)
        xT = SB("xT", [128, 8, T], F32)
        ARENA_W = 22568
        arena = SB("arena", [128, ARENA_W], F32)

        def aview(off_b, nbytes, dt):
            a = arena[:, off_b // 4:(off_b + nbytes) // 4]
            return a.bitcast(dt) if dt != F32 else a

        wchunk = [SB("wch%d" % i, [128, 1024], BF16) for i in range(4)]
        wbig = [SB("wbig%d" % i, [128, DFF], BF16) for i in range(2)]
        wg_sb = SB("wg_sb", [128, 640], BF16)
        t2k = [SB("t2k%d" % i, [128, 512], F32) for i in range(6)]
        sqb = [SB("sqb%d" % i, [128, 512], BF16) for i in range(2)]
        ident_f = SB("ident_f", [128, 128], F32)
        ident_b = SB("ident_b", [128, 128], BF16)
        ones_b = SB("ones_b", [128, 128], BF16)
        ones_f = SB("ones_f", [128, 128], F32)
        mask = SB("mask", [128, 2, 128], F32)
        norms_sb = SB("norms_sb", [128, 64], F32)
        convw_sb = SB("convw_sb", [128, 24], F32)
        mnorm_sb = SB("mnorm_sb", [128, 8], F32)
        bgate_sb = SB("bgate_sb", [40, 2], F32)
        negbf = SB("negbf", [40, 1], F32)
        sel = SB("sel", [40, 16], F32)
        xTh = SB("xTh", [128, 8, 2], F32)
        e_tok = SB("e_tok", [128, NCH, 16], F32)
        cl_tok = SB("cl_tok", [128, NCH, 16], F32)
        dec_rep = SB("dec_rep", [128, NCH, 16], F32)
        decp = SB("decp", [128, NCH, 2, 4], F32)
        g_amax = SB("g_amax", [40, NCH], F32)
        g_bn = SB("g_bn", [40, NCH], F32)
        g_m = SB("g_m", [40, NCH], F32)
        g_mp = SB("g_mp", [40, NCH], F32)
        g_mout = SB("g_mout", [40, 1], F32)
        g_dec = SB("g_dec", [40, NCH], F32)
        g_bd = SB("g_bd", [40, NCH, 16], F32)
        PT = [SB("PT%d" % i, [128, 128], BF16) for i in range(8)]
        kp = [SB("kp%d" % i, [128, 128], BF16) for i in range(6)]
        Cp = [SB("Cp%d" % i, [128, 258], BF16) for i in range(3)]
        S_f = SB("S_f", [128, 258], F32)
        S_b = SB("S_b", [128, 258], F32)
        hs = [SB("hs%d" % i, [128, 128], F32) for i in range(4)]
        hn = [SB("hn%d" % i, [128, 128], BF16) for i in range(4)]
        hjunk = SB("hjunk", [128, 128], BF16)
        sm = [SB("sm%d" % i, [128, 8], F32) for i in range(8)]
        flp_sb = SB("flp_sb", [128, 8], F32)
        flpB = SB("flpB", [128, 8], F32)
        fls_sb = SB("fls_sb", [40, 4], F32)
        flsB = SB("flsB", [40, 4], F32)
        g_sv = SB("g_sv", [40, 2], F32)
        g_bd2 = SB("g_bd2", [40, 2, 16], F32)
        svrep = SB("svrep", [128, 32], F32)
        sc_p = SB("sc_p", [128, 16], F32)
        mq = SB("mq", [128, 4, 8], F32)
        Fq = SB("Fq", [128, 4, 8], F32)
        a1 = SB("a1", [128, 4, 8], F32)
        a2 = SB("a2", [128, 4, 8], F32)
        cm = SB("cm", [128, 8], F32)
        ct1 = SB("ct1", [128, 8], F32)
        ct2 = SB("ct2", [128, 8], F32)
        svq = SB("svq", [40, 4, 2], F32)
        ms = SB("ms", [40, 1], F32)
        st1 = SB("st1", [40, 1], F32)
        st2 = SB("st2", [40, 1], F32)

        psb = [st.enter_context(nc.psum_tensor("psb%d" % i, [128, 512], F32)) for i in range(8)]

        rr = {}

        def rot(name, n):
            i = rr.get(name, 0)
            rr[name] = i + 1
            return i % n

        accmode = {"wide": False}

        def acc_bank():
            if accmode["wide"]:
                return (0, 1, 2, 3, 4, 5, 7)[rot("accw", 7)]
            return rot("acc", 5)
        STAT = 5
        MISC = 6
        MISC2 = 7

        def P(b):
            return ("ps", b)

        s.op("pool", lambda e: e.memset(ident_f[:], 0.0), writes=[("ident_f",)])
        s.op("pool", lambda e: e.affine_select(out=ident_f[:], in_=ident_f[:], pattern=[[-1, 128]],
                                               compare_op=ALU.not_equal, fill=1.0, base=0, channel_multiplier=1),
             reads=[("ident_f",)], writes=[("ident_f",)])
        s.op("pool", lambda e: e.tensor_copy(out=ident_b[:], in_=ident_f[:]), reads=[("ident_f",)], writes=[("ident_b",)])
        s.op("pool", lambda e: e.memset(ones_b[:], 1.0), writes=[("ones_b",)])
        s.op("pool", lambda e: e.memset(ones_f[:], 1.0), writes=[("ones_f",)])
        s.op("pool", lambda e: e.affine_select(out=mask[:, 0, :], in_=ones_f[:], pattern=[[1, 128]],
                                               compare_op=ALU.is_ge, fill=0.0, base=0, channel_multiplier=-1),
             reads=[("ones_f",)], writes=[("mask", 0)])
        s.op("pool", lambda e: e.affine_select(out=mask[:, 1, :], in_=ones_f[:], pattern=[[-1, 128]],
                                               compare_op=ALU.is_ge, fill=0.0, base=0, channel_multiplier=1),
             reads=[("ones_f",)], writes=[("mask", 1)])
        s.op("pool", lambda e: e.tensor_copy(out=sel[:, 0:8], in_=ident_f[0:40, 0:8]), reads=[("ident_f",)], writes=[("sel", 0)])
        s.op("pool", lambda e: e.tensor_copy(out=sel[:, 8:16], in_=ident_f[0:40, 32:40]), reads=[("ident_f",)], writes=[("sel", 1)])
        s.op("pool", lambda e: e.memset(g_m[:], 0.0), writes=[("g", "m")])
        s.op("pool", lambda e: e.memset(g_mout[:], 0.0), writes=[("g", "mp")])
        s.op("pool", lambda e: e.memset(g_amax[:], 0.0), writes=[("g", "amax")])
        s.op("pool", lambda e: e.memset(g_bn[:], 0.0), writes=[("g", "bn")])
        s.op("sp", lambda e: e.dma_start(out=norms_sb[:], in_=norms_d), writes=[("norms",)], dma="c_norms")
        s.op("sp", lambda e: e.dma_start(out=convw_sb[:], in_=convw_d), writes=[("convw",)], dma="c_convw")
        s.op("sp", lambda e: e.dma_start(out=mnorm_sb[:], in_=mnorm_d), writes=[("mnorm",)], dma="c_mnorm")
        s.op("sp", lambda e: e.dma_start(out=bgate_sb[:], in_=bgate_d), writes=[("bgate",)], dma="c_bgate")
        s.op("dve", lambda e: e.tensor_scalar(out=negbf[:], in0=bgate_sb[:, 1:2], scalar1=-1.0, scalar2=None, op0=ALU.mult),
             reads=[("bgate",)], writes=[("negbf",)])

        cast_order = ["w_cin", "w_cout", "w_f1:0", "w_f2:0", "w_g", "w_qk", "w_v", "w_o", "w_mout", "w_f1:1", "w_f2:1"]
        cast_pieces = []
        for item in cast_order:
            if ":" in item:
                nm, l = item.split(":")
                l = int(l)
                n = wspec[nm][0] // 2
                lo, hi = l * n, (l + 1) * n
            else:
                nm = item
                lo, hi = 0, wspec[nm][0]
            step = 8 if wspec[nm][1] <= 1024 else 4
            for a in range(lo, hi, step):
                cast_pieces.append((nm, a, min(hi, a + step)))

        def flush_cast(n=1, gate=True):
            for _ in range(n):
                if not cast_pieces:
                    return
                nm, a, b = cast_pieces.pop(0)
                aft = [s.last_pe] if (gate and s.last_pe is not None) else []
                s.op("pool", lambda e: e.dma_start(out=wb[nm][a:b], in_=wf[nm][a:b]),
                     writes=[("wb", nm, j) for j in range(a, b)], dma=("cast", nm, a), after=aft)

        flush_cast(2, gate=False)

        def load_chunk(nm, j):
            sl = rot("wch", 4)
            s.op("sp", lambda e: e.dma_start(out=wchunk[sl][:], in_=wb[nm][j]),
                 reads=[("wb", nm, j)], writes=[("wch", sl)], dma=("wch", sl))
            return wchunk[sl], ("wch", sl)

        def load_big(nm, j, width):
            sl = rot("wbig", 2)
            s.op("sp", lambda e: e.dma_start(out=wbig[sl][:, 0:width], in_=wb[nm][j]),
                 reads=[("wb", nm, j)], writes=[("wbig", sl)], dma=("wbig", sl))
            return wbig[sl], ("wbig", sl)

        gcol = lambda l, n: (l * 4 + n) * 8

        def rms_T(src, srckeys, n, gc, dst, dstkeys, npart=128):
            for c in range(8):
                q = rot("sqb", 2)
                s.op("act", lambda e, c=c, q=q: e.activation(out=sqb[q][:, 0:n], in_=src(c), func=AF.Square),
                     reads=[srckeys(c)], writes=[("sqb", q)])
                s.op("pe", lambda e, c=c, q=q: e.matmul(psb[STAT][:, 0:n], lhsT=ones_b[:], rhs=sqb[q][:, 0:n],
                                                        start=(c == 0), stop=(c == 7)),
                     reads=[("sqb", q), ("ones_b",)], writes=[P(STAT)])
            r = rot("t2k", 6)
            s.op("act", lambda e: e.activation(out=t2k[r][:, 0:n], in_=psb[STAT][:, 0:n], func=AF.Sqrt, bias=EPS, scale=1.0 / D),
                 reads=[P(STAT)], writes=[("t2k", r)])
            s.op("dve", lambda e: e.reciprocal(out=t2k[r][:, 0:n], in_=t2k[r][:, 0:n]), reads=[("t2k", r)], writes=[("t2k", r)])
            for c in range(8):
                s.op("dve", lambda e, c=c: e.scalar_tensor_tensor(out=dst(c), in0=src(c), scalar=norms_sb[:, gc + c:gc + c + 1],
                                                                  in1=t2k[r][:, 0:n], op0=ALU.mult, op1=ALU.mult),
                     reads=[srckeys(c), ("t2k", r), ("norms",)], writes=[dstkeys(c)])

        def out_proj_block(wres, wreskeys, nk, rhs, rhskeys, ybuf, ykey, col0, n, gc, statbank):
            for oc in range(8):
                b = acc_bank()
                for kc in range(nk):
                    s.op("pe", lambda e, oc=oc, kc=kc, b=b: e.matmul(psb[b][:, 0:n], lhsT=wres(oc, kc), rhs=rhs(kc),
                                                                     start=(kc == 0), stop=(kc == nk - 1)),
                         reads=[wreskeys(oc), rhskeys(kc)], writes=[P(b)])
                s.op("act", lambda e, oc=oc, b=b: e.copy(out=ybuf(oc), in_=psb[b][:, 0:n]), reads=[P(b)], writes=[(ykey, oc, col0)])
                q = rot("sqb", 2)
                s.op("act", lambda e, oc=oc, q=q, b=b: e.activation(out=sqb[q][:, 0:n], in_=psb[b][:, 0:n], func=AF.Square),
                     reads=[P(b)], writes=[("sqb", q)])
                s.op("pe", lambda e, oc=oc, q=q: e.matmul(psb[statbank][:, 0:n], lhsT=ones_b[:], rhs=sqb[q][:, 0:n],
                                                          start=(oc == 0), stop=(oc == 7)),
                     reads=[("sqb", q), ("ones_b",)], writes=[P(statbank)])

        def resid_block(ybuf, ykey, col0, n, gc, statbank):
            r = rot("t2k", 6)
            s.op("act", lambda e: e.activation(out=t2k[r][:, 0:n], in_=psb[statbank][:, 0:n], func=AF.Sqrt, bias=EPS, scale=1.0 / D),
                 reads=[P(statbank)], writes=[("t2k", r)])
            s.op("dve", lambda e: e.reciprocal(out=t2k[r][:, 0:n], in_=t2k[r][:, 0:n]), reads=[("t2k", r)], writes=[("t2k", r)])
            for oc in range(8):
                t = rot("t2k", 6)
                while t == r:
                    t = rot("t2k", 6)
                s.op("dve", lambda e, oc=oc, t=t: e.scalar_tensor_tensor(out=t2k[t][:, 0:n], in0=ybuf(oc),
                                                                         scalar=norms_sb[:, gc + oc:gc + oc + 1],
                                                                         in1=t2k[r][:, 0:n], op0=ALU.mult, op1=ALU.mult),
                     reads=[(ykey, oc, col0), ("t2k", r), ("norms",)], writes=[("t2k", t)])
                s.op("pool", lambda e, oc=oc, t=t: e.tensor_tensor(out=xT[:, oc, col0:col0 + n], in0=xT[:, oc, col0:col0 + n],
                                                                   in1=t2k[t][:, 0:n], op=ALU.add),
                     reads=[("t2k", t), ("xT", oc, col0 // 512)], writes=[("xT", oc, col0 // 512)])

        xkey = lambda c, blk: ("xT", c, blk)

        import os as _os
        XQ = _os.environ.get("KXQ", "pool")

        def load_blocks(seg, blks):
            for blk in blks:
                for tt in range(blk * 4, blk * 4 + 4):
                    for hf in range(2):
                        r = rot("t2k", 6)
                        s.op(XQ, lambda e: e.dma_start(out=t2k[r][:], in_=xin[seg, tt * 128:(tt + 1) * 128, hf * 512:(hf + 1) * 512]),
                             writes=[("t2k", r)], dma=("xl", r))
                        bk = acc_bank()
                        for q in range(4):
                            s.op("pe", lambda e: e.transpose(out=psb[bk][:, q * 128:(q + 1) * 128], in_=t2k[r][:, q * 128:(q + 1) * 128], identity=ident_f[:]),
                                 reads=[("t2k", r), ("ident_f",)], writes=[P(bk)])
                        outap = xT[:, hf * 4:(hf + 1) * 4, tt * 128:(tt + 1) * 128]
                        inap = psb[bk][:].rearrange("p (q t) -> p q t", q=4)
                        wk = [xkey(hf * 4 + q, tt // 4) for q in range(4)]
                        if (tt + hf) % 2 == 0:
                            s.op("act", lambda e: e.copy(out=outap, in_=inap), reads=[P(bk)], writes=wk)
                        else:
                            s.op("dve", lambda e: e.tensor_copy(out=outap, in_=inap), reads=[P(bk)], writes=wk)

        def load_halo(seg):
            for hf in range(2):
                r = rot("t2k", 6)
                s.op(XQ, lambda e: e.dma_start(out=t2k[r][0:2, :], in_=halo[seg, :, hf * 512:(hf + 1) * 512]), writes=[("t2k", r)], dma=("xl", r))
                for q in range(4):
                    c = hf * 4 + q
                    s.op("pe", lambda e: e.transpose(out=psb[MISC][:, c * 2:(c + 1) * 2], in_=t2k[r][0:2, q * 128:(q + 1) * 128],
                                                     identity=ident_f[0:2, 0:2]),
                         reads=[("t2k", r), ("ident_f",)], writes=[P(MISC)])
            s.op("act", lambda e: e.copy(out=xTh[:], in_=psb[MISC][:, 0:16].rearrange("p (c t) -> p c t", t=2)), reads=[P(MISC)], writes=[("xTh",)])

        def store_blocks(seg, blks):
            for blk in blks:
                for tt in range(blk * 4, blk * 4 + 4):
                    for hf in range(2):
                        bk = acc_bank()
                        for q in range(4):
                            c = hf * 4 + q
                            s.op("pe", lambda e: e.transpose(out=psb[bk][:, q * 128:(q + 1) * 128], in_=xT[:, c, tt * 128:(tt + 1) * 128], identity=ident_f[:]),
                                 reads=[xkey(c, tt // 4), ("ident_f",)], writes=[P(bk)])
                        r = rot("t2k", 6)
                        if (tt + hf) % 2 == 0:
                            s.op("act", lambda e: e.copy(out=t2k[r][:], in_=psb[bk][:]), reads=[P(bk)], writes=[("t2k", r)])
                        else:
                            s.op("dve", lambda e: e.tensor_copy(out=t2k[r][:], in_=psb[bk][:]), reads=[P(bk)], writes=[("t2k", r)])
                        s.op(XQ, lambda e: e.dma_start(out=yout[seg, tt * 128:(tt + 1) * 128, hf * 512:(hf + 1) * 512], in_=t2k[r][:]),
                             reads=[("t2k", r)], writes=[("yout", seg, tt, hf)], dma=("xs", r))

        for seg in range(n_seg):
            if seg == 0:
                load_blocks(0, range(NBLK))
                load_halo(0)

            for layer in range(n_layers):
                if layer == 0:
                    xnT = aview(0, 32800, BF16).rearrange("p (c t) -> p c t", c=8)
                    zT = aview(32800, 32768, BF16).rearrange("p (c t) -> p c t", c=8)
                    c_sb = aview(65568, 8200, F32)
                    u_sb = aview(73768, 8200, F32)
                    s.alias(["xnT", "zT", "c_sb", "u_sb"], ["xnT", "zT", "c_sb", "u_sb", "wres", "mix", "xn2", "act", "y", "hT", "gt", "pair"])
                    for blk in range(NBLK):
                        rms_T(lambda c, blk=blk: xT[:, c, blk * 512:(blk + 1) * 512], lambda c, blk=blk: xkey(c, blk), 512, gcol(0, 0),
                              lambda c, blk=blk: xnT[:, c, 1 + blk * 512:1 + (blk + 1) * 512], lambda c, blk=blk: ("xnT", c, blk))
                    rms_T(lambda c: xTh[:, c, :], lambda c: ("xTh",), 2, gcol(0, 0),
                          lambda c: xnT[:, c, 0:2050:2049], lambda c: ("xnT", c, "h"))
                    for cc in range(8):
                        if seg == 0:
                            flush_cast(1)
                        wc, kwc = load_chunk("w_cin", 3 * cc + 0)
                        wv_, kwv = load_chunk("w_cin", 3 * cc + 1)
                        wb_, kwb = load_chunk("w_cin", 3 * cc + 2)
                        for blk in range(NBLK):
                            b = acc_bank()
                            for kc in range(8):
                                s.op("pe", lambda e, kc=kc, b=b, blk=blk, wc=wc: e.matmul(psb[b][:], lhsT=wc[:, kc * 128:(kc + 1) * 128],
                                                                                       rhs=xnT[:, kc, 1 + blk * 512:1 + (blk + 1) * 512],
                                                                                       start=(kc == 0), stop=(kc == 7)),
                                     reads=[kwc, ("xnT", kc, blk)], writes=[P(b)])
                            s.op("act", lambda e, b=b, blk=blk: e.copy(out=c_sb[:, 1 + blk * 512:1 + (blk + 1) * 512], in_=psb[b][:]),
                                 reads=[P(b)], writes=[("c_sb", blk)])
                        for kc in range(8):
                            s.op("pe", lambda e, kc=kc, wc=wc: e.matmul(psb[MISC][:, 0:2], lhsT=wc[:, kc * 128:(kc + 1) * 128], rhs=xnT[:, kc, 0:2050:2049],
                                                                     start=(kc == 0), stop=(kc == 7)),
                                 reads=[kwc, ("xnT", kc, "h")], writes=[P(MISC)])
                        s.op("act", lambda e: e.copy(out=c_sb[:, 0:2050:2049], in_=psb[MISC][:, 0:2]), reads=[P(MISC)], writes=[("c_sb", "h")])
                        for blk in range(NBLK):
                            b = acc_bank()
                            for kc in range(8):
                                s.op("pe", lambda e, kc=kc, b=b, blk=blk, wv_=wv_: e.matmul(psb[b][:], lhsT=wv_[:, kc * 128:(kc + 1) * 128],
                                                                                         rhs=xnT[:, kc, 1 + blk * 512:1 + (blk + 1) * 512],
                                                                                         start=(kc == 0), stop=(kc == 7)),
                                     reads=[kwv, ("xnT", kc, blk)], writes=[P(b)])
                            s.op("dve", lambda e, b=b, blk=blk: e.tensor_tensor(out=u_sb[:, 1 + blk * 512:1 + (blk + 1) * 512], in0=psb[b][:],
                                                                               in1=c_sb[:, 1 + blk * 512:1 + (blk + 1) * 512], op=ALU.mult),
                                 reads=[P(b), ("c_sb", blk)], writes=[("u_sb", blk)])
                        for kc in range(8):
                            s.op("pe", lambda e, kc=kc, wv_=wv_: e.matmul(psb[MISC][:, 0:2], lhsT=wv_[:, kc * 128:(kc + 1) * 128], rhs=xnT[:, kc, 0:2050:2049],
                                                                       start=(kc == 0), stop=(kc == 7)),
                                 reads=[kwv, ("xnT", kc, "h")], writes=[P(MISC)])
                        s.op("dve", lambda e: e.tensor_tensor(out=u_sb[:, 0:2050:2049], in0=psb[MISC][:, 0:2], in1=c_sb[:, 0:2050:2049], op=ALU.mult),
                             reads=[P(MISC), ("c_sb", "h")], writes=[("u_sb", "h")])
                        ukeys = [("u_sb", k) for k in (0, 1, 2, 3, "h")]
                        ckeys = [("c_sb", k) for k in (0, 1, 2, 3, "h")]
                        cw = lambda j, cc=cc: convw_sb[:, cc * 3 + j:cc * 3 + j + 1]
                        s.op("act", lambda e, cw=cw: e.activation(out=c_sb[:, 1:2049], in_=u_sb[:, 0:2048], func=AF.Copy, scale=cw(0)),
                             reads=ukeys + [("convw",)], writes=ckeys)
                        s.op("dve", lambda e, cw=cw: e.scalar_tensor_tensor(out=c_sb[:, 1:2049], in0=u_sb[:, 1:2049], scalar=cw(1), in1=c_sb[:, 1:2049],
                                                                          op0=ALU.mult, op1=ALU.add),
                             reads=ukeys + ckeys + [("convw",)], writes=ckeys)
                        s.op("dve", lambda e, cw=cw: e.scalar_tensor_tensor(out=c_sb[:, 1:2049], in0=u_sb[:, 2:2050], scalar=cw(2), in1=c_sb[:, 1:2049],
                                                                          op0=ALU.mult, op1=ALU.add),
                             reads=ukeys + ckeys + [("convw",)], writes=ckeys)
                        for blk in range(NBLK):
                            b = acc_bank()
                            for kc in range(8):
                                s.op("pe", lambda e, kc=kc, b=b, blk=blk, wb_=wb_: e.matmul(psb[b][:], lhsT=wb_[:, kc * 128:(kc + 1) * 128],
                                                                                         rhs=xnT[:, kc, 1 + blk * 512:1 + (blk + 1) * 512],
                                                                                         start=(kc == 0), stop=(kc == 7)),
                                     reads=[kwb, ("xnT", kc, blk)], writes=[P(b)])
                            s.op("dve", lambda e, b=b, blk=blk, cc=cc: e.tensor_tensor(out=zT[:, cc, blk * 512:(blk + 1) * 512], in0=psb[b][:],
                                                                                      in1=c_sb[:, 1 + blk * 512:1 + (blk + 1) * 512], op=ALU.mult),
                                 reads=[P(b)] + ckeys, writes=[("zT", cc, blk)])
                    mixer_in = zT
                    mixer_key = "zT"
                    wout_name = "w_cout"
                else:
                    if seg == 0:
                        flush_cast(100)
                        s.op("sp", lambda e: e.dma_start(out=wg_sb[:], in_=wb["w_g"][0]), reads=[("wb", "w_g", 0)], writes=[("wg",)], dma="c_wg")
                    xnT = aview(0, 32768, BF16).rearrange("p (c t) -> p c t", c=8)
                    hT = aview(32800, 32768, BF16).rearrange("p (c t) -> p c t", c=8)
                    s.alias(["xnT", "hT", "gt", "pair"], ["xnT", "zT", "c_sb", "u_sb", "wres", "mix", "xn2", "act", "y", "hT", "gt", "pair"])
                    for blk in range(NBLK):
                        rms_T(lambda c, blk=blk: xT[:, c, blk * 512:(blk + 1) * 512], lambda c, blk=blk: xkey(c, blk), 512, gcol(1, 0),
                              lambda c, blk=blk: xnT[:, c, blk * 512:(blk + 1) * 512], lambda c, blk=blk: ("xnT", c, blk))
                    GI = aview(32800, 8192, F32)
                    CS = aview(32800 + 8192, 8192, F32)
                    SP = aview(32800 + 16384, 8192, F32)
                    GI3 = GI.rearrange("p (c t) -> p c t", t=128)
                    SP3 = SP.rearrange("p (c t) -> p c t", t=128)
                    CS3 = CS.rearrange("p (c t) -> p c t", t=128)
                    for blk in range(NBLK):
                        cols = slice(blk * 512, (blk + 1) * 512)
                        for gi in range(2):
                            b = acc_bank()
                            for kc in range(8):
                                s.op("pe", lambda e: e.matmul(psb[b][0:40, :], lhsT=wg_sb[:, gi * 320 + kc * 40:gi * 320 + (kc + 1) * 40],
                                                              rhs=xnT[:, kc, cols], start=(kc == 0), stop=(kc == 7)),
                                     reads=[("wg",), ("xnT", kc, blk)], writes=[P(b)])
                            if gi == 0:
                                s.op("act", lambda e: e.activation(out=GI[0:40, cols], in_=psb[b][0:40, :], func=AF.Identity, bias=bgate_sb[:, 0:1]),
                                     reads=[P(b), ("bgate",)], writes=[("gt", "GI")])
                            else:
                                s.op("act", lambda e: e.activation(out=SP[0:40, cols], in_=psb[b][0:40, :], func=AF.Exp, bias=negbf[:, 0:1], scale=-1.0),
                                     reads=[P(b), ("negbf",)], writes=[("gt", "SP")])
                    s.op("act", lambda e: e.activation(out=SP[0:40, :], in_=SP[0:40, :], func=AF.Ln, bias=1.0), reads=[("gt", "SP")], writes=[("gt", "SP")])
                    for c in range(NCH):
                        s.op("dve", lambda e: e.tensor_tensor_scan(out=CS[0:40, c * 128:(c + 1) * 128], data0=ones_f[0:40, :], data1=SP[0:40, c * 128:(c + 1) * 128],
                                                                   initial=0.0, op0=ALU.mult, op1=ALU.add),
                             reads=[("gt", "SP"), ("ones_f",)], writes=[("gt", "CS")])
                    s.op("dve", lambda e: e.tensor_copy(out=g_bn[:], in_=CS3[0:40, :, 127]), reads=[("gt", "CS")], writes=[("g", "bn")])
                    s.op("dve", lambda e: e.tensor_tensor(out=CS3[32:40], in0=g_bn[32:40, :].unsqueeze(2).broadcast_to([8, NCH, 128]), in1=CS3[32:40], op=ALU.subtract),
                         reads=[("gt", "CS"), ("g", "bn")], writes=[("gt", "CS")])
                    s.op("dve", lambda e: e.tensor_tensor(out=CS[32:40, :], in0=CS[32:40, :], in1=SP[32:40, :], op=ALU.add),
                         reads=[("gt", "CS"), ("gt", "SP")], writes=[("gt", "CS")])
                    s.op("dve", lambda e: e.tensor_tensor(out=GI[0:40, :], in0=GI[0:40, :], in1=CS[0:40, :], op=ALU.add),
                         reads=[("gt", "GI"), ("gt", "CS")], writes=[("gt", "GI")])
                    s.op("dve", lambda e: e.tensor_reduce(out=g_amax[:], in_=GI3[0:40], axis=AX.X, op=ALU.max), reads=[("gt", "GI")], writes=[("g", "amax")])
                    tabkeys = [("e_tok", h_, d_) for h_ in range(2) for d_ in range(2)]
                    clkeys = [("cl_tok", h_, d_) for h_ in range(2) for d_ in range(2)]
                    dpkeys = [("decp", 0), ("decp", 1)]

                    def gate_tables(m_in):
                        s.op("dve", lambda e: e.memset(g_mp[:], NEG), writes=[("g", "mp")])
                        if m_in is not None:
                            s.op("dve", lambda e: e.tensor_copy(out=g_mp[0:8, 0:1], in_=m_in[0:8, :]), reads=[("cs", "ms"), ("g", "mp")], writes=[("g", "mp")])
                            s.op("dve", lambda e: e.tensor_copy(out=g_mp[32:40, NCH - 1:NCH], in_=m_in[32:40, :]), reads=[("cs", "ms"), ("g", "mp")], writes=[("g", "mp")])
                        for c in range(NCH):
                            s.op("dve", lambda e: e.tensor_tensor(out=g_m[0:8, c:c + 1], in0=g_mp[0:8, c:c + 1], in1=g_amax[0:8, c:c + 1], op=ALU.max),
                                 reads=[("g", "mp"), ("g", "amax")], writes=[("g", "m")])
                            dst = g_mp[0:8, c + 1:c + 2] if c < NCH - 1 else g_mout[0:8, :]
                            s.op("dve", lambda e: e.tensor_tensor(out=dst, in0=g_m[0:8, c:c + 1], in1=g_bn[0:8, c:c + 1], op=ALU.subtract),
                                 reads=[("g", "m"), ("g", "bn")], writes=[("g", "mp")])
                        for c in range(NCH - 1, -1, -1):
                            s.op("dve", lambda e: e.tensor_tensor(out=g_m[32:40, c:c + 1], in0=g_mp[32:40, c:c + 1], in1=g_amax[32:40, c:c + 1], op=ALU.max),
                                 reads=[("g", "mp"), ("g", "amax")], writes=[("g", "m")])
                            dst = g_mp[32:40, c - 1:c] if c > 0 else g_mout[32:40, :]
                            s.op("dve", lambda e: e.tensor_tensor(out=dst, in0=g_m[32:40, c:c + 1], in1=g_bn[32:40, c:c + 1], op=ALU.subtract),
                                 reads=[("g", "m"), ("g", "bn")], writes=[("g", "mp")])
                        s.op("dve", lambda e: e.tensor_tensor(out=g_dec[:], in0=g_mp[:], in1=g_m[:], op=ALU.subtract), reads=[("g", "mp"), ("g", "m")], writes=[("g", "dec")])
                        s.op("dve", lambda e: e.tensor_scalar(out=g_dec[:], in0=g_dec[:], scalar1=-100.0, scalar2=None, op0=ALU.max), reads=[("g", "dec")], writes=[("g", "dec")])
                        s.op("act", lambda e: e.activation(out=g_dec[:], in_=g_dec[:], func=AF.Exp), reads=[("g", "dec")], writes=[("g", "dec")])
                        for src3, skey, dstt, dkey in ((GI3, ("gt", "GI"), e_tok, "e_tok"), (CS3, ("gt", "CS"), cl_tok, "cl_tok")):
                            s.op("dve", lambda e: e.tensor_tensor(out=SP3[0:40], in0=src3[0:40], in1=g_m[:].unsqueeze(2).broadcast_to([40, NCH, 128]), op=ALU.subtract),
                                 reads=[skey, ("g", "m"), ("gt", "SP")], writes=[("gt", "SP")])
                            s.op("act", lambda e: e.activation(out=SP[0:40, :], in_=SP[0:40, :], func=AF.Exp), reads=[("gt", "SP")], writes=[("gt", "SP")])
                            for half in range(2):
                                for c8 in range(8):
                                    c = half * 8 + c8
                                    s.op("pe", lambda e: e.transpose(out=psb[MISC][:, c8 * 40:(c8 + 1) * 40], in_=SP[0:40, c * 128:(c + 1) * 128],
                                                                     identity=ident_f[0:40, 0:40]),
                                         reads=[("gt", "SP"), ("ident_f",)], writes=[P(MISC)])
                                m3 = psb[MISC][:, 0:320].rearrange("p (c j) -> p c j", j=40)
                                for d in range(2):
                                    s.op("act", lambda e: e.copy(out=dstt[:, half * 8:(half + 1) * 8, d * 8:(d + 1) * 8], in_=m3[:, :, d * 32:d * 32 + 8]),
                                         reads=[P(MISC)], writes=[(dkey, half, d)])
                        s.op("dve", lambda e: e.tensor_tensor(out=g_bd[:], in0=g_dec[:].unsqueeze(2).broadcast_to([40, NCH, 16]),
                                                              in1=sel[:].unsqueeze(1).broadcast_to([40, NCH, 16]), op=ALU.mult),
                             reads=[("g", "dec"), ("sel", 0), ("sel", 1)], writes=[("g", "bd")])
                        s.op("pe", lambda e: e.matmul(psb[MISC2][:, 0:256], lhsT=ones_f[0:40, :], rhs=g_bd[:].rearrange("p c j -> p (c j)"), start=True, stop=True),
                             reads=[("g", "bd"), ("ones_f",)], writes=[P(MISC2)])
                        s.op("act", lambda e: e.copy(out=dec_rep[:].rearrange("p c j -> p (c j)"), in_=psb[MISC2][:, 0:256]), reads=[P(MISC2)], writes=[("dec_rep",)])
                        dr4 = dec_rep[:].rearrange("p c (d g two) -> p c d g two", d=2, two=2)
                        s.op("dve", lambda e: e.tensor_copy(out=decp[0:64], in_=dr4[0:64, :, :, :, 0]), reads=[("dec_rep",)], writes=[("decp", 0)])
                        s.op("dve", lambda e: e.tensor_copy(out=decp[64:128], in_=dr4[64:128, :, :, :, 1]), reads=[("dec_rep",)], writes=[("decp", 1)])

                    PB = 65568
                    qT = aview(PB, 4096, BF16)
                    kT = aview(PB + 4096, 4096, BF16)
                    v_aug = aview(PB + 8192, 8256, BF16).rearrange("p (c h f) -> p c h f", c=NCH, h=2)
                    Cb_st = aview(PB + 8192 + 8256, 8256, BF16).rearrange("p (c f) -> p c f", c=NCH)

                    qstate = {}

                    def qk_proj(g, which, blks=None):
                        dst, sc = ((qT, 0.125), (kT, 1.0))[which]
                        if blks is None or (g, which) not in qstate:
                            qstate[(g, which)] = load_chunk("w_qk", 2 * g + which)
                        wq, kwq = qstate[(g, which)]
                        for blk in (range(NBLK) if blks is None else blks):
                            b = acc_bank()
                            for kc in range(8):
                                s.op("pe", lambda e: e.matmul(psb[b][:], lhsT=wq[:, kc * 128:(kc + 1) * 128],
                                                              rhs=xnT[:, kc, blk * 512:(blk + 1) * 512], start=(kc == 0), stop=(kc == 7)),
                                     reads=[kwq, ("xnT", kc, blk)], writes=[P(b)])
                            s.op("act", lambda e: e.activation(out=dst[:, blk * 512:(blk + 1) * 512], in_=psb[b][:], func=AF.Copy, scale=sc),
                                 reads=[P(b)], writes=[("pair", "qk", which, blk)])

                    def v_proj(g):
                        wv_, kwv = load_big("w_v", g, 2048)
                        s.op("pool", lambda e: e.memset(v_aug[:, :, :, 128:129], 1.0), writes=[("pair", "v1")])
                        for c in range(NCH):
                            b = acc_bank()
                            for kc in range(8):
                                s.op("pe", lambda e: e.matmul(psb[b][:, 0:256], lhsT=xnT[:, kc, c * 128:(c + 1) * 128],
                                                              rhs=wv_[:, kc * 256:(kc + 1) * 256], start=(kc == 0), stop=(kc == 7)),
                                     reads=[kwv, ("xnT", kc, c // 4)], writes=[P(b)])
                            s.op("act", lambda e: e.copy(out=v_aug[:, c, :, 0:128], in_=psb[b][:, 0:256].rearrange("p (h f) -> p h f", h=2)),
                                 reads=[P(b)], writes=[("pair", "v", c)])

                    def kprime(g, c, d):
                        pbank = psb[MISC][:].bitcast(BF16)
                        s.op("pe", lambda e: e.transpose(out=pbank[:, 0:128], in_=kT[:, c * 128:(c + 1) * 128], identity=ident_b[:]),
                             reads=[("pair", "qk", 1, c // 4), ("ident_b",)], writes=[P(MISC)])
                        q = rot("kp", 6)
                        s.op("dve", lambda e: e.tensor_tensor(out=kp[q][:].rearrange("p (h f) -> p h f", h=2),
                                                              in0=pbank[:, 0:128].rearrange("p (h f) -> p h f", h=2),
                                                              in1=e_tok[:, c, d * 8 + 2 * g:d * 8 + 2 * g + 2].unsqueeze(2).broadcast_to([128, 2, 64]),
                                                              op=ALU.mult),
                             reads=[P(MISC)] + tabkeys, writes=[("kp", q)])
                        return q

                    def state_step(g, c, d, S, Skey, q):
                        b = acc_bank()
                        s.op("pe", lambda e: e.matmul(psb[b][:, 0:258], lhsT=kp[q][:], rhs=v_aug[:, c, :, :].rearrange("p h f -> p (h f)"), start=True, stop=True),
                             reads=[("kp", q), ("pair", "v", c), ("pair", "v1")], writes=[P(b)])
                        s.op("dve", lambda e: e.scalar_tensor_tensor(out=S[:], in0=S[:], scalar=decp[:, c, d, g:g + 1], in1=psb[b][:, 0:258],
                                                                     op0=ALU.mult, op1=ALU.add),
                             reads=[P(b), Skey] + dpkeys, writes=[Skey])

                    if do_mlstm:
                        qk_proj(0, 1)
                        v_proj(0)
                        if seg != 0:
                            qk_proj(0, 0)
                    gate_tables(None)
                    if seg == 0 and do_mlstm:
                        for g in range(4):
                            if g > 0:
                                qk_proj(g, 1)
                                v_proj(g)
                            s.op("pool", lambda e: e.memset(S_b[:], 0.0), writes=[("S_b",)])
                            for c in range(NCH - 1, -1, -1):
                                q = kprime(g, c, 1)
                                state_step(g, c, 1, S_b, ("S_b",), q)
                            s.op("sp", lambda e: e.dma_start(out=summ_b[:, g * 258:(g + 1) * 258], in_=S_b[:]), reads=[("S_b",)], writes=[("summ_in", 4 + g)], dma="sm")
                            s.op("pool", lambda e: e.memset(S_f[:], 0.0), writes=[("S_f",)])
                            for c in range(NCH):
                                q = kprime(g, c, 0)
                                state_step(g, c, 0, S_f, ("S_f",), q)
                            s.op("sp", lambda e: e.dma_start(out=summ_f[:, g * 258:(g + 1) * 258], in_=S_f[:]), reads=[("S_f",)], writes=[("summ_in", g)], dma="sm")
                        s.op("dve", lambda e: e.tensor_copy(out=g_sv[:, 0:1], in_=g_mout[:]), reads=[("g", "mp")], writes=[("cs", "sv")])
                        s.op("dve", lambda e: e.tensor_reduce(out=g_sv[:, 1:2], in_=g_bn[:], axis=AX.X, op=ALU.add, negate=True), reads=[("g", "bn"), ("cs", "sv")], writes=[("cs", "sv")])
                        s.op("dve", lambda e: e.tensor_tensor(out=g_bd2[:], in0=g_sv[:].unsqueeze(2).broadcast_to([40, 2, 16]),
                                                              in1=sel[:].unsqueeze(1).broadcast_to([40, 2, 16]), op=ALU.mult),
                             reads=[("cs", "sv"), ("sel", 0), ("sel", 1)], writes=[("cs", "bd2")])
                        s.op("pe", lambda e: e.matmul(psb[MISC2][:, 0:32], lhsT=ones_f[0:40, :], rhs=g_bd2[:].rearrange("p a j -> p (a j)"), start=True, stop=True),
                             reads=[("cs", "bd2"), ("ones_f",)], writes=[P(MISC2)])
                        s.op("act", lambda e: e.copy(out=svrep[:], in_=psb[MISC2][:, 0:32]), reads=[P(MISC2)], writes=[("cs", "svrep")])
                        sv4 = svrep[:].rearrange("p (a d g two) -> p a d g two", a=2, d=2, two=2)
                        scp4 = sc_p[:].rearrange("p (a d g) -> p a d g", a=2, d=2)
                        s.op("dve", lambda e: e.tensor_copy(out=scp4[0:64], in_=sv4[0:64, :, :, :, 0]), reads=[("cs", "svrep")], writes=[("cs", "scp", 0)])
                        s.op("dve", lambda e: e.tensor_copy(out=scp4[64:128], in_=sv4[64:128, :, :, :, 1]), reads=[("cs", "svrep")], writes=[("cs", "scp", 1)])
                        s.op("sp", lambda e: e.dma_start(out=summ_b[:, 1032:1048], in_=sc_p[:]), reads=[("cs", "scp", 0), ("cs", "scp", 1)], writes=[("summ_in", 8)], dma="sm")
                        s.op("pool", lambda e: e.memset(ct1[:], 0.0), writes=[("cs", "ct1")])
                        s.op("sp", lambda e: e.dma_start(out=summ_b[:, 1048:1056], in_=ct1[:]), reads=[("cs", "ct1")], writes=[("summ_in", 9)], dma="sm")
                        s.op("sp", lambda e: e.dma_start(out=summ_b[0:40, 1048:1050], in_=g_sv[:]), reads=[("cs", "sv")], writes=[("summ_in", 9)], dma="sm")
                        s.op("pool", lambda e: e.collective_compute("AllGather", ALU.bypass, replica_groups=[[0, 1, 2, 3], [4, 5, 6, 7]],
                                                                    ins=[summ_f], outs=[sout_f]),
                             reads=[("summ_in", j) for j in range(10)], writes=[("summ_out",)], dma="ag", inc=1)
                        s.op("pool", lambda e: e.collective_compute("AllGather", ALU.bypass, replica_groups=[[0, 1, 2, 3], [4, 5, 6, 7]],
                                                                    ins=[summ_b], outs=[sout_b]),
                             reads=[("summ_in", j) for j in range(10)], writes=[("summ_out",)], dma="ag2", inc=1)
                        s.op("sp", lambda e: e.dma_start(out=flp_sb[:], in_=flp_d), writes=[("cs", "flp")], dma="c_flp")
                        s.op("sp", lambda e: e.dma_start(out=fls_sb[:], in_=fls_d), writes=[("cs", "fls")], dma="c_fls")
                        s.op("dve", lambda e: e.tensor_scalar(out=flpB[:], in0=flp_sb[:], scalar1=-NEG, scalar2=NEG, op0=ALU.mult, op1=ALU.add), reads=[("cs", "flp")], writes=[("cs", "flpB")])
                        s.op("dve", lambda e: e.tensor_scalar(out=flsB[:], in0=fls_sb[:], scalar1=-NEG, scalar2=NEG, op0=ALU.mult, op1=ALU.add), reads=[("cs", "fls")], writes=[("cs", "flsB")])
                        s.op("dve", lambda e: e.memset(svq[:], 0.0), writes=[("cs", "svq")])
                        for k in range(4):
                            for d in range(2):
                                qm = k if d == 0 else 3 - k
                                s.op("sp", lambda e: e.dma_start(out=mq[:, k, d * 4:(d + 1) * 4], in_=sout_b[qm * 128:(qm + 1) * 128, 1032 + d * 4:1032 + (d + 1) * 4]),
                                     reads=[("summ_out",)], writes=[("cs", "mq", k, d)], dma="cq")
                                s.op("sp", lambda e: e.dma_start(out=Fq[:, k, d * 4:(d + 1) * 4], in_=sout_b[qm * 128:(qm + 1) * 128, 1040 + d * 4:1040 + (d + 1) * 4]),
                                     reads=[("summ_out",)], writes=[("cs", "Fq", k, d)], dma="cq")
                                r0 = d * 32
                                s.op("sp", lambda e: e.dma_start(out=svq[r0:r0 + 8, k, :], in_=sout_b[qm * 128 + r0:qm * 128 + r0 + 8, 1048:1050]),
                                     reads=[("summ_out",), ("cs", "svq")], writes=[("cs", "svq")], dma="cq")
                        cs_ = lambda nm: [("cs", nm)]
                        s.op("dve", lambda e: e.memset(cm[:], NEG), writes=cs_("cm"))
                        s.op("dve", lambda e: e.memset(ms[:], NEG), writes=cs_("ms"))
                        v3 = lambda t: t[:].rearrange("p (d g) -> p d g", d=2)
                        for k in range(4):
                            flb = flp_sb[:, k * 2:(k + 1) * 2].unsqueeze(2).broadcast_to([128, 2, 4])
                            flBb = flpB[:, k * 2:(k + 1) * 2].unsqueeze(2).broadcast_to([128, 2, 4])
                            mqk = mq[:, k, :].rearrange("p (d g) -> p d g", d=2)
                            Fqk = Fq[:, k, :].rearrange("p (d g) -> p d g", d=2)
                            rk = [("cs", "mq", k_, d_) for k_ in range(4) for d_ in range(2)] + [("cs", "Fq", k_, d_) for k_ in range(4) for d_ in range(2)] + [("cs", "svq"), ("cs", "flp"), ("cs", "flpB")]
                            s.op("dve", lambda e: e.tensor_tensor(out=v3(ct1), in0=Fqk, in1=flb, op=ALU.mult), reads=rk, writes=cs_("ct1"))
                            s.op("dve", lambda e: e.tensor_tensor(out=v3(ct2), in0=mqk, in1=flb, op=ALU.mult), reads=rk, writes=cs_("ct2"))
                            s.op("dve", lambda e: e.tensor_tensor(out=v3(ct2), in0=v3(ct2), in1=flBb, op=ALU.add), reads=rk + cs_("ct2"), writes=cs_("ct2"))
                            s.op("dve", lambda e: e.tensor_tensor(out=ct1[:], in0=ct1[:], in1=cm[:], op=ALU.add), reads=cs_("ct1") + cs_("cm"), writes=cs_("ct1"))
                            s.op("dve", lambda e: e.tensor_tensor(out=cm[:], in0=ct1[:], in1=ct2[:], op=ALU.max), reads=cs_("ct1") + cs_("ct2") + cs_("cm"), writes=cs_("cm"))
                            s.op("dve", lambda e: e.tensor_tensor(out=ct1[:], in0=ct1[:], in1=cm[:], op=ALU.subtract), reads=cs_("ct1") + cs_("cm"), writes=cs_("ct1"))
                            s.op("dve", lambda e: e.tensor_scalar(out=ct1[:], in0=ct1[:], scalar1=-100.0, scalar2=None, op0=ALU.max), reads=cs_("ct1"), writes=cs_("ct1"))
                            s.op("act", lambda e: e.activation(out=a1[:, k, :], in_=ct1[:], func=AF.Exp), reads=cs_("ct1"), writes=[("cs", "a1", k)])
                            s.op("dve", lambda e: e.tensor_tensor(out=ct2[:], in0=ct2[:], in1=cm[:], op=ALU.subtract), reads=cs_("ct2") + cs_("cm"), writes=cs_("ct2"))
                            s.op("dve", lambda e: e.tensor_scalar(out=ct2[:], in0=ct2[:], scalar1=-100.0, scalar2=None, op0=ALU.max), reads=cs_("ct2"), writes=cs_("ct2"))
                            s.op("act", lambda e: e.activation(out=ct2[:], in_=ct2[:], func=AF.Exp), reads=cs_("ct2"), writes=cs_("ct2"))
                            s.op("dve", lambda e: e.tensor_tensor(out=a2[:, k, :].rearrange("p (d g) -> p d g", d=2), in0=v3(ct2), in1=flb, op=ALU.mult),
                                 reads=cs_("ct2") + rk, writes=[("cs", "a2", k)])
                            rs = rk + [("cs", "fls"), ("cs", "flsB")]
                            s.op("dve", lambda e: e.tensor_tensor(out=st1[:], in0=svq[:, k, 1:2], in1=fls_sb[:, k:k + 1], op=ALU.mult), reads=rs, writes=cs_("st1"))
                            s.op("dve", lambda e: e.tensor_tensor(out=st2[:], in0=svq[:, k, 0:1], in1=fls_sb[:, k:k + 1], op=ALU.mult), reads=rs, writes=cs_("st2"))
                            s.op("dve", lambda e: e.tensor_tensor(out=st2[:], in0=st2[:], in1=flsB[:, k:k + 1], op=ALU.add), reads=rs + cs_("st2"), writes=cs_("st2"))
                            s.op("dve", lambda e: e.tensor_tensor(out=st1[:], in0=st1[:], in1=ms[:], op=ALU.add), reads=cs_("st1") + cs_("ms"), writes=cs_("st1"))
                            s.op("dve", lambda e: e.tensor_tensor(out=ms[:], in0=st1[:], in1=st2[:], op=ALU.max), reads=cs_("st1") + cs_("st2") + cs_("ms"), writes=cs_("ms"))
                        for d in range(2):
                            for g in range(4):
                                ra = rot("t2k", 6)
                                s.op("pool", lambda e: e.memset(t2k[ra][:, 0:258], 0.0), writes=[("t2k", ra)])
                                for k in range(4):
                                    qm = k if d == 0 else 3 - k
                                    rb = rot("t2k", 6)
                                    while rb == ra:
                                        rb = rot("t2k", 6)
                                    s.op(XQ, lambda e: e.dma_start(out=t2k[rb][:, 0:258], in_=(sout_f if d == 0 else sout_b)[qm * 128:(qm + 1) * 128, g * 258:(g + 1) * 258]),
                                         reads=[("summ_out",)], writes=[("t2k", rb)], dma=("xl", rb))
                                    col = d * 4 + g
                                    s.op("dve", lambda e: e.tensor_scalar(out=t2k[rb][:, 0:258], in0=t2k[rb][:, 0:258], scalar1=a2[:, k, col:col + 1], scalar2=None, op0=ALU.mult),
                                         reads=[("t2k", rb), ("cs", "a2", k)], writes=[("t2k", rb)])
                                    s.op("dve", lambda e: e.scalar_tensor_tensor(out=t2k[ra][:, 0:258], in0=t2k[ra][:, 0:258], scalar=a1[:, k, col:col + 1], in1=t2k[rb][:, 0:258],
                                                                                 op0=ALU.mult, op1=ALU.add),
                                         reads=[("t2k", ra), ("t2k", rb), ("cs", "a1", k)], writes=[("t2k", ra)])
                                s.op(XQ, lambda e: e.dma_start(out=cin_d[:, (d * 4 + g) * 258:(d * 4 + g + 1) * 258], in_=t2k[ra][:, 0:258]),
                                     reads=[("t2k", ra)], writes=[("cin", d, g)], dma=("xs", ra))
                        gate_tables(ms)

                    s.alias(["hT"], ["gt"])
                    accmode["wide"] = True
                    for g in range(4 if do_mlstm else 0):
                        q_interleave = True
                        if g > 0 or seg == 0:
                            qk_proj(g, 1)
                            v_proj(g)
                        else:
                            q_interleave = False
                        if seg == 0:
                            s.op("sp", lambda e: e.dma_start(out=S_b[:], in_=cin_d[:, (4 + g) * 258:(5 + g) * 258]), reads=[("cin", 1, g)], writes=[("S_b",)], dma="c_sb")
                        else:
                            s.op("pool", lambda e: e.memset(S_b[:], 0.0), writes=[("S_b",)])
                        for c in range(NCH - 1, -1, -1):
                            s.op("act", lambda e, c=c: e.activation(out=Cb_st[:, c, :], in_=S_b[:], func=AF.Copy, scale=decp[:, c, 1, g:g + 1]),
                                 reads=[("S_b",)] + dpkeys, writes=[("pair", "Cb", c)])
                            if c > 0:
                                q = kprime(g, c, 1)
                                state_step(g, c, 1, S_b, ("S_b",), q)
                            if q_interleave and c % 4 == 0:
                                qk_proj(g, 0, blks=[3 - c // 4])
                        if seg == 0:
                            s.op("sp", lambda e: e.dma_start(out=S_f[:], in_=cin_d[:, g * 258:(g + 1) * 258]), reads=[("cin", 0, g)], writes=[("S_f",)], dma="c_sf")
                        else:
                            s.op("pool", lambda e: e.memset(S_f[:], 0.0), writes=[("S_f",)])
                        for c in range(0, NCH, 2):
                            if c % 4 == 0:
                                so = []
                                for hh in range(2):
                                    wo, kwo = load_chunk("w_o", 2 * g + hh)
                                    b = acc_bank()
                                    for kc in range(8):
                                        s.op("pe", lambda e, kc=kc, b=b, wo=wo: e.matmul(psb[b][:], lhsT=wo[:, kc * 128:(kc + 1) * 128],
                                                                                      rhs=xnT[:, kc, (c // 4) * 512:(c // 4 + 1) * 512], start=(kc == 0), stop=(kc == 7)),
                                             reads=[kwo, ("xnT", kc, c // 4)], writes=[P(b)])
                                    r = rot("t2k", 6)
                                    s.op("act", lambda e, b=b, r=r: e.activation(out=t2k[r][:], in_=psb[b][:], func=AF.Sigmoid), reads=[P(b)], writes=[("t2k", r)])
                                    so.append(r)
                            U = []
                            for cc_ in (c, c + 1):
                                cq_ = rot("Cp", 3)
                                s.op("act", lambda e: e.activation(out=Cp[cq_][:], in_=S_f[:], func=AF.Copy, scale=decp[:, cc_, 0, g:g + 1]),
                                     reads=[("S_f",)] + dpkeys, writes=[("Cp", cq_)])
                                for hh in range(2):
                                    U.append(dict(c=cc_, csl=slice(cc_ * 128, (cc_ + 1) * 128), cq=cq_, hh=hh, hd=2 * g + hh,
                                                  rows=slice(hh * 64, (hh + 1) * 64), bS=acc_bank()))
                                if cc_ < NCH - 1:
                                    q = kprime(g, cc_, 0)
                                    state_step(g, cc_, 0, S_f, ("S_f",), q)
                            for u in U:
                                s.op("pe", lambda e: e.matmul(psb[u["bS"]][:, 0:128], lhsT=kT[u["rows"], u["csl"]], rhs=qT[u["rows"], u["csl"]], start=True, stop=True),
                                     reads=[("pair", "qk", 0, u["c"] // 4), ("pair", "qk", 1, u["c"] // 4)], writes=[P(u["bS"])])
                            for u in U:
                                u["pts"] = []
                                for d in range(2):
                                    pq = rot("PT", 8)
                                    s.op("dve", lambda e: e.scalar_tensor_tensor(out=PT[pq][:], in0=psb[u["bS"]][:, 0:128],
                                                                                 scalar=e_tok[:, u["c"], d * 8 + u["hd"]:d * 8 + u["hd"] + 1],
                                                                                 in1=mask[:, d, :], op0=ALU.mult, op1=ALU.mult),
                                         reads=[P(u["bS"]), ("mask", d)] + tabkeys, writes=[("PT", pq)])
                                    u["pts"].append(pq)
                            for u in U:
                                u["bO"] = acc_bank()
                                bO, hh, rows = u["bO"], u["hh"], u["rows"]
                                c, csl, cq = u["c"], u["csl"], u["cq"]
                                vrhs = v_aug[:, c, hh, :]
                                s.op("pe", lambda e: e.matmul(psb[bO][:, 0:129], lhsT=PT[u["pts"][0]][:], rhs=vrhs, start=True, stop=False),
                                     reads=[("PT", u["pts"][0]), ("pair", "v", c), ("pair", "v1")], writes=[P(bO)])
                                s.op("pe", lambda e: e.matmul(psb[bO][:, 0:129], lhsT=qT[rows, csl], rhs=Cp[cq][rows, hh * 129:(hh + 1) * 129], start=False, stop=True),
                                     reads=[("pair", "qk", 0, c // 4), ("Cp", cq)], writes=[P(bO)])
                                s.op("pe", lambda e: e.matmul(psb[bO][:, 129:258], lhsT=PT[u["pts"][1]][:], rhs=vrhs, start=True, stop=False),
                                     reads=[("PT", u["pts"][1]), ("pair", "v", c), ("pair", "v1")], writes=[P(bO)])
                                s.op("pe", lambda e: e.matmul(psb[bO][:, 129:258], lhsT=qT[rows, csl], rhs=Cb_st[rows, c, hh * 129:(hh + 1) * 129], start=False, stop=True),
                                     reads=[("pair", "qk", 0, c // 4), ("pair", "Cb", c)], writes=[P(bO)])
                            for u in U:
                                u["m"] = rot("sm", 8)
                                m, bO, hd = u["m"], u["bO"], u["hd"]
                                c = u["c"]
                                den = psb[bO][:, 128:258:129]
                                s.op("dve", lambda e: e.scalar_tensor_tensor(out=sm[m][:, 0:2], in0=den, scalar=-1.0, in1=cl_tok[:, c, hd:16:8], op0=ALU.mult, op1=ALU.max),
                                     reads=[P(bO)] + clkeys, writes=[("sm", m)])
                                s.op("dve", lambda e: e.tensor_tensor(out=sm[m][:, 0:2], in0=sm[m][:, 0:2], in1=den, op=ALU.max), reads=[P(bO), ("sm", m)], writes=[("sm", m)])
                                s.op("dve", lambda e: e.reciprocal(out=sm[m][:, 0:2], in_=sm[m][:, 0:2]), reads=[("sm", m)], writes=[("sm", m)])
                            for u in U:
                                u["hq"] = rot("hq", 4)
                                m, bO, hq = u["m"], u["bO"], u["hq"]
                                s.op("dve", lambda e: e.tensor_scalar(out=hs[hq][:], in0=psb[bO][:, 0:128], scalar1=sm[m][:, 0:1], scalar2=None, op0=ALU.mult),
                                     reads=[P(bO), ("sm", m)], writes=[("hs", hq)])
                                s.op("dve", lambda e: e.scalar_tensor_tensor(out=hs[hq][:], in0=psb[bO][:, 129:257], scalar=sm[m][:, 1:2], in1=hs[hq][:], op0=ALU.mult, op1=ALU.add),
                                     reads=[P(bO), ("sm", m), ("hs", hq)], writes=[("hs", hq)])
                            for u in U:
                                m, hq = u["m"], u["hq"]
                                s.op("act", lambda e: e.activation(out=hjunk[:], in_=hs[hq][:], func=AF.Square, accum_out=sm[m][:, 2:3]),
                                     reads=[("hs", hq)], writes=[("sm", m), ("hjunk",)])
                                s.op("act", lambda e: e.activation(out=sm[m][:, 3:4], in_=sm[m][:, 2:3], func=AF.Sqrt, bias=EPS, scale=1.0 / 128),
                                     reads=[("sm", m)], writes=[("sm", m)])
                            for u in U:
                                m, hq = u["m"], u["hq"]
                                s.op("dve", lambda e: e.reciprocal(out=sm[m][:, 3:4], in_=sm[m][:, 3:4]), reads=[("sm", m)], writes=[("sm", m)])
                                s.op("dve", lambda e: e.tensor_scalar(out=hn[hq][:], in0=hs[hq][:], scalar1=sm[m][:, 3:4], scalar2=None, op0=ALU.mult),
                                     reads=[("hs", hq), ("sm", m)], writes=[("hn", hq)])
                            for u in U:
                                pO = psb[u["bO"]][:].bitcast(BF16)
                                s.op("pe", lambda e: e.transpose(out=pO[:, 768:896], in_=hn[u["hq"]][:], identity=ident_b[:]),
                                     reads=[("hn", u["hq"]), ("ident_b",)], writes=[P(u["bO"])])
                            for u in U:
                                pO = psb[u["bO"]][:].bitcast(BF16)
                                hd, r = u["hd"], so[u["hh"]]
                                c, csl = u["c"], u["csl"]
                                s.op("dve", lambda e: e.scalar_tensor_tensor(out=hT[:, hd, csl], in0=pO[:, 768:896], scalar=mnorm_sb[:, hd:hd + 1],
                                                                             in1=t2k[r][:, (c % 4) * 128:(c % 4 + 1) * 128], op0=ALU.mult, op1=ALU.mult),
                                     reads=[P(u["bO"]), ("t2k", r), ("mnorm",)], writes=[("hT", hd, c // 4)])

                    accmode["wide"] = False
                    mixer_in = hT
                    mixer_key = "hT"
                    wout_name = "w_mout"

                wres = aview(0, 16384, BF16).rearrange("p (o k) -> p o k", o=8)
                mix = aview(16384, 16384, F32).rearrange("p (o t) -> p o t", o=8)
                s.alias(["wres", "mix"], ["xnT"])
                for oc in range(8):
                    s.op("sp", lambda e, oc=oc: e.dma_start(out=wres[:, oc, :], in_=wb[wout_name][oc]),
                         reads=[("wb", wout_name, oc)], writes=[("wres", oc)], dma=("wres", oc))
                for blk in range(NBLK):
                    cols = slice(blk * 512, (blk + 1) * 512)
                    out_proj_block(lambda oc, kc: wres[:, oc, kc * 128:(kc + 1) * 128], lambda oc: ("wres", oc), 8,
                                   lambda kc, cols=cols: mixer_in[:, kc, cols], lambda kc, blk=blk: (mixer_key, kc, blk),
                                   lambda oc: mix[:, oc, :], "mix", 0, 512, gcol(layer, 1), STAT)
                    r = rot("t2k", 6)
                    s.op("act", lambda e, r=r: e.activation(out=t2k[r][:], in_=psb[STAT][:], func=AF.Sqrt, bias=EPS, scale=1.0 / D),
                         reads=[P(STAT)], writes=[("t2k", r)])
                    s.op("dve", lambda e, r=r: e.reciprocal(out=t2k[r][:], in_=t2k[r][:]), reads=[("t2k", r)], writes=[("t2k", r)])
                    for oc in range(8):
                        t = rot("t2k", 6)
                        while t == r:
                            t = rot("t2k", 6)
                        gc = gcol(layer, 1)
                        s.op("dve", lambda e, oc=oc, t=t, r=r, gc=gc: e.scalar_tensor_tensor(out=t2k[t][:], in0=mix[:, oc, :], scalar=norms_sb[:, gc + oc:gc + oc + 1],
                                                                                       in1=t2k[r][:], op0=ALU.mult, op1=ALU.mult),
                             reads=[("mix", oc, 0), ("t2k", r), ("norms",)], writes=[("t2k", t)])
                        s.op("dve", lambda e, oc=oc, t=t, cols=cols: e.tensor_tensor(out=xT[:, oc, cols], in0=xT[:, oc, cols], in1=t2k[t][:], op=ALU.add),
                             reads=[("t2k", t), xkey(oc, blk)], writes=[xkey(oc, blk)])

                xn2 = aview(0, 16384, BF16).rearrange("p (c t) -> p c t", c=8)
                ybuf = aview(0, 32768, F32).rearrange("p (o t) -> p o t", o=8)
                act = aview(32800, 45056, BF16).rearrange("p (k t) -> p k t", k=KF)
                for half in range(2):
                    s.alias(["xn2", "act"], ["wres", "mix", "zT", "hT", "y", "xn2", "act", "xnT", "c_sb", "u_sb", "gt", "pair"])
                    for sub in range(2):
                        blk = half * 2 + sub
                        rms_T(lambda c, blk=blk: xT[:, c, blk * 512:(blk + 1) * 512], lambda c, blk=blk: xkey(c, blk), 512, gcol(layer, 2),
                              lambda c, sub=sub: xn2[:, c, sub * 512:(sub + 1) * 512], lambda c, sub=sub: ("xn2", c, sub))
                    for j in range(KF):
                        if seg == 0 and (half * KF + j) % 3 == 0:
                            flush_cast(1)
                        wg_, kwg = load_chunk("w_f1", layer * 44 + 2 * j)
                        wu_, kwu = load_chunk("w_f1", layer * 44 + 2 * j + 1)
                        for sub in range(2):
                            cols = slice(sub * 512, (sub + 1) * 512)
                            b = acc_bank()
                            for kc in range(8):
                                s.op("pe", lambda e, kc=kc, b=b, cols=cols, wg_=wg_: e.matmul(psb[b][:], lhsT=wg_[:, kc * 128:(kc + 1) * 128], rhs=xn2[:, kc, cols],
                                                                                           start=(kc == 0), stop=(kc == 7)),
                                     reads=[kwg, ("xn2", kc, sub)], writes=[P(b)])
                            r = rot("t2k", 6)
                            s.op("act", lambda e, b=b, r=r: e.activation(out=t2k[r][:], in_=psb[b][:], func=AF.Silu), reads=[P(b)], writes=[("t2k", r)])
                            b2 = acc_bank()
                            for kc in range(8):
                                s.op("pe", lambda e, kc=kc, b2=b2, cols=cols, wu_=wu_: e.matmul(psb[b2][:], lhsT=wu_[:, kc * 128:(kc + 1) * 128], rhs=xn2[:, kc, cols],
                                                                                             start=(kc == 0), stop=(kc == 7)),
                                     reads=[kwu, ("xn2", kc, sub)], writes=[P(b2)])
                            s.op("dve", lambda e, b2=b2, r=r, j=j, cols=cols: e.tensor_tensor(out=act[:, j, cols], in0=psb[b2][:], in1=t2k[r][:], op=ALU.mult),
                                 reads=[P(b2), ("t2k", r)], writes=[("act", j, sub)])
                    s.alias(["y"], ["xn2"])
                    for oc in range(8):
                        w2, kw2 = load_big("w_f2", layer * 8 + oc, DFF)
                        for sub in range(2):
                            cols = slice(sub * 512, (sub + 1) * 512)
                            statbank = STAT if sub == 0 else MISC
                            b = acc_bank()
                            for kc in range(KF):
                                s.op("pe", lambda e, kc=kc, b=b, cols=cols, w2=w2: e.matmul(psb[b][:], lhsT=w2[:, kc * 128:(kc + 1) * 128], rhs=act[:, kc, cols],
                                                                                         start=(kc == 0), stop=(kc == KF - 1)),
                                     reads=[kw2, ("act", kc, sub)], writes=[P(b)])
                            s.op("act", lambda e, b=b, oc=oc, cols=cols: e.copy(out=ybuf[:, oc, cols], in_=psb[b][:]), reads=[P(b)], writes=[("y", oc, sub)])
                            q = rot("sqb", 2)
                            s.op("act", lambda e, q=q, b=b: e.activation(out=sqb[q][:], in_=psb[b][:], func=AF.Square),
                                 reads=[P(b)], writes=[("sqb", q)])
                            s.op("pe", lambda e, oc=oc, q=q, statbank=statbank: e.matmul(psb[statbank][:], lhsT=ones_b[:], rhs=sqb[q][:], start=(oc == 0), stop=(oc == 7)),
                                 reads=[("sqb", q), ("ones_b",)], writes=[P(statbank)])
                    for sub in range(2):
                        blk = half * 2 + sub
                        cols = slice(sub * 512, (sub + 1) * 512)
                        xcols = slice(blk * 512, (blk + 1) * 512)
                        statbank = STAT if sub == 0 else MISC
                        r = rot("t2k", 6)
                        s.op("act", lambda e, r=r, statbank=statbank: e.activation(out=t2k[r][:], in_=psb[statbank][:], func=AF.Sqrt, bias=EPS, scale=1.0 / D),
                             reads=[P(statbank)], writes=[("t2k", r)])
                        s.op("dve", lambda e, r=r: e.reciprocal(out=t2k[r][:], in_=t2k[r][:]), reads=[("t2k", r)], writes=[("t2k", r)])
                        gc = gcol(layer, 3)
                        for oc in range(8):
                            t = rot("t2k", 6)
                            while t == r:
                                t = rot("t2k", 6)
                            s.op("dve", lambda e, oc=oc, t=t, r=r, gc=gc, cols=cols: e.scalar_tensor_tensor(out=t2k[t][:], in0=ybuf[:, oc, cols], scalar=norms_sb[:, gc + oc:gc + oc + 1],
                                                                                                    in1=t2k[r][:], op0=ALU.mult, op1=ALU.mult),
                                 reads=[("y", oc, sub), ("t2k", r), ("norms",)], writes=[("t2k", t)])
                            s.op("dve", lambda e, oc=oc, t=t, xcols=xcols: e.tensor_tensor(out=xT[:, oc, xcols], in0=xT[:, oc, xcols], in1=t2k[t][:], op=ALU.add),
                                 reads=[("t2k", t), xkey(oc, blk)], writes=[xkey(oc, blk)])
                    if layer == n_layers - 1:
                        store_blocks(seg, [half * 2, half * 2 + 1])
                        if seg + 1 < n_seg:
                            load_blocks(seg + 1, [half * 2, half * 2 + 1])
                            if half == 1:
                                load_halo(seg + 1)

            if n_layers == 0:
                store_blocks(seg, range(NBLK))
                if seg + 1 < n_seg:
                    load_blocks(seg + 1, range(NBLK))
                    load_halo(seg + 1)
        s.emit(st)
        import os
        if os.environ.get("KDEBUG"):
            print("SCHED", s.stats)
    return nc


def _chunks(W, col_lists):
    K = W.shape[0]
    kcn = K // 128
    out = []
    for cols in col_lists:
        sub = W[:, cols]
        w = sub.shape[1]
        out.append(sub.reshape(kcn, 128, w).transpose(1, 0, 2).reshape(128, kcn * w))
    return np.ascontiguousarray(np.stack(out, 0), dtype=np.float32)


def prep_weights(inp):
    r = lambda a, b: list(range(a, b))
    cw_in = inp["conv_w_in"][0]
    cin_lists = []
    for cc in range(8):
        cin_lists += [r(1024 + cc * 128, 1024 + (cc + 1) * 128), r(2048 + cc * 128, 2048 + (cc + 1) * 128), r(cc * 128, (cc + 1) * 128)]
    w = {}
    w["w_cin"] = _chunks(cw_in, cin_lists)
    w["w_cout"] = _chunks(inp["conv_w_out"][0], [r(o * 128, (o + 1) * 128) for o in range(8)])
    mw = inp["mlstm_w_in"][0]
    qk_lists = []
    for g in range(4):
        qk_lists += [r(g * 128, (g + 1) * 128), r(512 + g * 128, 512 + (g + 1) * 128)]
    w["w_qk"] = _chunks(mw, qk_lists)
    w["w_o"] = _chunks(mw, [r(2048 + h * 128, 2048 + (h + 1) * 128) for h in range(8)])
    w["w_v"] = _chunks(mw, [r(1024 + g * 256, 1024 + (g + 1) * 256) for g in range(4)])
    gcols = mw[:, 3072:3104]
    gi = np.zeros((1024, 40), np.float32)
    gf = np.zeros((1024, 40), np.float32)
    gi[:, 0:8] = gcols[:, 0:8]
    gf[:, 0:8] = gcols[:, 8:16]
    gi[:, 32:40] = gcols[:, 16:24]
    gf[:, 32:40] = gcols[:, 24:32]
    wgi = _chunks(gi, [r(0, 40)])[0]
    wgf = _chunks(gf, [r(0, 40)])[0]
    w["w_g"] = np.ascontiguousarray(np.concatenate([wgi, wgf], axis=1)[None], dtype=np.float32)
    w["w_mout"] = _chunks(inp["mlstm_w_out"][0], [r(o * 128, (o + 1) * 128) for o in range(8)])
    f1 = []
    for l in range(2):
        lists = []
        for j in range(KF):
            lists += [r(j * 128, (j + 1) * 128), r(DFF + j * 128, DFF + (j + 1) * 128)]
        f1.append(_chunks(inp["ffn_w_in"][l], lists))
    w["w_f1"] = np.ascontiguousarray(np.concatenate(f1, 0))
    f2 = [_chunks(inp["ffn_w_out"][l], [r(o * 128, (o + 1) * 128) for o in range(8)]) for l in range(2)]
    w["w_f2"] = np.ascontiguousarray(np.concatenate(f2, 0))
    nr = inp["norms"].reshape(8, 8, 128)
    w["norms_t"] = np.ascontiguousarray(nr.transpose(2, 0, 1).reshape(128, 64), dtype=np.float32)
    cw = inp["conv_w"][0].reshape(3, 8, 128)
    w["convw_t"] = np.ascontiguousarray(cw.transpose(2, 1, 0).reshape(128, 24), dtype=np.float32)
    w["mnorm_t"] = np.ascontiguousarray(inp["mlstm_norm"][0].reshape(8, 128).T, dtype=np.float32)
    bg = inp["mlstm_b_gate"][0]
    bt = np.zeros((40, 2), np.float32)
    bt[0:8, 0] = bg[0:8]
    bt[0:8, 1] = bg[8:16]
    bt[32:40, 0] = bg[16:24]
    bt[32:40, 1] = bg[24:32]
    w["bgate_t"] = bt
    return w


def kernel(x_prompt, x_sample, norms, conv_w_in, conv_w, conv_w_out, mlstm_w_in, mlstm_b_gate,
           mlstm_norm, mlstm_w_out, ffn_w_in, ffn_w_out, _n_layers=2, _do_mlstm=True, _n_seg=NSEG):
    inp = dict(norms=np.asarray(norms, np.float32), conv_w_in=np.asarray(conv_w_in, np.float32),
               conv_w=np.asarray(conv_w, np.float32), conv_w_out=np.asarray(conv_w_out, np.float32),
               mlstm_w_in=np.asarray(mlstm_w_in, np.float32), mlstm_b_gate=np.asarray(mlstm_b_gate, np.float32),
               mlstm_norm=np.asarray(mlstm_norm, np.float32), mlstm_w_out=np.asarray(mlstm_w_out, np.float32),
               ffn_w_in=np.asarray(ffn_w_in, np.float32), ffn_w_out=np.asarray(ffn_w_out, np.float32))
    xp = np.asarray(x_prompt, np.float32)
    xs = np.asarray(x_sample, np.float32)
    w = prep_weights(inp)
    in_maps = []
    for r in range(NCORES):
        b, qd = r // 4, r % 4
        xin = np.empty((NSEG, T, D), np.float32)
        halo = np.zeros((NSEG, 2, D), np.float32)
        xin[0] = xp[b, qd * T:(qd + 1) * T]
        if qd > 0:
            halo[0, 0] = xp[b, qd * T - 1]
        if qd < 3:
            halo[0, 1] = xp[b, (qd + 1) * T]
        xin[1] = xs[2 * r]
        xin[2] = xs[2 * r + 1]
        m = dict(w)
        flp = np.zeros((128, 4, 2), np.float32)
        fls = np.zeros((40, 4), np.float32)
        for k in range(4):
            ff = 1.0 if k < qd else 0.0
            fb = 1.0 if (3 - k) > qd else 0.0
            flp[:, k, 0] = ff
            flp[:, k, 1] = fb
            fls[0:8, k] = ff
            fls[32:40, k] = fb
        m["flp"] = flp.reshape(128, 8)
        m["fls"] = fls
        m["xin"] = xin
        m["halo"] = halo
        in_maps.append(m)
    nc = build_program(n_layers=_n_layers, do_mlstm=_do_mlstm, n_seg=_n_seg)
    res = run_bass_kernel_spmd(nc, in_maps, core_ids=list(range(NCORES)))
    y_prompt = np.empty_like(xp)
    y_sample = np.empty_like(xs)
    for r in range(NCORES):
        y = res.results[r]["yout"]
        b, qd = r // 4, r % 4
        y_prompt[b, qd * T:(qd + 1) * T] = y[0]
        y_sample[2 * r] = y[1]
        y_sample[2 * r + 1] = y[2]
    return (y_prompt, y_sample)
```

```python
import contextlib
import numpy as np
import concourse.bass as bass
import concourse.mybir as mybir
from concourse.bass_utils import run_bass_kernel_spmd

F32 = mybir.dt.float32
BF16 = mybir.dt.bfloat16
AF = mybir.ActivationFunctionType
ALU = mybir.AluOpType
AX = mybir.AxisListType

D = 1024
T = 2048
NSEG = 3
NBLK = 4
NCH = 16
DFF = 2816
KF = 22
EPS = 1e-6
NEG = -1.0e30
NCORES = 8


class _Rec:
    def __getattr__(self, name):
        def f(*a, **kw):
            self.call = (name, a, kw)
            return self
        return f


class Sched:
    ENGS = ("pe", "act", "dve", "pool", "sp")

    def __init__(self, nc):
        self.nc = nc
        self.ops = []
        self.lastw = {}
        self.readers = {}
        self.dma_keys = []
        self.pending = {}
        self.touched = set()
        self.tags = []
        self.reorder = True
        self.last_pe = None
        self.window = 64
        import os
        self.reorder_engs = tuple(os.environ.get("KREORDER", "pe,act,dve,pool").split(","))
        self.xlat = 1000.0

    def alias(self, new_names, old_names):
        old = set(old_names)
        dset = set()
        for k, w in self.lastw.items():
            if k[0] in old:
                dset.add(w)
        for k, rs in self.readers.items():
            if k[0] in old:
                dset.update(rs)
        for n in new_names:
            self.pending[n] = set(self.pending.get(n, set())) | dset
            self.touched = {k for k in self.touched if k[0] != n}

    @staticmethod
    def _cost(eng, name, a, kw, dma):
        out = kw.get("out", a[0] if a else None)
        try:
            shp = out.shape
            free = 1
            for d_ in shp[1:]:
                free *= d_
        except Exception:
            free = 512
        if name == "collective_compute":
            return (500.0, 40000.0)
        if dma is not None:
            try:
                nbytes = out.nbytes()
            except Exception:
                nbytes = free * 4 * 128
            return (150.0, 2500.0 + nbytes / 120.0)
        if eng == "pe":
            return (max(64, free) * 0.50 + 35.0, 250.0)
        if eng == "act":
            return (210.0 + free * 0.65, 250.0)
        if eng == "dve":
            return (90.0 + free * 0.95, 250.0)
        if eng == "pool":
            return (250.0 + free * 2.6, 300.0)
        return (100.0, 100.0)

    def op(self, eng, fn, reads=(), writes=(), dma=None, inc=16, after=()):
        deps = set(after)
        for k in list(reads) + list(writes):
            if k[0] in self.pending and k not in self.touched:
                deps |= self.pending[k[0]]
                self.touched.add(k)
        for k in reads:
            w = self.lastw.get(k)
            if w is not None:
                deps.add(w)
        for k in writes:
            w = self.lastw.get(k)
            if w is not None:
                deps.add(w)
            for r in self.readers.get(k, ()):
                deps.add(r)
        i = len(self.ops)
        rec = _Rec()
        fn(rec)
        name, a, kw = rec.call
        fn = (lambda e, name=name, a=a, kw=kw: getattr(e, name)(*a, **kw))
        self.ops.append(dict(eng=eng, fn=fn, deps=deps, dma=dma, inc=inc))
        if eng == "pe":
            self.last_pe = i
        if dma is not None and dma not in self.dma_keys:
            self.dma_keys.append(dma)
        tag = ("d", dma) if dma is not None else ("e", eng)
        order = set()
        for k in reads:
            lst = self.readers.setdefault(k, [])
            for r in lst:
                if self.tags[r] == tag:
                    order.add(r)
            lst[:] = [r for r in lst if self.tags[r] != tag]
            lst.append(i)
        self.tags.append(tag)
        self.ops[i]["order"] = order
        self.ops[i]["cost"] = self._cost(eng, name, a, kw, dma)
        for k in writes:
            self.lastw[k] = i
            self.readers[k] = []
        return i

    def _list_schedule(self):
        ops = self.ops
        n = len(ops)
        full = {e: [] for e in self.ENGS}
        for i, o in enumerate(ops):
            full[o["eng"]].append(i)
        nxt = {e: 0 for e in self.ENGS}
        win = {e: [] for e in self.ENGS}
        done = [False] * n
        fin = [0.0] * n
        free_t = {e: 0.0 for e in self.ENGS}
        out = {e: [] for e in self.ENGS}
        preds = [list(o["deps"] | o["order"]) for o in ops]
        succ_eng = [set() for _ in range(n)]
        for i, o in enumerate(ops):
            for d in preds[i]:
                succ_eng[d].add(o["eng"])
        W = self.window
        xlat = self.xlat

        def refill(e):
            Wl = W if e in self.reorder_engs else 1
            w = win[e]
            f = full[e]
            while len(w) < Wl and nxt[e] < len(f):
                w.append(f[nxt[e]])
                nxt[e] += 1

        def best(e):
            bi = None
            bt = None
            ft = free_t[e]
            for i in win[e]:
                ok = True
                rt = 0.0
                for d in preds[i]:
                    if not done[d]:
                        ok = False
                        break
                    od = ops[d]
                    t = fin[d] + (0.0 if (od["eng"] == e and od["dma"] is None) else xlat)
                    if t > rt:
                        rt = t
                if not ok:
                    continue
                st = rt if rt > ft else ft
                if bt is None or st < bt - 1e-9:
                    bi, bt = i, st
                    if st <= ft + 1e-9:
                        break
            return bi, bt

        for e in self.ENGS:
            refill(e)
        cand = {}
        remaining = n
        while remaining:
            choice = None
            for e in self.ENGS:
                if not win[e]:
                    continue
                if e not in cand:
                    cand[e] = best(e)
                bi, bt = cand[e]
                if bi is None:
                    continue
                if choice is None or bt < choice[1]:
                    choice = (bi, bt, e)
            assert choice is not None, "scheduler deadlock"
            i, st, e = choice
            busy, lat = ops[i]["cost"]
            free_t[e] = st + busy
            fin[i] = st + busy + lat
            done[i] = True
            win[e].remove(i)
            refill(e)
            out[e].append(i)
            remaining -= 1
            cand.pop(e, None)
            for e2 in succ_eng[i]:
                cand.pop(e2, None)
        self.sim_time = max(fin) if fin else 0.0
        return out

    def emit(self, stack):
        nc = self.nc
        ops = self.ops
        per_eng_sched = self._list_schedule() if self.reorder else None
        needed = [False] * len(ops)
        for o in ops:
            if o["eng"] == "pe":
                o["wdeps"] = {d for d in o["deps"] if not (ops[d]["eng"] == "pe" and ops[d]["dma"] is None)}
            else:
                o["wdeps"] = o["deps"]
            for d in o["wdeps"]:
                needed[d] = True
        esem = {e: stack.enter_context(nc.semaphore("s_" + e)) for e in self.ENGS}
        dsem = {k: stack.enter_context(nc.semaphore("d_%d" % i)) for i, k in enumerate(self.dma_keys)}
        cnt = {e: 0 for e in self.ENGS}
        dcnt = {k: 0 for k in self.dma_keys}
        token = [None] * len(ops)
        per_eng = {e: [] for e in self.ENGS}
        if per_eng_sched is not None:
            seq = [i for e in self.ENGS for i in per_eng_sched[e]]
        else:
            seq = list(range(len(ops)))
        for i in seq:
            o = ops[i]
            per_eng[o["eng"]].append(i)
            if o["dma"] is not None:
                dcnt[o["dma"]] += o["inc"]
                token[i] = (("d", o["dma"]), dsem[o["dma"]], dcnt[o["dma"]])
            elif needed[i]:
                cnt[o["eng"]] += 1
                token[i] = (("e", o["eng"]), esem[o["eng"]], cnt[o["eng"]])
        self.stats = dict(sim_ms=getattr(self, "sim_time", 0.0) / 1e6, nops=len(ops), cnt=dict(cnt), ndma=len(self.dma_keys),
                          per_eng={e: len(v) for e, v in per_eng.items()})
        self._last_order = per_eng
        self._last_token = token
        block = stack.enter_context(nc.Block())
        handles = {"pe": block.tensor, "act": block.scalar, "dve": block.vector,
                   "pool": block.gpsimd, "sp": block.sync}
        final_d = dict(dcnt)
        final_e = dict(cnt)

        def make_body(e):
            def body(eng):
                seen = {}
                for i in per_eng[e]:
                    o = ops[i]
                    waits = {}
                    for d in o["wdeps"]:
                        t = token[d]
                        if t is None:
                            continue
                        name, sem, val = t
                        if seen.get(name, 0) >= val:
                            continue
                        if name not in waits or waits[name][1] < val:
                            waits[name] = (sem, val)
                    if o["dma"] is not None:
                        name, sem, val = token[i]
                        prev = val - o["inc"]
                        if prev > 0 and seen.get(name, 0) < prev:
                            waits[name] = (sem, prev)
                    for name, (sem, val) in waits.items():
                        eng.wait_ge(sem, val)
                        seen[name] = val
                    ins = o["fn"](eng)
                    t = token[i]
                    if t is not None:
                        ins.then_inc(t[1], o["inc"] if o["dma"] is not None else 1)
                if e == "sp":
                    for k, v in final_d.items():
                        if v:
                            eng.wait_ge(dsem[k], v)
                    for e2, v in final_e.items():
                        if v and e2 != "sp":
                            eng.wait_ge(esem[e2], v)
            return body

        for e in self.ENGS:
            handles[e](make_body(e))


def build_program(n_layers=2, do_mlstm=True, n_seg=NSEG):
    nc = bass.Bass("TRN2", target_bir_lowering=False)
    dr = lambda name, shape, dt=F32, kind="ExternalInput": nc.dram_tensor(name, shape, dt, kind=kind).ap()
    xin = dr("xin", [NSEG, T, D])
    halo = dr("halo", [NSEG, 2, D])
    yout = dr("yout", [NSEG, T, D], kind="ExternalOutput")
    wspec = {
        "w_cin": (24, 1024), "w_cout": (8, 1024), "w_qk": (8, 1024), "w_o": (8, 1024),
        "w_v": (4, 2048), "w_g": (1, 640), "w_mout": (8, 1024),
        "w_f1": (88, 1024), "w_f2": (16, DFF),
    }
    wf = {k: dr(k, [n, 128, w]) for k, (n, w) in wspec.items()}
    wb = {k: nc.dram_tensor("b" + k, [n, 128, w], BF16).ap() for k, (n, w) in wspec.items()}
    norms_d = dr("norms_t", [128, 64])
    convw_d = dr("convw_t", [128, 24])
    mnorm_d = dr("mnorm_t", [128, 8])
    bgate_d = dr("bgate_t", [40, 2])
    flp_d = dr("flp", [128, 8])
    fls_d = dr("fls", [40, 4])
    summ_f = nc.dram_tensor("summ_f", [128, 1032], F32).ap()
    summ_b = nc.dram_tensor("summ_b", [128, 1056], F32).ap()
    sout_f = nc.dram_tensor("sout_f", [512, 1032], F32).ap()
    sout_b = nc.dram_tensor("sout_b", [512, 1056], F32).ap()
    cin_d = nc.dram_tensor("cin_d", [128, 2064], F32).ap()

    st = contextlib.ExitStack()
    with st:
        SB = lambda name, shape, dt: st.enter_context(nc.sbuf_tensor(name, shape, dt))
        s = Sched(nc)
        xT = SB("xT", [128, 8, T], F32)
        ARENA_W = 22568
        arena = SB("arena", [128, ARENA_W], F32)

        def aview(off_b, nbytes, dt):
            a = arena[:, off_b // 4:(off_b + nbytes) // 4]
            return a.bitcast(dt) if dt != F32 else a

        wchunk = [SB("wch%d" % i, [128, 1024], BF16) for i in range(4)]
        wbig = [SB("wbig%d" % i, [128, DFF], BF16) for i in range(2)]
        wg_sb = SB("wg_sb", [128, 640], BF16)
        t2k = [SB("t2k%d" % i, [128, 512], F32) for i in range(6)]
        sqb = [SB("sqb%d" % i, [128, 512], BF16) for i in range(2)]
        ident_f = SB("ident_f", [128, 128], F32)
        ident_b = SB("ident_b", [128, 128], BF16)
        ones_b = SB("ones_b", [128, 128], BF16)
        ones_f = SB("ones_f", [128, 128], F32)
        mask = SB("mask", [128, 2, 128], F32)
        norms_sb = SB("norms_sb", [128, 64], F32)
        convw_sb = SB("convw_sb", [128, 24], F32)
        mnorm_sb = SB("mnorm_sb", [128, 8], F32)
        bgate_sb = SB("bgate_sb", [40, 2], F32)
        negbf = SB("negbf", [40, 1], F32)
        sel = SB("sel", [40, 16], F32)
        xTh = SB("xTh", [128, 8, 2], F32)
        e_tok = SB("e_tok", [128, NCH, 16], F32)
        cl_tok = SB("cl_tok", [128, NCH, 16], F32)
        dec_rep = SB("dec_rep", [128, NCH, 16], F32)
        decp = SB("decp", [128, NCH, 2, 4], F32)
        g_amax = SB("g_amax", [40, NCH], F32)
        g_bn = SB("g_bn", [40, NCH], F32)
        g_m = SB("g_m", [40, NCH], F32)
        g_mp = SB("g_mp", [40, NCH], F32)
        g_mout = SB("g_mout", [40, 1], F32)
        g_dec = SB("g_dec", [40, NCH], F32)
        g_bd = SB("g_bd", [40, NCH, 16], F32)
        PT = [SB("PT%d" % i, [128, 128], BF16) for i in range(8)]
        kp = [SB("kp%d" % i, [128, 128], BF16) for i in range(6)]
        Cp = [SB("Cp%d" % i, [128, 258], BF16) for i in range(3)]
        S_f = SB("S_f", [128, 258], F32)
        S_b = SB("S_b", [128, 258], F32)
        hs = [SB("hs%d" % i, [128, 128], F32) for i in range(4)]
        hn = [SB("hn%d" % i, [128, 128], BF16) for i in range(4)]
        hjunk = SB("hjunk", [128, 128], BF16)
        sm = [SB("sm%d" % i, [128, 8], F32) for i in range(8)]
        flp_sb = SB("flp_sb", [128, 8], F32)
        flpB = SB("flpB", [128, 8], F32)
        fls_sb = SB("fls_sb", [40, 4], F32)
        flsB = SB("flsB", [40, 4], F32)
        g_sv = SB("g_sv", [40, 2], F32)
        g_bd2 = SB("g_bd2", [40, 2, 16], F32)
        svrep = SB("svrep", [128, 32], F32)
        sc_p = SB("sc_p", [128, 16], F32)
        mq = SB("mq", [128, 4, 8], F32)
        Fq = SB("Fq", [128, 4, 8], F32)
        a1 = SB("a1", [128, 4, 8], F32)
        a2 = SB("a2", [128, 4, 8], F32)
        cm = SB("cm", [128, 8], F32)
        ct1 = SB("ct1", [128, 8], F32)
        ct2 = SB("ct2", [128, 8], F32)
        svq = SB("svq", [40, 4, 2], F32)
        ms = SB("ms", [40, 1], F32)
        st1 = SB("st1", [40, 1], F32)
        st2 = SB("st2", [40, 1], F32)

        psb = [st.enter_context(nc.psum_tensor("psb%d" % i, [128, 512], F32)) for i in range(8)]

        rr = {}

        def rot(name, n):
            i = rr.get(name, 0)
            rr[name] = i + 1
            return i % n

        accmode = {"wide": False}

        def acc_bank():
            if accmode["wide"]:
                return (0, 1, 2, 3, 4, 5, 7)[rot("accw", 7)]
            return rot("acc", 5)
        STAT = 5
        MISC = 6
        MISC2 = 7

        def P(b):
            return ("ps", b)

        s.op("pool", lambda e: e.memset(ident_f[:], 0.0), writes=[("ident_f",)])
        s.op("pool", lambda e: e.affine_select(out=ident_f[:], in_=ident_f[:], pattern=[[-1, 128]],
                                               compare_op=ALU.not_equal, fill=1.0, base=0, channel_multiplier=1),
             reads=[("ident_f",)], writes=[("ident_f",)])
        s.op("pool", lambda e: e.tensor_copy(out=ident_b[:], in_=ident_f[:]), reads=[("ident_f",)], writes=[("ident_b",)])
        s.op("pool", lambda e: e.memset(ones_b[:], 1.0), writes=[("ones_b",)])
        s.op("pool", lambda e: e.memset(ones_f[:], 1.0), writes=[("ones_f",)])
        s.op("pool", lambda e: e.affine_select(out=mask[:, 0, :], in_=ones_f[:], pattern=[[1, 128]],
                                               compare_op=ALU.is_ge, fill=0.0, base=0, channel_multiplier=-1),
             reads=[("ones_f",)], writes=[("mask", 0)])
        s.op("pool", lambda e: e.affine_select(out=mask[:, 1, :], in_=ones_f[:], pattern=[[-1, 128]],
                                               compare_op=ALU.is_ge, fill=0.0, base=0, channel_multiplier=1),
             reads=[("ones_f",)], writes=[("mask", 1)])
        s.op("pool", lambda e: e.tensor_copy(out=sel[:, 0:8], in_=ident_f[0:40, 0:8]), reads=[("ident_f",)], writes=[("sel", 0)])
        s.op("pool", lambda e: e.tensor_copy(out=sel[:, 8:16], in_=ident_f[0:40, 32:40]), reads=[("ident_f",)], writes=[("sel", 1)])
        s.op("pool", lambda e: e.memset(g_m[:], 0.0), writes=[("g", "m")])
        s.op("pool", lambda e: e.memset(g_mout[:], 0.0), writes=[("g", "mp")])
        s.op("pool", lambda e: e.memset(g_amax[:], 0.0), writes=[("g", "amax")])
        s.op("pool", lambda e: e.memset(g_bn[:], 0.0), writes=[("g", "bn")])
        s.op("sp", lambda e: e.dma_start(out=norms_sb[:], in_=norms_d), writes=[("norms",)], dma="c_norms")
        s.op("sp", lambda e: e.dma_start(out=convw_sb[:], in_=convw_d), writes=[("convw",)], dma="c_convw")
        s.op("sp", lambda e: e.dma_start(out=mnorm_sb[:], in_=mnorm_d), writes=[("mnorm",)], dma="c_mnorm")
        s.op("sp", lambda e: e.dma_start(out=bgate_sb[:], in_=bgate_d), writes=[("bgate",)], dma="c_bgate")
        s.op("dve", lambda e: e.tensor_scalar(out=negbf[:], in0=bgate_sb[:, 1:2], scalar1=-1.0, scalar2=None, op0=ALU.mult),
             reads=[("bgate",)], writes=[("negbf",)])

        cast_order = ["w_cin", "w_cout", "w_f1:0", "w_f2:0", "w_g", "w_qk", "w_v", "w_o", "w_mout", "w_f1:1", "w_f2:1"]
        cast_pieces = []
        for item in cast_order:
            if ":" in item:
                nm, l = item.split(":")
                l = int(l)
                n = wspec[nm][0] // 2
                lo, hi = l * n, (l + 1) * n
            else:
                nm = item
                lo, hi = 0, wspec[nm][0]
            step = 8 if wspec[nm][1] <= 1024 else 4
            for a in range(lo, hi, step):
                cast_pieces.append((nm, a, min(hi, a + step)))

        def flush_cast(n=1, gate=True):
            for _ in range(n):
                if not cast_pieces:
                    return
                nm, a, b = cast_pieces.pop(0)
                aft = [s.last_pe] if (gate and s.last_pe is not None) else []
                s.op("pool", lambda e: e.dma_start(out=wb[nm][a:b], in_=wf[nm][a:b]),
                     writes=[("wb", nm, j) for j in range(a, b)], dma=("cast", nm, a), after=aft)

        flush_cast(2, gate=False)

        def load_chunk(nm, j):
            sl = rot("wch", 4)
            s.op("sp", lambda e: e.dma_start(out=wchunk[sl][:], in_=wb[nm][j]),
                 reads=[("wb", nm, j)], writes=[("wch", sl)], dma=("wch", sl))
            return wchunk[sl], ("wch", sl)

        def load_big(nm, j, width):
            sl = rot("wbig", 2)
            s.op("sp", lambda e: e.dma_start(out=wbig[sl][:, 0:width], in_=wb[nm][j]),
                 reads=[("wb", nm, j)], writes=[("wbig", sl)], dma=("wbig", sl))
            return wbig[sl], ("wbig", sl)

        gcol = lambda l, n: (l * 4 + n) * 8

        def rms_T(src, srckeys, n, gc, dst, dstkeys, npart=128):
            for c in range(8):
                q = rot("sqb", 2)
                s.op("act", lambda e, c=c, q=q: e.activation(out=sqb[q][:, 0:n], in_=src(c), func=AF.Square),
                     reads=[srckeys(c)], writes=[("sqb", q)])
                s.op("pe", lambda e, c=c, q=q: e.matmul(psb[STAT][:, 0:n], lhsT=ones_b[:], rhs=sqb[q][:, 0:n],
                                                        start=(c == 0), stop=(c == 7)),
                     reads=[("sqb", q), ("ones_b",)], writes=[P(STAT)])
            r = rot("t2k", 6)
            s.op("act", lambda e: e.activation(out=t2k[r][:, 0:n], in_=psb[STAT][:, 0:n], func=AF.Sqrt, bias=EPS, scale=1.0 / D),
                 reads=[P(STAT)], writes=[("t2k", r)])
            s.op("dve", lambda e: e.reciprocal(out=t2k[r][:, 0:n], in_=t2k[r][:, 0:n]), reads=[("t2k", r)], writes=[("t2k", r)])
            for c in range(8):
                s.op("dve", lambda e, c=c: e.scalar_tensor_tensor(out=dst(c), in0=src(c), scalar=norms_sb[:, gc + c:gc + c + 1],
                                                                  in1=t2k[r][:, 0:n], op0=ALU.mult, op1=ALU.mult),
                     reads=[srckeys(c), ("t2k", r), ("norms",)], writes=[dstkeys(c)])

        def out_proj_block(wres, wreskeys, nk, rhs, rhskeys, ybuf, ykey, col0, n, gc, statbank):
            for oc in range(8):
                b = acc_bank()
                for kc in range(nk):
                    s.op("pe", lambda e, oc=oc, kc=kc, b=b: e.matmul(psb[b][:, 0:n], lhsT=wres(oc, kc), rhs=rhs(kc),
                                                                     start=(kc == 0), stop=(kc == nk - 1)),
                         reads=[wreskeys(oc), rhskeys(kc)], writes=[P(b)])
                s.op("act", lambda e, oc=oc, b=b: e.copy(out=ybuf(oc), in_=psb[b][:, 0:n]), reads=[P(b)], writes=[(ykey, oc, col0)])
                q = rot("sqb", 2)
                s.op("act", lambda e, oc=oc, q=q, b=b: e.activation(out=sqb[q][:, 0:n], in_=psb[b][:, 0:n], func=AF.Square),
                     reads=[P(b)], writes=[("sqb", q)])
                s.op("pe", lambda e, oc=oc, q=q: e.matmul(psb[statbank][:, 0:n], lhsT=ones_b[:], rhs=sqb[q][:, 0:n],
                                                          start=(oc == 0), stop=(oc == 7)),
                     reads=[("sqb", q), ("ones_b",)], writes=[P(statbank)])

        def resid_block(ybuf, ykey, col0, n, gc, statbank):
            r = rot("t2k", 6)
            s.op("act", lambda e: e.activation(out=t2k[r][:, 0:n], in_=psb[statbank][:, 0:n], func=AF.Sqrt, bias=EPS, scale=1.0 / D),
                 reads=[P(statbank)], writes=[("t2k", r)])
            s.op("dve", lambda e: e.reciprocal(out=t2k[r][:, 0:n], in_=t2k[r][:, 0:n]), reads=[("t2k", r)], writes=[("t2k", r)])
            for oc in range(8):
                t = rot("t2k", 6)
                while t == r:
                    t = rot("t2k", 6)
                s.op("dve", lambda e, oc=oc, t=t: e.scalar_tensor_tensor(out=t2k[t][:, 0:n], in0=ybuf(oc),
                                                                         scalar=norms_sb[:, gc + oc:gc + oc + 1],
                                                                         in1=t2k[r][:, 0:n], op0=ALU.mult, op1=ALU.mult),
                     reads=[(ykey, oc, col0), ("t2k", r), ("norms",)], writes=[("t2k", t)])
                s.op("pool", lambda e, oc=oc, t=t: e.tensor_tensor(out=xT[:, oc, col0:col0 + n], in0=xT[:, oc, col0:col0 + n],
                                                                   in1=t2k[t][:, 0:n], op=ALU.add),
                     reads=[("t2k", t), ("xT", oc, col0 // 512)], writes=[("xT", oc, col0 // 512)])

        xkey = lambda c, blk: ("xT", c, blk)

        import os as _os
        XQ = _os.environ.get("KXQ", "pool")

        def load_blocks(seg, blks):
            for blk in blks:
                for tt in range(blk * 4, blk * 4 + 4):
                    for hf in range(2):
                        r = rot("t2k", 6)
                        s.op(XQ, lambda e: e.dma_start(out=t2k[r][:], in_=xin[seg, tt * 128:(tt + 1) * 128, hf * 512:(hf + 1) * 512]),
                             writes=[("t2k", r)], dma=("xl", r))
                        bk = acc_bank()
                        for q in range(4):
                            s.op("pe", lambda e: e.transpose(out=psb[bk][:, q * 128:(q + 1) * 128], in_=t2k[r][:, q * 128:(q + 1) * 128], identity=ident_f[:]),
                                 reads=[("t2k", r), ("ident_f",)], writes=[P(bk)])
                        outap = xT[:, hf * 4:(hf + 1) * 4, tt * 128:(tt + 1) * 128]
                        inap = psb[bk][:].rearrange("p (q t) -> p q t", q=4)
                        wk = [xkey(hf * 4 + q, tt // 4) for q in range(4)]
                        if (tt + hf) % 2 == 0:
                            s.op("act", lambda e: e.copy(out=outap, in_=inap), reads=[P(bk)], writes=wk)
                        else:
                            s.op("dve", lambda e: e.tensor_copy(out=outap, in_=inap), reads=[P(bk)], writes=wk)

        def load_halo(seg):
            for hf in range(2):
                r = rot("t2k", 6)
                s.op(XQ, lambda e: e.dma_start(out=t2k[r][0:2, :], in_=halo[seg, :, hf * 512:(hf + 1) * 512]), writes=[("t2k", r)], dma=("xl", r))
                for q in range(4):
                    c = hf * 4 + q
                    s.op("pe", lambda e: e.transpose(out=psb[MISC][:, c * 2:(c + 1) * 2], in_=t2k[r][0:2, q * 128:(q + 1) * 128],
                                                     identity=ident_f[0:2, 0:2]),
                         reads=[("t2k", r), ("ident_f",)], writes=[P(MISC)])
            s.op("act", lambda e: e.copy(out=xTh[:], in_=psb[MISC][:, 0:16].rearrange("p (c t) -> p c t", t=2)), reads=[P(MISC)], writes=[("xTh",)])

        def store_blocks(seg, blks):
            for blk in blks:
                for tt in range(blk * 4, blk * 4 + 4):
                    for hf in range(2):
                        bk = acc_bank()
                        for q in range(4):
                            c = hf * 4 + q
                            s.op("pe", lambda e: e.transpose(out=psb[bk][:, q * 128:(q + 1) * 128], in_=xT[:, c, tt * 128:(tt + 1) * 128], identity=ident_f[:]),
                                 reads=[xkey(c, tt // 4), ("ident_f",)], writes=[P(bk)])
                        r = rot("t2k", 6)
                        if (tt + hf) % 2 == 0:
                            s.op("act", lambda e: e.copy(out=t2k[r][:], in_=psb[bk][:]), reads=[P(bk)], writes=[("t2k", r)])
                        else:
                            s.op("dve", lambda e: e.tensor_copy(out=t2k[r][:], in_=psb[bk][:]), reads=[P(bk)], writes=[("t2k", r)])
                        s.op(XQ, lambda e: e.dma_start(out=yout[seg, tt * 128:(tt + 1) * 128, hf * 512:(hf + 1) * 512], in_=t2k[r][:]),
                             reads=[("t2k", r)], writes=[("yout", seg, tt, hf)], dma=("xs", r))

        for seg in range(n_seg):
            if seg == 0:
                load_blocks(0, range(NBLK))
                load_halo(0)

            for layer in range(n_layers):
                if layer == 0:
                    xnT = aview(0, 32800, BF16).rearrange("p (c t) -> p c t", c=8)
                    zT = aview(32800, 32768, BF16).rearrange("p (c t) -> p c t", c=8)
                    c_sb = aview(65568, 8200, F32)
                    u_sb = aview(73768, 8200, F32)
                    s.alias(["xnT", "zT", "c_sb", "u_sb"], ["xnT", "zT", "c_sb", "u_sb", "wres", "mix", "xn2", "act", "y", "hT", "gt", "pair"])
                    for blk in range(NBLK):
                        rms_T(lambda c, blk=blk: xT[:, c, blk * 512:(blk + 1) * 512], lambda c, blk=blk: xkey(c, blk), 512, gcol(0, 0),
                              lambda c, blk=blk: xnT[:, c, 1 + blk * 512:1 + (blk + 1) * 512], lambda c, blk=blk: ("xnT", c, blk))
                    rms_T(lambda c: xTh[:, c, :], lambda c: ("xTh",), 2, gcol(0, 0),
                          lambda c: xnT[:, c, 0:2050:2049], lambda c: ("xnT", c, "h"))
                    for cc in range(8):
                        if seg == 0:
                            flush_cast(1)
                        wc, kwc = load_chunk("w_cin", 3 * cc + 0)
                        wv_, kwv = load_chunk("w_cin", 3 * cc + 1)
                        wb_, kwb = load_chunk("w_cin", 3 * cc + 2)
                        for blk in range(NBLK):
                            b = acc_bank()
                            for kc in range(8):
                                s.op("pe", lambda e, kc=kc, b=b, blk=blk, wc=wc: e.matmul(psb[b][:], lhsT=wc[:, kc * 128:(kc + 1) * 128],
                                                                                       rhs=xnT[:, kc, 1 + blk * 512:1 + (blk + 1) * 512],
                                                                                       start=(kc == 0), stop=(kc == 7)),
                                     reads=[kwc, ("xnT", kc, blk)], writes=[P(b)])
                            s.op("act", lambda e, b=b, blk=blk: e.copy(out=c_sb[:, 1 + blk * 512:1 + (blk + 1) * 512], in_=psb[b][:]),
                                 reads=[P(b)], writes=[("c_sb", blk)])
                        for kc in range(8):
                            s.op("pe", lambda e, kc=kc, wc=wc: e.matmul(psb[MISC][:, 0:2], lhsT=wc[:, kc * 128:(kc + 1) * 128], rhs=xnT[:, kc, 0:2050:2049],
                                                                     start=(kc == 0), stop=(kc == 7)),
                                 reads=[kwc, ("xnT", kc, "h")], writes=[P(MISC)])
                        s.op("act", lambda e: e.copy(out=c_sb[:, 0:2050:2049], in_=psb[MISC][:, 0:2]), reads=[P(MISC)], writes=[("c_sb", "h")])
                        for blk in range(NBLK):
                            b = acc_bank()
                            for kc in range(8):
                                s.op("pe", lambda e, kc=kc, b=b, blk=blk, wv_=wv_: e.matmul(psb[b][:], lhsT=wv_[:, kc * 128:(kc + 1) * 128],
                                                                                         rhs=xnT[:, kc, 1 + blk * 512:1 + (blk + 1) * 512],
                                                                                         start=(kc == 0), stop=(kc == 7)),
                                     reads=[kwv, ("xnT", kc, blk)], writes=[P(b)])
                            s.op("dve", lambda e, b=b, blk=blk: e.tensor_tensor(out=u_sb[:, 1 + blk * 512:1 + (blk + 1) * 512], in0=psb[b][:],
                                                                               in1=c_sb[:, 1 + blk * 512:1 + (blk + 1) * 512], op=ALU.mult),
                                 reads=[P(b), ("c_sb", blk)], writes=[("u_sb", blk)])
                        for kc in range(8):
                            s.op("pe", lambda e, kc=kc, wv_=wv_: e.matmul(psb[MISC][:, 0:2], lhsT=wv_[:, kc * 128:(kc + 1) * 128], rhs=xnT[:, kc, 0:2050:2049],
                                                                       start=(kc == 0), stop=(kc == 7)),
                                 reads=[kwv, ("xnT", kc, "h")], writes=[P(MISC)])
                        s.op("dve", lambda e: e.tensor_tensor(out=u_sb[:, 0:2050:2049], in0=psb[MISC][:, 0:2], in1=c_sb[:, 0:2050:2049], op=ALU.mult),
                             reads=[P(MISC), ("c_sb", "h")], writes=[("u_sb", "h")])
                        ukeys = [("u_sb", k) for k in (0, 1, 2, 3, "h")]
                        ckeys = [("c_sb", k) for k in (0, 1, 2, 3, "h")]
                        cw = lambda j, cc=cc: convw_sb[:, cc * 3 + j:cc * 3 + j + 1]
                        s.op("act", lambda e, cw=cw: e.activation(out=c_sb[:, 1:2049], in_=u_sb[:, 0:2048], func=AF.Copy, scale=cw(0)),
                             reads=ukeys + [("convw",)], writes=ckeys)
                        s.op("dve", lambda e, cw=cw: e.scalar_tensor_tensor(out=c_sb[:, 1:2049], in0=u_sb[:, 1:2049], scalar=cw(1), in1=c_sb[:, 1:2049],
                                                                          op0=ALU.mult, op1=ALU.add),
                             reads=ukeys + ckeys + [("convw",)], writes=ckeys)
                        s.op("dve", lambda e, cw=cw: e.scalar_tensor_tensor(out=c_sb[:, 1:2049], in0=u_sb[:, 2:2050], scalar=cw(2), in1=c_sb[:, 1:2049],
                                                                          op0=ALU.mult, op1=ALU.add),
                             reads=ukeys + ckeys + [("convw",)], writes=ckeys)
                        for blk in range(NBLK):
                            b = acc_bank()
                            for kc in range(8):
                                s.op("pe", lambda e, kc=kc, b=b, blk=blk, wb_=wb_: e.matmul(psb[b][:], lhsT=wb_[:, kc * 128:(kc + 1) * 128],
                                                                                         rhs=xnT[:, kc, 1 + blk * 512:1 + (blk + 1) * 512],
                                                                                         start=(kc == 0), stop=(kc == 7)),
                                     reads=[kwb, ("xnT", kc, blk)], writes=[P(b)])
                            s.op("dve", lambda e, b=b, blk=blk, cc=cc: e.tensor_tensor(out=zT[:, cc, blk * 512:(blk + 1) * 512], in0=psb[b][:],
                                                                                      in1=c_sb[:, 1 + blk * 512:1 + (blk + 1) * 512], op=ALU.mult),
                                 reads=[P(b)] + ckeys, writes=[("zT", cc, blk)])
                    mixer_in = zT
                    mixer_key = "zT"
                    wout_name = "w_cout"
                else:
                    if seg == 0:
                        flush_cast(100)
                        s.op("sp", lambda e: e.dma_start(out=wg_sb[:], in_=wb["w_g"][0]), reads=[("wb", "w_g", 0)], writes=[("wg",)], dma="c_wg")
                    xnT = aview(0, 32768, BF16).rearrange("p (c t) -> p c t", c=8)
                    hT = aview(32800, 32768, BF16).rearrange("p (c t) -> p c t", c=8)
                    s.alias(["xnT", "hT", "gt", "pair"], ["xnT", "zT", "c_sb", "u_sb", "wres", "mix", "xn2", "act", "y", "hT", "gt", "pair"])
                    for blk in range(NBLK):
                        rms_T(lambda c, blk=blk: xT[:, c, blk * 512:(blk + 1) * 512], lambda c, blk=blk: xkey(c, blk), 512, gcol(1, 0),
                              lambda c, blk=blk: xnT[:, c, blk * 512:(blk + 1) * 512], lambda c, blk=blk: ("xnT", c, blk))
                    GI = aview(32800, 8192, F32)
                    CS = aview(32800 + 8192, 8192, F32)
                    SP = aview(32800 + 16384, 8192, F32)
                    GI3 = GI.rearrange("p (c t) -> p c t", t=128)
                    SP3 = SP.rearrange("p (c t) -> p c t", t=128)
                    CS3 = CS.rearrange("p (c t) -> p c t", t=128)
                    for blk in range(NBLK):
                        cols = slice(blk * 512, (blk + 1) * 512)
                        for gi in range(2):
                            b = acc_bank()
                            for kc in range(8):
                                s.op("pe", lambda e: e.matmul(psb[b][0:40, :], lhsT=wg_sb[:, gi * 320 + kc * 40:gi * 320 + (kc + 1) * 40],
                                                              rhs=xnT[:, kc, cols], start=(kc == 0), stop=(kc == 7)),
                                     reads=[("wg",), ("xnT", kc, blk)], writes=[P(b)])
                            if gi == 0:
                                s.op("act", lambda e: e.activation(out=GI[0:40, cols], in_=psb[b][0:40, :], func=AF.Identity, bias=bgate_sb[:, 0:1]),
                                     reads=[P(b), ("bgate",)], writes=[("gt", "GI")])
                            else:
                                s.op("act", lambda e: e.activation(out=SP[0:40, cols], in_=psb[b][0:40, :], func=AF.Exp, bias=negbf[:, 0:1], scale=-1.0),
                                     reads=[P(b), ("negbf",)], writes=[("gt", "SP")])
                    s.op("act", lambda e: e.activation(out=SP[0:40, :], in_=SP[0:40, :], func=AF.Ln, bias=1.0), reads=[("gt", "SP")], writes=[("gt", "SP")])
                    for c in range(NCH):
                        s.op("dve", lambda e: e.tensor_tensor_scan(out=CS[0:40, c * 128:(c + 1) * 128], data0=ones_f[0:40, :], data1=SP[0:40, c * 128:(c + 1) * 128],
                                                                   initial=0.0, op0=ALU.mult, op1=ALU.add),
                             reads=[("gt", "SP"), ("ones_f",)], writes=[("gt", "CS")])
                    s.op("dve", lambda e: e.tensor_copy(out=g_bn[:], in_=CS3[0:40, :, 127]), reads=[("gt", "CS")], writes=[("g", "bn")])
                    s.op("dve", lambda e: e.tensor_tensor(out=CS3[32:40], in0=g_bn[32:40, :].unsqueeze(2).broadcast_to([8, NCH, 128]), in1=CS3[32:40], op=ALU.subtract),
                         reads=[("gt", "CS"), ("g", "bn")], writes=[("gt", "CS")])
                    s.op("dve", lambda e: e.tensor_tensor(out=CS[32:40, :], in0=CS[32:40, :], in1=SP[32:40, :], op=ALU.add),
                         reads=[("gt", "CS"), ("gt", "SP")], writes=[("gt", "CS")])
                    s.op("dve", lambda e: e.tensor_tensor(out=GI[0:40, :], in0=GI[0:40, :], in1=CS[0:40, :], op=ALU.add),
                         reads=[("gt", "GI"), ("gt", "CS")], writes=[("gt", "GI")])
                    s.op("dve", lambda e: e.tensor_reduce(out=g_amax[:], in_=GI3[0:40], axis=AX.X, op=ALU.max), reads=[("gt", "GI")], writes=[("g", "amax")])
                    tabkeys = [("e_tok", h_, d_) for h_ in range(2) for d_ in range(2)]
                    clkeys = [("cl_tok", h_, d_) for h_ in range(2) for d_ in range(2)]
                    dpkeys = [("decp", 0), ("decp", 1)]

                    def gate_tables(m_in):
                        s.op("dve", lambda e: e.memset(g_mp[:], NEG), writes=[("g", "mp")])
                        if m_in is not None:
                            s.op("dve", lambda e: e.tensor_copy(out=g_mp[0:8, 0:1], in_=m_in[0:8, :]), reads=[("cs", "ms"), ("g", "mp")], writes=[("g", "mp")])
                            s.op("dve", lambda e: e.tensor_copy(out=g_mp[32:40, NCH - 1:NCH], in_=m_in[32:40, :]), reads=[("cs", "ms"), ("g", "mp")], writes=[("g", "mp")])
                        for c in range(NCH):
                            s.op("dve", lambda e: e.tensor_tensor(out=g_m[0:8, c:c + 1], in0=g_mp[0:8, c:c + 1], in1=g_amax[0:8, c:c + 1], op=ALU.max),
                                 reads=[("g", "mp"), ("g", "amax")], writes=[("g", "m")])
                            dst = g_mp[0:8, c + 1:c + 2] if c < NCH - 1 else g_mout[0:8, :]
                            s.op("dve", lambda e: e.tensor_tensor(out=dst, in0=g_m[0:8, c:c + 1], in1=g_bn[0:8, c:c + 1], op=ALU.subtract),
                                 reads=[("g", "m"), ("g", "bn")], writes=[("g", "mp")])
                        for c in range(NCH - 1, -1, -1):
                            s.op("dve", lambda e: e.tensor_tensor(out=g_m[32:40, c:c + 1], in0=g_mp[32:40, c:c + 1], in1=g_amax[32:40, c:c + 1], op=ALU.max),
                                 reads=[("g", "mp"), ("g", "amax")], writes=[("g", "m")])
                            dst = g_mp[32:40, c - 1:c] if c > 0 else g_mout[32:40, :]
                            s.op("dve", lambda e: e.tensor_tensor(out=dst, in0=g_m[32:40, c:c + 1], in1=g_bn[32:40, c:c + 1], op=ALU.subtract),
                                 reads=[("g", "m"), ("g", "bn")], writes=[("g", "mp")])
                        s.op("dve", lambda e: e.tensor_tensor(out=g_dec[:], in0=g_mp[:], in1=g_m[:], op=ALU.subtract), reads=[("g", "mp"), ("g", "m")], writes=[("g", "dec")])
                        s.op("dve", lambda e: e.tensor_scalar(out=g_dec[:], in0=g_dec[:], scalar1=-100.0, scalar2=None, op0=ALU.max), reads=[("g", "dec")], writes=[("g", "dec")])
                        s.op("act", lambda e: e.activation(out=g_dec[:], in_=g_dec[:], func=AF.Exp), reads=[("g", "dec")], writes=[("g", "dec")])
                        for src3, skey, dstt, dkey in ((GI3, ("gt", "GI"), e_tok, "e_tok"), (CS3, ("gt", "CS"), cl_tok, "cl_tok")):
                            s.op("dve", lambda e: e.tensor_tensor(out=SP3[0:40], in0=src3[0:40], in1=g_m[:].unsqueeze(2).broadcast_to([40, NCH, 128]), op=ALU.subtract),
                                 reads=[skey, ("g", "m"), ("gt", "SP")], writes=[("gt", "SP")])
                            s.op("act", lambda e: e.activation(out=SP[0:40, :], in_=SP[0:40, :], func=AF.Exp), reads=[("gt", "SP")], writes=[("gt", "SP")])
                            for half in range(2):
                                for c8 in range(8):
                                    c = half * 8 + c8
                                    s.op("pe", lambda e: e.transpose(out=psb[MISC][:, c8 * 40:(c8 + 1) * 40], in_=SP[0:40, c * 128:(c + 1) * 128],
                                                                     identity=ident_f[0:40, 0:40]),
                                         reads=[("gt", "SP"), ("ident_f",)], writes=[P(MISC)])
                                m3 = psb[MISC][:, 0:320].rearrange("p (c j) -> p c j", j=40)
                                for d in range(2):
                                    s.op("act", lambda e: e.copy(out=dstt[:, half * 8:(half + 1) * 8, d * 8:(d + 1) * 8], in_=m3[:, :, d * 32:d * 32 + 8]),
                                         reads=[P(MISC)], writes=[(dkey, half, d)])
                        s.op("dve", lambda e: e.tensor_tensor(out=g_bd[:], in0=g_dec[:].unsqueeze(2).broadcast_to([40, NCH, 16]),
                                                              in1=sel[:].unsqueeze(1).broadcast_to([40, NCH, 16]), op=ALU.mult),
                             reads=[("g", "dec"), ("sel", 0), ("sel", 1)], writes=[("g", "bd")])
                        s.op("pe", lambda e: e.matmul(psb[MISC2][:, 0:256], lhsT=ones_f[0:40, :], rhs=g_bd[:].rearrange("p c j -> p (c j)"), start=True, stop=True),
                             reads=[("g", "bd"), ("ones_f",)], writes=[P(MISC2)])
                        s.op("act", lambda e: e.copy(out=dec_rep[:].rearrange("p c j -> p (c j)"), in_=psb[MISC2][:, 0:256]), reads=[P(MISC2)], writes=[("dec_rep",)])
                        dr4 = dec_rep[:].rearrange("p c (d g two) -> p c d g two", d=2, two=2)
                        s.op("dve", lambda e: e.tensor_copy(out=decp[0:64], in_=dr4[0:64, :, :, :, 0]), reads=[("dec_rep",)], writes=[("decp", 0)])
                        s.op("dve", lambda e: e.tensor_copy(out=decp[64:128], in_=dr4[64:128, :, :, :, 1]), reads=[("dec_rep",)], writes=[("decp", 1)])

                    PB = 65568
                    qT = aview(PB, 4096, BF16)
                    kT = aview(PB + 4096, 4096, BF16)
                    v_aug = aview(PB + 8192, 8256, BF16).rearrange("p (c h f) -> p c h f", c=NCH, h=2)
                    Cb_st = aview(PB + 8192 + 8256, 8256, BF16).rearrange("p (c f) -> p c f", c=NCH)

                    qstate = {}

                    def qk_proj(g, which, blks=None):
                        dst, sc = ((qT, 0.125), (kT, 1.0))[which]
                        if blks is None or (g, which) not in qstate:
                            qstate[(g, which)] = load_chunk("w_qk", 2 * g + which)
                        wq, kwq = qstate[(g, which)]
                        for blk in (range(NBLK) if blks is None else blks):
                            b = acc_bank()
                            for kc in range(8):
                                s.op("pe", lambda e: e.matmul(psb[b][:], lhsT=wq[:, kc * 128:(kc + 1) * 128],
                                                              rhs=xnT[:, kc, blk * 512:(blk + 1) * 512], start=(kc == 0), stop=(kc == 7)),
                                     reads=[kwq, ("xnT", kc, blk)], writes=[P(b)])
                            s.op("act", lambda e: e.activation(out=dst[:, blk * 512:(blk + 1) * 512], in_=psb[b][:], func=AF.Copy, scale=sc),
                                 reads=[P(b)], writes=[("pair", "qk", which, blk)])

                    def v_proj(g):
                        wv_, kwv = load_big("w_v", g, 2048)
                        s.op("pool", lambda e: e.memset(v_aug[:, :, :, 128:129], 1.0), writes=[("pair", "v1")])
                        for c in range(NCH):
                            b = acc_bank()
                            for kc in range(8):
                                s.op("pe", lambda e: e.matmul(psb[b][:, 0:256], lhsT=xnT[:, kc, c * 128:(c + 1) * 128],
                                                              rhs=wv_[:, kc * 256:(kc + 1) * 256], start=(kc == 0), stop=(kc == 7)),
                                     reads=[kwv, ("xnT", kc, c // 4)], writes=[P(b)])
                            s.op("act", lambda e: e.copy(out=v_aug[:, c, :, 0:128], in_=psb[b][:, 0:256].rearrange("p (h f) -> p h f", h=2)),
                                 reads=[P(b)], writes=[("pair", "v", c)])

                    def kprime(g, c, d):
                        pbank = psb[MISC][:].bitcast(BF16)
                        s.op("pe", lambda e: e.transpose(out=pbank[:, 0:128], in_=kT[:, c * 128:(c + 1) * 128], identity=ident_b[:]),
                             reads=[("pair", "qk", 1, c // 4), ("ident_b",)], writes=[P(MISC)])
                        q = rot("kp", 6)
                        s.op("dve", lambda e: e.tensor_tensor(out=kp[q][:].rearrange("p (h f) -> p h f", h=2),
                                                              in0=pbank[:, 0:128].rearrange("p (h f) -> p h f", h=2),
                                                              in1=e_tok[:, c, d * 8 + 2 * g:d * 8 + 2 * g + 2].unsqueeze(2).broadcast_to([128, 2, 64]),
                                                              op=ALU.mult),
                             reads=[P(MISC)] + tabkeys, writes=[("kp", q)])
                        return q

                    def state_step(g, c, d, S, Skey, q):
                        b = acc_bank()
                        s.op("pe", lambda e: e.matmul(psb[b][:, 0:258], lhsT=kp[q][:], rhs=v_aug[:, c, :, :].rearrange("p h f -> p (h f)"), start=True, stop=True),
                             reads=[("kp", q), ("pair", "v", c), ("pair", "v1")], writes=[P(b)])
                        s.op("dve", lambda e: e.scalar_tensor_tensor(out=S[:], in0=S[:], scalar=decp[:, c, d, g:g + 1], in1=psb[b][:, 0:258],
                                                                     op0=ALU.mult, op1=ALU.add),
                             reads=[P(b), Skey] + dpkeys, writes=[Skey])

                    if do_mlstm:
                        qk_proj(0, 1)
                        v_proj(0)
                        if seg != 0:
                            qk_proj(0, 0)
                    gate_tables(None)
                    if seg == 0 and do_mlstm:
                        for g in range(4):
                            if g > 0:
                                qk_proj(g, 1)
                                v_proj(g)
                            s.op("pool", lambda e: e.memset(S_b[:], 0.0), writes=[("S_b",)])
                            for c in range(NCH - 1, -1, -1):
                                q = kprime(g, c, 1)
                                state_step(g, c, 1, S_b, ("S_b",), q)
                            s.op("sp", lambda e: e.dma_start(out=summ_b[:, g * 258:(g + 1) * 258], in_=S_b[:]), reads=[("S_b",)], writes=[("summ_in", 4 + g)], dma="sm")
                            s.op("pool", lambda e: e.memset(S_f[:], 0.0), writes=[("S_f",)])
                            for c in range(NCH):
                                q = kprime(g, c, 0)
                                state_step(g, c, 0, S_f, ("S_f",), q)
                            s.op("sp", lambda e: e.dma_start(out=summ_f[:, g * 258:(g + 1) * 258], in_=S_f[:]), reads=[("S_f",)], writes=[("summ_in", g)], dma="sm")
                        s.op("dve", lambda e: e.tensor_copy(out=g_sv[:, 0:1], in_=g_mout[:]), reads=[("g", "mp")], writes=[("cs", "sv")])
                        s.op("dve", lambda e: e.tensor_reduce(out=g_sv[:, 1:2], in_=g_bn[:], axis=AX.X, op=ALU.add, negate=True), reads=[("g", "bn"), ("cs", "sv")], writes=[("cs", "sv")])
                        s.op("dve", lambda e: e.tensor_tensor(out=g_bd2[:], in0=g_sv[:].unsqueeze(2).broadcast_to([40, 2, 16]),
                                                              in1=sel[:].unsqueeze(1).broadcast_to([40, 2, 16]), op=ALU.mult),
                             reads=[("cs", "sv"), ("sel", 0), ("sel", 1)], writes=[("cs", "bd2")])
                        s.op("pe", lambda e: e.matmul(psb[MISC2][:, 0:32], lhsT=ones_f[0:40, :], rhs=g_bd2[:].rearrange("p a j -> p (a j)"), start=True, stop=True),
                             reads=[("cs", "bd2"), ("ones_f",)], writes=[P(MISC2)])
                        s.op("act", lambda e: e.copy(out=svrep[:], in_=psb[MISC2][:, 0:32]), reads=[P(MISC2)], writes=[("cs", "svrep")])
                        sv4 = svrep[:].rearrange("p (a d g two) -> p a d g two", a=2, d=2, two=2)
                        scp4 = sc_p[:].rearrange("p (a d g) -> p a d g", a=2, d=2)
                        s.op("dve", lambda e: e.tensor_copy(out=scp4[0:64], in_=sv4[0:64, :, :, :, 0]), reads=[("cs", "svrep")], writes=[("cs", "scp", 0)])
                        s.op("dve", lambda e: e.tensor_copy(out=scp4[64:128], in_=sv4[64:128, :, :, :, 1]), reads=[("cs", "svrep")], writes=[("cs", "scp", 1)])
                        s.op("sp", lambda e: e.dma_start(out=summ_b[:, 1032:1048], in_=sc_p[:]), reads=[("cs", "scp", 0), ("cs", "scp", 1)], writes=[("summ_in", 8)], dma="sm")
                        s.op("pool", lambda e: e.memset(ct1[:], 0.0), writes=[("cs", "ct1")])
                        s.op("sp", lambda e: e.dma_start(out=summ_b[:, 1048:1056], in_=ct1[:]), reads=[("cs", "ct1")], writes=[("summ_in", 9)], dma="sm")
                        s.op("sp", lambda e: e.dma_start(out=summ_b[0:40, 1048:1050], in_=g_sv[:]), reads=[("cs", "sv")], writes=[("summ_in", 9)], dma="sm")
                        s.op("pool", lambda e: e.collective_compute("AllGather", ALU.bypass, replica_groups=[[0, 1, 2, 3], [4, 5, 6, 7]],
                                                                    ins=[summ_f], outs=[sout_f]),
                             reads=[("summ_in", j) for j in range(10)], writes=[("summ_out",)], dma="ag", inc=1)
                        s.op("pool", lambda e: e.collective_compute("AllGather", ALU.bypass, replica_groups=[[0, 1, 2, 3], [4, 5, 6, 7]],
                                                                    ins=[summ_b], outs=[sout_b]),
                             reads=[("summ_in", j) for j in range(10)], writes=[("summ_out",)], dma="ag2", inc=1)
                        s.op("sp", lambda e: e.dma_start(out=flp_sb[:], in_=flp_d), writes=[("cs", "flp")], dma="c_flp")
                        s.op("sp", lambda e: e.dma_start(out=fls_sb[:], in_=fls_d), writes=[("cs", "fls")], dma="c_fls")
                        s.op("dve", lambda e: e.tensor_scalar(out=flpB[:], in0=flp_sb[:], scalar1=-NEG, scalar2=NEG, op0=ALU.mult, op1=ALU.add), reads=[("cs", "flp")], writes=[("cs", "flpB")])
                        s.op("dve", lambda e: e.tensor_scalar(out=flsB[:], in0=fls_sb[:], scalar1=-NEG, scalar2=NEG, op0=ALU.mult, op1=ALU.add), reads=[("cs", "fls")], writes=[("cs", "flsB")])
                        s.op("dve", lambda e: e.memset(svq[:], 0.0), writes=[("cs", "svq")])
                        for k in range(4):
                            for d in range(2):
                                qm = k if d == 0 else 3 - k
                                s.op("sp", lambda e: e.dma_start(out=mq[:, k, d * 4:(d + 1) * 4], in_=sout_b[qm * 128:(qm + 1) * 128, 1032 + d * 4:1032 + (d + 1) * 4]),
                                     reads=[("summ_out",)], writes=[("cs", "mq", k, d)], dma="cq")
                                s.op("sp", lambda e: e.dma_start(out=Fq[:, k, d * 4:(d + 1) * 4], in_=sout_b[qm * 128:(qm + 1) * 128, 1040 + d * 4:1040 + (d + 1) * 4]),
                                     reads=[("summ_out",)], writes=[("cs", "Fq", k, d)], dma="cq")
                                r0 = d * 32
                                s.op("sp", lambda e: e.dma_start(out=svq[r0:r0 + 8, k, :], in_=sout_b[qm * 128 + r0:qm * 128 + r0 + 8, 1048:1050]),
                                     reads=[("summ_out",), ("cs", "svq")], writes=[("cs", "svq")], dma="cq")
                        cs_ = lambda nm: [("cs", nm)]
                        s.op("dve", lambda e: e.memset(cm[:], NEG), writes=cs_("cm"))
                        s.op("dve", lambda e: e.memset(ms[:], NEG), writes=cs_("ms"))
                        v3 = lambda t: t[:].rearrange("p (d g) -> p d g", d=2)
                        for k in range(4):
                            flb = flp_sb[:, k * 2:(k + 1) * 2].unsqueeze(2).broadcast_to([128, 2, 4])
                            flBb = flpB[:, k * 2:(k + 1) * 2].unsqueeze(2).broadcast_to([128, 2, 4])
                            mqk = mq[:, k, :].rearrange("p (d g) -> p d g", d=2)
                            Fqk = Fq[:, k, :].rearrange("p (d g) -> p d g", d=2)
                            rk = [("cs", "mq", k_, d_) for k_ in range(4) for d_ in range(2)] + [("cs", "Fq", k_, d_) for k_ in range(4) for d_ in range(2)] + [("cs", "svq"), ("cs", "flp"), ("cs", "flpB")]
                            s.op("dve", lambda e: e.tensor_tensor(out=v3(ct1), in0=Fqk, in1=flb, op=ALU.mult), reads=rk, writes=cs_("ct1"))
                            s.op("dve", lambda e: e.tensor_tensor(out=v3(ct2), in0=mqk, in1=flb, op=ALU.mult), reads=rk, writes=cs_("ct2"))
                            s.op("dve", lambda e: e.tensor_tensor(out=v3(ct2), in0=v3(ct2), in1=flBb, op=ALU.add), reads=rk + cs_("ct2"), writes=cs_("ct2"))
                            s.op("dve", lambda e: e.tensor_tensor(out=ct1[:], in0=ct1[:], in1=cm[:], op=ALU.add), reads=cs_("ct1") + cs_("cm"), writes=cs_("ct1"))
                            s.op("dve", lambda e: e.tensor_tensor(out=cm[:], in0=ct1[:], in1=ct2[:], op=ALU.max), reads=cs_("ct1") + cs_("ct2") + cs_("cm"), writes=cs_("cm"))
                            s.op("dve", lambda e: e.tensor_tensor(out=ct1[:], in0=ct1[:], in1=cm[:], op=ALU.subtract), reads=cs_("ct1") + cs_("cm"), writes=cs_("ct1"))
                            s.op("dve", lambda e: e.tensor_scalar(out=ct1[:], in0=ct1[:], scalar1=-100.0, scalar2=None, op0=ALU.max), reads=cs_("ct1"), writes=cs_("ct1"))
                            s.op("act", lambda e: e.activation(out=a1[:, k, :], in_=ct1[:], func=AF.Exp), reads=cs_("ct1"), writes=[("cs", "a1", k)])
                            s.op("dve", lambda e: e.tensor_tensor(out=ct2[:], in0=ct2[:], in1=cm[:], op=ALU.subtract), reads=cs_("ct2") + cs_("cm"), writes=cs_("ct2"))
                            s.op("dve", lambda e: e.tensor_scalar(out=ct2[:], in0=ct2[:], scalar1=-100.0, scalar2=None, op0=ALU.max), reads=cs_("ct2"), writes=cs_("ct2"))
                            s.op("act", lambda e: e.activation(out=ct2[:], in_=ct2[:], func=AF.Exp), reads=cs_("ct2"), writes=cs_("ct2"))
                            s.op("dve", lambda e: e.tensor_tensor(out=a2[:, k, :].rearrange("p (d g) -> p d g", d=2), in0=v3(ct2), in1=flb, op=ALU.mult),
                                 reads=cs_("ct2") + rk, writes=[("cs", "a2", k)])
                            rs = rk + [("cs", "fls"), ("cs", "flsB")]
                            s.op("dve", lambda e: e.tensor_tensor(out=st1[:], in0=svq[:, k, 1:2], in1=fls_sb[:, k:k + 1], op=ALU.mult), reads=rs, writes=cs_("st1"))
                            s.op("dve", lambda e: e.tensor_tensor(out=st2[:], in0=svq[:, k, 0:1], in1=fls_sb[:, k:k + 1], op=ALU.mult), reads=rs, writes=cs_("st2"))
                            s.op("dve", lambda e: e.tensor_tensor(out=st2[:], in0=st2[:], in1=flsB[:, k:k + 1], op=ALU.add), reads=rs + cs_("st2"), writes=cs_("st2"))
                            s.op("dve", lambda e: e.tensor_tensor(out=st1[:], in0=st1[:], in1=ms[:], op=ALU.add), reads=cs_("st1") + cs_("ms"), writes=cs_("st1"))
                            s.op("dve", lambda e: e.tensor_tensor(out=ms[:], in0=st1[:], in1=st2[:], op=ALU.max), reads=cs_("st1") + cs_("st2") + cs_("ms"), writes=cs_("ms"))
                        for d in range(2):
                            for g in range(4):
                                ra = rot("t2k", 6)
                                s.op("pool", lambda e: e.memset(t2k[ra][:, 0:258], 0.0), writes=[("t2k", ra)])
                                for k in range(4):
                                    qm = k if d == 0 else 3 - k
                                    rb = rot("t2k", 6)
                                    while rb == ra:
                                        rb = rot("t2k", 6)
                                    s.op(XQ, lambda e: e.dma_start(out=t2k[rb][:, 0:258], in_=(sout_f if d == 0 else sout_b)[qm * 128:(qm + 1) * 128, g * 258:(g + 1) * 258]),
                                         reads=[("summ_out",)], writes=[("t2k", rb)], dma=("xl", rb))
                                    col = d * 4 + g
                                    s.op("dve", lambda e: e.tensor_scalar(out=t2k[rb][:, 0:258], in0=t2k[rb][:, 0:258], scalar1=a2[:, k, col:col + 1], scalar2=None, op0=ALU.mult),
                                         reads=[("t2k", rb), ("cs", "a2", k)], writes=[("t2k", rb)])
                                    s.op("dve", lambda e: e.scalar_tensor_tensor(out=t2k[ra][:, 0:258], in0=t2k[ra][:, 0:258], scalar=a1[:, k, col:col + 1], in1=t2k[rb][:, 0:258],
                                                                                 op0=ALU.mult, op1=ALU.add),
                                         reads=[("t2k", ra), ("t2k", rb), ("cs", "a1", k)], writes=[("t2k", ra)])
                                s.op(XQ, lambda e: e.dma_start(out=cin_d[:, (d * 4 + g) * 258:(d * 4 + g + 1) * 258], in_=t2k[ra][:, 0:258]),
                                     reads=[("t2k", ra)], writes=[("cin", d, g)], dma=("xs", ra))
                        gate_tables(ms)

                    s.alias(["hT"], ["gt"])
                    accmode["wide"] = True
                    for g in range(4 if do_mlstm else 0):
                        q_interleave = True
                        if g > 0 or seg == 0:
                            qk_proj(g, 1)
                            v_proj(g)
                        else:
                            q_interleave = False
                        if seg == 0:
                            s.op("sp", lambda e: e.dma_start(out=S_b[:], in_=cin_d[:, (4 + g) * 258:(5 + g) * 258]), reads=[("cin", 1, g)], writes=[("S_b",)], dma="c_sb")
                        else:
                            s.op("pool", lambda e: e.memset(S_b[:], 0.0), writes=[("S_b",)])
                        for c in range(NCH - 1, -1, -1):
                            s.op("act", lambda e, c=c: e.activation(out=Cb_st[:, c, :], in_=S_b[:], func=AF.Copy, scale=decp[:, c, 1, g:g + 1]),
                                 reads=[("S_b",)] + dpkeys, writes=[("pair", "Cb", c)])
                            if c > 0:
                                q = kprime(g, c, 1)
                                state_step(g, c, 1, S_b, ("S_b",), q)
                            if q_interleave and c % 4 == 0:
                                qk_proj(g, 0, blks=[3 - c // 4])
                        if seg == 0:
                            s.op("sp", lambda e: e.dma_start(out=S_f[:], in_=cin_d[:, g * 258:(g + 1) * 258]), reads=[("cin", 0, g)], writes=[("S_f",)], dma="c_sf")
                        else:
                            s.op("pool", lambda e: e.memset(S_f[:], 0.0), writes=[("S_f",)])
                        for c in range(0, NCH, 2):
                            if c % 4 == 0:
                                so = []
                                for hh in range(2):
                                    wo, kwo = load_chunk("w_o", 2 * g + hh)
                                    b = acc_bank()
                                    for kc in range(8):
                                        s.op("pe", lambda e, kc=kc, b=b, wo=wo: e.matmul(psb[b][:], lhsT=wo[:, kc * 128:(kc + 1) * 128],
                                                                                      rhs=xnT[:, kc, (c // 4) * 512:(c // 4 + 1) * 512], start=(kc == 0), stop=(kc == 7)),
                                             reads=[kwo, ("xnT", kc, c // 4)], writes=[P(b)])
                                    r = rot("t2k", 6)
                                    s.op("act", lambda e, b=b, r=r: e.activation(out=t2k[r][:], in_=psb[b][:], func=AF.Sigmoid), reads=[P(b)], writes=[("t2k", r)])
                                    so.append(r)
                            U = []
                            for cc_ in (c, c + 1):
                                cq_ = rot("Cp", 3)
                                s.op("act", lambda e: e.activation(out=Cp[cq_][:], in_=S_f[:], func=AF.Copy, scale=decp[:, cc_, 0, g:g + 1]),
                                     reads=[("S_f",)] + dpkeys, writes=[("Cp", cq_)])
                                for hh in range(2):
                                    U.append(dict(c=cc_, csl=slice(cc_ * 128, (cc_ + 1) * 128), cq=cq_, hh=hh, hd=2 * g + hh,
                                                  rows=slice(hh * 64, (hh + 1) * 64), bS=acc_bank()))
                                if cc_ < NCH - 1:
                                    q = kprime(g, cc_, 0)
                                    state_step(g, cc_, 0, S_f, ("S_f",), q)
                            for u in U:
                                s.op("pe", lambda e: e.matmul(psb[u["bS"]][:, 0:128], lhsT=kT[u["rows"], u["csl"]], rhs=qT[u["rows"], u["csl"]], start=True, stop=True),
                                     reads=[("pair", "qk", 0, u["c"] // 4), ("pair", "qk", 1, u["c"] // 4)], writes=[P(u["bS"])])
                            for u in U:
                                u["pts"] = []
                                for d in range(2):
                                    pq = rot("PT", 8)
                                    s.op("dve", lambda e: e.scalar_tensor_tensor(out=PT[pq][:], in0=psb[u["bS"]][:, 0:128],
                                                                                 scalar=e_tok[:, u["c"], d * 8 + u["hd"]:d * 8 + u["hd"] + 1],
                                                                                 in1=mask[:, d, :], op0=ALU.mult, op1=ALU.mult),
                                         reads=[P(u["bS"]), ("mask", d)] + tabkeys, writes=[("PT", pq)])
                                    u["pts"].append(pq)
                            for u in U:
                                u["bO"] = acc_bank()
                                bO, hh, rows = u["bO"], u["hh"], u["rows"]
                                c, csl, cq = u["c"], u["csl"], u["cq"]
                                vrhs = v_aug[:, c, hh, :]
                                s.op("pe", lambda e: e.matmul(psb[bO][:, 0:129], lhsT=PT[u["pts"][0]][:], rhs=vrhs, start=True, stop=False),
                                     reads=[("PT", u["pts"][0]), ("pair", "v", c), ("pair", "v1")], writes=[P(bO)])
                                s.op("pe", lambda e: e.matmul(psb[bO][:, 0:129], lhsT=qT[rows, csl], rhs=Cp[cq][rows, hh * 129:(hh + 1) * 129], start=False, stop=True),
                                     reads=[("pair", "qk", 0, c // 4), ("Cp", cq)], writes=[P(bO)])
                                s.op("pe", lambda e: e.matmul(psb[bO][:, 129:258], lhsT=PT[u["pts"][1]][:], rhs=vrhs, start=True, stop=False),
                                     reads=[("PT", u["pts"][1]), ("pair", "v", c), ("pair", "v1")], writes=[P(bO)])
                                s.op("pe", lambda e: e.matmul(psb[bO][:, 129:258], lhsT=qT[rows, csl], rhs=Cb_st[rows, c, hh * 129:(hh + 1) * 129], start=False, stop=True),
                                     reads=[("pair", "qk", 0, c // 4), ("pair", "Cb", c)], writes=[P(bO)])
                            for u in U:
                                u["m"] = rot("sm", 8)
                                m, bO, hd = u["m"], u["bO"], u["hd"]
                                c = u["c"]
                                den = psb[bO][:, 128:258:129]
                                s.op("dve", lambda e: e.scalar_tensor_tensor(out=sm[m][:, 0:2], in0=den, scalar=-1.0, in1=cl_tok[:, c, hd:16:8], op0=ALU.mult, op1=ALU.max),
                                     reads=[P(bO)] + clkeys, writes=[("sm", m)])
                                s.op("dve", lambda e: e.tensor_tensor(out=sm[m][:, 0:2], in0=sm[m][:, 0:2], in1=den, op=ALU.max), reads=[P(bO), ("sm", m)], writes=[("sm", m)])
                                s.op("dve", lambda e: e.reciprocal(out=sm[m][:, 0:2], in_=sm[m][:, 0:2]), reads=[("sm", m)], writes=[("sm", m)])
                            for u in U:
                                u["hq"] = rot("hq", 4)
                                m, bO, hq = u["m"], u["bO"], u["hq"]
                                s.op("act", lambda e: e.activation(out=hs[hq][:], in_=psb[bO][:, 0:128], func=AF.Copy, scale=sm[m][:, 0:1]),
                                     reads=[P(bO), ("sm", m)], writes=[("hs", hq)])
                                s.op("dve", lambda e: e.scalar_tensor_tensor(out=hs[hq][:], in0=psb[bO][:, 129:257], scalar=sm[m][:, 1:2], in1=hs[hq][:], op0=ALU.mult, op1=ALU.add),
                                     reads=[P(bO), ("sm", m), ("hs", hq)], writes=[("hs", hq)])
                            for u in U:
                                m, hq = u["m"], u["hq"]
                                s.op("act", lambda e: e.activation(out=hjunk[:], in_=hs[hq][:], func=AF.Square, accum_out=sm[m][:, 2:3]),
                                     reads=[("hs", hq)], writes=[("sm", m), ("hjunk",)])
                                s.op("act", lambda e: e.activation(out=sm[m][:, 3:4], in_=sm[m][:, 2:3], func=AF.Sqrt, bias=EPS, scale=1.0 / 128),
                                     reads=[("sm", m)], writes=[("sm", m)])
                            for u in U:
                                m, hq = u["m"], u["hq"]
                                s.op("dve", lambda e: e.reciprocal(out=sm[m][:, 3:4], in_=sm[m][:, 3:4]), reads=[("sm", m)], writes=[("sm", m)])
                                s.op("dve", lambda e: e.tensor_scalar(out=hn[hq][:], in0=hs[hq][:], scalar1=sm[m][:, 3:4], scalar2=None, op0=ALU.mult),
                                     reads=[("hs", hq), ("sm", m)], writes=[("hn", hq)])
                            for u in U:
                                pO = psb[u["bO"]][:].bitcast(BF16)
                                s.op("pe", lambda e: e.transpose(out=pO[:, 768:896], in_=hn[u["hq"]][:], identity=ident_b[:]),
                                     reads=[("hn", u["hq"]), ("ident_b",)], writes=[P(u["bO"])])
                            for u in U:
                                pO = psb[u["bO"]][:].bitcast(BF16)
                                hd, r = u["hd"], so[u["hh"]]
                                c, csl = u["c"], u["csl"]
                                s.op("dve", lambda e: e.scalar_tensor_tensor(out=hT[:, hd, csl], in0=pO[:, 768:896], scalar=mnorm_sb[:, hd:hd + 1],
                                                                             in1=t2k[r][:, (c % 4) * 128:(c % 4 + 1) * 128], op0=ALU.mult, op1=ALU.mult),
                                     reads=[P(u["bO"]), ("t2k", r), ("mnorm",)], writes=[("hT", hd, c // 4)])

                    accmode["wide"] = False
                    mixer_in = hT
                    mixer_key = "hT"
                    wout_name = "w_mout"

                wres = aview(0, 16384, BF16).rearrange("p (o k) -> p o k", o=8)
                mix = aview(16384, 16384, F32).rearrange("p (o t) -> p o t", o=8)
                s.alias(["wres", "mix"], ["xnT"])
                for oc in range(8):
                    s.op("sp", lambda e, oc=oc: e.dma_start(out=wres[:, oc, :], in_=wb[wout_name][oc]),
                         reads=[("wb", wout_name, oc)], writes=[("wres", oc)], dma=("wres", oc))
                for blk in range(NBLK):
                    cols = slice(blk * 512, (blk + 1) * 512)
                    out_proj_block(lambda oc, kc: wres[:, oc, kc * 128:(kc + 1) * 128], lambda oc: ("wres", oc), 8,
                                   lambda kc, cols=cols: mixer_in[:, kc, cols], lambda kc, blk=blk: (mixer_key, kc, blk),
                                   lambda oc: mix[:, oc, :], "mix", 0, 512, gcol(layer, 1), STAT)
                    r = rot("t2k", 6)
                    s.op("act", lambda e, r=r: e.activation(out=t2k[r][:], in_=psb[STAT][:], func=AF.Sqrt, bias=EPS, scale=1.0 / D),
                         reads=[P(STAT)], writes=[("t2k", r)])
                    s.op("dve", lambda e, r=r: e.reciprocal(out=t2k[r][:], in_=t2k[r][:]), reads=[("t2k", r)], writes=[("t2k", r)])
                    for oc in range(8):
                        t = rot("t2k", 6)
                        while t == r:
                            t = rot("t2k", 6)
                        gc = gcol(layer, 1)
                        s.op("dve", lambda e, oc=oc, t=t, r=r, gc=gc: e.scalar_tensor_tensor(out=t2k[t][:], in0=mix[:, oc, :], scalar=norms_sb[:, gc + oc:gc + oc + 1],
                                                                                       in1=t2k[r][:], op0=ALU.mult, op1=ALU.mult),
                             reads=[("mix", oc, 0), ("t2k", r), ("norms",)], writes=[("t2k", t)])
                        s.op("dve", lambda e, oc=oc, t=t, cols=cols: e.tensor_tensor(out=xT[:, oc, cols], in0=xT[:, oc, cols], in1=t2k[t][:], op=ALU.add),
                             reads=[("t2k", t), xkey(oc, blk)], writes=[xkey(oc, blk)])

                xn2 = aview(0, 16384, BF16).rearrange("p (c t) -> p c t", c=8)
                ybuf = aview(0, 32768, F32).rearrange("p (o t) -> p o t", o=8)
                act = aview(32800, 45056, BF16).rearrange("p (k t) -> p k t", k=KF)
                for half in range(2):
                    s.alias(["xn2", "act"], ["wres", "mix", "zT", "hT", "y", "xn2", "act", "xnT", "c_sb", "u_sb", "gt", "pair"])
                    for sub in range(2):
                        blk = half * 2 + sub
                        rms_T(lambda c, blk=blk: xT[:, c, blk * 512:(blk + 1) * 512], lambda c, blk=blk: xkey(c, blk), 512, gcol(layer, 2),
                              lambda c, sub=sub: xn2[:, c, sub * 512:(sub + 1) * 512], lambda c, sub=sub: ("xn2", c, sub))
                    for j in range(KF):
                        if seg == 0 and (half * KF + j) % 3 == 0:
                            flush_cast(1)
                        wg_, kwg = load_chunk("w_f1", layer * 44 + 2 * j)
                        wu_, kwu = load_chunk("w_f1", layer * 44 + 2 * j + 1)
                        for sub in range(2):
                            cols = slice(sub * 512, (sub + 1) * 512)
                            b = acc_bank()
                            for kc in range(8):
                                s.op("pe", lambda e, kc=kc, b=b, cols=cols, wg_=wg_: e.matmul(psb[b][:], lhsT=wg_[:, kc * 128:(kc + 1) * 128], rhs=xn2[:, kc, cols],
                                                                                           start=(kc == 0), stop=(kc == 7)),
                                     reads=[kwg, ("xn2", kc, sub)], writes=[P(b)])
                            r = rot("t2k", 6)
                            s.op("act", lambda e, b=b, r=r: e.activation(out=t2k[r][:], in_=psb[b][:], func=AF.Silu), reads=[P(b)], writes=[("t2k", r)])
                            b2 = acc_bank()
                            for kc in range(8):
                                s.op("pe", lambda e, kc=kc, b2=b2, cols=cols, wu_=wu_: e.matmul(psb[b2][:], lhsT=wu_[:, kc * 128:(kc + 1) * 128], rhs=xn2[:, kc, cols],
                                                                                             start=(kc == 0), stop=(kc == 7)),
                                     reads=[kwu, ("xn2", kc, sub)], writes=[P(b2)])
                            s.op("dve", lambda e, b2=b2, r=r, j=j, cols=cols: e.tensor_tensor(out=act[:, j, cols], in0=psb[b2][:], in1=t2k[r][:], op=ALU.mult),
                                 reads=[P(b2), ("t2k", r)], writes=[("act", j, sub)])
                    s.alias(["y"], ["xn2"])
                    for oc in range(8):
                        w2, kw2 = load_big("w_f2", layer * 8 + oc, DFF)
                        for sub in range(2):
                            cols = slice(sub * 512, (sub + 1) * 512)
                            statbank = STAT if sub == 0 else MISC
                            b = acc_bank()
                            for kc in range(KF):
                                s.op("pe", lambda e, kc=kc, b=b, cols=cols, w2=w2: e.matmul(psb[b][:], lhsT=w2[:, kc * 128:(kc + 1) * 128], rhs=act[:, kc, cols],
                                                                                         start=(kc == 0), stop=(kc == KF - 1)),
                                     reads=[kw2, ("act", kc, sub)], writes=[P(b)])
                            s.op("act", lambda e, b=b, oc=oc, cols=cols: e.copy(out=ybuf[:, oc, cols], in_=psb[b][:]), reads=[P(b)], writes=[("y", oc, sub)])
                            q = rot("sqb", 2)
                            s.op("act", lambda e, q=q, b=b: e.activation(out=sqb[q][:], in_=psb[b][:], func=AF.Square),
                                 reads=[P(b)], writes=[("sqb", q)])
                            s.op("pe", lambda e, oc=oc, q=q, statbank=statbank: e.matmul(psb[statbank][:], lhsT=ones_b[:], rhs=sqb[q][:], start=(oc == 0), stop=(oc == 7)),
                                 reads=[("sqb", q), ("ones_b",)], writes=[P(statbank)])
                    for sub in range(2):
                        blk = half * 2 + sub
                        cols = slice(sub * 512, (sub + 1) * 512)
                        xcols = slice(blk * 512, (blk + 1) * 512)
                        statbank = STAT if sub == 0 else MISC
                        r = rot("t2k", 6)
                        s.op("act", lambda e, r=r, statbank=statbank: e.activation(out=t2k[r][:], in_=psb[statbank][:], func=AF.Sqrt, bias=EPS, scale=1.0 / D),
                             reads=[P(statbank)], writes=[("t2k", r)])
                        s.op("dve", lambda e, r=r: e.reciprocal(out=t2k[r][:], in_=t2k[r][:]), reads=[("t2k", r)], writes=[("t2k", r)])
                        gc = gcol(layer, 3)
                        for oc in range(8):
                            t = rot("t2k", 6)
                            while t == r:
                                t = rot("t2k", 6)
                            s.op("dve", lambda e, oc=oc, t=t, r=r, gc=gc, cols=cols: e.scalar_tensor_tensor(out=t2k[t][:], in0=ybuf[:, oc, cols], scalar=norms_sb[:, gc + oc:gc + oc + 1],
                                                                                                    in1=t2k[r][:], op0=ALU.mult, op1=ALU.mult),
                                 reads=[("y", oc, sub), ("t2k", r), ("norms",)], writes=[("t2k", t)])
                            s.op("dve", lambda e, oc=oc, t=t, xcols=xcols: e.tensor_tensor(out=xT[:, oc, xcols], in0=xT[:, oc, xcols], in1=t2k[t][:], op=ALU.add),
                                 reads=[("t2k", t), xkey(oc, blk)], writes=[xkey(oc, blk)])
                    if layer == n_layers - 1:
                        store_blocks(seg, [half * 2, half * 2 + 1])
                        if seg + 1 < n_seg:
                            load_blocks(seg + 1, [half * 2, half * 2 + 1])
                            if half == 1:
                                load_halo(seg + 1)

            if n_layers == 0:
                store_blocks(seg, range(NBLK))
                if seg + 1 < n_seg:
                    load_blocks(seg + 1, range(NBLK))
                    load_halo(seg + 1)
        s.emit(st)
        import os
        if os.environ.get("KDEBUG"):
            print("SCHED", s.stats)
    return nc


def _chunks(W, col_lists):
    K = W.shape[0]
    kcn = K // 128
    out = []
    for cols in col_lists:
        sub = W[:, cols]
        w = sub.shape[1]
        out.append(sub.reshape(kcn, 128, w).transpose(1, 0, 2).reshape(128, kcn * w))
    return np.ascontiguousarray(np.stack(out, 0), dtype=np.float32)


def prep_weights(inp):
    r = lambda a, b: list(range(a, b))
    cw_in = inp["conv_w_in"][0]
    cin_lists = []
    for cc in range(8):
        cin_lists += [r(1024 + cc * 128, 1024 + (cc + 1) * 128), r(2048 + cc * 128, 2048 + (cc + 1) * 128), r(cc * 128, (cc + 1) * 128)]
    w = {}
    w["w_cin"] = _chunks(cw_in, cin_lists)
    w["w_cout"] = _chunks(inp["conv_w_out"][0], [r(o * 128, (o + 1) * 128) for o in range(8)])
    mw = inp["mlstm_w_in"][0]
    qk_lists = []
    for g in range(4):
        qk_lists += [r(g * 128, (g + 1) * 128), r(512 + g * 128, 512 + (g + 1) * 128)]
    w["w_qk"] = _chunks(mw, qk_lists)
    w["w_o"] = _chunks(mw, [r(2048 + h * 128, 2048 + (h + 1) * 128) for h in range(8)])
    w["w_v"] = _chunks(mw, [r(1024 + g * 256, 1024 + (g + 1) * 256) for g in range(4)])
    gcols = mw[:, 3072:3104]
    gi = np.zeros((1024, 40), np.float32)
    gf = np.zeros((1024, 40), np.float32)
    gi[:, 0:8] = gcols[:, 0:8]
    gf[:, 0:8] = gcols[:, 8:16]
    gi[:, 32:40] = gcols[:, 16:24]
    gf[:, 32:40] = gcols[:, 24:32]
    wgi = _chunks(gi, [r(0, 40)])[0]
    wgf = _chunks(gf, [r(0, 40)])[0]
    w["w_g"] = np.ascontiguousarray(np.concatenate([wgi, wgf], axis=1)[None], dtype=np.float32)
    w["w_mout"] = _chunks(inp["mlstm_w_out"][0], [r(o * 128, (o + 1) * 128) for o in range(8)])
    f1 = []
    for l in range(2):
        lists = []
        for j in range(KF):
            lists += [r(j * 128, (j + 1) * 128), r(DFF + j * 128, DFF + (j + 1) * 128)]
        f1.append(_chunks(inp["ffn_w_in"][l], lists))
    w["w_f1"] = np.ascontiguousarray(np.concatenate(f1, 0))
    f2 = [_chunks(inp["ffn_w_out"][l], [r(o * 128, (o + 1) * 128) for o in range(8)]) for l in range(2)]
    w["w_f2"] = np.ascontiguousarray(np.concatenate(f2, 0))
    nr = inp["norms"].reshape(8, 8, 128)
    w["norms_t"] = np.ascontiguousarray(nr.transpose(2, 0, 1).reshape(128, 64), dtype=np.float32)
    cw = inp["conv_w"][0].reshape(3, 8, 128)
    w["convw_t"] = np.ascontiguousarray(cw.transpose(2, 1, 0).reshape(128, 24), dtype=np.float32)
    w["mnorm_t"] = np.ascontiguousarray(inp["mlstm_norm"][0].reshape(8, 128).T, dtype=np.float32)
    bg = inp["mlstm_b_gate"][0]
    bt = np.zeros((40, 2), np.float32)
    bt[0:8, 0] = bg[0:8]
    bt[0:8, 1] = bg[8:16]
    bt[32:40, 0] = bg[16:24]
    bt[32:40, 1] = bg[24:32]
    w["bgate_t"] = bt
    return w


def kernel(x_prompt, x_sample, norms, conv_w_in, conv_w, conv_w_out, mlstm_w_in, mlstm_b_gate,
           mlstm_norm, mlstm_w_out, ffn_w_in, ffn_w_out, _n_layers=2, _do_mlstm=True, _n_seg=NSEG):
    inp = dict(norms=np.asarray(norms, np.float32), conv_w_in=np.asarray(conv_w_in, np.float32),
               conv_w=np.asarray(conv_w, np.float32), conv_w_out=np.asarray(conv_w_out, np.float32),
               mlstm_w_in=np.asarray(mlstm_w_in, np.float32), mlstm_b_gate=np.asarray(mlstm_b_gate, np.float32),
               mlstm_norm=np.asarray(mlstm_norm, np.float32), mlstm_w_out=np.asarray(mlstm_w_out, np.float32),
               ffn_w_in=np.asarray(ffn_w_in, np.float32), ffn_w_out=np.asarray(ffn_w_out, np.float32))
    xp = np.asarray(x_prompt, np.float32)
    xs = np.asarray(x_sample, np.float32)
    w = prep_weights(inp)
    in_maps = []
    for r in range(NCORES):
        b, qd = r // 4, r % 4
        xin = np.empty((NSEG, T, D), np.float32)
        halo = np.zeros((NSEG, 2, D), np.float32)
        xin[0] = xp[b, qd * T:(qd + 1) * T]
        if qd > 0:
            halo[0, 0] = xp[b, qd * T - 1]
        if qd < 3:
            halo[0, 1] = xp[b, (qd + 1) * T]
        xin[1] = xs[2 * r]
        xin[2] = xs[2 * r + 1]
        m = dict(w)
        flp = np.zeros((128, 4, 2), np.float32)
        fls = np.zeros((40, 4), np.float32)
        for k in range(4):
            ff = 1.0 if k < qd else 0.0
            fb = 1.0 if (3 - k) > qd else 0.0
            flp[:, k, 0] = ff
            flp[:, k, 1] = fb
            fls[0:8, k] = ff
            fls[32:40, k] = fb
        m["flp"] = flp.reshape(128, 8)
        m["fls"] = fls
        m["xin"] = xin
        m["halo"] = halo
        in_maps.append(m)
    nc = build_program(n_layers=_n_layers, do_mlstm=_do_mlstm, n_seg=_n_seg)
    res = run_bass_kernel_spmd(nc, in_maps, core_ids=list(range(NCORES)))
    y_prompt = np.empty_like(xp)
    y_sample = np.empty_like(xs)
    for r in range(NCORES):
        y = res.results[r]["yout"]
        b, qd = r // 4, r % 4
        y_prompt[b, qd * T:(qd + 1) * T] = y[0]
        y_sample[2 * r] = y[1]
        y_sample[2 * r + 1] = y[2]
    return (y_prompt, y_sample)
```
